# Optimizing a Trainium2 kernel written in Bass

```python
import jax
import jax.numpy as jnp
from jax import lax
import numpy as np

D_MODEL = 1024
BATCH = 2
SEQ = 8192
DEPTH = 2

GRID_W = 64
CTX_LEN = 256
HEAD_DIM = 64
BRANCH_W = 256
N_BRANCH = 4
A_Q_HEADS = 4
A_KV_HEADS = 2
D_Q_HEADS = 4
D_KV_HEADS = 2
FNET_GROUPS = 4
SGU_GROUPS = 4
GROUP_W = 64
CHUNK = 128
Q_BLOCK = 128
WINDOW = 128
ROPE_BASE = 10000.0
EPS = 1e-6
KV_W = 128
IN_SIZES = (BRANCH_W, KV_W, KV_W, BRANCH_W,
            BRANCH_W, KV_W, KV_W, BRANCH_W,
            BRANCH_W, BRANCH_W,
            BRANCH_W, BRANCH_W, BRANCH_W)
IN_W = 9 * BRANCH_W + 4 * KV_W

kernel_name = "hybrid_prefix_dit_block"


def _rms(x, w):
    xf = x.astype(jnp.float32)
    y = xf * lax.rsqrt(jnp.mean(xf * xf, axis=-1, keepdims=True) + EPS)
    return (y * w.astype(jnp.float32)).astype(x.dtype)


def _split_in(p):
    idx, acc = [], 0
    for s in IN_SIZES[:-1]:
        acc += s
        idx.append(acc)
    return jnp.split(p, idx, axis=-1)


def _modulation(cvec, w_ada, b_ada):
    m = jax.nn.silu(cvec) @ w_ada + b_ada
    return jnp.split(m, 3, axis=-1)


def _axial_rope_tables(rows):
    t = jnp.arange(rows * GRID_W, dtype=jnp.int32)
    r = (t // GRID_W).astype(jnp.float32)
    col = (t % GRID_W).astype(jnp.float32)
    nf = HEAD_DIM // 4
    inv = ROPE_BASE ** (-jnp.arange(nf, dtype=jnp.float32) / nf)
    ar = r[:, None] * inv[None, :]
    ac = col[:, None] * inv[None, :]
    return (jnp.cos(ar), jnp.sin(ar), jnp.cos(ac), jnp.sin(ac))


def _rot_half(x, cos, sin):
    nf = cos.shape[-1]
    x1, x2 = x[..., :nf], x[..., nf:]
    cos = cos[None, :, None, :]
    sin = sin[None, :, None, :]
    return jnp.concatenate([x1 * cos - x2 * sin, x1 * sin + x2 * cos], axis=-1)


def _axial_rope(x, tabs):
    cr, sr, cc, sc = tabs
    xf = x.astype(jnp.float32)
    half = HEAD_DIM // 2
    y = jnp.concatenate([_rot_half(xf[..., :half], cr, sr),
                         _rot_half(xf[..., half:], cc, sc)], axis=-1)
    return y.astype(x.dtype)


def _heads(t, n):
    return t.reshape(t.shape[0], t.shape[1], n, HEAD_DIM)


def _flat(o):
    return o.reshape(o.shape[0], o.shape[1], -1)


def _q_groups(q, qn, n_q, n_kv, tabs):
    b, n = q.shape[:2]
    q = _rms(_heads(q, n_q), qn)
    if tabs is not None:
        q = _axial_rope(q, tabs)
    return (q * HEAD_DIM ** -0.5).reshape(b, n, n_kv, n_q // n_kv, HEAD_DIM)


def _k_heads(k, kn, n_kv, tabs):
    k = _rms(_heads(k, n_kv), kn)
    if tabs is not None:
        k = _axial_rope(k, tabs)
    return k


def _attend(q, k, v, sink):
    s = jnp.einsum('bqkgd,bnkd->bkgqn', q, k, preferred_element_type=jnp.float32)
    if sink is not None:
        s_sink = jnp.broadcast_to(sink.astype(jnp.float32)[None, :, :, None, None], s.shape[:-1] + (1,))
        p = jax.nn.softmax(jnp.concatenate([s, s_sink], axis=-1), axis=-1)[..., :-1]
    else:
        p = jax.nn.softmax(s, axis=-1)
    return jnp.einsum('bkgqn,bnkd->bqkgd', p.astype(v.dtype), v)


def _attend_blocked(q, k, v):
    b, s = q.shape[:2]
    nb = s // Q_BLOCK
    qb = jnp.moveaxis(q.reshape(b, nb, Q_BLOCK, *q.shape[2:]), 1, 0)
    ob = lax.map(lambda qq: _attend(qq, k, v, None), qb)
    return jnp.moveaxis(ob, 0, 1).reshape(b, s, -1)


def _attend_window_sink(q, k, v, kc, vc, sink):
    b, s = q.shape[:2]
    nb = s // Q_BLOCK
    nc = kc.shape[1]
    pad = ((0, 0), (Q_BLOCK, Q_BLOCK), (0, 0), (0, 0))

    def band(t):
        tb = jnp.pad(t, pad).reshape(b, nb + 2, Q_BLOCK, *t.shape[2:])
        bt = jnp.concatenate([tb[:, :-2], tb[:, 1:-1], tb[:, 2:]], axis=2)
        return jnp.moveaxis(bt, 1, 0)

    kb, vb = band(k), band(v)
    qb = jnp.moveaxis(q.reshape(b, nb, Q_BLOCK, *q.shape[2:]), 1, 0)
    a_idx = jnp.arange(Q_BLOCK)[:, None]
    j_idx = jnp.arange(3 * Q_BLOCK)[None, :]
    rel = j_idx - Q_BLOCK - a_idx
    sink_f = sink.astype(jnp.float32)

    def one(args):
        qq, kk, vv, i = args
        kpos = (i - 1) * Q_BLOCK + j_idx
        mask = (jnp.abs(rel) <= WINDOW) & (kpos >= 0) & (kpos < s)
        s_loc = jnp.einsum('bqkgd,bjkd->bkgqj', qq, kk, preferred_element_type=jnp.float32)
        s_loc = jnp.where(mask, s_loc, -jnp.inf)
        s_ctx = jnp.einsum('bqkgd,bnkd->bkgqn', qq, kc, preferred_element_type=jnp.float32)
        s_sink = jnp.broadcast_to(sink_f[None, :, :, None, None], s_loc.shape[:-1] + (1,))
        p = jax.nn.softmax(jnp.concatenate([s_ctx, s_loc, s_sink], axis=-1), axis=-1).astype(vv.dtype)
        return (jnp.einsum('bkgqn,bnkd->bqkgd', p[..., :nc], vc)
                + jnp.einsum('bkgqj,bjkd->bqkgd', p[..., nc:nc + 3 * Q_BLOCK], vv))

    ob = lax.map(one, (qb, kb, vb, jnp.arange(nb)))
    return jnp.moveaxis(ob, 0, 1).reshape(b, s, -1)


def _fourier_mix(f, w_fnet):
    b, n = f.shape[:2]
    fg = f.astype(jnp.float32).reshape(b, n, FNET_GROUPS, GROUP_W)
    y = jnp.fft.fft2(fg, axes=(1, 3), norm='ortho').real.astype(f.dtype)
    return jnp.einsum('bngc,gcd->bngd', y, w_fnet).reshape(b, n, BRANCH_W)


def _spatial_gate(u, v, w_sp, b_sp):
    b, n = u.shape[:2]
    nch = n // CHUNK
    u = jax.nn.gelu(u, approximate=False).reshape(b, nch, CHUNK, SGU_GROUPS, GROUP_W)
    v = jax.nn.gelu(v, approximate=False).reshape(b, nch, CHUNK, SGU_GROUPS, GROUP_W)
    sp = jnp.einsum('gpq,bmqgc->bmpgc', w_sp, v) + b_sp.T[None, None, :, :, None]
    return (u * sp).reshape(b, n, BRANCH_W)


def _merge(h, outs, gates, w_br, w_merge, b_merge, w_out):
    b, n = h.shape[:2]
    yb = jnp.stack([o * jax.nn.silu(z) for o, z in zip(outs, gates)], axis=2)
    y = jnp.einsum('bnrw,rwd->bnrd', yb, w_br)
    g = jax.nn.sigmoid(h @ w_merge + b_merge).reshape(b, n, N_BRANCH, D_MODEL)
    return jnp.sum(g * y, axis=2) @ w_out


def _layer(x, xc, c, c_ctx, tabs, norm_w, w_ada, b_ada, w_in, qn_a, kn_a, qn_d, kn_d,
           sink_d, w_fnet, w_sp, b_sp, w_br, w_merge, b_merge, w_out, need_ctx_out):
    sh, sc, gt = _modulation(c, w_ada, b_ada)
    sh_c, sc_c, gt_c = _modulation(c_ctx, w_ada, b_ada)
    h = _rms(x, norm_w) * (1.0 + sc[:, None, :]) + sh[:, None, :]
    hc = _rms(xc, norm_w) * (1.0 + sc_c) + sh_c
    aq, ak, av, az, dq, dk, dv, dz, bf, bz, cu, cv, cz = _split_in(h @ w_in)
    aqc, akc, avc, azc, dqc, dkc, dvc, dzc, bfc, bzc, cuc, cvc, czc = _split_in(hc @ w_in)
    sink = sink_d.reshape(D_KV_HEADS, D_Q_HEADS // D_KV_HEADS)

    ka_c = _k_heads(akc, kn_a, A_KV_HEADS, None)
    va_c = _heads(avc, A_KV_HEADS)
    kd_c = _k_heads(dkc, kn_d, D_KV_HEADS, None)
    vd_c = _heads(dvc, D_KV_HEADS)

    qa = _q_groups(aq, qn_a, A_Q_HEADS, A_KV_HEADS, tabs)
    ka = jnp.concatenate([ka_c, _k_heads(ak, kn_a, A_KV_HEADS, tabs)], axis=1)
    va = jnp.concatenate([va_c, _heads(av, A_KV_HEADS)], axis=1)
    o_a = _attend_blocked(qa, ka, va)
    qd = _q_groups(dq, qn_d, D_Q_HEADS, D_KV_HEADS, tabs)
    o_d = _attend_window_sink(qd, _k_heads(dk, kn_d, D_KV_HEADS, tabs), _heads(dv, D_KV_HEADS),
                              kd_c, vd_c, sink)
    o_b = _fourier_mix(bf, w_fnet)
    o_c = _spatial_gate(cu, cv, w_sp, b_sp)
    y = _merge(h, (o_a, o_d, o_b, o_c), (az, dz, bz, cz), w_br, w_merge, b_merge, w_out)
    x = x + gt[:, None, :] * y

    if need_ctx_out:
        qa_c = _q_groups(aqc, qn_a, A_Q_HEADS, A_KV_HEADS, None)
        o_ac = _flat(_attend(qa_c, ka_c, va_c, None))
        qd_c = _q_groups(dqc, qn_d, D_Q_HEADS, D_KV_HEADS, None)
        o_dc = _flat(_attend(qd_c, kd_c, vd_c, sink))
        o_bc = _fourier_mix(bfc, w_fnet)
        o_cc = _spatial_gate(cuc, cvc, w_sp, b_sp)
        yc = _merge(hc, (o_ac, o_dc, o_bc, o_cc), (azc, dzc, bzc, czc), w_br, w_merge, b_merge, w_out)
        xc = xc + gt_c * yc
    return x, xc


def setup_inputs(seed: int = 0) -> dict:
    key = jax.random.key(seed)
    ks = jax.random.split(key, 24)
    f32 = jnp.float32
    nrm = lambda k, shape, s: jax.random.normal(k, shape, f32) * s
    return {
        'x': nrm(ks[0], (BATCH, SEQ, D_MODEL), 1.0),
        'c': nrm(ks[1], (BATCH, D_MODEL), 1.0),
        'ctx': nrm(ks[2], (BATCH, CTX_LEN, D_MODEL), 1.0),
        'c_ctx': nrm(ks[3], (D_MODEL,), 1.0),
        'norm_w': 1.0 + nrm(ks[4], (DEPTH, D_MODEL), 0.02),
        'w_ada': nrm(ks[5], (DEPTH, D_MODEL, 3 * D_MODEL), 0.5 * D_MODEL ** -0.5),
        'b_ada': nrm(ks[6], (DEPTH, 3 * D_MODEL), 0.02),
        'w_in': nrm(ks[7], (DEPTH, D_MODEL, IN_W), D_MODEL ** -0.5),
        'qn_a': 1.0 + nrm(ks[8], (DEPTH, HEAD_DIM), 0.02),
        'kn_a': 1.0 + nrm(ks[9], (DEPTH, HEAD_DIM), 0.02),
        'qn_d': 1.0 + nrm(ks[10], (DEPTH, HEAD_DIM), 0.02),
        'kn_d': 1.0 + nrm(ks[11], (DEPTH, HEAD_DIM), 0.02),
        'sink_d': nrm(ks[12], (DEPTH, D_Q_HEADS), 0.5),
        'w_fnet': nrm(ks[13], (DEPTH, FNET_GROUPS, GROUP_W, GROUP_W), GROUP_W ** -0.5),
        'w_sp': nrm(ks[14], (DEPTH, SGU_GROUPS, CHUNK, CHUNK), CHUNK ** -0.5),
        'b_sp': 1.0 + nrm(ks[15], (DEPTH, SGU_GROUPS, CHUNK), 0.02),
        'w_br': nrm(ks[16], (DEPTH, N_BRANCH, BRANCH_W, D_MODEL), BRANCH_W ** -0.5),
        'w_merge': nrm(ks[17], (DEPTH, D_MODEL, N_BRANCH * D_MODEL), D_MODEL ** -0.5),
        'b_merge': nrm(ks[18], (DEPTH, N_BRANCH * D_MODEL), 0.02),
        'w_out': nrm(ks[19], (DEPTH, D_MODEL, D_MODEL), D_MODEL ** -0.5),
    }


def reference(x, c, ctx, c_ctx, norm_w, w_ada, b_ada, w_in, qn_a, kn_a, qn_d, kn_d, sink_d,
              w_fnet, w_sp, b_sp, w_br, w_merge, b_merge, w_out):
    rows = x.shape[1] // GRID_W
    tabs = _axial_rope_tables(rows)
    xc = ctx
    for l in range(DEPTH):
        x, xc = _layer(x, xc, c, c_ctx, tabs, norm_w[l], w_ada[l], b_ada[l], w_in[l],
                       qn_a[l], kn_a[l], qn_d[l], kn_d[l], sink_d[l], w_fnet[l], w_sp[l],
                       b_sp[l], w_br[l], w_merge[l], b_merge[l], w_out[l],
                       need_ctx_out=(l < DEPTH - 1))
    return x
```

```python
import numpy as np
import ml_dtypes
import concourse.bass as bass
import concourse.mybir as mybir
from concourse.bass_utils import run_bass_kernel_spmd

F32 = mybir.dt.float32
BF = mybir.dt.bfloat16
AF = mybir.ActivationFunctionType
ALU = mybir.AluOpType
AX = mybir.AxisListType
NPBF = ml_dtypes.bfloat16

NCORES = 8
TOK = 2048
NT = 16
CT = 2
TT = NT + CT
TALL = TT * 128
D = 1024
KC = 8
EPS = 1e-6
SEQ = 8192
INW = 2816
GROUPS = [(0, 512), (512, 512), (1024, 512), (1536, 512), (2048, 256)]


class Buf:
    __slots__ = ("name", "w", "rs")

    def __init__(self, name):
        self.name = name
        self.w = None
        self.rs = []


class Op:
    __slots__ = ("eng", "fn", "deps", "kind", "sem", "semval", "signal")

    def __init__(self, eng, fn, kind):
        self.eng = eng
        self.fn = fn
        self.kind = kind
        self.deps = []
        self.sem = None
        self.semval = 0
        self.signal = False


class Prog:
    ENGS = ("pe", "act", "dve", "pool", "sp")
    NS = 24

    def __init__(self, nc):
        self.nc = nc
        self.ops = {e: [] for e in self.ENGS}
        self.dma_count = 0
        self.dma_last = [None] * self.NS
        self.ncoll = 0
        self.bufs = {}
        self.pending = {e: [] for e in self.ENGS}
        self.last_c = {e: None for e in self.ENGS}
        self.colls = []

    def buf(self, name):
        b = self.bufs.get(name)
        if b is None:
            b = self.bufs[name] = Buf(name)
        return b

    def _add(self, op, d):
        if d is op:
            return
        op.deps.append(d)
        if d.kind == "c":
            d.signal = True

    def _deps(self, op, reads, writes):
        deps = []
        for b in reads:
            b = self.buf(b)
            if b.w is not None:
                deps.append(("raw", b.w))
            b.rs.append(op)
        for b in writes:
            b = self.buf(b)
            if b.w is not None:
                deps.append(("waw", b.w))
            for r in b.rs:
                if r is not op:
                    deps.append(("war", r))
            b.rs = []
            b.w = op
        for kind, d in deps:
            if d is op:
                continue
            if d.kind == "c" and op.kind == "c" and d.eng == op.eng:
                if op.eng == "pe" or kind == "war":
                    continue
            self._add(op, d)
        if self.pending[op.eng]:
            for d in self.pending[op.eng]:
                self._add(op, d)
            self.pending[op.eng] = []

    def op(self, eng, meth, *args, R=(), W=(), **kw):
        fn = (lambda e, meth=meth, args=args, kw=kw: getattr(e, meth)(*args, **kw))
        o = Op(eng, fn, "c")
        self._deps(o, R, W)
        self.ops[eng].append(o)
        self.last_c[eng] = o
        return o

    def dma(self, q, out, in_, reads=(), writes=(), **kw):
        o = Op(q, (lambda e, out=out, in_=in_, kw=kw: e.dma_start(out=out, in_=in_, **kw)), "d")
        self._deps(o, reads, writes)
        slot = self.dma_count % self.NS
        self.dma_count += 1
        prev = self.dma_last[slot]
        o.sem = slot
        o.semval = (prev.semval if prev else 0) + 16
        if prev is not None:
            o.deps.append(prev)
        self.dma_last[slot] = o
        self.ops[q].append(o)
        return o

    def collective(self, fn, reads=(), writes=()):
        o = Op("pool", fn, "x")
        self._deps(o, reads, writes)
        o.sem = self.ncoll
        self.ncoll += 1
        self.ops["pool"].append(o)
        self.colls.append(o)
        return o

    def barrier(self):
        lasts = [o for o in self.last_c.values() if o is not None]
        lasts += [o for o in self.dma_last if o is not None]
        lasts += list(self.colls)
        for e in self.ENGS:
            self.pending[e] = list(lasts)

    def emit(self):
        nc = self.nc
        for e in self.ENGS:
            cnt = 0
            for o in self.ops[e]:
                if o.kind == "c" and o.signal:
                    cnt += 1
                    o.semval = cnt
        esem = {e: nc.alloc_semaphore("es_" + e) for e in self.ENGS}
        dsem = [nc.alloc_semaphore("ds_%d" % i) for i in range(self.NS)]
        csem = [nc.alloc_semaphore("cs_%d" % i) for i in range(self.ncoll)]

        def semof(d):
            if d.kind == "c":
                return esem[d.eng], d.semval
            if d.kind == "d":
                return dsem[d.sem], d.semval
            return csem[d.sem], 1

        def gen(ename):
            def body(eng):
                waited = {}
                for o in self.ops[ename]:
                    for d in o.deps:
                        s, v = semof(d)
                        if waited.get(id(s), 0) >= v:
                            continue
                        waited[id(s)] = v
                        eng.wait_ge(s, v)
                    ins = o.fn(eng)
                    if o.kind == "c":
                        if o.signal:
                            ins.then_inc(esem[ename], 1)
                    elif o.kind == "d":
                        ins.then_inc(dsem[o.sem], 16)
                    else:
                        ins.then_inc(csem[o.sem])
                last = {}
                for o in self.ops[ename]:
                    if o.kind in ("d", "x"):
                        s, v = semof(o)
                        if v > last.get(id(s), (None, 0))[1]:
                            last[id(s)] = (s, v)
                for s, v in last.values():
                    if waited.get(id(s), 0) < v:
                        eng.wait_ge(s, v)
            return body

        with nc.Block() as block:
            block.sync(gen("sp"))
            block.scalar(gen("act"))
            block.vector(gen("dve"))
            block.tensor(gen("pe"))
            block.gpsimd(gen("pool"))


def host_consts(core):
    b, q = divmod(core, 4)
    t = (q * TOK + np.arange(TOK)).astype(np.int64)
    r = (t // 64).astype(np.float32)
    col = (t % 64).astype(np.float32)
    inv = (np.float32(10000.0) ** (-np.arange(16, dtype=np.float32) / np.float32(16))).astype(np.float32)
    ar = r[:, None] * inv[None, :]
    ac = col[:, None] * inv[None, :]
    cr, sr, cc, sc = np.cos(ar), np.sin(ar), np.cos(ac), np.sin(ac)
    cos64 = np.concatenate([cr, cr, cc, cc], 1).astype(np.float32)
    sin64 = np.concatenate([-sr, sr, -sc, sc], 1).astype(np.float32)
    cos64 = np.ascontiguousarray(cos64.reshape(NT, 128, 64).transpose(1, 0, 2))
    sin64 = np.ascontiguousarray(sin64.reshape(NT, 128, 64).transpose(1, 0, 2))
    n1 = np.arange(64)
    th = 2 * np.pi * np.outer(n1, n1) / 64.0
    C, S = np.cos(th), np.sin(th)
    sc1 = 1.0 / np.sqrt(SEQ * 64.0)
    m1 = np.zeros((128, 128), np.float64)
    m1[:64, :64] = C
    m1[64:, :64] = -S
    m1[:64, 64:] = S
    m1[64:, 64:] = C
    m1 *= sc1
    n2 = np.arange(128)[:, None, None]
    k1 = np.arange(64)[None, :, None]
    k2 = (np.arange(32) + 32 * q)[None, None, :]
    ph = 2 * np.pi * n2 * (64 * k2 + k1) / float(SEQ)
    mc = np.cos(ph).reshape(128, 64 * 32)
    ms = (-np.sin(ph)).reshape(128, 64 * 32)
    n = np.arange(256)
    t256 = 2 * np.pi * np.outer(n, n) / 256.0
    sc2 = 1.0 / np.sqrt(256 * 64.0)
    c256 = (np.cos(t256) * sc2).reshape(2, 128, 256).transpose(1, 0, 2)
    s256n = (-np.sin(t256) * sc2).reshape(2, 128, 256).transpose(1, 0, 2)
    c64d = np.concatenate([C, C], 1).astype(np.float32)
    s64d = np.concatenate([S, S], 1).astype(np.float32)
    pk = np.arange(128)[:, None]
    pq = np.arange(128)[None, :]
    mP = (pk >= pq).astype(np.float32)
    mN = (pk <= pq).astype(np.float32)
    masks = np.zeros((128, 10, 128), np.float32)
    masks[:, 0] = mP
    masks[:, 1] = mN
    for s in range(4):
        if s == q - 1:
            masks[:, 2 + s] = mP
        if s == q + 1:
            masks[:, 6 + s] = mN
    sel2 = np.zeros((2, 256), np.float32)
    sel2[0, :128] = 1.0
    sel2[1, 128:] = 1.0
    return {
        "cos64": cos64, "sin64": sin64,
        "m1": m1.astype(NPBF), "mc": mc.astype(NPBF), "ms": ms.astype(NPBF),
        "c256": np.ascontiguousarray(c256).astype(NPBF), "s256n": np.ascontiguousarray(s256n).astype(NPBF),
        "c64d": c64d, "s64d": s64d,
        "masks": masks.astype(NPBF), "sel2": sel2,
        "ident": np.eye(128, dtype=np.float32),
    }


CONST_SPECS = {
    "cos64": ([128, NT, 64], F32), "sin64": ([128, NT, 64], F32),
    "m1": ([128, 128], BF), "mc": ([128, 2048], BF), "ms": ([128, 2048], BF),
    "c256": ([128, 2, 256], BF), "s256n": ([128, 2, 256], BF),
    "c64d": ([64, 128], F32), "s64d": ([64, 128], F32),
    "masks": ([128, 10, 128], BF), "sel2": ([2, 256], F32), "ident": ([128, 128], F32),
}

WEIGHT_SPECS = {
    "norm_w": [2, D], "w_ada": [2, D, 3 * D], "b_ada": [2, 3 * D], "w_in": [2, D, INW],
    "qn_a": [2, 64], "kn_a": [2, 64], "qn_d": [2, 64], "kn_d": [2, 64], "sink_d": [2, 4],
    "w_fnet": [2, 4, 64, 64], "w_spT": [2, 128, 4, 128], "b_sp": [2, 4, 128],
    "w_br": [2, 4, 256, D], "w_merge": [2, D, 4 * D], "b_mergeT": [2, 128, 32], "w_out": [2, D, D],
}


class Builder:
    def __init__(self, layers=(0, 1), first=True, last=True, debug=(), stop=None):
        self.stop = stop
        self.layers = tuple(layers)
        self.first = first
        self.last = last
        self.debug = set(debug)
        nc = self.nc = bass.Bass("TRN2", target_bir_lowering=False)
        self.P = Prog(nc)
        self.dbg_outs = {}
        self._uid = 0

    def uid(self, p="t"):
        self._uid += 1
        return "%s%d" % (p, self._uid)

    def dram_in(self, name, shape, dt=F32):
        return self.nc.dram_tensor(name, list(shape), dt, kind="ExternalInput").ap()

    def dram_out(self, name, shape, dt=F32):
        return self.nc.dram_tensor(name, list(shape), dt, kind="ExternalOutput").ap()

    def sb(self, name, shape, dt):
        return self.nc.alloc_sbuf_tensor("s_" + name, list(shape), dt)

    def tap(self, name, view, shape, reads, dt=F32):
        if name not in self.debug:
            return
        o = self.dram_out("dbg_" + name, shape, dt)
        self.dbg_outs["dbg_" + name] = (shape, dt)
        self.P.dma("sp", o, view, reads=reads)

    def build(self):
        nc, P = self.nc, self.P
        if self.first:
            self.x_in = self.dram_in("x", [TOK, D])
            self.ctx_in = self.dram_in("ctx", [256, D])
        else:
            self.xs_in = self.dram_in("xs_in", [TALL, D])
        self.cvecT = self.dram_in("cvecT", [128, KC, 2])
        self.W = {k: self.dram_in(k, s) for k, s in WEIGHT_SPECS.items()}
        self.C = {k: self.dram_in(k, s, dt) for k, (s, dt) in CONST_SPECS.items()}
        if self.last:
            self.out = self.dram_out("out", [TOK, D])
        else:
            self.xs_out = self.dram_out("xs_out", [TALL, D])
        self.xs = nc.dram_tensor("xs", [TALL, D], F32).ap()
        self.payload = [[nc.dram_tensor("payload%d_%d" % (l, j), [256, TOK], BF) for j in range(4)] for l in range(2)]
        self.gathered = [[nc.dram_tensor("gathered%d_%d" % (l, j), [1024, TOK], BF) for j in range(4)] for l in range(2)]

        self.HT = self.sb("HT", [128, KC, TALL], BF)
        self.YB = [self.sb("YB%d" % r, [128, 2, TALL], BF) for r in range(4)]
        self.WSF = [self.sb("WSF%d" % i, [128, 2048], F32) for i in range(2)]
        self.wsf_i = 0
        self.cos64 = self.sb("cos64", [128, NT, 64], F32)
        self.sin64 = self.sb("sin64", [128, NT, 64], F32)
        self.identf = self.sb("identf", [128, 128], F32)
        self.ident = self.sb("ident", [128, 128], BF)
        self.m1 = self.sb("m1", [128, 128], BF)
        self.c256 = self.sb("c256", [128, 2, 256], BF)
        self.s256n = self.sb("s256n", [128, 2, 256], BF)
        self.c64d = self.sb("c64d", [64, 128], F32)
        self.s64d = self.sb("s64d", [64, 128], F32)
        self.masks = self.sb("masks", [128, 10, 128], BF)
        self.sel2 = self.sb("sel2", [2, 256], F32)
        self.ones64 = self.sb("ones64", [128, 64], BF)
        self.scT = self.sb("scT", [128, KC, 2], F32)
        self.gate2 = self.sb("gate2", [2, D], F32)
        self.knw = [self.sb("knw%d" % i, [128, 128], F32) for i in range(2)]
        self.qnw = [self.sb("qnw%d" % i, [128, 256], F32) for i in range(2)]
        self.esink = self.sb("esink", [128, 2], F32)
        self.KTc = [self.sb("KTc%d" % i, [128, 256], BF) for i in range(2)]
        self.Vc = [self.sb("Vc%d" % i, [128, 2, 128], BF) for i in range(2)]
        self.UVc = self.sb("UVc", [128, 2, 2, 256], BF)
        self.BA = self.sb("BA", [128, 2, 2, 128], BF)
        self.stat = self.sb("stat", [128, 64], F32)
        self.stat_i = 0
        self.PS = [nc.alloc_psum_tensor("ps%d" % i, [128, 512], F32) for i in range(6)]
        self.PST = [nc.alloc_psum_tensor("pst%d" % i, [128, 1024], BF) for i in range(2)]
        self.ps_i = 0
        self.pst_i = 0
        self.ARENA_BYTES = 88 * 1024
        self.arena = self.sb("arena", [128, self.ARENA_BYTES // 2], BF)
        self.ar_off = 0
        self.ar_gen = 0

        for k in ("cos64", "sin64", "m1", "c256", "s256n", "c64d", "s64d", "masks", "sel2"):
            P.dma("sp", getattr(self, k)[:], self.C[k], writes=[k])
        P.dma("sp", self.identf[:], self.C["ident"], writes=["identf"])
        P.op("dve", "tensor_copy", self.ident[:], self.identf[:], R=["identf"], W=["ident"])
        P.op("pool", "memset", self.ones64[:], 1.0, R=[], W=["ones64"])
        P.dma("sp", self.scT[:], self.cvecT, writes=["scT"])
        P.op("act", "activation", self.scT[:], self.scT[:], AF.Silu, R=["scT"], W=["scT"])

        for li, l in enumerate(self.layers):
            is_first = self.first and li == 0
            is_last = self.last and li == len(self.layers) - 1
            self.layer(l, is_first, is_last, x_from_input=is_first,
                       x_src_xs_in=(not self.first and li == 0))
        if self.stop is not None:
            dst = self.out if self.last else self.xs_out
            P.dma("sp", dst[0:128, :], self.identf[:, :].unsqueeze(1).broadcast_to([128, 8, 128]) if False else self.cos64[:, :, :], reads=["cos64"])
        P.emit()
        return nc

    def phase(self):
        self.P.barrier()
        self.ar_off = 0
        self.ar_gen += 1

    def ar(self, name, shape, dt):
        n = int(np.prod(shape[1:]))
        nbytes = n * (4 if dt == F32 else 2)
        nbytes = (nbytes + 63) // 64 * 64
        assert self.ar_off + nbytes <= self.ARENA_BYTES, (name, self.ar_off, nbytes)
        e0 = self.ar_off // 2
        v = self.arena[:, e0:e0 + nbytes // 2]
        if dt == F32:
            v = v.bitcast(F32)
        v = v[:, 0:n]
        self.ar_off += nbytes
        if shape[0] != 128:
            v = v[0:shape[0]]
        if len(shape) == 3:
            v = v.rearrange("p (a b) -> p a b", a=shape[1])
        elif len(shape) == 4:
            v = v.rearrange("p (a b c) -> p a b c", a=shape[1], b=shape[2])
        return v, "%s_g%d" % (name, self.ar_gen)

    def ps(self):
        i = self.ps_i % 4
        self.ps_i += 1
        return self.PS[i], "ps%d" % i

    def pst(self):
        i = self.pst_i % 2
        self.pst_i += 1
        return self.PST[i], "pst%d" % i

    def statcol(self, n=1):
        c = (self.stat_i % 16) * 4
        self.stat_i += 1
        return self.stat[:, c:c + n], "stat%d" % c

    def load_w(self, src, dst, dstbuf, shape3, q="sp"):
        P = self.P
        i = self.wsf_i % 2
        self.wsf_i += 1
        a, b = shape3
        st = self.WSF[i][:, 0:a * b].rearrange("p (a b) -> p a b", a=a)
        P.dma(q, st, src, writes=["wsf%d" % i])
        P.op("pool", "tensor_copy", dst, st, R=["wsf%d" % i], W=[dstbuf])

    def w_cols(self, name, l, c0, n):
        return self.W[name][l, :, c0:c0 + n].rearrange("(kc p) n -> p kc n", p=128)

    def proj_fm(self, wb, wbuf, c0, nchunks, tiles, consume):
        P = self.P
        for c in range(nchunks):
            for (t0, n) in GROUPS:
                if t0 // 128 >= tiles:
                    continue
                ps, pb = self.ps()
                htb = ["HT%d" % i for i in range(t0 // 128, (t0 + n) // 128)]
                for kc in range(KC):
                    P.op("pe", "matmul", ps[:, 0:n], wb[:, kc, c0 + c * 128:c0 + (c + 1) * 128], self.HT[:, kc, t0:t0 + n],
                        start=(kc == 0), stop=(kc == KC - 1), R=[wbuf] + htb, W=[pb])
                consume(ps, pb, c, t0, n)

    def proj_tm(self, wb, wbuf, c0, ncols, tile):
        P = self.P
        ps, pb = self.ps()
        for kc in range(KC):
            P.op("pe", "matmul", ps[:, 0:ncols], self.HT[:, kc, tile * 128:(tile + 1) * 128], wb[:, kc, c0:c0 + ncols],
                start=(kc == 0), stop=(kc == KC - 1), R=[wbuf, "HT%d" % tile], W=[pb])
        return ps, pb

    def headnorm_rope(self, ps, pb, c0, nh, wtile, wbuf, tile, tmp, out_bf, outbuf, permute):
        P = self.P
        n = nh * 64
        sq, t1, t2 = tmp
        sqb, t1b, t2b = [self.uid("hn") for _ in range(3)]
        P.op("act", "activation", sq[:, 0:n], ps[:, c0:c0 + n], AF.Square, R=[pb], W=[sqb])
        ssq, ssqb = self.statcol(4)
        rt, rtb = self.statcol(4)
        rr, rrb = self.statcol(4)
        P.op("dve", "tensor_reduce", ssq[:, 0:nh], sq[:, 0:n].rearrange("p (h d) -> p h d", h=nh), AX.X, ALU.add, R=[sqb], W=[ssqb])
        P.op("act", "activation", rt[:, 0:nh], ssq[:, 0:nh], AF.Sqrt, bias=64.0 * EPS, scale=1.0, R=[ssqb], W=[rtb])
        P.op("dve", "reciprocal", rr[:, 0:nh], rt[:, 0:nh], R=[rtb], W=[rrb])
        P.op("dve", "tensor_tensor", t1[:, 0:n].rearrange("p (h d) -> p h d", h=nh), ps[:, c0:c0 + n].rearrange("p (h d) -> p h d", h=nh),
            rr[:, 0:nh].unsqueeze(2).broadcast_to([128, nh, 64]), ALU.mult, R=[pb, rrb], W=[t1b])
        if permute:
            ov = out_bf[:, 0:n].rearrange("p (g kv d) -> p kv g d", g=2, kv=2)
        if tile >= NT:
            if permute:
                P.op("pool", "tensor_tensor", ov, t1[:, 0:n].rearrange("p (kv g d) -> p kv g d", kv=2, g=2),
                    wtile[:, 0:n].rearrange("p (kv g d) -> p kv g d", kv=2, g=2), ALU.mult, R=[t1b, wbuf], W=[outbuf])
            else:
                P.op("pool", "tensor_tensor", out_bf[:, 0:n], t1[:, 0:n], wtile[:, 0:n], ALU.mult, R=[t1b, wbuf], W=[outbuf])
            return
        P.op("pool", "tensor_tensor", sq[:, 0:n], t1[:, 0:n], wtile[:, 0:n], ALU.mult, R=[t1b, wbuf, sqb], W=[sqb])
        cosb = self.cos64[:, tile, :].unsqueeze(1).broadcast_to([128, nh, 64])
        sinv = self.sin64[:, tile, :].rearrange("p (t j) -> p t j", t=4)
        xv = sq[:, 0:n].rearrange("p (h t j) -> p h t j", h=nh, t=4)
        t2v = t2[:, 0:n].rearrange("p (h t j) -> p h t j", h=nh, t=4)
        P.op("dve", "tensor_tensor", t1[:, 0:n].rearrange("p (h d) -> p h d", h=nh),
                                              sq[:, 0:n].rearrange("p (h d) -> p h d", h=nh), cosb, ALU.mult, R=[sqb, "cos64", t1b], W=[t1b])
        P.op("pool", "tensor_tensor", t2v[:, :, 0::2, :], xv[:, :, 1::2, :],
                                               sinv[:, 0::2, :].unsqueeze(1).broadcast_to([128, nh, 2, 16]), ALU.mult, R=[sqb, "sin64"], W=[t2b + "a"])
        P.op("pool", "tensor_tensor", t2v[:, :, 1::2, :], xv[:, :, 0::2, :],
                                               sinv[:, 1::2, :].unsqueeze(1).broadcast_to([128, nh, 2, 16]), ALU.mult, R=[sqb, "sin64"], W=[t2b + "b"])
        if permute:
            P.op("dve", "tensor_tensor", ov, t1[:, 0:n].rearrange("p (kv g d) -> p kv g d", kv=2, g=2),
                t2[:, 0:n].rearrange("p (kv g d) -> p kv g d", kv=2, g=2), ALU.add, R=[t1b, t2b + "a", t2b + "b"], W=[outbuf])
        else:
            P.op("dve", "tensor_tensor", out_bf[:, 0:n], t1[:, 0:n], t2[:, 0:n], ALU.add, R=[t1b, t2b + "a", t2b + "b"], W=[outbuf])

    def transpose_to(self, src_bf, srcbuf, nblk, dst_fn, dstbuf_fn, eng="act"):
        P = self.P
        pt, ptb = self.pst()
        for j in range(nblk):
            P.op("pe", "transpose", pt[:, j * 128:(j + 1) * 128], src_bf[:, j * 128:(j + 1) * 128], self.ident[:], R=[srcbuf, "ident"], W=[ptb])
        for j in range(nblk):
            if eng == "act":
                P.op("act", "copy", dst_fn(j), pt[:, j * 128:(j + 1) * 128], R=[ptb], W=[dstbuf_fn(j)])
            else:
                P.op("dve", "tensor_copy", dst_fn(j), pt[:, j * 128:(j + 1) * 128], R=[ptb], W=[dstbuf_fn(j)])

    def layer(self, l, is_first, is_last, x_from_input, x_src_xs_in):
        nc, P, W = self.nc, self.P, self.W
        do_ctx_out = not is_last_layer(l)
        ntl = TT if do_ctx_out else NT
        pay, gat = self.payload[l], self.gathered[l]

        def x_tile_src(i):
            if x_from_input:
                return self.x_in[i * 128:(i + 1) * 128, :] if i < NT else self.ctx_in[(i - NT) * 128:(i - NT + 1) * 128, :]
            if x_src_xs_in:
                return self.xs_in[i * 128:(i + 1) * 128, :]
            return self.xs[i * 128:(i + 1) * 128, :]

        self.phase()
        L = "L%d" % l
        mod, modb = self.ar("mod", [2, 3 * D], F32)
        arow, arowb = self.ar("arow", [2, D], F32)
        normw2, normw2b = self.ar("normw2", [2, D], F32)
        bada2, bada2b = self.ar("bada2", [2, 3 * D], F32)
        P.dma("sp", normw2, W["norm_w"][l:l + 1, :].partition_broadcast(2), writes=[normw2b])
        P.dma("sp", bada2, W["b_ada"][l:l + 1, :].partition_broadcast(2), writes=[bada2b])
        for i, (kn, qn) in enumerate((("kn_a", "qn_a"), ("kn_d", "qn_d"))):
            for h in range(2):
                P.dma("sp", self.knw[i][:, h * 64:(h + 1) * 64], W[kn][l:l + 1, :].partition_broadcast(128), writes=["knw%d" % i])
            for h in range(4):
                P.dma("sp", self.qnw[i][:, h * 64:(h + 1) * 64], W[qn][l:l + 1, :].partition_broadcast(128), writes=["qnw%d" % i])
            P.op("act", "mul", self.knw[i][:], self.knw[i][:], 8.0, R=["knw%d" % i], W=["knw%d" % i])
        for g in range(2):
            for kv in range(2):
                P.dma("sp", self.esink[g * 64:(g + 1) * 64, kv:kv + 1],
                      W["sink_d"][l:l + 1, 2 * kv + g:2 * kv + g + 1].partition_broadcast(64), writes=["esink"])
        P.op("act", "activation", self.esink[:], self.esink[:], AF.Exp, R=["esink"], W=["esink"])

        for blk in range(6):
            ps, pb = self.ps()
            for half in range(2):
                i = self.wsf_i % 2
                self.wsf_i += 1
                st = self.WSF[i][:].rearrange("p (a b) -> p a b", a=KC)
                c0 = blk * 512 + half * 256
                P.dma("sp", st, self.w_cols("w_ada", l, c0, 256), writes=["wsf%d" % i])
                for kc in range(KC):
                    P.op("pe", "matmul", ps[0:2, half * 256:(half + 1) * 256], self.scT[:, kc, :], st[:, kc, :],
                        start=(kc == 0), stop=(kc == KC - 1), R=["scT", "wsf%d" % i], W=[pb])
            P.op("dve", "tensor_tensor", mod[0:2, blk * 512:(blk + 1) * 512], ps[0:2, :], bada2[0:2, blk * 512:(blk + 1) * 512], ALU.add, R=[pb, bada2b], W=[modb])
        self.tap(L + "mod", mod, [2, 3 * D], [modb])
        P.op("dve", "scalar_tensor_tensor", arow, mod[0:2, D:2 * D], 1.0, normw2, ALU.add, ALU.mult, R=[modb, normw2b], W=[arowb])
        P.op("act", "copy", self.gate2[:], mod[0:2, 2 * D:3 * D], R=[modb], W=["gate2"])
        a_bc, a_bcb = self.ar("a_bc", [128, 2, D], F32)
        sh_bc, sh_bcb = self.ar("sh_bc", [128, 2, D], F32)
        for r in range(2):
            for half in range(2):
                for which, (dst, dbuf, src, sbuf) in enumerate(((a_bc, a_bcb, arow, arowb), (sh_bc, sh_bcb, mod, modb))):
                    ps, pb = self.ps()
                    P.op("pe", "matmul", ps[:, :], self.sel2[0:2, r * 128:(r + 1) * 128], src[0:2, half * 512:(half + 1) * 512],
                        start=True, stop=True, R=["sel2", sbuf], W=[pb])
                    P.op("act", "copy", dst[:, r, half * 512:(half + 1) * 512], ps[:, :], R=[pb], W=[dbuf])

        XT = [self.ar("XT%d" % i, [128, D], F32) for i in range(2)]
        TMPF = [self.ar("TMPF%d" % i, [128, D], F32) for i in range(2)]
        HB = [self.ar("HB%d" % i, [128, D], BF) for i in range(2)]
        for i in range(TT):
            s = i % 2
            r = 0 if i < NT else 1
            (xt, xtb), (tf, tfb), (hb, hbb) = XT[s], TMPF[s], HB[s]
            P.dma("sp", xt, x_tile_src(i), writes=[xtb])
            ssq, ssqb = self.statcol()
            rt, rtb = self.statcol()
            rs, rsb = self.statcol()
            P.op("act", "activation", tf, xt, AF.Square, accum_out=ssq, R=[xtb], W=[tfb, ssqb])
            P.op("act", "activation", rt, ssq, AF.Sqrt, bias=EPS, scale=1.0 / D, R=[ssqb], W=[rtb])
            P.op("dve", "reciprocal", rs, rt, R=[rtb], W=[rsb])
            P.op("dve", "scalar_tensor_tensor", tf, xt, rs, a_bc[:, r, :], ALU.mult, ALU.mult, R=[xtb, rsb, a_bcb, tfb], W=[tfb])
            P.op("pool", "tensor_tensor", hb, tf, sh_bc[:, r, :], ALU.add, R=[tfb, sh_bcb], W=[hbb])
            if i == 0:
                self.tap(L + "h0", hb, [128, D], [hbb], BF)
            pt, ptb = self.pst()
            for kc in range(KC):
                P.op("pe", "transpose", pt[:, kc * 128:(kc + 1) * 128], hb[:, kc * 128:(kc + 1) * 128], self.ident[:], R=[hbb, "ident"], W=[ptb])
            P.op("act", "copy", self.HT[:, :, i * 128:(i + 1) * 128], pt[:].rearrange("p (k t) -> p k t", k=KC), R=[ptb], W=["HT%d" % i])

        if self.stop == "A1":
            return
        self.phase()
        WB, WBb = self.ar("WBkv", [128, KC, 256], BF)
        KR = [self.ar("kr%d" % j, [128, 128], BF) for j in range(2)]
        KTs = [self.ar("KTs%d" % a, [128, TOK], BF) for a in range(2)]
        Vs = [self.ar("Vs%d" % a, [128, NT, 128], BF) for a in range(2)]
        tmp = [self.ar("hnt%d" % j, [128, 256], F32)[0] for j in range(3)]
        for a, c0 in enumerate((256, 1024)):
            self.load_w(self.w_cols("w_in", l, c0, 256), WB, WBb, (KC, 256))
            for i in range(TT):
                ps, pb = self.proj_tm(WB, WBb, 0, 256, i)
                kr, krb = KR[i % 2]
                self.headnorm_rope(ps, pb, 0, 2, self.knw[a], "knw%d" % a, i, tmp, kr, krb, permute=False)
                if i < NT:
                    kts, ktsb = KTs[a]
                    vs, vsb = Vs[a]
                    self.transpose_to(kr, krb, 1, lambda j, kts=kts, i=i: kts[:, i * 128:(i + 1) * 128], lambda j, ktsb=ktsb: ktsb)
                    P.op("act", "copy", vs[:, i, :], ps[:, 128:256], R=[pb], W=[vsb])
                else:
                    j = i - NT
                    self.transpose_to(kr, krb, 1, lambda jj, a=a, j=j: self.KTc[a][:, j * 128:(j + 1) * 128], lambda jj, a=a: "KTc%d" % a)
                    P.op("act", "copy", self.Vc[a][:, j, :], ps[:, 128:256], R=[pb], W=["Vc%d" % a])
            kts, ktsb = KTs[a]
            vs, vsb = Vs[a]
            P.dma("sp", pay[0].ap()[a * 128:(a + 1) * 128, :], kts, reads=[ktsb], writes=["pay_k%d" % a])
            P.dma("sp", pay[3].ap()[a * 128:(a + 1) * 128, :].rearrange("p (t d) -> p t d", d=128), vs, reads=[vsb],
                  writes=["pay_v%d" % a])
        if l == self.layers[0]:
            self.tap(L + "KTsA", KTs[0][0], [128, TOK], [KTs[0][1]], BF)
            self.tap(L + "VsA", Vs[0][0], [128, NT, 128], [Vs[0][1]], BF)
            self.tap(L + "KTcA", self.KTc[0][:], [128, 256], ["KTc0"], BF)

        self.load_w(self.w_cols("w_in", l, 1536, 256), WB, WBb, (KC, 256))
        BFT, BFTb = self.ar("BFT", [128, 2, TALL], BF)

        def bf_consume(ps, pb, c, t0, n):
            P.op("act", "copy", BFT[:, c, t0:t0 + n], ps[:, 0:n], R=[pb], W=[BFTb])
        self.proj_fm(WB, WBb, 0, 2, TT, bf_consume)
        wf, wfb = self.ar("wf", [64, 4, 64], F32)
        P.dma("sp", wf, W["w_fnet"][l].rearrange("g j c -> j g c"), writes=[wfb])
        P.op("pool", "memset", self.BA[:], 0.0, R=[], W=["BA"])
        for cs, tab in enumerate((self.c64d, self.s64d)):
            for g in range(4):
                ps, pb = self.ps()
                P.op("pe", "matmul", ps[:, 0:64], tab[0:64, :], wf[0:64, g, :], start=True, stop=True, R=[wfb, "c64d", "s64d"], W=[pb])
                rows = slice((g % 2) * 64, (g % 2) * 64 + 64)
                P.op("dve", "tensor_copy", self.BA[rows, cs, g // 2, (g % 2) * 64:(g % 2) * 64 + 64], ps[rows, 0:64], R=[pb, "BA"], W=["BA"])
        UVT, UVTb = self.ar("UVT", [128, 2, 2, TOK], BF)
        for cs in range(2):
            for c in range(2):
                for (t0, n) in GROUPS[:4]:
                    ps, pb = self.ps()
                    P.op("pe", "matmul", ps[:, 0:n], self.BA[:, cs, c, :], BFT[:, c, t0:t0 + n], start=True, stop=True, R=["BA", BFTb], W=[pb])
                    P.op("act", "copy", UVT[:, cs, c, t0:t0 + n], ps[:, 0:n], R=[pb], W=[UVTb])
                for j in range(2):
                    ps, pb = self.ps()
                    P.op("pe", "matmul", ps[:, 0:128], BFT[:, c, TOK + j * 128:TOK + (j + 1) * 128], self.BA[:, cs, c, :], start=True, stop=True, R=["BA", BFTb], W=[pb])
                    P.op("dve", "tensor_copy", self.UVc[:, cs, j, c * 128:(c + 1) * 128], ps[:, 0:128], R=[pb], W=["UVc"])
        for cs in range(2):
            P.dma("sp", pay[1 + cs].ap().rearrange("(c p) t -> p c t", p=128), UVT[:, cs, :, :],
                  reads=[UVTb], writes=["pay_uv%d" % cs])
        self.tap(L + "UVT", UVT, [128, 2, 2, TOK], [UVTb], BF)

        if self.stop == "A":
            return
        payb = [["pay_k0", "pay_k1"], ["pay_uv0"], ["pay_uv1"], ["pay_v0", "pay_v1"]]
        for j in (0, 3, 1, 2):
            P.collective(lambda e, j=j: e.collective_compute("AllGather", ALU.bypass, replica_groups=[[0, 1, 2, 3], [4, 5, 6, 7]],
                                                             ins=[pay[j].ap().opt()], outs=[gat[j].ap().opt()]),
                         reads=payb[j], writes=["gat%d" % j])
        G = [g.ap() for g in gat]

        if self.stop == "G":
            return
        self.phase()
        self.attention(l, 0, ntl, G, pay)
        if self.stop == "B2":
            return
        self.phase()
        self.attention(l, 1, ntl, G, pay)
        if self.stop == "B3":
            return
        self.phase()
        self.fnet(l, ntl, G)
        if self.stop == "B4":
            return
        self.phase()
        self.sgu(l, ntl)
        if l == self.layers[0]:
            for r in range(4):
                self.tap(L + "YB%d" % r, self.YB[r][:], [128, 2, TALL], ["YB%d_%d_%d" % (r, c, i) for c in range(2) for i in range(TT)], BF)
        if self.stop == "B5":
            return
        self.phase()
        self.merge(l, ntl, is_last, x_tile_src)

    def attention(self, l, which, ntl, G, pay):
        P, W = self.P, self.W
        q0 = 0 if which == 0 else 768
        z0 = 512 if which == 0 else 1280
        YB = self.YB[which]
        nqg = [g for g in GROUPS if g[0] // 128 < ntl]
        WQ, WQb = self.ar("WQ", [128, KC, 256], BF)
        WZ, WZb = self.ar("WZ", [128, KC, 256], BF)
        self.load_w(self.w_cols("w_in", l, q0, 256), WQ, WQb, (KC, 256))
        self.load_w(self.w_cols("w_in", l, z0, 256), WZ, WZb, (KC, 256))
        QT = [self.ar("QT%d" % g, [128, TALL], BF) for g in range(2)]
        if which == 0:
            NKB = 2 + 64
            KT, KTb = self.ar("KT", [128, NKB * 128], BF)
            V, Vb = self.ar("V", [128, NKB, 128], BF)
            P.op("act", "copy", KT[:, 0:256], self.KTc[0][:], R=["KTc0"], W=[KTb + "c"])
            P.op("act", "copy", V[:, 0:2, :], self.Vc[0][:], R=["Vc0"], W=[Vb + "c"])
            for s in range(4):
                P.dma("sp", KT[:, 256 + s * TOK:256 + (s + 1) * TOK], G[0][s * 256:s * 256 + 128, :], reads=["gat0"], writes=[KTb + str(s)])
                P.dma("sp", V[:, 2 + s * NT:2 + (s + 1) * NT, :],
                      G[3][s * 256:s * 256 + 128, :].rearrange("p (t d) -> p t d", d=128), reads=["gat3"], writes=[Vb + str(s)])

            def kbufs(kb):
                return [KTb + ("c" if kb < 2 else str((kb - 2) // NT))], [Vb + ("c" if kb < 2 else str((kb - 2) // NT))]
        else:
            NKB = 2 + NT + 8
            KT, KTb = self.ar("KT", [128, NKB * 128], BF)
            V, Vb = self.ar("V", [128, NKB, 128], BF)
            P.op("act", "copy", KT[:, 0:256], self.KTc[1][:], R=["KTc1"], W=[KTb])
            P.op("act", "copy", V[:, 0:2, :], self.Vc[1][:], R=["Vc1"], W=[Vb])
            P.dma("sp", KT[:, 256:256 + TOK], pay[0].ap()[128:256, :], reads=["pay_k1"], writes=[KTb])
            P.dma("sp", V[:, 2:2 + NT, :], pay[3].ap()[128:256, :].rearrange("p (t d) -> p t d", d=128), reads=["pay_v1"], writes=[Vb])
            for s in range(4):
                kb = 2 + NT + s
                P.dma("sp", KT[:, kb * 128:(kb + 1) * 128], G[0][s * 256 + 128:s * 256 + 256, TOK - 128:TOK], reads=["gat0"], writes=[KTb])
                P.dma("sp", V[:, kb, :], G[3][s * 256 + 128:s * 256 + 256, TOK - 128:TOK], reads=["gat3"], writes=[Vb])
                kb = 2 + NT + 4 + s
                P.dma("sp", KT[:, kb * 128:(kb + 1) * 128], G[0][s * 256 + 128:s * 256 + 256, 0:128], reads=["gat0"], writes=[KTb])
                P.dma("sp", V[:, kb, :], G[3][s * 256 + 128:s * 256 + 256, 0:128], reads=["gat3"], writes=[Vb])

            def kbufs(kb):
                return [KTb], [Vb]

        tmp = [self.ar("hnt%d" % j, [128, 256], F32)[0] for j in range(3)]
        QR = [self.ar("QR%d" % j, [128, 256], BF) for j in range(2)]
        for i in range(ntl):
            ps, pb = self.proj_tm(WQ, WQb, 0, 256, i)
            qr, qrb = QR[i % 2]
            self.headnorm_rope(ps, pb, 0, 4, self.qnw[which], "qnw%d" % which, i, tmp, qr, qrb, permute=True)
            self.transpose_to(qr, qrb, 2, lambda j, i=i: QT[j][0][:, i * 128:(i + 1) * 128], lambda j, i=i: QT[j][1] + "_%d" % i)

        def z_consume(ps, pb, c, t0, n):
            P.op("act", "activation", YB[:, c, t0:t0 + n], ps[:, 0:n], AF.Silu, R=[pb], W=["YB%d_%d_%d" % (which, c, i) for i in range(t0 // 128, (t0 + n) // 128)])
        self.proj_fm(WZ, WZb, 0, 2, ntl, z_consume)
        if which == 0 and l == self.layers[0]:
            self.tap("L%dQT0" % l, QT[0][0], [128, TALL], [QT[0][1] + "_%d" % i for i in range(ntl)], BF)

        PT = [self.ar("PT%d" % j, [128, 512], BF) for j in range(3)]
        fin = [self.ar("fin%d" % j, [128, 512], F32) for j in range(2)]
        pt_i = 0
        acc_i = 0

        def run_head(g, kv, t0, n, keylist):
            nonlocal pt_i, acc_i
            rows = slice(g * 64, (g + 1) * 64)
            krows = slice(kv * 64, (kv + 1) * 64)
            qt, qtb = QT[g]
            qbufs = [qtb + "_%d" % i for i in range(t0 // 128, (t0 + n) // 128)]
            acc_o, acc_d = self.PS[4], self.PS[5]
            aob, adb = "ps4", "ps5"
            nk = len(keylist)
            for idx, (kb, mi) in enumerate(keylist):
                ps, pb = self.ps()
                kb_k, kb_v = kbufs(kb)
                P.op("pe", "matmul", ps[:, 0:n], KT[krows, kb * 128:(kb + 1) * 128], qt[krows, t0:t0 + n],
                                                           start=True, stop=True, R=kb_k + qbufs, W=[pb])
                pt, ptb = PT[pt_i % 3]
                pt_i += 1
                P.op("act", "activation", pt[:, 0:n], ps[:, 0:n], AF.Exp, R=[pb], W=[ptb])
                if mi is not None:
                    P.op("pool", "tensor_tensor", pt[:, 0:n], pt[:, 0:n], self.masks[:, mi, 0:n], ALU.mult, R=[ptb, "masks"], W=[ptb])
                P.op("pe", "matmul", acc_o[rows, 0:n], V[:, kb, krows], pt[:, 0:n], start=(idx == 0), stop=(idx == nk - 1), tile_position=(0, g * 64), R=kb_v + [ptb], W=[aob])
                P.op("pe", "matmul", acc_d[rows, 0:n], self.ones64[:, :], pt[:, 0:n], start=(idx == 0), stop=(idx == nk - 1), tile_position=(0, g * 64), R=["ones64", ptb], W=[adb])
            (f0, f0b), (f1, f1b) = fin
            if which == 1:
                P.op("dve", "tensor_scalar", f0[rows, 0:n], acc_d[rows, 0:n], self.esink[rows, kv:kv + 1], None, ALU.add, R=[adb, "esink"], W=[f0b])
                P.op("dve", "reciprocal", f0[rows, 0:n], f0[rows, 0:n], R=[f0b], W=[f0b])
            else:
                P.op("dve", "reciprocal", f0[rows, 0:n], acc_d[rows, 0:n], R=[adb], W=[f0b])
            P.op("dve", "tensor_tensor", f1[rows, 0:n], acc_o[rows, 0:n], f0[rows, 0:n], ALU.mult, R=[aob, f0b], W=[f1b])
            ybb = ["YB%d_%d_%d" % (which, kv, i) for i in range(t0 // 128, (t0 + n) // 128)]
            P.op("pool", "tensor_tensor", YB[rows, kv, t0:t0 + n], f1[rows, 0:n], YB[rows, kv, t0:t0 + n], ALU.mult, R=[f1b] + ybb, W=ybb)

        if which == 0:
            for (t0, n) in nqg:
                keys = [(kb, None) for kb in range(NKB)] if t0 < TOK else [(0, None), (1, None)]
                for g in range(2):
                    for kv in range(2):
                        run_head(g, kv, t0, n, keys)
        else:
            for i in range(ntl):
                if i >= NT:
                    keys = [(0, None), (1, None)]
                else:
                    keys = [(0, None), (1, None), (2 + i, None)]
                    if i > 0:
                        keys.append((2 + i - 1, 0))
                    else:
                        keys += [(2 + NT + s, 2 + s) for s in range(4)]
                    if i < NT - 1:
                        keys.append((2 + i + 1, 1))
                    else:
                        keys += [(2 + NT + 4 + s, 6 + s) for s in range(4)]
                for g in range(2):
                    for kv in range(2):
                        run_head(g, kv, i * 128, 128, keys)

    def fnet(self, l, ntl, G):
        P, W = self.P, self.W
        YB = self.YB[2]
        WZ, WZb = self.ar("WZ", [128, KC, 256], BF)
        self.load_w(self.w_cols("w_in", l, 1792, 256), WZ, WZb, (KC, 256))

        def z_consume(ps, pb, c, t0, n):
            P.op("act", "activation", YB[:, c, t0:t0 + n], ps[:, 0:n], AF.Silu, R=[pb], W=["YB2_%d_%d" % (c, i) for i in range(t0 // 128, (t0 + n) // 128)])
        self.proj_fm(WZ, WZb, 0, 2, ntl, z_consume)
        MC, MCb = self.ar("MC", [128, 64, 32], BF)
        MS, MSb = self.ar("MS", [128, 64, 32], BF)
        P.dma("sp", MC, self.C["mc"].rearrange("p (a b) -> p a b", a=64), writes=[MCb])
        P.dma("sp", MS, self.C["ms"].rearrange("p (a b) -> p a b", a=64), writes=[MSb])
        Z = [self.ar("Z%d" % j, [128, 64, 128], BF) for j in range(2)]
        T3, T3b = self.ar("T3", [128, 64, 128], BF)
        sp = 0
        for c in range(2):
            for hf in range(2):
                z, zb = Z[sp % 2]
                sp += 1
                for s in range(4):
                    for ri in range(2):
                        r0 = s * 256 + c * 128 + hf * 64
                        P.dma("sp", z[ri * 64 + s * 16:ri * 64 + (s + 1) * 16, :, :],
                              G[1 + ri][r0:r0 + 64, :].rearrange("ch (t n) -> t ch n", n=128), reads=["gat%d" % (1 + ri)], writes=[zb])
                for ch4 in range(16):
                    ps, pb = self.ps()
                    for j in range(4):
                        ch = ch4 * 4 + j
                        P.op("pe", "matmul", ps[:, j * 128:(j + 1) * 128], z[:, ch, :], self.m1[:, :],
                                                                            start=True, stop=True, R=[zb, "m1"], W=[pb])
                    if ch4 % 2 == 0:
                        P.op("act", "copy", T3[:, ch4 * 4:ch4 * 4 + 4, :], ps[:].rearrange("p (a b) -> p a b", a=4), R=[pb], W=[T3b])
                    else:
                        P.op("dve", "tensor_copy", T3[:, ch4 * 4:ch4 * 4 + 4, :], ps[:].rearrange("p (a b) -> p a b", a=4), R=[pb], W=[T3b])
                rows = slice(hf * 64, (hf + 1) * 64)
                for bank in range(4):
                    ps, pb = self.ps()
                    for kk in range(16):
                        k1 = bank * 16 + kk
                        P.op("pe", "matmul", ps[rows, kk * 32:(kk + 1) * 32], T3[:, :, k1], MC[:, k1, :],
                                                                         start=True, stop=False, tile_position=(0, hf * 64), R=[T3b, MCb], W=[pb])
                        P.op("pe", "matmul", ps[rows, kk * 32:(kk + 1) * 32], T3[:, :, 64 + k1], MS[:, k1, :],
                                                                         start=False, stop=True, tile_position=(0, hf * 64), R=[T3b, MSb], W=[pb])
                    yv = YB[rows, c, 0:TOK].rearrange("p (k2 k1) -> p k1 k2", k1=64)[:, bank * 16:(bank + 1) * 16, :]
                    ybb = ["YB2_%d_%d" % (c, i) for i in range(NT)]
                    P.op("dve", "tensor_tensor", yv, ps[rows, :].rearrange("p (a b) -> p a b", a=16), yv, ALU.mult, R=[pb] + ybb, W=ybb)
        if ntl > NT:
            for c in range(2):
                ps, pb = self.ps()
                k = 0
                for uv, tab in ((0, self.c256), (1, self.s256n)):
                    for j in range(2):
                        P.op("pe", "matmul", ps[:, 0:256], self.UVc[:, uv, j, c * 128:(c + 1) * 128], tab[:, j, :], start=(k == 0), stop=(k == 3), R=["UVc", "c256", "s256n"], W=[pb])
                        k += 1
                ybb = ["YB2_%d_%d" % (c, i) for i in range(NT, TT)]
                P.op("dve", "tensor_tensor", YB[:, c, TOK:TALL], ps[:, 0:256], YB[:, c, TOK:TALL], ALU.mult, R=[pb] + ybb, W=ybb)

    def sgu(self, l, ntl):
        P, W = self.P, self.W
        YB = self.YB[3]
        WU, WUb = self.ar("WU", [128, KC, 256], BF)
        WV, WVb = self.ar("WV", [128, KC, 256], BF)
        WZ, WZb = self.ar("WZ", [128, KC, 256], BF)
        self.load_w(self.w_cols("w_in", l, 2048, 256), WU, WUb, (KC, 256))
        self.load_w(self.w_cols("w_in", l, 2304, 256), WV, WVb, (KC, 256))
        self.load_w(self.w_cols("w_in", l, 2560, 256), WZ, WZb, (KC, 256))
        GU, GUb = self.ar("GU", [128, 2, TALL], BF)
        GV, GVb = self.ar("GV", [128, TT, 256], BF)
        wsp, wspb = self.ar("wsp", [128, 4, 128], BF)
        bsp, bspb = self.ar("bsp", [128, 2, 128], F32)
        self.load_w(W["w_spT"][l], wsp, wspb, (4, 128))
        for g in range(4):
            P.dma("sp", bsp[(g % 2) * 64:(g % 2) * 64 + 64, g // 2, :], W["b_sp"][l, g:g + 1, :].partition_broadcast(64), writes=[bspb])

        def u_consume(ps, pb, c, t0, n):
            P.op("act", "activation", GU[:, c, t0:t0 + n], ps[:, 0:n], AF.Gelu, R=[pb], W=[GUb])

        def z_consume(ps, pb, c, t0, n):
            P.op("act", "activation", YB[:, c, t0:t0 + n], ps[:, 0:n], AF.Silu, R=[pb], W=["YB3_%d_%d" % (c, i) for i in range(t0 // 128, (t0 + n) // 128)])
        self.proj_fm(WU, WUb, 0, 2, ntl, u_consume)
        self.proj_fm(WZ, WZb, 0, 2, ntl, z_consume)
        for i in range(ntl):
            ps, pb = self.proj_tm(WV, WVb, 0, 256, i)
            P.op("act", "activation", GV[:, i, :], ps[:, 0:256], AF.Gelu, R=[pb], W=[GVb + "_%d" % i])
        t1, t1b = self.ar("sg1", [128, 512], F32)
        t2, t2b = self.ar("sg2", [128, 512], F32)
        for i0 in range(0, ntl, 4):
            nt = min(4, ntl - i0)
            n = nt * 128
            for c in range(2):
                ps, pb = self.ps()
                for j in range(nt):
                    i = i0 + j
                    for gg in range(2):
                        g = 2 * c + gg
                        P.op("pe", "matmul", ps[gg * 64:(gg + 1) * 64, j * 128:(j + 1) * 128], GV[:, i, g * 64:(g + 1) * 64], wsp[:, g, :],
                            start=True, stop=True, tile_position=(0, gg * 64), R=[GVb + "_%d" % i, wspb], W=[pb])
                P.op("dve", "tensor_tensor", t1[:, 0:nt * 128].rearrange("p (a b) -> p a b", a=nt), ps[:, 0:nt * 128].rearrange("p (a b) -> p a b", a=nt),
                    bsp[:, c, :].unsqueeze(1).broadcast_to([128, nt, 128]), ALU.add, R=[pb, bspb, t1b], W=[t1b])
                P.op("pool", "tensor_tensor", t2[:, 0:n], t1[:, 0:n], GU[:, c, i0 * 128:i0 * 128 + n], ALU.mult, R=[t1b, GUb, t2b], W=[t2b])
                ybb = ["YB3_%d_%d" % (c, i) for i in range(i0, i0 + nt)]
                P.op("dve", "tensor_tensor", YB[:, c, i0 * 128:i0 * 128 + n], t2[:, 0:n], YB[:, c, i0 * 128:i0 * 128 + n], ALU.mult, R=[t2b] + ybb, W=ybb)

    def merge(self, l, ntl, is_last, x_tile_src):
        P, W = self.P, self.W
        groups = [g for g in GROUPS if g[0] // 128 < ntl]
        MT, MTb = self.ar("MT", [128, KC, TALL], BF)
        WBR = [self.ar("WBR%d" % r, [128, 2, D], BF) for r in range(4)]
        for r in range(4):
            self.load_w(W["w_br"][l, r].rearrange("(kc p) n -> p kc n", p=128), WBR[r][0], WBR[r][1], (2, D))
        bm, bmb = self.ar("bm", [128, 32], F32)
        P.dma("sp", bm, W["b_mergeT"][l], writes=[bmb])
        WM = [self.ar("WM%d" % j, [128, 4, KC, 128], BF) for j in range(2)]
        sg = [self.ar("sg%d" % j, [128, 512], F32) for j in range(2)]
        acc = [self.ar("acc%d" % j, [128, 512], F32) for j in range(2)]
        tmpm = [self.ar("tmpm%d" % j, [128, 512], F32) for j in range(2)]
        cnt = 0
        for dc in range(KC):
            wm, wmb = WM[dc % 2]
            for r in range(4):
                i = self.wsf_i % 2
                self.wsf_i += 1
                st = self.WSF[i][:, 0:1024].rearrange("p (a b) -> p a b", a=KC)
                P.dma("sp", st, self.w_cols("w_merge", l, r * D + dc * 128, 128), writes=["wsf%d" % i])
                P.op("pool", "tensor_copy", wm[:, r, :, :], st, R=["wsf%d" % i], W=[wmb + "_%d" % r])
            for (t0, n) in groups:
                htb = ["HT%d" % i for i in range(t0 // 128, (t0 + n) // 128)]
                a_, ab = acc[cnt % 2]
                for r in range(4):
                    psg, pgb = self.ps()
                    for kc in range(KC):
                        P.op("pe", "matmul", psg[:, 0:n], wm[:, r, kc, :], self.HT[:, kc, t0:t0 + n], start=(kc == 0), stop=(kc == KC - 1), R=[wmb + "_%d" % r] + htb, W=[pgb])
                    psy, pyb = self.ps()
                    ybb = ["YB%d_%d_%d" % (r, c, i) for c in range(2) for i in range(t0 // 128, (t0 + n) // 128)]
                    for k2 in range(2):
                        P.op("pe", "matmul", psy[:, 0:n], WBR[r][0][:, k2, dc * 128:(dc + 1) * 128], self.YB[r][:, k2, t0:t0 + n],
                            start=(k2 == 0), stop=(k2 == 1), R=[WBR[r][1]] + ybb, W=[pyb])
                    s_, sb_ = sg[(cnt * 4 + r) % 2]
                    P.op("act", "activation", s_[:, 0:n], psg[:, 0:n], AF.Sigmoid,
                                                                         bias=bm[:, r * 8 + dc:r * 8 + dc + 1], scale=1.0, R=[pgb, bmb], W=[sb_])
                    if r == 0:
                        P.op("dve", "tensor_tensor", a_[:, 0:n], psy[:, 0:n], s_[:, 0:n], ALU.mult, R=[pyb, sb_, ab], W=[ab])
                    else:
                        t_, tb_ = tmpm[r % 2]
                        P.op("dve", "tensor_tensor", t_[:, 0:n], psy[:, 0:n], s_[:, 0:n], ALU.mult, R=[pyb, sb_, tb_], W=[tb_])
                        if r < 3:
                            P.op("pool", "tensor_tensor", a_[:, 0:n], a_[:, 0:n], t_[:, 0:n], ALU.add, R=[ab, tb_], W=[ab])
                        else:
                            P.op("pool", "tensor_tensor", MT[:, dc, t0:t0 + n], a_[:, 0:n], t_[:, 0:n], ALU.add, R=[ab, tb_], W=[MTb + "_%d_%d" % (dc, t0)])
                cnt += 1
        self.tap("L%dMT" % l, MT, [128, KC, TALL], [MTb + "_%d_%d" % (dc, g[0]) for dc in range(KC) for g in groups], BF)
        if self.stop == "C2":
            return
        self.P.barrier()
        self.ar_off = (KC * TALL * 2 + 63) // 64 * 64
        self.ar_gen += 1
        WO, WOb = self.ar("WO", [128, KC, D], BF)
        for j in range(4):
            self.load_w(self.w_cols("w_out", l, j * 256, 256), WO[:, :, j * 256:(j + 1) * 256], WOb + "_%d" % j, (KC, 256))
        gt_bc, gtb = self.ar("gt_bc", [128, 2, D], F32)
        for r in range(2 if ntl > NT else 1):
            for half in range(2):
                ps, pb = self.ps()
                P.op("pe", "matmul", ps[:, :], self.sel2[0:2, r * 128:(r + 1) * 128], self.gate2[0:2, half * 512:(half + 1) * 512],
                    start=True, stop=True, R=["sel2", "gate2"], W=[pb])
                P.op("act", "copy", gt_bc[:, r, half * 512:(half + 1) * 512], ps[:, :], R=[pb], W=[gtb])
        XT = [self.ar("XT%d" % i, [128, D], F32) for i in range(2)]
        XO = [self.ar("XO%d" % i, [128, D], F32) for i in range(2)]
        for i in range(ntl):
            r = 0 if i < NT else 1
            xt, xtb = XT[i % 2]
            xo, xob = XO[i % 2]
            P.dma("sp", xt, x_tile_src(i), writes=[xtb])
            for half in range(2):
                ps, pb = self.ps()
                mtb = [MTb + "_%d_%d" % (dc, g[0]) for dc in range(KC) for g in GROUPS if g[0] <= i * 128 < g[0] + g[1]]
                for kc in range(KC):
                    P.op("pe", "matmul", ps[:, :], MT[:, kc, i * 128:(i + 1) * 128], WO[:, kc, half * 512:(half + 1) * 512],
                        start=(kc == 0), stop=(kc == KC - 1), R=mtb + [WOb + "_%d" % (2 * half), WOb + "_%d" % (2 * half + 1)], W=[pb])
                P.op("dve", "tensor_tensor", xo[:, half * 512:(half + 1) * 512], ps[:, :], gt_bc[:, r, half * 512:(half + 1) * 512], ALU.mult, R=[pb, gtb, xob], W=[xob + "h%d" % half])
                P.op("pool", "tensor_tensor", xo[:, half * 512:(half + 1) * 512], xo[:, half * 512:(half + 1) * 512], xt[:, half * 512:(half + 1) * 512], ALU.add, R=[xob + "h%d" % half, xtb], W=[xob + "h%d" % half])
            if is_last:
                dst = self.out[i * 128:(i + 1) * 128, :]
            elif self.last or l != self.layers[-1]:
                dst = self.xs[i * 128:(i + 1) * 128, :]
            else:
                dst = self.xs_out[i * 128:(i + 1) * 128, :]
            P.dma("sp", dst, xo, reads=[xob + "h0", xob + "h1"], writes=["xs%d" % i])


def is_last_layer(l):
    return l == 1


_CONST_CACHE = {}


def _consts(core):
    if core not in _CONST_CACHE:
        _CONST_CACHE[core] = host_consts(core)
    return _CONST_CACHE[core]


def _weights_layout(inp):
    f = lambda a: np.ascontiguousarray(np.asarray(a, dtype=np.float32))
    w = {k: f(inp[k]) for k in ("norm_w", "w_ada", "b_ada", "w_in", "qn_a", "kn_a", "qn_d", "kn_d", "sink_d",
                                "w_fnet", "b_sp", "w_br", "w_merge", "w_out")}
    w["w_spT"] = np.ascontiguousarray(f(inp["w_sp"]).transpose(0, 3, 1, 2))
    w["b_mergeT"] = np.ascontiguousarray(f(inp["b_merge"]).reshape(2, 32, 128).transpose(0, 2, 1))
    return w


def _run(builder_kwargs, per_core_extra, inp):
    b = Builder(**builder_kwargs)
    nc = b.build()
    w = _weights_layout(inp)
    c = np.asarray(inp["c"], np.float32)
    c_ctx = np.asarray(inp["c_ctx"], np.float32)
    in_maps = []
    for core in range(NCORES):
        bi = core // 4
        cv = np.stack([c[bi], c_ctx], 0)
        cvT = np.ascontiguousarray(cv.reshape(2, KC, 128).transpose(2, 1, 0))
        m = {"cvecT": cvT}
        m.update(w)
        m.update(_consts(core))
        m.update(per_core_extra(core))
        in_maps.append(m)
    res = run_bass_kernel_spmd(nc, in_maps, core_ids=list(range(NCORES)))
    return res.results, b


def kernel(**inp):
    x = np.asarray(inp["x"], np.float32)
    ctx = np.asarray(inp["ctx"], np.float32)

    def extra(core):
        bi, q = divmod(core, 4)
        return {"x": np.ascontiguousarray(x[bi, q * TOK:(q + 1) * TOK]), "ctx": np.ascontiguousarray(ctx[bi])}
    results, _ = _run(dict(layers=(0, 1), first=True, last=True), extra, inp)
    out = np.empty((2, SEQ, D), np.float32)
    for core in range(NCORES):
        bi, q = divmod(core, 4)
        out[bi, q * TOK:(q + 1) * TOK] = results[core]["out"]
    return out
```

```python
import numpy as np
import ml_dtypes
import concourse.bass as bass
import concourse.mybir as mybir
from concourse.bass_utils import run_bass_kernel_spmd

F32 = mybir.dt.float32
BF = mybir.dt.bfloat16
AF = mybir.ActivationFunctionType
ALU = mybir.AluOpType
AX = mybir.AxisListType
NPBF = ml_dtypes.bfloat16

NCORES = 8
TOK = 2048
NT = 16
CT = 2
TT = NT + CT
TALL = TT * 128
D = 1024
KC = 8
EPS = 1e-6
SEQ = 8192
INW = 2816
GROUPS = [(0, 512), (512, 512), (1024, 512), (1536, 512), (2048, 256)]


class Buf:
    __slots__ = ("name", "w", "rs")

    def __init__(self, name):
        self.name = name
        self.w = None
        self.rs = []


class Op:
    __slots__ = ("eng", "fn", "deps", "kind", "sem", "semval", "signal")

    def __init__(self, eng, fn, kind):
        self.eng = eng
        self.fn = fn
        self.kind = kind
        self.deps = []
        self.sem = None
        self.semval = 0
        self.signal = False


class Prog:
    ENGS = ("pe", "act", "dve", "pool", "sp")
    NS = 24

    def __init__(self, nc):
        self.nc = nc
        self.ops = {e: [] for e in self.ENGS}
        self.dma_count = 0
        self.dma_last = [None] * self.NS
        self.ncoll = 0
        self.bufs = {}
        self.pending = {e: [] for e in self.ENGS}
        self.last_c = {e: None for e in self.ENGS}
        self.colls = []

    def buf(self, name):
        b = self.bufs.get(name)
        if b is None:
            b = self.bufs[name] = Buf(name)
        return b

    def _add(self, op, d):
        if d is op:
            return
        op.deps.append(d)
        if d.kind == "c":
            d.signal = True

    def _deps(self, op, reads, writes):
        deps = []
        for b in reads:
            b = self.buf(b)
            if b.w is not None:
                deps.append(("raw", b.w))
            b.rs.append(op)
        for b in writes:
            b = self.buf(b)
            if b.w is not None:
                deps.append(("waw", b.w))
            for r in b.rs:
                if r is not op:
                    deps.append(("war", r))
            b.rs = []
            b.w = op
        for kind, d in deps:
            if d is op:
                continue
            if d.kind == "c" and op.kind == "c" and d.eng == op.eng:
                if op.eng == "pe" or kind == "war":
                    continue
            self._add(op, d)
        if self.pending[op.eng]:
            for d in self.pending[op.eng]:
                self._add(op, d)
            self.pending[op.eng] = []

    def op(self, eng, meth, *args, R=(), W=(), **kw):
        fn = (lambda e, meth=meth, args=args, kw=kw: getattr(e, meth)(*args, **kw))
        o = Op(eng, fn, "c")
        self._deps(o, R, W)
        self.ops[eng].append(o)
        self.last_c[eng] = o
        return o

    def dma(self, q, out, in_, reads=(), writes=(), **kw):
        o = Op(q, (lambda e, out=out, in_=in_, kw=kw: e.dma_start(out=out, in_=in_, **kw)), "d")
        self._deps(o, reads, writes)
        slot = self.dma_count % self.NS
        self.dma_count += 1
        prev = self.dma_last[slot]
        o.sem = slot
        o.semval = (prev.semval if prev else 0) + 16
        if prev is not None:
            o.deps.append(prev)
        self.dma_last[slot] = o
        self.ops[q].append(o)
        return o

    def collective(self, fn, reads=(), writes=()):
        o = Op("pool", fn, "x")
        self._deps(o, reads, writes)
        o.sem = self.ncoll
        self.ncoll += 1
        self.ops["pool"].append(o)
        self.colls.append(o)
        return o

    def barrier(self):
        lasts = [o for o in self.last_c.values() if o is not None]
        lasts += [o for o in self.dma_last if o is not None]
        lasts += list(self.colls)
        for e in self.ENGS:
            self.pending[e] = list(lasts)

    def emit(self):
        nc = self.nc
        for e in self.ENGS:
            cnt = 0
            for o in self.ops[e]:
                if o.kind == "c" and o.signal:
                    cnt += 1
                    o.semval = cnt
        esem = {e: nc.alloc_semaphore("es_" + e) for e in self.ENGS}
        dsem = [nc.alloc_semaphore("ds_%d" % i) for i in range(self.NS)]
        csem = [nc.alloc_semaphore("cs_%d" % i) for i in range(self.ncoll)]

        def semof(d):
            if d.kind == "c":
                return esem[d.eng], d.semval
            if d.kind == "d":
                return dsem[d.sem], d.semval
            return csem[d.sem], 1

        def gen(ename):
            def body(eng):
                waited = {}
                for o in self.ops[ename]:
                    for d in o.deps:
                        s, v = semof(d)
                        if waited.get(id(s), 0) >= v:
                            continue
                        waited[id(s)] = v
                        eng.wait_ge(s, v)
                    ins = o.fn(eng)
                    if o.kind == "c":
                        if o.signal:
                            ins.then_inc(esem[ename], 1)
                    elif o.kind == "d":
                        ins.then_inc(dsem[o.sem], 16)
                    else:
                        ins.then_inc(csem[o.sem])
                last = {}
                for o in self.ops[ename]:
                    if o.kind in ("d", "x"):
                        s, v = semof(o)
                        if v > last.get(id(s), (None, 0))[1]:
                            last[id(s)] = (s, v)
                for s, v in last.values():
                    if waited.get(id(s), 0) < v:
                        eng.wait_ge(s, v)
            return body

        with nc.Block() as block:
            block.sync(gen("sp"))
            block.scalar(gen("act"))
            block.vector(gen("dve"))
            block.tensor(gen("pe"))
            block.gpsimd(gen("pool"))


def host_consts(core):
    b, q = divmod(core, 4)
    t = (q * TOK + np.arange(TOK)).astype(np.int64)
    r = (t // 64).astype(np.float32)
    col = (t % 64).astype(np.float32)
    inv = (np.float32(10000.0) ** (-np.arange(16, dtype=np.float32) / np.float32(16))).astype(np.float32)
    ar = r[:, None] * inv[None, :]
    ac = col[:, None] * inv[None, :]
    cr, sr, cc, sc = np.cos(ar), np.sin(ar), np.cos(ac), np.sin(ac)
    cos64 = np.concatenate([cr, cr, cc, cc], 1).astype(np.float32)
    sin64 = np.concatenate([-sr, sr, -sc, sc], 1).astype(np.float32)
    cos64 = np.ascontiguousarray(cos64.reshape(NT, 128, 64).transpose(1, 0, 2))
    sin64 = np.ascontiguousarray(sin64.reshape(NT, 128, 64).transpose(1, 0, 2))
    n1 = np.arange(64)
    th = 2 * np.pi * np.outer(n1, n1) / 64.0
    C, S = np.cos(th), np.sin(th)
    sc1 = 1.0 / np.sqrt(SEQ * 64.0)
    m1 = np.zeros((128, 128), np.float64)
    m1[:64, :64] = C
    m1[64:, :64] = -S
    m1[:64, 64:] = S
    m1[64:, 64:] = C
    m1 *= sc1
    n2 = np.arange(128)[:, None, None]
    k1 = np.arange(64)[None, :, None]
    k2 = (np.arange(32) + 32 * q)[None, None, :]
    ph = 2 * np.pi * n2 * (64 * k2 + k1) / float(SEQ)
    mc = np.cos(ph).reshape(128, 64 * 32)
    ms = (-np.sin(ph)).reshape(128, 64 * 32)
    n = np.arange(256)
    t256 = 2 * np.pi * np.outer(n, n) / 256.0
    sc2 = 1.0 / np.sqrt(256 * 64.0)
    c256 = (np.cos(t256) * sc2).reshape(2, 128, 256).transpose(1, 0, 2)
    s256n = (-np.sin(t256) * sc2).reshape(2, 128, 256).transpose(1, 0, 2)
    c64d = np.concatenate([C, C], 1).astype(np.float32)
    s64d = np.concatenate([S, S], 1).astype(np.float32)
    pk = np.arange(128)[:, None]
    pq = np.arange(128)[None, :]
    mP = (pk >= pq).astype(np.float32)
    mN = (pk <= pq).astype(np.float32)
    masks = np.zeros((128, 10, 128), np.float32)
    masks[:, 0] = mP
    masks[:, 1] = mN
    for s in range(4):
        if s == q - 1:
            masks[:, 2 + s] = mP
        if s == q + 1:
            masks[:, 6 + s] = mN
    sel2 = np.zeros((2, 256), np.float32)
    sel2[0, :128] = 1.0
    sel2[1, 128:] = 1.0
    return {
        "cos64": cos64, "sin64": sin64,
        "m1": m1.astype(NPBF), "mc": mc.astype(NPBF), "ms": ms.astype(NPBF),
        "c256": np.ascontiguousarray(c256).astype(NPBF), "s256n": np.ascontiguousarray(s256n).astype(NPBF),
        "c64d": c64d, "s64d": s64d,
        "masks": masks.astype(NPBF), "sel2": sel2,
        "ident": np.eye(128, dtype=np.float32),
    }


CONST_SPECS = {
    "cos64": ([128, NT, 64], F32), "sin64": ([128, NT, 64], F32),
    "m1": ([128, 128], BF), "mc": ([128, 2048], BF), "ms": ([128, 2048], BF),
    "c256": ([128, 2, 256], BF), "s256n": ([128, 2, 256], BF),
    "c64d": ([64, 128], F32), "s64d": ([64, 128], F32),
    "masks": ([128, 10, 128], BF), "sel2": ([2, 256], F32), "ident": ([128, 128], F32),
}

WEIGHT_SPECS = {
    "norm_w": [2, D], "w_ada": [2, D, 3 * D], "b_ada": [2, 3 * D], "w_in": [2, D, INW],
    "qn_a": [2, 64], "kn_a": [2, 64], "qn_d": [2, 64], "kn_d": [2, 64], "sink_d": [2, 4],
    "w_fnet": [2, 4, 64, 64], "w_spT": [2, 128, 4, 128], "b_sp": [2, 4, 128],
    "w_br": [2, 4, 256, D], "w_merge": [2, D, 4 * D], "b_mergeT": [2, 128, 32], "w_out": [2, D, D],
}


class Builder:
    def __init__(self, layers=(0, 1), first=True, last=True, debug=(), stop=None):
        self.stop = stop
        self.layers = tuple(layers)
        self.first = first
        self.last = last
        self.debug = set(debug)
        nc = self.nc = bass.Bass("TRN2", target_bir_lowering=False)
        self.P = Prog(nc)
        self.dbg_outs = {}
        self._uid = 0

    def uid(self, p="t"):
        self._uid += 1
        return "%s%d" % (p, self._uid)

    def dram_in(self, name, shape, dt=F32):
        return self.nc.dram_tensor(name, list(shape), dt, kind="ExternalInput").ap()

    def dram_out(self, name, shape, dt=F32):
        return self.nc.dram_tensor(name, list(shape), dt, kind="ExternalOutput").ap()

    def sb(self, name, shape, dt):
        return self.nc.alloc_sbuf_tensor("s_" + name, list(shape), dt)

    def tap(self, name, view, shape, reads, dt=F32):
        if name not in self.debug:
            return
        o = self.dram_out("dbg_" + name, shape, dt)
        self.dbg_outs["dbg_" + name] = (shape, dt)
        self.P.dma("sp", o, view, reads=reads)

    def build(self):
        nc, P = self.nc, self.P
        if self.first:
            self.x_in = self.dram_in("x", [TOK, D])
            self.ctx_in = self.dram_in("ctx", [256, D])
        else:
            self.xs_in = self.dram_in("xs_in", [TALL, D])
        self.cvecT = self.dram_in("cvecT", [128, KC, 2])
        self.W = {k: self.dram_in(k, s) for k, s in WEIGHT_SPECS.items()}
        self.C = {k: self.dram_in(k, s, dt) for k, (s, dt) in CONST_SPECS.items()}
        if self.last:
            self.out = self.dram_out("out", [TOK, D])
        else:
            self.xs_out = self.dram_out("xs_out", [TALL, D])
        self.xs = nc.dram_tensor("xs", [TALL, D], F32).ap()
        self.payload = [[nc.dram_tensor("payload%d_%d" % (l, j), [256, TOK], BF) for j in range(4)] for l in range(2)]
        self.gathered = [[nc.dram_tensor("gathered%d_%d" % (l, j), [1024, TOK], BF) for j in range(4)] for l in range(2)]

        self.HT = self.sb("HT", [128, KC, TALL], BF)
        self.YB = [self.sb("YB%d" % r, [128, 2, TALL], BF) for r in range(4)]
        self.WSF = [self.sb("WSF%d" % i, [128, 2048], F32) for i in range(2)]
        self.wsf_i = 0
        self.cos64 = self.sb("cos64", [128, NT, 64], F32)
        self.sin64 = self.sb("sin64", [128, NT, 64], F32)
        self.identf = self.sb("identf", [128, 128], F32)
        self.ident = self.sb("ident", [128, 128], BF)
        self.m1 = self.sb("m1", [128, 128], BF)
        self.c256 = self.sb("c256", [128, 2, 256], BF)
        self.s256n = self.sb("s256n", [128, 2, 256], BF)
        self.c64d = self.sb("c64d", [64, 128], F32)
        self.s64d = self.sb("s64d", [64, 128], F32)
        self.masks = self.sb("masks", [128, 10, 128], BF)
        self.sel2 = self.sb("sel2", [2, 256], F32)
        self.ones64 = self.sb("ones64", [128, 64], BF)
        self.scT = self.sb("scT", [128, KC, 2], F32)
        self.gate2 = self.sb("gate2", [2, D], F32)
        self.knw = [self.sb("knw%d" % i, [128, 128], F32) for i in range(2)]
        self.qnw = [self.sb("qnw%d" % i, [128, 256], F32) for i in range(2)]
        self.esink = self.sb("esink", [128, 2], F32)
        self.KTc = [self.sb("KTc%d" % i, [128, 256], BF) for i in range(2)]
        self.Vc = [self.sb("Vc%d" % i, [128, 2, 128], BF) for i in range(2)]
        self.UVc = self.sb("UVc", [128, 2, 2, 256], BF)
        self.BA = self.sb("BA", [128, 2, 2, 128], BF)
        self.stat = self.sb("stat", [128, 64], F32)
        self.stat_i = 0
        self.PS = [nc.alloc_psum_tensor("ps%d" % i, [128, 512], F32) for i in range(6)]
        self.PST = [nc.alloc_psum_tensor("pst%d" % i, [128, 1024], BF) for i in range(2)]
        self.ps_i = 0
        self.pst_i = 0
        self.ARENA_BYTES = 88 * 1024
        self.arena = self.sb("arena", [128, self.ARENA_BYTES // 2], BF)
        self.ar_off = 0
        self.ar_gen = 0

        for k in ("cos64", "sin64", "m1", "c256", "s256n", "c64d", "s64d", "masks", "sel2"):
            P.dma("sp", getattr(self, k)[:], self.C[k], writes=[k])
        P.dma("sp", self.identf[:], self.C["ident"], writes=["identf"])
        P.op("dve", "tensor_copy", self.ident[:], self.identf[:], R=["identf"], W=["ident"])
        P.op("pool", "memset", self.ones64[:], 1.0, R=[], W=["ones64"])
        P.dma("sp", self.scT[:], self.cvecT, writes=["scT"])
        P.op("act", "activation", self.scT[:], self.scT[:], AF.Silu, R=["scT"], W=["scT"])

        for li, l in enumerate(self.layers):
            is_first = self.first and li == 0
            is_last = self.last and li == len(self.layers) - 1
            self.layer(l, is_first, is_last, x_from_input=is_first,
                       x_src_xs_in=(not self.first and li == 0))
        if self.stop is not None:
            dst = self.out if self.last else self.xs_out
            P.dma("sp", dst[0:128, :], self.identf[:, :].unsqueeze(1).broadcast_to([128, 8, 128]) if False else self.cos64[:, :, :], reads=["cos64"])
        P.emit()
        return nc

    def phase(self):
        self.P.barrier()
        self.ar_off = 0
        self.ar_gen += 1

    def ar(self, name, shape, dt):
        n = int(np.prod(shape[1:]))
        nbytes = n * (4 if dt == F32 else 2)
        nbytes = (nbytes + 63) // 64 * 64
        assert self.ar_off + nbytes <= self.ARENA_BYTES, (name, self.ar_off, nbytes)
        e0 = self.ar_off // 2
        v = self.arena[:, e0:e0 + nbytes // 2]
        if dt == F32:
            v = v.bitcast(F32)
        v = v[:, 0:n]
        self.ar_off += nbytes
        if shape[0] != 128:
            v = v[0:shape[0]]
        if len(shape) == 3:
            v = v.rearrange("p (a b) -> p a b", a=shape[1])
        elif len(shape) == 4:
            v = v.rearrange("p (a b c) -> p a b c", a=shape[1], b=shape[2])
        return v, "%s_g%d" % (name, self.ar_gen)

    def ps(self):
        i = self.ps_i % 4
        self.ps_i += 1
        return self.PS[i], "ps%d" % i

    def pst(self):
        i = self.pst_i % 2
        self.pst_i += 1
        return self.PST[i], "pst%d" % i

    def statcol(self, n=1):
        c = (self.stat_i % 16) * 4
        self.stat_i += 1
        return self.stat[:, c:c + n], "stat%d" % c

    def load_w(self, src, dst, dstbuf, shape3, q="sp"):
        P = self.P
        i = self.wsf_i % 2
        self.wsf_i += 1
        a, b = shape3
        st = self.WSF[i][:, 0:a * b].rearrange("p (a b) -> p a b", a=a)
        P.dma(q, st, src, writes=["wsf%d" % i])
        P.op("pool", "tensor_copy", dst, st, R=["wsf%d" % i], W=[dstbuf])

    def w_cols(self, name, l, c0, n):
        return self.W[name][l, :, c0:c0 + n].rearrange("(kc p) n -> p kc n", p=128)

    def proj_fm(self, wb, wbuf, c0, nchunks, tiles, consume):
        P = self.P
        for c in range(nchunks):
            for (t0, n) in GROUPS:
                if t0 // 128 >= tiles:
                    continue
                ps, pb = self.ps()
                htb = ["HT%d" % i for i in range(t0 // 128, (t0 + n) // 128)]
                for kc in range(KC):
                    P.op("pe", "matmul", ps[:, 0:n], wb[:, kc, c0 + c * 128:c0 + (c + 1) * 128], self.HT[:, kc, t0:t0 + n],
                        start=(kc == 0), stop=(kc == KC - 1), R=[wbuf] + htb, W=[pb])
                consume(ps, pb, c, t0, n)

    def proj_tm(self, wb, wbuf, c0, ncols, tile):
        P = self.P
        ps, pb = self.ps()
        for kc in range(KC):
            P.op("pe", "matmul", ps[:, 0:ncols], self.HT[:, kc, tile * 128:(tile + 1) * 128], wb[:, kc, c0:c0 + ncols],
                start=(kc == 0), stop=(kc == KC - 1), R=[wbuf, "HT%d" % tile], W=[pb])
        return ps, pb

    def headnorm_rope(self, ps, pb, c0, nh, wtile, wbuf, tile, tmp, out_bf, outbuf, permute):
        P = self.P
        n = nh * 64
        sq, t1, t2 = tmp
        sqb, t1b, t2b = [self.uid("hn") for _ in range(3)]
        P.op("act", "activation", sq[:, 0:n], ps[:, c0:c0 + n], AF.Square, R=[pb], W=[sqb])
        ssq, ssqb = self.statcol(4)
        rt, rtb = self.statcol(4)
        rr, rrb = self.statcol(4)
        P.op("dve", "tensor_reduce", ssq[:, 0:nh], sq[:, 0:n].rearrange("p (h d) -> p h d", h=nh), AX.X, ALU.add, R=[sqb], W=[ssqb])
        P.op("act", "activation", rt[:, 0:nh], ssq[:, 0:nh], AF.Sqrt, bias=64.0 * EPS, scale=1.0, R=[ssqb], W=[rtb])
        P.op("dve", "reciprocal", rr[:, 0:nh], rt[:, 0:nh], R=[rtb], W=[rrb])
        P.op("dve", "tensor_tensor", t1[:, 0:n].rearrange("p (h d) -> p h d", h=nh), ps[:, c0:c0 + n].rearrange("p (h d) -> p h d", h=nh),
            rr[:, 0:nh].unsqueeze(2).broadcast_to([128, nh, 64]), ALU.mult, R=[pb, rrb], W=[t1b])
        if permute:
            ov = out_bf[:, 0:n].rearrange("p (g kv d) -> p kv g d", g=2, kv=2)
        if tile >= NT:
            if permute:
                P.op("pool", "tensor_tensor", ov, t1[:, 0:n].rearrange("p (kv g d) -> p kv g d", kv=2, g=2),
                    wtile[:, 0:n].rearrange("p (kv g d) -> p kv g d", kv=2, g=2), ALU.mult, R=[t1b, wbuf], W=[outbuf])
            else:
                P.op("pool", "tensor_tensor", out_bf[:, 0:n], t1[:, 0:n], wtile[:, 0:n], ALU.mult, R=[t1b, wbuf], W=[outbuf])
            return
        P.op("pool", "tensor_tensor", sq[:, 0:n], t1[:, 0:n], wtile[:, 0:n], ALU.mult, R=[t1b, wbuf, sqb], W=[sqb])
        cosb = self.cos64[:, tile, :].unsqueeze(1).broadcast_to([128, nh, 64])
        sinv = self.sin64[:, tile, :].rearrange("p (t j) -> p t j", t=4)
        xv = sq[:, 0:n].rearrange("p (h t j) -> p h t j", h=nh, t=4)
        t2v = t2[:, 0:n].rearrange("p (h t j) -> p h t j", h=nh, t=4)
        P.op("dve", "tensor_tensor", t1[:, 0:n].rearrange("p (h d) -> p h d", h=nh),
                                              sq[:, 0:n].rearrange("p (h d) -> p h d", h=nh), cosb, ALU.mult, R=[sqb, "cos64", t1b], W=[t1b])
        P.op("pool", "tensor_tensor", t2v[:, :, 0::2, :], xv[:, :, 1::2, :],
                                               sinv[:, 0::2, :].unsqueeze(1).broadcast_to([128, nh, 2, 16]), ALU.mult, R=[sqb, "sin64"], W=[t2b + "a"])
        P.op("pool", "tensor_tensor", t2v[:, :, 1::2, :], xv[:, :, 0::2, :],
                                               sinv[:, 1::2, :].unsqueeze(1).broadcast_to([128, nh, 2, 16]), ALU.mult, R=[sqb, "sin64"], W=[t2b + "b"])
        if permute:
            P.op("dve", "tensor_tensor", ov, t1[:, 0:n].rearrange("p (kv g d) -> p kv g d", kv=2, g=2),
                t2[:, 0:n].rearrange("p (kv g d) -> p kv g d", kv=2, g=2), ALU.add, R=[t1b, t2b + "a", t2b + "b"], W=[outbuf])
        else:
            P.op("dve", "tensor_tensor", out_bf[:, 0:n], t1[:, 0:n], t2[:, 0:n], ALU.add, R=[t1b, t2b + "a", t2b + "b"], W=[outbuf])

    def transpose_to(self, src_bf, srcbuf, nblk, dst_fn, dstbuf_fn, eng="act"):
        P = self.P
        pt, ptb = self.pst()
        for j in range(nblk):
            P.op("pe", "transpose", pt[:, j * 128:(j + 1) * 128], src_bf[:, j * 128:(j + 1) * 128], self.ident[:], R=[srcbuf, "ident"], W=[ptb])
        for j in range(nblk):
            if eng == "act":
                P.op("act", "copy", dst_fn(j), pt[:, j * 128:(j + 1) * 128], R=[ptb], W=[dstbuf_fn(j)])
            else:
                P.op("dve", "tensor_copy", dst_fn(j), pt[:, j * 128:(j + 1) * 128], R=[ptb], W=[dstbuf_fn(j)])

    def layer(self, l, is_first, is_last, x_from_input, x_src_xs_in):
        nc, P, W = self.nc, self.P, self.W
        do_ctx_out = not is_last_layer(l)
        ntl = TT if do_ctx_out else NT
        pay, gat = self.payload[l], self.gathered[l]

        def x_tile_src(i):
            if x_from_input:
                return self.x_in[i * 128:(i + 1) * 128, :] if i < NT else self.ctx_in[(i - NT) * 128:(i - NT + 1) * 128, :]
            if x_src_xs_in:
                return self.xs_in[i * 128:(i + 1) * 128, :]
            return self.xs[i * 128:(i + 1) * 128, :]

        self.phase()
        L = "L%d" % l
        mod, modb = self.ar("mod", [2, 3 * D], F32)
        arow, arowb = self.ar("arow", [2, D], F32)
        normw2, normw2b = self.ar("normw2", [2, D], F32)
        bada2, bada2b = self.ar("bada2", [2, 3 * D], F32)
        P.dma("sp", normw2, W["norm_w"][l:l + 1, :].partition_broadcast(2), writes=[normw2b])
        P.dma("sp", bada2, W["b_ada"][l:l + 1, :].partition_broadcast(2), writes=[bada2b])
        for i, (kn, qn) in enumerate((("kn_a", "qn_a"), ("kn_d", "qn_d"))):
            for h in range(2):
                P.dma("sp", self.knw[i][:, h * 64:(h + 1) * 64], W[kn][l:l + 1, :].partition_broadcast(128), writes=["knw%d" % i])
            for h in range(4):
                P.dma("sp", self.qnw[i][:, h * 64:(h + 1) * 64], W[qn][l:l + 1, :].partition_broadcast(128), writes=["qnw%d" % i])
            P.op("act", "mul", self.knw[i][:], self.knw[i][:], 8.0, R=["knw%d" % i], W=["knw%d" % i])
        for g in range(2):
            for kv in range(2):
                P.dma("sp", self.esink[g * 64:(g + 1) * 64, kv:kv + 1],
                      W["sink_d"][l:l + 1, 2 * kv + g:2 * kv + g + 1].partition_broadcast(64), writes=["esink"])
        P.op("act", "activation", self.esink[:], self.esink[:], AF.Exp, R=["esink"], W=["esink"])

        for blk in range(6):
            ps, pb = self.ps()
            for half in range(2):
                i = self.wsf_i % 2
                self.wsf_i += 1
                st = self.WSF[i][:].rearrange("p (a b) -> p a b", a=KC)
                c0 = blk * 512 + half * 256
                P.dma("sp", st, self.w_cols("w_ada", l, c0, 256), writes=["wsf%d" % i])
                for kc in range(KC):
                    P.op("pe", "matmul", ps[0:2, half * 256:(half + 1) * 256], self.scT[:, kc, :], st[:, kc, :],
                        start=(kc == 0), stop=(kc == KC - 1), R=["scT", "wsf%d" % i], W=[pb])
            P.op("dve", "tensor_tensor", mod[0:2, blk * 512:(blk + 1) * 512], ps[0:2, :], bada2[0:2, blk * 512:(blk + 1) * 512], ALU.add, R=[pb, bada2b], W=[modb])
        self.tap(L + "mod", mod, [2, 3 * D], [modb])
        P.op("dve", "scalar_tensor_tensor", arow, mod[0:2, D:2 * D], 1.0, normw2, ALU.add, ALU.mult, R=[modb, normw2b], W=[arowb])
        P.op("act", "copy", self.gate2[:], mod[0:2, 2 * D:3 * D], R=[modb], W=["gate2"])
        a_bc, a_bcb = self.ar("a_bc", [128, 2, D], F32)
        sh_bc, sh_bcb = self.ar("sh_bc", [128, 2, D], F32)
        for r in range(2):
            for half in range(2):
                for which, (dst, dbuf, src, sbuf) in enumerate(((a_bc, a_bcb, arow, arowb), (sh_bc, sh_bcb, mod, modb))):
                    ps, pb = self.ps()
                    P.op("pe", "matmul", ps[:, :], self.sel2[0:2, r * 128:(r + 1) * 128], src[0:2, half * 512:(half + 1) * 512],
                        start=True, stop=True, R=["sel2", sbuf], W=[pb])
                    P.op("act", "copy", dst[:, r, half * 512:(half + 1) * 512], ps[:, :], R=[pb], W=[dbuf])

        XT = [self.ar("XT%d" % i, [128, D], F32) for i in range(2)]
        TMPF = [self.ar("TMPF%d" % i, [128, D], F32) for i in range(2)]
        HB = [self.ar("HB%d" % i, [128, D], BF) for i in range(2)]
        for i in range(TT):
            s = i % 2
            r = 0 if i < NT else 1
            (xt, xtb), (tf, tfb), (hb, hbb) = XT[s], TMPF[s], HB[s]
            P.dma("sp", xt, x_tile_src(i), writes=[xtb])
            ssq, ssqb = self.statcol()
            rt, rtb = self.statcol()
            rs, rsb = self.statcol()
            P.op("act", "activation", tf, xt, AF.Square, accum_out=ssq, R=[xtb], W=[tfb, ssqb])
            P.op("act", "activation", rt, ssq, AF.Sqrt, bias=EPS, scale=1.0 / D, R=[ssqb], W=[rtb])
            P.op("dve", "reciprocal", rs, rt, R=[rtb], W=[rsb])
            P.op("dve", "scalar_tensor_tensor", tf, xt, rs, a_bc[:, r, :], ALU.mult, ALU.mult, R=[xtb, rsb, a_bcb, tfb], W=[tfb])
            P.op("pool", "tensor_tensor", hb, tf, sh_bc[:, r, :], ALU.add, R=[tfb, sh_bcb], W=[hbb])
            if i == 0:
                self.tap(L + "h0", hb, [128, D], [hbb], BF)
            pt, ptb = self.pst()
            for kc in range(KC):
                P.op("pe", "transpose", pt[:, kc * 128:(kc + 1) * 128], hb[:, kc * 128:(kc + 1) * 128], self.ident[:], R=[hbb, "ident"], W=[ptb])
            P.op("act", "copy", self.HT[:, :, i * 128:(i + 1) * 128], pt[:].rearrange("p (k t) -> p k t", k=KC), R=[ptb], W=["HT%d" % i])

        if self.stop == "A1":
            return
        self.phase()
        WB, WBb = self.ar("WBkv", [128, KC, 256], BF)
        KR = [self.ar("kr%d" % j, [128, 128], BF) for j in range(2)]
        KTs = [self.ar("KTs%d" % a, [128, TOK], BF) for a in range(2)]
        Vs = [self.ar("Vs%d" % a, [128, NT, 128], BF) for a in range(2)]
        tmp = [self.ar("hnt%d" % j, [128, 256], F32)[0] for j in range(3)]
        for a, c0 in enumerate((256, 1024)):
            self.load_w(self.w_cols("w_in", l, c0, 256), WB, WBb, (KC, 256))
            for i in range(TT):
                ps, pb = self.proj_tm(WB, WBb, 0, 256, i)
                kr, krb = KR[i % 2]
                self.headnorm_rope(ps, pb, 0, 2, self.knw[a], "knw%d" % a, i, tmp, kr, krb, permute=False)
                if i < NT:
                    kts, ktsb = KTs[a]
                    vs, vsb = Vs[a]
                    self.transpose_to(kr, krb, 1, lambda j, kts=kts, i=i: kts[:, i * 128:(i + 1) * 128], lambda j, ktsb=ktsb: ktsb)
                    P.op("act", "copy", vs[:, i, :], ps[:, 128:256], R=[pb], W=[vsb])
                else:
                    j = i - NT
                    self.transpose_to(kr, krb, 1, lambda jj, a=a, j=j: self.KTc[a][:, j * 128:(j + 1) * 128], lambda jj, a=a: "KTc%d" % a)
                    P.op("act", "copy", self.Vc[a][:, j, :], ps[:, 128:256], R=[pb], W=["Vc%d" % a])
            kts, ktsb = KTs[a]
            vs, vsb = Vs[a]
            P.dma("sp", pay[0].ap()[a * 128:(a + 1) * 128, :], kts, reads=[ktsb], writes=["pay_k%d" % a])
            P.dma("sp", pay[3].ap()[a * 128:(a + 1) * 128, :].rearrange("p (t d) -> p t d", d=128), vs, reads=[vsb],
                  writes=["pay_v%d" % a])
        if l == self.layers[0]:
            self.tap(L + "KTsA", KTs[0][0], [128, TOK], [KTs[0][1]], BF)
            self.tap(L + "VsA", Vs[0][0], [128, NT, 128], [Vs[0][1]], BF)
            self.tap(L + "KTcA", self.KTc[0][:], [128, 256], ["KTc0"], BF)

        self.load_w(self.w_cols("w_in", l, 1536, 256), WB, WBb, (KC, 256))
        BFT, BFTb = self.ar("BFT", [128, 2, TALL], BF)

        def bf_consume(ps, pb, c, t0, n):
            P.op("act", "copy", BFT[:, c, t0:t0 + n], ps[:, 0:n], R=[pb], W=[BFTb])
        self.proj_fm(WB, WBb, 0, 2, TT, bf_consume)
        wf, wfb = self.ar("wf", [64, 4, 64], F32)
        P.dma("sp", wf, W["w_fnet"][l].rearrange("g j c -> j g c"), writes=[wfb])
        P.op("pool", "memset", self.BA[:], 0.0, R=[], W=["BA"])
        for cs, tab in enumerate((self.c64d, self.s64d)):
            for g in range(4):
                ps, pb = self.ps()
                P.op("pe", "matmul", ps[:, 0:64], tab[0:64, :], wf[0:64, g, :], start=True, stop=True, R=[wfb, "c64d", "s64d"], W=[pb])
                rows = slice((g % 2) * 64, (g % 2) * 64 + 64)
                P.op("dve", "tensor_copy", self.BA[rows, cs, g // 2, (g % 2) * 64:(g % 2) * 64 + 64], ps[rows, 0:64], R=[pb, "BA"], W=["BA"])
        UVT, UVTb = self.ar("UVT", [128, 2, 2, TOK], BF)
        for cs in range(2):
            for c in range(2):
                for (t0, n) in GROUPS[:4]:
                    ps, pb = self.ps()
                    P.op("pe", "matmul", ps[:, 0:n], self.BA[:, cs, c, :], BFT[:, c, t0:t0 + n], start=True, stop=True, R=["BA", BFTb], W=[pb])
                    P.op("act", "copy", UVT[:, cs, c, t0:t0 + n], ps[:, 0:n], R=[pb], W=[UVTb])
                for j in range(2):
                    ps, pb = self.ps()
                    P.op("pe", "matmul", ps[:, 0:128], BFT[:, c, TOK + j * 128:TOK + (j + 1) * 128], self.BA[:, cs, c, :], start=True, stop=True, R=["BA", BFTb], W=[pb])
                    P.op("dve", "tensor_copy", self.UVc[:, cs, j, c * 128:(c + 1) * 128], ps[:, 0:128], R=[pb], W=["UVc"])
        for cs in range(2):
            P.dma("sp", pay[1 + cs].ap().rearrange("(c p) t -> p c t", p=128), UVT[:, cs, :, :],
                  reads=[UVTb], writes=["pay_uv%d" % cs])
        self.tap(L + "UVT", UVT, [128, 2, 2, TOK], [UVTb], BF)

        if self.stop == "A":
            return
        payb = [["pay_k0", "pay_k1"], ["pay_uv0"], ["pay_uv1"], ["pay_v0", "pay_v1"]]
        for j in (0, 3, 1, 2):
            P.collective(lambda e, j=j: e.collective_compute("AllGather", ALU.bypass, replica_groups=[[0, 1, 2, 3], [4, 5, 6, 7]],
                                                             ins=[pay[j].ap().opt()], outs=[gat[j].ap().opt()]),
                         reads=payb[j], writes=["gat%d" % j])
        G = [g.ap() for g in gat]

        if self.stop == "G":
            return
        self.phase()
        self.attention(l, 0, ntl, G, pay)
        if self.stop == "B2":
            return
        self.phase()
        self.attention(l, 1, ntl, G, pay)
        if self.stop == "B3":
            return
        self.phase()
        self.fnet(l, ntl, G)
        if self.stop == "B4":
            return
        self.phase()
        self.sgu(l, ntl)
        if l == self.layers[0]:
            for r in range(4):
                self.tap(L + "YB%d" % r, self.YB[r][:], [128, 2, TALL], ["YB%d_%d_%d" % (r, c, i) for c in range(2) for i in range(TT)], BF)
        if self.stop == "B5":
            return
        self.phase()
        self.merge(l, ntl, is_last, x_tile_src)

    def attention(self, l, which, ntl, G, pay):
        P, W = self.P, self.W
        q0 = 0 if which == 0 else 768
        z0 = 512 if which == 0 else 1280
        YB = self.YB[which]
        nqg = [g for g in GROUPS if g[0] // 128 < ntl]
        WQ, WQb = self.ar("WQ", [128, KC, 256], BF)
        WZ, WZb = self.ar("WZ", [128, KC, 256], BF)
        self.load_w(self.w_cols("w_in", l, q0, 256), WQ, WQb, (KC, 256))
        self.load_w(self.w_cols("w_in", l, z0, 256), WZ, WZb, (KC, 256))
        QT = [self.ar("QT%d" % g, [128, TALL], BF) for g in range(2)]
        if which == 0:
            NKB = 2 + 64
            KT, KTb = self.ar("KT", [128, NKB * 128], BF)
            V, Vb = self.ar("V", [128, NKB, 128], BF)
            P.op("act", "copy", KT[:, 0:256], self.KTc[0][:], R=["KTc0"], W=[KTb + "c"])
            P.op("act", "copy", V[:, 0:2, :], self.Vc[0][:], R=["Vc0"], W=[Vb + "c"])
            for s in range(4):
                P.dma("sp", KT[:, 256 + s * TOK:256 + (s + 1) * TOK], G[0][s * 256:s * 256 + 128, :], reads=["gat0"], writes=[KTb + str(s)])
                P.dma("sp", V[:, 2 + s * NT:2 + (s + 1) * NT, :],
                      G[3][s * 256:s * 256 + 128, :].rearrange("p (t d) -> p t d", d=128), reads=["gat3"], writes=[Vb + str(s)])

            def kbufs(kb):
                return [KTb + ("c" if kb < 2 else str((kb - 2) // NT))], [Vb + ("c" if kb < 2 else str((kb - 2) // NT))]
        else:
            NKB = 2 + NT + 8
            KT, KTb = self.ar("KT", [128, NKB * 128], BF)
            V, Vb = self.ar("V", [128, NKB, 128], BF)
            P.op("act", "copy", KT[:, 0:256], self.KTc[1][:], R=["KTc1"], W=[KTb])
            P.op("act", "copy", V[:, 0:2, :], self.Vc[1][:], R=["Vc1"], W=[Vb])
            P.dma("sp", KT[:, 256:256 + TOK], pay[0].ap()[128:256, :], reads=["pay_k1"], writes=[KTb])
            P.dma("sp", V[:, 2:2 + NT, :], pay[3].ap()[128:256, :].rearrange("p (t d) -> p t d", d=128), reads=["pay_v1"], writes=[Vb])
            for s in range(4):
                kb = 2 + NT + s
                P.dma("sp", KT[:, kb * 128:(kb + 1) * 128], G[0][s * 256 + 128:s * 256 + 256, TOK - 128:TOK], reads=["gat0"], writes=[KTb])
                P.dma("sp", V[:, kb, :], G[3][s * 256 + 128:s * 256 + 256, TOK - 128:TOK], reads=["gat3"], writes=[Vb])
                kb = 2 + NT + 4 + s
                P.dma("sp", KT[:, kb * 128:(kb + 1) * 128], G[0][s * 256 + 128:s * 256 + 256, 0:128], reads=["gat0"], writes=[KTb])
                P.dma("sp", V[:, kb, :], G[3][s * 256 + 128:s * 256 + 256, 0:128], reads=["gat3"], writes=[Vb])

            def kbufs(kb):
                return [KTb], [Vb]

        tmp = [self.ar("hnt%d" % j, [128, 256], F32)[0] for j in range(3)]
        QR = [self.ar("QR%d" % j, [128, 256], BF) for j in range(2)]
        for i in range(ntl):
            ps, pb = self.proj_tm(WQ, WQb, 0, 256, i)
            qr, qrb = QR[i % 2]
            self.headnorm_rope(ps, pb, 0, 4, self.qnw[which], "qnw%d" % which, i, tmp, qr, qrb, permute=True)
            self.transpose_to(qr, qrb, 2, lambda j, i=i: QT[j][0][:, i * 128:(i + 1) * 128], lambda j, i=i: QT[j][1] + "_%d" % i)

        def z_consume(ps, pb, c, t0, n):
            P.op("act", "activation", YB[:, c, t0:t0 + n], ps[:, 0:n], AF.Silu, R=[pb], W=["YB%d_%d_%d" % (which, c, i) for i in range(t0 // 128, (t0 + n) // 128)])
        self.proj_fm(WZ, WZb, 0, 2, ntl, z_consume)
        if which == 0 and l == self.layers[0]:
            self.tap("L%dQT0" % l, QT[0][0], [128, TALL], [QT[0][1] + "_%d" % i for i in range(ntl)], BF)

        PT = [self.ar("PT%d" % j, [128, 512], BF) for j in range(3)]
        fin = [self.ar("fin%d" % j, [128, 512], F32) for j in range(2)]

        work = []

        def add_head(g, kv, t0, n, keylist):
            cw = 512 // n
            chunks = [keylist[c:c + cw] for c in range(0, len(keylist), cw)]
            for ci, ch in enumerate(chunks):
                work.append(dict(g=g, kv=kv, t0=t0, n=n, keys=ch, first=(ci == 0), last=(ci == len(chunks) - 1)))

        if which == 0:
            for (t0, n) in nqg:
                keys = [(kb, None) for kb in range(NKB)] if t0 < TOK else [(0, None), (1, None)]
                for g in range(2):
                    for kv in range(2):
                        add_head(g, kv, t0, n, keys)
        else:
            for i in range(ntl):
                if i >= NT:
                    keys = [(0, None), (1, None)]
                else:
                    keys = [(0, None), (1, None), (2 + i, None)]
                    if i > 0:
                        keys.append((2 + i - 1, 0))
                    else:
                        keys += [(2 + NT + s, 2 + s) for s in range(4)]
                    if i < NT - 1:
                        keys.append((2 + i + 1, 1))
                    else:
                        keys += [(2 + NT + 4 + s, 6 + s) for s in range(4)]
                for g in range(2):
                    for kv in range(2):
                        add_head(g, kv, i * 128, 128, keys)

        acc_o, acc_d = self.PS[4], self.PS[5]
        aob, adb = "ps4", "ps5"

        def stage1(w, slot):
            g, kv, t0, n = w["g"], w["kv"], w["t0"], w["n"]
            krows = slice(kv * 64, (kv + 1) * 64)
            qt, qtb = QT[g]
            qbufs = [qtb + "_%d" % i for i in range(t0 // 128, (t0 + n) // 128)]
            ps, pb = self.ps()
            pt, ptb = PT[slot % 3]
            cnt = len(w["keys"])
            for j, (kb, mi) in enumerate(w["keys"]):
                kb_k, _ = kbufs(kb)
                P.op("pe", "matmul", ps[:, j * n:(j + 1) * n], KT[krows, kb * 128:(kb + 1) * 128], qt[krows, t0:t0 + n],
                     start=True, stop=True, R=kb_k + qbufs, W=[pb])
            P.op("act", "activation", pt[:, 0:cnt * n], ps[:, 0:cnt * n], AF.Exp, R=[pb], W=[ptb])
            for j, (kb, mi) in enumerate(w["keys"]):
                if mi is not None:
                    P.op("pool", "tensor_tensor", pt[:, j * n:(j + 1) * n], pt[:, j * n:(j + 1) * n], self.masks[:, mi, 0:n], ALU.mult,
                         R=[ptb, "masks"], W=[ptb])

        def stage2(w, slot):
            g, kv, t0, n = w["g"], w["kv"], w["t0"], w["n"]
            rows = slice(g * 64, (g + 1) * 64)
            krows = slice(kv * 64, (kv + 1) * 64)
            pt, ptb = PT[slot % 3]
            cnt = len(w["keys"])
            for j, (kb, mi) in enumerate(w["keys"]):
                _, kb_v = kbufs(kb)
                st = w["first"] and j == 0
                sp = w["last"] and j == cnt - 1
                P.op("pe", "matmul", acc_o[rows, 0:n], V[:, kb, krows], pt[:, j * n:(j + 1) * n], start=st, stop=sp,
                     tile_position=(0, g * 64), R=kb_v + [ptb], W=[aob])
                P.op("pe", "matmul", acc_d[rows, 0:n], self.ones64[:, :], pt[:, j * n:(j + 1) * n], start=st, stop=sp,
                     tile_position=(0, g * 64), R=["ones64", ptb], W=[adb])
            if not w["last"]:
                return
            (f0, f0b), (f1, f1b) = fin
            if which == 1:
                P.op("dve", "tensor_scalar", f0[rows, 0:n], acc_d[rows, 0:n], self.esink[rows, kv:kv + 1], None, ALU.add,
                     R=[adb, "esink"], W=[f0b])
                P.op("dve", "tensor_copy", f1[rows, 0:n], acc_o[rows, 0:n], R=[aob], W=[f1b])
                P.op("dve", "reciprocal", f0[rows, 0:n], f0[rows, 0:n], R=[f0b], W=[f0b])
            else:
                P.op("dve", "reciprocal", f0[rows, 0:n], acc_d[rows, 0:n], R=[adb], W=[f0b])
                P.op("dve", "tensor_copy", f1[rows, 0:n], acc_o[rows, 0:n], R=[aob], W=[f1b])
            P.op("pool", "tensor_tensor", f1[rows, 0:n], f1[rows, 0:n], f0[rows, 0:n], ALU.mult, R=[f1b, f0b], W=[f1b])
            ybb = ["YB%d_%d_%d" % (which, kv, i) for i in range(t0 // 128, (t0 + n) // 128)]
            P.op("pool", "tensor_tensor", YB[rows, kv, t0:t0 + n], f1[rows, 0:n], YB[rows, kv, t0:t0 + n], ALU.mult,
                 R=[f1b] + ybb, W=ybb)

        LOOK = 1
        for k in range(len(work) + LOOK):
            if k < len(work):
                stage1(work[k], k)
            if k >= LOOK:
                stage2(work[k - LOOK], k - LOOK)

    def fnet(self, l, ntl, G):
        P, W = self.P, self.W
        YB = self.YB[2]
        WZ, WZb = self.ar("WZ", [128, KC, 256], BF)
        self.load_w(self.w_cols("w_in", l, 1792, 256), WZ, WZb, (KC, 256))

        def z_consume(ps, pb, c, t0, n):
            P.op("act", "activation", YB[:, c, t0:t0 + n], ps[:, 0:n], AF.Silu, R=[pb], W=["YB2_%d_%d" % (c, i) for i in range(t0 // 128, (t0 + n) // 128)])
        self.proj_fm(WZ, WZb, 0, 2, ntl, z_consume)
        MC, MCb = self.ar("MC", [128, 64, 32], BF)
        MS, MSb = self.ar("MS", [128, 64, 32], BF)
        P.dma("sp", MC, self.C["mc"].rearrange("p (a b) -> p a b", a=64), writes=[MCb])
        P.dma("sp", MS, self.C["ms"].rearrange("p (a b) -> p a b", a=64), writes=[MSb])
        Z = [self.ar("Z%d" % j, [128, 64, 128], BF) for j in range(2)]
        T3, T3b = self.ar("T3", [128, 64, 128], BF)
        sp = 0
        for c in range(2):
            for hf in range(2):
                z, zb = Z[sp % 2]
                sp += 1
                for s in range(4):
                    for ri in range(2):
                        r0 = s * 256 + c * 128 + hf * 64
                        P.dma("sp", z[ri * 64 + s * 16:ri * 64 + (s + 1) * 16, :, :],
                              G[1 + ri][r0:r0 + 64, :].rearrange("ch (t n) -> t ch n", n=128), reads=["gat%d" % (1 + ri)], writes=[zb])
                for ch4 in range(16):
                    ps, pb = self.ps()
                    for j in range(4):
                        ch = ch4 * 4 + j
                        P.op("pe", "matmul", ps[:, j * 128:(j + 1) * 128], z[:, ch, :], self.m1[:, :],
                                                                            start=True, stop=True, R=[zb, "m1"], W=[pb])
                    if ch4 % 2 == 0:
                        P.op("act", "copy", T3[:, ch4 * 4:ch4 * 4 + 4, :], ps[:].rearrange("p (a b) -> p a b", a=4), R=[pb], W=[T3b])
                    else:
                        P.op("dve", "tensor_copy", T3[:, ch4 * 4:ch4 * 4 + 4, :], ps[:].rearrange("p (a b) -> p a b", a=4), R=[pb], W=[T3b])
                rows = slice(hf * 64, (hf + 1) * 64)
                for bank in range(4):
                    ps, pb = self.ps()
                    for kk in range(16):
                        k1 = bank * 16 + kk
                        P.op("pe", "matmul", ps[rows, kk * 32:(kk + 1) * 32], T3[:, :, k1], MC[:, k1, :],
                                                                         start=True, stop=False, tile_position=(0, hf * 64), R=[T3b, MCb], W=[pb])
                        P.op("pe", "matmul", ps[rows, kk * 32:(kk + 1) * 32], T3[:, :, 64 + k1], MS[:, k1, :],
                                                                         start=False, stop=True, tile_position=(0, hf * 64), R=[T3b, MSb], W=[pb])
                    yv = YB[rows, c, 0:TOK].rearrange("p (k2 k1) -> p k1 k2", k1=64)[:, bank * 16:(bank + 1) * 16, :]
                    ybb = ["YB2_%d_%d" % (c, i) for i in range(NT)]
                    P.op("dve", "tensor_tensor", yv, ps[rows, :].rearrange("p (a b) -> p a b", a=16), yv, ALU.mult, R=[pb] + ybb, W=ybb)
        if ntl > NT:
            for c in range(2):
                ps, pb = self.ps()
                k = 0
                for uv, tab in ((0, self.c256), (1, self.s256n)):
                    for j in range(2):
                        P.op("pe", "matmul", ps[:, 0:256], self.UVc[:, uv, j, c * 128:(c + 1) * 128], tab[:, j, :], start=(k == 0), stop=(k == 3), R=["UVc", "c256", "s256n"], W=[pb])
                        k += 1
                ybb = ["YB2_%d_%d" % (c, i) for i in range(NT, TT)]
                P.op("dve", "tensor_tensor", YB[:, c, TOK:TALL], ps[:, 0:256], YB[:, c, TOK:TALL], ALU.mult, R=[pb] + ybb, W=ybb)

    def sgu(self, l, ntl):
        P, W = self.P, self.W
        YB = self.YB[3]
        WU, WUb = self.ar("WU", [128, KC, 256], BF)
        WV, WVb = self.ar("WV", [128, KC, 256], BF)
        WZ, WZb = self.ar("WZ", [128, KC, 256], BF)
        self.load_w(self.w_cols("w_in", l, 2048, 256), WU, WUb, (KC, 256))
        self.load_w(self.w_cols("w_in", l, 2304, 256), WV, WVb, (KC, 256))
        self.load_w(self.w_cols("w_in", l, 2560, 256), WZ, WZb, (KC, 256))
        GU, GUb = self.ar("GU", [128, 2, TALL], BF)
        GV, GVb = self.ar("GV", [128, TT, 256], BF)
        wsp, wspb = self.ar("wsp", [128, 4, 128], BF)
        bsp, bspb = self.ar("bsp", [128, 2, 128], F32)
        self.load_w(W["w_spT"][l], wsp, wspb, (4, 128))
        for g in range(4):
            P.dma("sp", bsp[(g % 2) * 64:(g % 2) * 64 + 64, g // 2, :], W["b_sp"][l, g:g + 1, :].partition_broadcast(64), writes=[bspb])

        def u_consume(ps, pb, c, t0, n):
            P.op("act", "activation", GU[:, c, t0:t0 + n], ps[:, 0:n], AF.Gelu, R=[pb], W=[GUb])

        def z_consume(ps, pb, c, t0, n):
            P.op("act", "activation", YB[:, c, t0:t0 + n], ps[:, 0:n], AF.Silu, R=[pb], W=["YB3_%d_%d" % (c, i) for i in range(t0 // 128, (t0 + n) // 128)])
        self.proj_fm(WU, WUb, 0, 2, ntl, u_consume)
        self.proj_fm(WZ, WZb, 0, 2, ntl, z_consume)
        for i in range(ntl):
            ps, pb = self.proj_tm(WV, WVb, 0, 256, i)
            P.op("act", "activation", GV[:, i, :], ps[:, 0:256], AF.Gelu, R=[pb], W=[GVb + "_%d" % i])
        t1, t1b = self.ar("sg1", [128, 512], F32)
        t2, t2b = self.ar("sg2", [128, 512], F32)
        for i0 in range(0, ntl, 4):
            nt = min(4, ntl - i0)
            n = nt * 128
            for c in range(2):
                ps, pb = self.ps()
                for j in range(nt):
                    i = i0 + j
                    for gg in range(2):
                        g = 2 * c + gg
                        P.op("pe", "matmul", ps[gg * 64:(gg + 1) * 64, j * 128:(j + 1) * 128], GV[:, i, g * 64:(g + 1) * 64], wsp[:, g, :],
                            start=True, stop=True, tile_position=(0, gg * 64), R=[GVb + "_%d" % i, wspb], W=[pb])
                P.op("dve", "tensor_tensor", t1[:, 0:nt * 128].rearrange("p (a b) -> p a b", a=nt), ps[:, 0:nt * 128].rearrange("p (a b) -> p a b", a=nt),
                    bsp[:, c, :].unsqueeze(1).broadcast_to([128, nt, 128]), ALU.add, R=[pb, bspb, t1b], W=[t1b])
                P.op("pool", "tensor_tensor", t2[:, 0:n], t1[:, 0:n], GU[:, c, i0 * 128:i0 * 128 + n], ALU.mult, R=[t1b, GUb, t2b], W=[t2b])
                ybb = ["YB3_%d_%d" % (c, i) for i in range(i0, i0 + nt)]
                P.op("dve", "tensor_tensor", YB[:, c, i0 * 128:i0 * 128 + n], t2[:, 0:n], YB[:, c, i0 * 128:i0 * 128 + n], ALU.mult, R=[t2b] + ybb, W=ybb)

    def merge(self, l, ntl, is_last, x_tile_src):
        P, W = self.P, self.W
        groups = [g for g in GROUPS if g[0] // 128 < ntl]
        MT, MTb = self.ar("MT", [128, KC, TALL], BF)
        WBR = [self.ar("WBR%d" % r, [128, 2, D], BF) for r in range(4)]
        for r in range(4):
            self.load_w(W["w_br"][l, r].rearrange("(kc p) n -> p kc n", p=128), WBR[r][0], WBR[r][1], (2, D))
        bm, bmb = self.ar("bm", [128, 32], F32)
        P.dma("sp", bm, W["b_mergeT"][l], writes=[bmb])
        WM = [self.ar("WM%d" % j, [128, 4, KC, 128], BF) for j in range(2)]
        sg = [self.ar("sg%d" % j, [128, 512], F32) for j in range(2)]
        acc = [self.ar("acc%d" % j, [128, 512], F32) for j in range(2)]
        tmpm = [self.ar("tmpm%d" % j, [128, 512], F32) for j in range(2)]
        cnt = 0
        for dc in range(KC):
            wm, wmb = WM[dc % 2]
            for r in range(4):
                i = self.wsf_i % 2
                self.wsf_i += 1
                st = self.WSF[i][:, 0:1024].rearrange("p (a b) -> p a b", a=KC)
                P.dma("sp", st, self.w_cols("w_merge", l, r * D + dc * 128, 128), writes=["wsf%d" % i])
                P.op("pool", "tensor_copy", wm[:, r, :, :], st, R=["wsf%d" % i], W=[wmb + "_%d" % r])
            for (t0, n) in groups:
                htb = ["HT%d" % i for i in range(t0 // 128, (t0 + n) // 128)]
                a_, ab = acc[cnt % 2]
                for r in range(4):
                    psg, pgb = self.ps()
                    for kc in range(KC):
                        P.op("pe", "matmul", psg[:, 0:n], wm[:, r, kc, :], self.HT[:, kc, t0:t0 + n], start=(kc == 0), stop=(kc == KC - 1), R=[wmb + "_%d" % r] + htb, W=[pgb])
                    psy, pyb = self.ps()
                    ybb = ["YB%d_%d_%d" % (r, c, i) for c in range(2) for i in range(t0 // 128, (t0 + n) // 128)]
                    for k2 in range(2):
                        P.op("pe", "matmul", psy[:, 0:n], WBR[r][0][:, k2, dc * 128:(dc + 1) * 128], self.YB[r][:, k2, t0:t0 + n],
                            start=(k2 == 0), stop=(k2 == 1), R=[WBR[r][1]] + ybb, W=[pyb])
                    s_, sb_ = sg[(cnt * 4 + r) % 2]
                    P.op("act", "activation", s_[:, 0:n], psg[:, 0:n], AF.Sigmoid,
                                                                         bias=bm[:, r * 8 + dc:r * 8 + dc + 1], scale=1.0, R=[pgb, bmb], W=[sb_])
                    if r == 0:
                        P.op("dve", "tensor_tensor", a_[:, 0:n], psy[:, 0:n], s_[:, 0:n], ALU.mult, R=[pyb, sb_, ab], W=[ab])
                    else:
                        t_, tb_ = tmpm[r % 2]
                        P.op("dve", "tensor_tensor", t_[:, 0:n], psy[:, 0:n], s_[:, 0:n], ALU.mult, R=[pyb, sb_, tb_], W=[tb_])
                        if r < 3:
                            P.op("pool", "tensor_tensor", a_[:, 0:n], a_[:, 0:n], t_[:, 0:n], ALU.add, R=[ab, tb_], W=[ab])
                        else:
                            P.op("pool", "tensor_tensor", MT[:, dc, t0:t0 + n], a_[:, 0:n], t_[:, 0:n], ALU.add, R=[ab, tb_], W=[MTb + "_%d_%d" % (dc, t0)])
                cnt += 1
        self.tap("L%dMT" % l, MT, [128, KC, TALL], [MTb + "_%d_%d" % (dc, g[0]) for dc in range(KC) for g in groups], BF)
        if self.stop == "C2":
            return
        self.P.barrier()
        self.ar_off = (KC * TALL * 2 + 63) // 64 * 64
        self.ar_gen += 1
        WO, WOb = self.ar("WO", [128, KC, D], BF)
        for j in range(4):
            self.load_w(self.w_cols("w_out", l, j * 256, 256), WO[:, :, j * 256:(j + 1) * 256], WOb + "_%d" % j, (KC, 256))
        gt_bc, gtb = self.ar("gt_bc", [128, 2, D], F32)
        for r in range(2 if ntl > NT else 1):
            for half in range(2):
                ps, pb = self.ps()
                P.op("pe", "matmul", ps[:, :], self.sel2[0:2, r * 128:(r + 1) * 128], self.gate2[0:2, half * 512:(half + 1) * 512],
                    start=True, stop=True, R=["sel2", "gate2"], W=[pb])
                P.op("act", "copy", gt_bc[:, r, half * 512:(half + 1) * 512], ps[:, :], R=[pb], W=[gtb])
        XT = [self.ar("XT%d" % i, [128, D], F32) for i in range(2)]
        XO = [self.ar("XO%d" % i, [128, D], F32) for i in range(2)]
        for i in range(ntl):
            r = 0 if i < NT else 1
            xt, xtb = XT[i % 2]
            xo, xob = XO[i % 2]
            P.dma("sp", xt, x_tile_src(i), writes=[xtb])
            for half in range(2):
                ps, pb = self.ps()
                mtb = [MTb + "_%d_%d" % (dc, g[0]) for dc in range(KC) for g in GROUPS if g[0] <= i * 128 < g[0] + g[1]]
                for kc in range(KC):
                    P.op("pe", "matmul", ps[:, :], MT[:, kc, i * 128:(i + 1) * 128], WO[:, kc, half * 512:(half + 1) * 512],
                        start=(kc == 0), stop=(kc == KC - 1), R=mtb + [WOb + "_%d" % (2 * half), WOb + "_%d" % (2 * half + 1)], W=[pb])
                P.op("dve", "tensor_tensor", xo[:, half * 512:(half + 1) * 512], ps[:, :], gt_bc[:, r, half * 512:(half + 1) * 512], ALU.mult, R=[pb, gtb, xob], W=[xob + "h%d" % half])
                P.op("pool", "tensor_tensor", xo[:, half * 512:(half + 1) * 512], xo[:, half * 512:(half + 1) * 512], xt[:, half * 512:(half + 1) * 512], ALU.add, R=[xob + "h%d" % half, xtb], W=[xob + "h%d" % half])
            if is_last:
                dst = self.out[i * 128:(i + 1) * 128, :]
            elif self.last or l != self.layers[-1]:
                dst = self.xs[i * 128:(i + 1) * 128, :]
            else:
                dst = self.xs_out[i * 128:(i + 1) * 128, :]
            P.dma("sp", dst, xo, reads=[xob + "h0", xob + "h1"], writes=["xs%d" % i])


def is_last_layer(l):
    return l == 1


_CONST_CACHE = {}


def _consts(core):
    if core not in _CONST_CACHE:
        _CONST_CACHE[core] = host_consts(core)
    return _CONST_CACHE[core]


def _weights_layout(inp):
    f = lambda a: np.ascontiguousarray(np.asarray(a, dtype=np.float32))
    w = {k: f(inp[k]) for k in ("norm_w", "w_ada", "b_ada", "w_in", "qn_a", "kn_a", "qn_d", "kn_d", "sink_d",
                                "w_fnet", "b_sp", "w_br", "w_merge", "w_out")}
    w["w_spT"] = np.ascontiguousarray(f(inp["w_sp"]).transpose(0, 3, 1, 2))
    w["b_mergeT"] = np.ascontiguousarray(f(inp["b_merge"]).reshape(2, 32, 128).transpose(0, 2, 1))
    return w


def _run(builder_kwargs, per_core_extra, inp):
    b = Builder(**builder_kwargs)
    nc = b.build()
    w = _weights_layout(inp)
    c = np.asarray(inp["c"], np.float32)
    c_ctx = np.asarray(inp["c_ctx"], np.float32)
    in_maps = []
    for core in range(NCORES):
        bi = core // 4
        cv = np.stack([c[bi], c_ctx], 0)
        cvT = np.ascontiguousarray(cv.reshape(2, KC, 128).transpose(2, 1, 0))
        m = {"cvecT": cvT}
        m.update(w)
        m.update(_consts(core))
        m.update(per_core_extra(core))
        in_maps.append(m)
    res = run_bass_kernel_spmd(nc, in_maps, core_ids=list(range(NCORES)))
    return res.results, b


def kernel(**inp):
    x = np.asarray(inp["x"], np.float32)
    ctx = np.asarray(inp["ctx"], np.float32)

    def extra(core):
        bi, q = divmod(core, 4)
        return {"x": np.ascontiguousarray(x[bi, q * TOK:(q + 1) * TOK]), "ctx": np.ascontiguousarray(ctx[bi])}
    results, _ = _run(dict(layers=(0, 1), first=True, last=True), extra, inp)
    out = np.empty((2, SEQ, D), np.float32)
    for core in range(NCORES):
        bi, q = divmod(core, 4)
        out[bi, q * TOK:(q + 1) * TOK] = results[core]["out"]
    return out
```

```python
import numpy as np
import ml_dtypes
import concourse.bass as bass
import concourse.mybir as mybir
from concourse.bass_utils import run_bass_kernel_spmd

F32 = mybir.dt.float32
BF = mybir.dt.bfloat16
AF = mybir.ActivationFunctionType
ALU = mybir.AluOpType
AX = mybir.AxisListType
NPBF = ml_dtypes.bfloat16

NCORES = 8
TOK = 2048
NT = 16
CT = 2
TT = NT + CT
TALL = TT * 128
D = 1024
KC = 8
EPS = 1e-6
SEQ = 8192
INW = 2816
GROUPS = [(0, 512), (512, 512), (1024, 512), (1536, 512), (2048, 256)]


class Buf:
    __slots__ = ("name", "w", "rs")

    def __init__(self, name):
        self.name = name
        self.w = None
        self.rs = []


class Op:
    __slots__ = ("eng", "fn", "deps", "kind", "sem", "semval", "signal")

    def __init__(self, eng, fn, kind):
        self.eng = eng
        self.fn = fn
        self.kind = kind
        self.deps = []
        self.sem = None
        self.semval = 0
        self.signal = False


class Prog:
    ENGS = ("pe", "act", "dve", "pool", "sp")
    NS = 24

    def __init__(self, nc):
        self.nc = nc
        self.ops = {e: [] for e in self.ENGS}
        self.dma_count = 0
        self.dma_last = [None] * self.NS
        self.ncoll = 0
        self.bufs = {}
        self.pending = {e: [] for e in self.ENGS}
        self.last_c = {e: None for e in self.ENGS}
        self.colls = []

    def buf(self, name):
        b = self.bufs.get(name)
        if b is None:
            b = self.bufs[name] = Buf(name)
        return b

    def _add(self, op, d):
        if d is op:
            return
        op.deps.append(d)
        if d.kind == "c":
            d.signal = True

    def _deps(self, op, reads, writes):
        deps = []
        for b in reads:
            b = self.buf(b)
            if b.w is not None:
                deps.append(("raw", b.w))
            b.rs.append(op)
        for b in writes:
            b = self.buf(b)
            if b.w is not None:
                deps.append(("waw", b.w))
            for r in b.rs:
                if r is not op:
                    deps.append(("war", r))
            b.rs = []
            b.w = op
        for kind, d in deps:
            if d is op:
                continue
            if d.kind == "c" and op.kind == "c" and d.eng == op.eng:
                if op.eng == "pe":
                    continue
            self._add(op, d)
        if self.pending[op.eng]:
            for d in self.pending[op.eng]:
                self._add(op, d)
            self.pending[op.eng] = []

    def op(self, eng, meth, *args, R=(), W=(), **kw):
        fn = (lambda e, meth=meth, args=args, kw=kw: getattr(e, meth)(*args, **kw))
        o = Op(eng, fn, "c")
        self._deps(o, R, W)
        self.ops[eng].append(o)
        self.last_c[eng] = o
        return o

    def dma(self, q, out, in_, reads=(), writes=(), **kw):
        o = Op(q, (lambda e, out=out, in_=in_, kw=kw: e.dma_start(out=out, in_=in_, **kw)), "d")
        self._deps(o, reads, writes)
        slot = self.dma_count % self.NS
        self.dma_count += 1
        prev = self.dma_last[slot]
        o.sem = slot
        o.semval = (prev.semval if prev else 0) + 16
        if prev is not None:
            o.deps.append(prev)
        self.dma_last[slot] = o
        self.ops[q].append(o)
        return o

    def collective(self, fn, reads=(), writes=()):
        o = Op("pool", fn, "x")
        self._deps(o, reads, writes)
        o.sem = self.ncoll
        self.ncoll += 1
        self.ops["pool"].append(o)
        self.colls.append(o)
        return o

    def barrier(self):
        lasts = [o for o in self.last_c.values() if o is not None]
        lasts += [o for o in self.dma_last if o is not None]
        lasts += list(self.colls)
        for e in self.ENGS:
            self.pending[e] = list(lasts)

    def emit(self):
        nc = self.nc
        for e in self.ENGS:
            cnt = 0
            for o in self.ops[e]:
                if o.kind == "c" and o.signal:
                    cnt += 1
                    o.semval = cnt
        esem = {e: nc.alloc_semaphore("es_" + e) for e in self.ENGS}
        dsem = [nc.alloc_semaphore("ds_%d" % i) for i in range(self.NS)]
        csem = [nc.alloc_semaphore("cs_%d" % i) for i in range(self.ncoll)]

        def semof(d):
            if d.kind == "c":
                return esem[d.eng], d.semval
            if d.kind == "d":
                return dsem[d.sem], d.semval
            return csem[d.sem], 1

        def gen(ename):
            def body(eng):
                waited = {}
                for o in self.ops[ename]:
                    for d in o.deps:
                        s, v = semof(d)
                        if waited.get(id(s), 0) >= v:
                            continue
                        waited[id(s)] = v
                        eng.wait_ge(s, v)
                    ins = o.fn(eng)
                    if o.kind == "c":
                        if o.signal:
                            ins.then_inc(esem[ename], 1)
                    elif o.kind == "d":
                        ins.then_inc(dsem[o.sem], 16)
                    else:
                        ins.then_inc(csem[o.sem])
                last = {}
                for o in self.ops[ename]:
                    if o.kind in ("d", "x"):
                        s, v = semof(o)
                        if v > last.get(id(s), (None, 0))[1]:
                            last[id(s)] = (s, v)
                for s, v in last.values():
                    if waited.get(id(s), 0) < v:
                        eng.wait_ge(s, v)
            return body

        with nc.Block() as block:
            block.sync(gen("sp"))
            block.scalar(gen("act"))
            block.vector(gen("dve"))
            block.tensor(gen("pe"))
            block.gpsimd(gen("pool"))


def host_consts(core):
    b, q = divmod(core, 4)
    t = (q * TOK + np.arange(TOK)).astype(np.int64)
    r = (t // 64).astype(np.float32)
    col = (t % 64).astype(np.float32)
    inv = (np.float32(10000.0) ** (-np.arange(16, dtype=np.float32) / np.float32(16))).astype(np.float32)
    ar = r[:, None] * inv[None, :]
    ac = col[:, None] * inv[None, :]
    cr, sr, cc, sc = np.cos(ar), np.sin(ar), np.cos(ac), np.sin(ac)
    cos64 = np.concatenate([cr, cr, cc, cc], 1).astype(np.float32)
    sin64 = np.concatenate([-sr, sr, -sc, sc], 1).astype(np.float32)
    cos64 = np.ascontiguousarray(cos64.reshape(NT, 128, 64).transpose(1, 0, 2))
    sin64 = np.ascontiguousarray(sin64.reshape(NT, 128, 64).transpose(1, 0, 2))
    n1 = np.arange(64)
    th = 2 * np.pi * np.outer(n1, n1) / 64.0
    C, S = np.cos(th), np.sin(th)
    sc1 = 1.0 / np.sqrt(SEQ * 64.0)
    m1 = np.zeros((128, 128), np.float64)
    m1[:64, :64] = C
    m1[64:, :64] = -S
    m1[:64, 64:] = S
    m1[64:, 64:] = C
    m1 *= sc1
    n2 = np.arange(128)[:, None, None]
    k1 = np.arange(64)[None, :, None]
    k2 = (np.arange(32) + 32 * q)[None, None, :]
    ph = 2 * np.pi * n2 * (64 * k2 + k1) / float(SEQ)
    mc = np.cos(ph).reshape(128, 64 * 32)
    ms = (-np.sin(ph)).reshape(128, 64 * 32)
    n = np.arange(256)
    t256 = 2 * np.pi * np.outer(n, n) / 256.0
    sc2 = 1.0 / np.sqrt(256 * 64.0)
    c256 = (np.cos(t256) * sc2).reshape(2, 128, 256).transpose(1, 0, 2)
    s256n = (-np.sin(t256) * sc2).reshape(2, 128, 256).transpose(1, 0, 2)
    c64d = np.concatenate([C, C], 1).astype(np.float32)
    s64d = np.concatenate([S, S], 1).astype(np.float32)
    pk = np.arange(128)[:, None]
    pq = np.arange(128)[None, :]
    mP = (pk >= pq).astype(np.float32)
    mN = (pk <= pq).astype(np.float32)
    masks = np.zeros((128, 10, 128), np.float32)
    masks[:, 0] = mP
    masks[:, 1] = mN
    for s in range(4):
        if s == q - 1:
            masks[:, 2 + s] = mP
        if s == q + 1:
            masks[:, 6 + s] = mN
    sel2 = np.zeros((2, 256), np.float32)
    sel2[0, :128] = 1.0
    sel2[1, 128:] = 1.0
    return {
        "cos64": cos64, "sin64": sin64,
        "m1": m1.astype(NPBF), "mc": mc.astype(NPBF), "ms": ms.astype(NPBF),
        "c256": np.ascontiguousarray(c256).astype(NPBF), "s256n": np.ascontiguousarray(s256n).astype(NPBF),
        "c64d": c64d, "s64d": s64d,
        "masks": masks.astype(NPBF), "sel2": sel2,
        "ident": np.eye(128, dtype=np.float32),
    }


CONST_SPECS = {
    "cos64": ([128, NT, 64], F32), "sin64": ([128, NT, 64], F32),
    "m1": ([128, 128], BF), "mc": ([128, 2048], BF), "ms": ([128, 2048], BF),
    "c256": ([128, 2, 256], BF), "s256n": ([128, 2, 256], BF),
    "c64d": ([64, 128], F32), "s64d": ([64, 128], F32),
    "masks": ([128, 10, 128], BF), "sel2": ([2, 256], F32), "ident": ([128, 128], F32),
}

WEIGHT_SPECS = {
    "norm_w": [2, D], "w_ada": [2, D, 3 * D], "b_ada": [2, 3 * D], "w_in": [2, D, INW],
    "qn_a": [2, 64], "kn_a": [2, 64], "qn_d": [2, 64], "kn_d": [2, 64], "sink_d": [2, 4],
    "w_fnet": [2, 4, 64, 64], "w_spT": [2, 128, 4, 128], "b_sp": [2, 4, 128],
    "w_br": [2, 4, 256, D], "w_merge": [2, D, 4 * D], "b_mergeT": [2, 128, 32], "w_out": [2, D, D],
}


class Builder:
    def __init__(self, layers=(0, 1), first=True, last=True, debug=(), stop=None):
        self.stop = stop
        self.layers = tuple(layers)
        self.first = first
        self.last = last
        self.debug = set(debug)
        nc = self.nc = bass.Bass("TRN2", target_bir_lowering=False)
        self.P = Prog(nc)
        self.dbg_outs = {}
        self._uid = 0

    def uid(self, p="t"):
        self._uid += 1
        return "%s%d" % (p, self._uid)

    def dram_in(self, name, shape, dt=F32):
        return self.nc.dram_tensor(name, list(shape), dt, kind="ExternalInput").ap()

    def dram_out(self, name, shape, dt=F32):
        return self.nc.dram_tensor(name, list(shape), dt, kind="ExternalOutput").ap()

    def sb(self, name, shape, dt):
        return self.nc.alloc_sbuf_tensor("s_" + name, list(shape), dt)

    def tap(self, name, view, shape, reads, dt=F32):
        if name not in self.debug:
            return
        o = self.dram_out("dbg_" + name, shape, dt)
        self.dbg_outs["dbg_" + name] = (shape, dt)
        self.P.dma("sp", o, view, reads=reads)

    def build(self):
        nc, P = self.nc, self.P
        if self.first:
            self.x_in = self.dram_in("x", [TOK, D])
            self.ctx_in = self.dram_in("ctx", [256, D])
        else:
            self.xs_in = self.dram_in("xs_in", [TALL, D])
        self.cvecT = self.dram_in("cvecT", [128, KC, 2])
        self.W = {k: self.dram_in(k, s) for k, s in WEIGHT_SPECS.items()}
        self.C = {k: self.dram_in(k, s, dt) for k, (s, dt) in CONST_SPECS.items()}
        if self.last:
            self.out = self.dram_out("out", [TOK, D])
        else:
            self.xs_out = self.dram_out("xs_out", [TALL, D])
        self.xs = nc.dram_tensor("xs", [TALL, D], F32).ap()
        self.payload = [[nc.dram_tensor("payload%d_%d" % (l, j), [256, TOK], BF) for j in range(4)] for l in range(2)]
        self.gathered = [[nc.dram_tensor("gathered%d_%d" % (l, j), [1024, TOK], BF) for j in range(4)] for l in range(2)]

        self.HT = self.sb("HT", [128, KC, TALL], BF)
        self.YB = [self.sb("YB%d" % r, [128, 2, TALL], BF) for r in range(4)]
        self.WSF = [self.sb("WSF%d" % i, [128, 2048], F32) for i in range(2)]
        self.wsf_i = 0
        self.cos64 = self.sb("cos64", [128, NT, 64], F32)
        self.sin64 = self.sb("sin64", [128, NT, 64], F32)
        self.identf = self.sb("identf", [128, 128], F32)
        self.ident = self.sb("ident", [128, 128], BF)
        self.m1 = self.sb("m1", [128, 128], BF)
        self.c256 = self.sb("c256", [128, 2, 256], BF)
        self.s256n = self.sb("s256n", [128, 2, 256], BF)
        self.c64d = self.sb("c64d", [64, 128], F32)
        self.s64d = self.sb("s64d", [64, 128], F32)
        self.masks = self.sb("masks", [128, 10, 128], BF)
        self.sel2 = self.sb("sel2", [2, 256], F32)
        self.ones64 = self.sb("ones64", [128, 64], BF)
        self.scT = self.sb("scT", [128, KC, 2], F32)
        self.gate2 = self.sb("gate2", [2, D], F32)
        self.knw = [self.sb("knw%d" % i, [128, 128], F32) for i in range(2)]
        self.qnw = [self.sb("qnw%d" % i, [128, 256], F32) for i in range(2)]
        self.esink = self.sb("esink", [128, 4], F32)
        self.KTc = [self.sb("KTc%d" % i, [128, 256], BF) for i in range(2)]
        self.Vc = [self.sb("Vc%d" % i, [128, 2, 128], BF) for i in range(2)]
        self.UVc = self.sb("UVc", [128, 2, 2, 256], BF)
        self.BA = self.sb("BA", [128, 2, 2, 128], BF)
        self.stat = self.sb("stat", [128, 64], F32)
        self.stat_i = 0
        self.PSW = [nc.alloc_psum_tensor("psw%d" % i, [128, 1024], F32) for i in range(2)]
        self.PS = [self.PSW[i // 2][:, (i % 2) * 512:(i % 2 + 1) * 512] for i in range(4)]
        self.PS += [nc.alloc_psum_tensor("ps%d" % i, [128, 512], F32) for i in (4, 5)]
        self.PST = [nc.alloc_psum_tensor("pst%d" % i, [128, 1024], BF) for i in range(2)]
        self.ps_i = 0
        self.pst_i = 0
        self.ARENA_BYTES = 88 * 1024
        self.arena = self.sb("arena", [128, self.ARENA_BYTES // 2], BF)
        self.ar_off = 0
        self.ar_gen = 0

        for k in ("cos64", "sin64", "m1", "c256", "s256n", "c64d", "s64d", "masks", "sel2"):
            P.dma("sp", getattr(self, k)[:], self.C[k], writes=[k])
        P.dma("sp", self.identf[:], self.C["ident"], writes=["identf"])
        P.op("dve", "tensor_copy", self.ident[:], self.identf[:], R=["identf"], W=["ident"])
        P.op("pool", "memset", self.ones64[:], 1.0, R=[], W=["ones64"])
        P.dma("sp", self.scT[:], self.cvecT, writes=["scT"])
        P.op("act", "activation", self.scT[:], self.scT[:], AF.Silu, R=["scT"], W=["scT"])

        for li, l in enumerate(self.layers):
            is_first = self.first and li == 0
            is_last = self.last and li == len(self.layers) - 1
            self.layer(l, is_first, is_last, x_from_input=is_first,
                       x_src_xs_in=(not self.first and li == 0))
        if self.stop is not None:
            dst = self.out if self.last else self.xs_out
            P.dma("sp", dst[0:128, :], self.identf[:, :].unsqueeze(1).broadcast_to([128, 8, 128]) if False else self.cos64[:, :, :], reads=["cos64"])
        P.emit()
        return nc

    def phase(self):
        self.P.barrier()
        self.ar_off = 0
        self.ar_gen += 1

    def ar(self, name, shape, dt):
        n = int(np.prod(shape[1:]))
        nbytes = n * (4 if dt == F32 else 2)
        nbytes = (nbytes + 63) // 64 * 64
        assert self.ar_off + nbytes <= self.ARENA_BYTES, (name, self.ar_off, nbytes)
        e0 = self.ar_off // 2
        v = self.arena[:, e0:e0 + nbytes // 2]
        if dt == F32:
            v = v.bitcast(F32)
        v = v[:, 0:n]
        self.ar_off += nbytes
        if shape[0] != 128:
            v = v[0:shape[0]]
        if len(shape) == 3:
            v = v.rearrange("p (a b) -> p a b", a=shape[1])
        elif len(shape) == 4:
            v = v.rearrange("p (a b c) -> p a b c", a=shape[1], b=shape[2])
        return v, "%s_g%d" % (name, self.ar_gen)

    def ps(self):
        i = self.ps_i % 4
        self.ps_i += 1
        return self.PS[i], "ps%d" % i

    def pst(self):
        i = self.pst_i % 2
        self.pst_i += 1
        return self.PST[i], "pst%d" % i

    def statcol(self, n=1):
        c = (self.stat_i % 16) * 4
        self.stat_i += 1
        return self.stat[:, c:c + n], "stat%d" % c

    def load_w(self, src, dst, dstbuf, shape3, q="sp"):
        P = self.P
        i = self.wsf_i % 2
        self.wsf_i += 1
        a, b = shape3
        st = self.WSF[i][:, 0:a * b].rearrange("p (a b) -> p a b", a=a)
        P.dma(q, st, src, writes=["wsf%d" % i])
        P.op("pool", "tensor_copy", dst, st, R=["wsf%d" % i], W=[dstbuf])

    def w_cols(self, name, l, c0, n):
        return self.W[name][l, :, c0:c0 + n].rearrange("(kc p) n -> p kc n", p=128)

    def proj_fm(self, wb, wbuf, c0, nchunks, tiles, consume):
        P = self.P
        for c in range(nchunks):
            for (t0, n) in GROUPS:
                if t0 // 128 >= tiles:
                    continue
                ps, pb = self.ps()
                htb = ["HT%d" % i for i in range(t0 // 128, (t0 + n) // 128)]
                for kc in range(KC):
                    P.op("pe", "matmul", ps[:, 0:n], wb[:, kc, c0 + c * 128:c0 + (c + 1) * 128], self.HT[:, kc, t0:t0 + n],
                        start=(kc == 0), stop=(kc == KC - 1), R=[wbuf] + htb, W=[pb])
                consume(ps, pb, c, t0, n)

    def proj_tm(self, wb, wbuf, c0, ncols, tile):
        P = self.P
        ps, pb = self.ps()
        for kc in range(KC):
            P.op("pe", "matmul", ps[:, 0:ncols], self.HT[:, kc, tile * 128:(tile + 1) * 128], wb[:, kc, c0:c0 + ncols],
                start=(kc == 0), stop=(kc == KC - 1), R=[wbuf, "HT%d" % tile], W=[pb])
        return ps, pb

    def headnorm_rope(self, ps, pb, c0, nh, wtile, wbuf, tile, tmp, out_bf, outbuf, permute):
        P = self.P
        n = nh * 64
        sq, t1, t2 = tmp
        sqb, t1b, t2b = [self.uid("hn") for _ in range(3)]
        P.op("act", "activation", sq[:, 0:n], ps[:, c0:c0 + n], AF.Square, R=[pb], W=[sqb])
        ssq, ssqb = self.statcol(4)
        rt, rtb = self.statcol(4)
        rr, rrb = self.statcol(4)
        P.op("dve", "tensor_reduce", ssq[:, 0:nh], sq[:, 0:n].rearrange("p (h d) -> p h d", h=nh), AX.X, ALU.add, R=[sqb], W=[ssqb])
        P.op("act", "activation", rt[:, 0:nh], ssq[:, 0:nh], AF.Sqrt, bias=64.0 * EPS, scale=1.0, R=[ssqb], W=[rtb])
        P.op("dve", "reciprocal", rr[:, 0:nh], rt[:, 0:nh], R=[rtb], W=[rrb])
        P.op("dve", "tensor_tensor", t1[:, 0:n].rearrange("p (h d) -> p h d", h=nh), ps[:, c0:c0 + n].rearrange("p (h d) -> p h d", h=nh),
            rr[:, 0:nh].unsqueeze(2).broadcast_to([128, nh, 64]), ALU.mult, R=[pb, rrb], W=[t1b])
        if permute:
            ov = out_bf[:, 0:n].rearrange("p (g kv d) -> p kv g d", g=2, kv=2)
        if tile >= NT:
            if permute:
                P.op("pool", "tensor_tensor", ov, t1[:, 0:n].rearrange("p (kv g d) -> p kv g d", kv=2, g=2),
                    wtile[:, 0:n].rearrange("p (kv g d) -> p kv g d", kv=2, g=2), ALU.mult, R=[t1b, wbuf], W=[outbuf])
            else:
                P.op("pool", "tensor_tensor", out_bf[:, 0:n], t1[:, 0:n], wtile[:, 0:n], ALU.mult, R=[t1b, wbuf], W=[outbuf])
            return
        P.op("pool", "tensor_tensor", sq[:, 0:n], t1[:, 0:n], wtile[:, 0:n], ALU.mult, R=[t1b, wbuf, sqb], W=[sqb])
        cosb = self.cos64[:, tile, :].unsqueeze(1).broadcast_to([128, nh, 64])
        sinv = self.sin64[:, tile, :].rearrange("p (t j) -> p t j", t=4)
        xv = sq[:, 0:n].rearrange("p (h t j) -> p h t j", h=nh, t=4)
        t2v = t2[:, 0:n].rearrange("p (h t j) -> p h t j", h=nh, t=4)
        P.op("dve", "tensor_tensor", t1[:, 0:n].rearrange("p (h d) -> p h d", h=nh),
                                              sq[:, 0:n].rearrange("p (h d) -> p h d", h=nh), cosb, ALU.mult, R=[sqb, "cos64", t1b], W=[t1b])
        P.op("pool", "tensor_tensor", t2v[:, :, 0::2, :], xv[:, :, 1::2, :],
                                               sinv[:, 0::2, :].unsqueeze(1).broadcast_to([128, nh, 2, 16]), ALU.mult, R=[sqb, "sin64"], W=[t2b + "a"])
        P.op("pool", "tensor_tensor", t2v[:, :, 1::2, :], xv[:, :, 0::2, :],
                                               sinv[:, 1::2, :].unsqueeze(1).broadcast_to([128, nh, 2, 16]), ALU.mult, R=[sqb, "sin64"], W=[t2b + "b"])
        if permute:
            P.op("dve", "tensor_tensor", ov, t1[:, 0:n].rearrange("p (kv g d) -> p kv g d", kv=2, g=2),
                t2[:, 0:n].rearrange("p (kv g d) -> p kv g d", kv=2, g=2), ALU.add, R=[t1b, t2b + "a", t2b + "b"], W=[outbuf])
        else:
            P.op("dve", "tensor_tensor", out_bf[:, 0:n], t1[:, 0:n], t2[:, 0:n], ALU.add, R=[t1b, t2b + "a", t2b + "b"], W=[outbuf])

    def transpose_to(self, src_bf, srcbuf, nblk, dst_fn, dstbuf_fn, eng="act"):
        P = self.P
        pt, ptb = self.pst()
        for j in range(nblk):
            P.op("pe", "transpose", pt[:, j * 128:(j + 1) * 128], src_bf[:, j * 128:(j + 1) * 128], self.ident[:], R=[srcbuf, "ident"], W=[ptb])
        for j in range(nblk):
            if eng == "act":
                P.op("act", "copy", dst_fn(j), pt[:, j * 128:(j + 1) * 128], R=[ptb], W=[dstbuf_fn(j)])
            else:
                P.op("dve", "tensor_copy", dst_fn(j), pt[:, j * 128:(j + 1) * 128], R=[ptb], W=[dstbuf_fn(j)])

    def layer(self, l, is_first, is_last, x_from_input, x_src_xs_in):
        nc, P, W = self.nc, self.P, self.W
        do_ctx_out = not is_last_layer(l)
        ntl = TT if do_ctx_out else NT
        pay, gat = self.payload[l], self.gathered[l]

        def x_tile_src(i):
            if x_from_input:
                return self.x_in[i * 128:(i + 1) * 128, :] if i < NT else self.ctx_in[(i - NT) * 128:(i - NT + 1) * 128, :]
            if x_src_xs_in:
                return self.xs_in[i * 128:(i + 1) * 128, :]
            return self.xs[i * 128:(i + 1) * 128, :]

        self.phase()
        L = "L%d" % l
        mod, modb = self.ar("mod", [2, 3 * D], F32)
        arow, arowb = self.ar("arow", [2, D], F32)
        normw2, normw2b = self.ar("normw2", [2, D], F32)
        bada2, bada2b = self.ar("bada2", [2, 3 * D], F32)
        P.dma("sp", normw2, W["norm_w"][l:l + 1, :].partition_broadcast(2), writes=[normw2b])
        P.dma("sp", bada2, W["b_ada"][l:l + 1, :].partition_broadcast(2), writes=[bada2b])
        for i, (kn, qn) in enumerate((("kn_a", "qn_a"), ("kn_d", "qn_d"))):
            for h in range(2):
                P.dma("sp", self.knw[i][:, h * 64:(h + 1) * 64], W[kn][l:l + 1, :].partition_broadcast(128), writes=["knw%d" % i])
            for h in range(4):
                P.dma("sp", self.qnw[i][:, h * 64:(h + 1) * 64], W[qn][l:l + 1, :].partition_broadcast(128), writes=["qnw%d" % i])
            P.op("act", "mul", self.knw[i][:], self.knw[i][:], 8.0, R=["knw%d" % i], W=["knw%d" % i])
        P.dma("sp", self.esink[:], W["sink_d"][l:l + 1, :].partition_broadcast(128), writes=["esink"])
        P.op("act", "activation", self.esink[:], self.esink[:], AF.Exp, R=["esink"], W=["esink"])

        for blk in range(6):
            ps, pb = self.ps()
            for half in range(2):
                i = self.wsf_i % 2
                self.wsf_i += 1
                st = self.WSF[i][:].rearrange("p (a b) -> p a b", a=KC)
                c0 = blk * 512 + half * 256
                P.dma("sp", st, self.w_cols("w_ada", l, c0, 256), writes=["wsf%d" % i])
                for kc in range(KC):
                    P.op("pe", "matmul", ps[0:2, half * 256:(half + 1) * 256], self.scT[:, kc, :], st[:, kc, :],
                        start=(kc == 0), stop=(kc == KC - 1), R=["scT", "wsf%d" % i], W=[pb])
            P.op("dve", "tensor_tensor", mod[0:2, blk * 512:(blk + 1) * 512], ps[0:2, :], bada2[0:2, blk * 512:(blk + 1) * 512], ALU.add, R=[pb, bada2b], W=[modb])
        self.tap(L + "mod", mod, [2, 3 * D], [modb])
        P.op("dve", "scalar_tensor_tensor", arow, mod[0:2, D:2 * D], 1.0, normw2, ALU.add, ALU.mult, R=[modb, normw2b], W=[arowb])
        P.op("act", "copy", self.gate2[:], mod[0:2, 2 * D:3 * D], R=[modb], W=["gate2"])
        a_bc, a_bcb = self.ar("a_bc", [128, 2, D], F32)
        sh_bc, sh_bcb = self.ar("sh_bc", [128, 2, D], F32)
        for r in range(2):
            for half in range(2):
                for which, (dst, dbuf, src, sbuf) in enumerate(((a_bc, a_bcb, arow, arowb), (sh_bc, sh_bcb, mod, modb))):
                    ps, pb = self.ps()
                    P.op("pe", "matmul", ps[:, :], self.sel2[0:2, r * 128:(r + 1) * 128], src[0:2, half * 512:(half + 1) * 512],
                        start=True, stop=True, R=["sel2", sbuf], W=[pb])
                    P.op("act", "copy", dst[:, r, half * 512:(half + 1) * 512], ps[:, :], R=[pb], W=[dbuf])

        XT = [self.ar("XT%d" % i, [128, D], F32) for i in range(2)]
        TMPF = [self.ar("TMPF%d" % i, [128, D], F32) for i in range(2)]
        HB = [self.ar("HB%d" % i, [128, D], BF) for i in range(2)]
        for i in range(TT):
            s = i % 2
            r = 0 if i < NT else 1
            (xt, xtb), (tf, tfb), (hb, hbb) = XT[s], TMPF[s], HB[s]
            P.dma("sp", xt, x_tile_src(i), writes=[xtb])
            ssq, ssqb = self.statcol()
            rt, rtb = self.statcol()
            rs, rsb = self.statcol()
            P.op("act", "activation", tf, xt, AF.Square, accum_out=ssq, R=[xtb], W=[tfb, ssqb])
            P.op("act", "activation", rt, ssq, AF.Sqrt, bias=EPS, scale=1.0 / D, R=[ssqb], W=[rtb])
            P.op("dve", "reciprocal", rs, rt, R=[rtb], W=[rsb])
            P.op("dve", "scalar_tensor_tensor", tf, xt, rs, a_bc[:, r, :], ALU.mult, ALU.mult, R=[xtb, rsb, a_bcb, tfb], W=[tfb])
            P.op("pool", "tensor_tensor", hb, tf, sh_bc[:, r, :], ALU.add, R=[tfb, sh_bcb], W=[hbb])
            if i == 0:
                self.tap(L + "h0", hb, [128, D], [hbb], BF)
            pt, ptb = self.pst()
            for kc in range(KC):
                P.op("pe", "transpose", pt[:, kc * 128:(kc + 1) * 128], hb[:, kc * 128:(kc + 1) * 128], self.ident[:], R=[hbb, "ident"], W=[ptb])
            P.op("act", "copy", self.HT[:, :, i * 128:(i + 1) * 128], pt[:].rearrange("p (k t) -> p k t", k=KC), R=[ptb], W=["HT%d" % i])

        if self.stop == "A1":
            return
        self.phase()
        WB, WBb = self.ar("WBkv", [128, KC, 256], BF)
        KR = [self.ar("kr%d" % j, [128, 128], BF) for j in range(2)]
        KTs = [self.ar("KTs%d" % a, [128, TOK], BF) for a in range(2)]
        Vs = [self.ar("Vs%d" % a, [128, NT, 128], BF) for a in range(2)]
        tmp = [self.ar("hnt%d" % j, [128, 256], F32)[0] for j in range(3)]
        for a, c0 in enumerate((256, 1024)):
            self.load_w(self.w_cols("w_in", l, c0, 256), WB, WBb, (KC, 256))
            for i in range(TT):
                ps, pb = self.proj_tm(WB, WBb, 0, 256, i)
                kr, krb = KR[i % 2]
                self.headnorm_rope(ps, pb, 0, 2, self.knw[a], "knw%d" % a, i, tmp, kr, krb, permute=False)
                if i < NT:
                    kts, ktsb = KTs[a]
                    vs, vsb = Vs[a]
                    self.transpose_to(kr, krb, 1, lambda j, kts=kts, i=i: kts[:, i * 128:(i + 1) * 128], lambda j, ktsb=ktsb: ktsb)
                    P.op("act", "copy", vs[:, i, :], ps[:, 128:256], R=[pb], W=[vsb])
                else:
                    j = i - NT
                    self.transpose_to(kr, krb, 1, lambda jj, a=a, j=j: self.KTc[a][:, j * 128:(j + 1) * 128], lambda jj, a=a: "KTc%d" % a)
                    P.op("act", "copy", self.Vc[a][:, j, :], ps[:, 128:256], R=[pb], W=["Vc%d" % a])
            kts, ktsb = KTs[a]
            vs, vsb = Vs[a]
            P.dma("sp", pay[0].ap()[a * 128:(a + 1) * 128, :], kts, reads=[ktsb], writes=["pay_k%d" % a])
            P.dma("sp", pay[3].ap()[a * 128:(a + 1) * 128, :].rearrange("p (t d) -> p t d", d=128), vs, reads=[vsb],
                  writes=["pay_v%d" % a])
        if l == self.layers[0]:
            self.tap(L + "KTsA", KTs[0][0], [128, TOK], [KTs[0][1]], BF)
            self.tap(L + "VsA", Vs[0][0], [128, NT, 128], [Vs[0][1]], BF)
            self.tap(L + "KTcA", self.KTc[0][:], [128, 256], ["KTc0"], BF)

        self.load_w(self.w_cols("w_in", l, 1536, 256), WB, WBb, (KC, 256))
        BFT, BFTb = self.ar("BFT", [128, 2, TALL], BF)

        def bf_consume(ps, pb, c, t0, n):
            P.op("act", "copy", BFT[:, c, t0:t0 + n], ps[:, 0:n], R=[pb], W=[BFTb])
        self.proj_fm(WB, WBb, 0, 2, TT, bf_consume)
        wf, wfb = self.ar("wf", [64, 4, 64], F32)
        P.dma("sp", wf, W["w_fnet"][l].rearrange("g j c -> j g c"), writes=[wfb])
        P.op("pool", "memset", self.BA[:], 0.0, R=[], W=["BA"])
        for cs, tab in enumerate((self.c64d, self.s64d)):
            for g in range(4):
                ps, pb = self.ps()
                P.op("pe", "matmul", ps[:, 0:64], tab[0:64, :], wf[0:64, g, :], start=True, stop=True, R=[wfb, "c64d", "s64d"], W=[pb])
                rows = slice((g % 2) * 64, (g % 2) * 64 + 64)
                P.op("dve", "tensor_copy", self.BA[rows, cs, g // 2, (g % 2) * 64:(g % 2) * 64 + 64], ps[rows, 0:64], R=[pb, "BA"], W=["BA"])
        UVT, UVTb = self.ar("UVT", [128, 2, 2, TOK], BF)
        for cs in range(2):
            for c in range(2):
                for (t0, n) in GROUPS[:4]:
                    ps, pb = self.ps()
                    P.op("pe", "matmul", ps[:, 0:n], self.BA[:, cs, c, :], BFT[:, c, t0:t0 + n], start=True, stop=True, R=["BA", BFTb], W=[pb])
                    P.op("act", "copy", UVT[:, cs, c, t0:t0 + n], ps[:, 0:n], R=[pb], W=[UVTb])
                for j in range(2):
                    ps, pb = self.ps()
                    P.op("pe", "matmul", ps[:, 0:128], BFT[:, c, TOK + j * 128:TOK + (j + 1) * 128], self.BA[:, cs, c, :], start=True, stop=True, R=["BA", BFTb], W=[pb])
                    P.op("dve", "tensor_copy", self.UVc[:, cs, j, c * 128:(c + 1) * 128], ps[:, 0:128], R=[pb], W=["UVc"])
        for cs in range(2):
            P.dma("sp", pay[1 + cs].ap().rearrange("(c p) t -> p c t", p=128), UVT[:, cs, :, :],
                  reads=[UVTb], writes=["pay_uv%d" % cs])
        self.tap(L + "UVT", UVT, [128, 2, 2, TOK], [UVTb], BF)

        if self.stop == "A":
            return
        payb = [["pay_k0", "pay_k1"], ["pay_uv0"], ["pay_uv1"], ["pay_v0", "pay_v1"]]
        for j in (0, 3, 1, 2):
            P.collective(lambda e, j=j: e.collective_compute("AllGather", ALU.bypass, replica_groups=[[0, 1, 2, 3], [4, 5, 6, 7]],
                                                             ins=[pay[j].ap().opt()], outs=[gat[j].ap().opt()]),
                         reads=payb[j], writes=["gat%d" % j])
        G = [g.ap() for g in gat]

        if self.stop == "G":
            return
        self.phase()
        self.attention(l, 0, ntl, G, pay)
        if self.stop == "B2":
            return
        self.phase()
        self.attention(l, 1, ntl, G, pay)
        if self.stop == "B3":
            return
        self.phase()
        self.fnet(l, ntl, G)
        if self.stop == "B4":
            return
        self.phase()
        self.sgu(l, ntl)
        if l == self.layers[0]:
            for r in range(4):
                self.tap(L + "YB%d" % r, self.YB[r][:], [128, 2, TALL], ["YB%d_%d_%d" % (r, c, i) for c in range(2) for i in range(TT)], BF)
        if self.stop == "B5":
            return
        self.phase()
        self.merge(l, ntl, is_last, x_tile_src)

    def attention(self, l, which, ntl, G, pay):
        P, W = self.P, self.W
        q0 = 0 if which == 0 else 768
        z0 = 512 if which == 0 else 1280
        YB = self.YB[which]
        nqg = [g for g in GROUPS if g[0] // 128 < ntl]
        WQ, WQb = self.ar("WQ", [128, KC, 256], BF)
        WZ, WZb = self.ar("WZ", [128, KC, 256], BF)
        self.load_w(self.w_cols("w_in", l, q0, 256), WQ, WQb, (KC, 256))
        self.load_w(self.w_cols("w_in", l, z0, 256), WZ, WZb, (KC, 256))
        QT = [self.ar("QT%d" % g, [128, TALL], BF) for g in range(2)]
        def vview(V, k0, k1):
            return V[:, k0:k1, :].rearrange("p t (kv x) -> p t kv x", kv=2)[:, :, :, 0:64]

        if which == 0:
            NKB = 2 + 64
            KT, KTb = self.ar("KT", [128, NKB * 128], BF)
            V, Vb = self.ar("V", [128, NKB, 256], BF)
            P.op("pool", "memset", V[:, :, :].rearrange("p t (kv x) -> p t kv x", kv=2)[:, :, :, 64:128], 1.0, R=[], W=[Vb + "ones"])
            P.op("act", "copy", KT[:, 0:256], self.KTc[0][:], R=["KTc0"], W=[KTb + "c"])
            P.op("act", "copy", vview(V, 0, 2), self.Vc[0][:].rearrange("p t (kv d) -> p t kv d", kv=2), R=["Vc0"], W=[Vb + "c"])
            for s_ in range(4):
                P.dma("sp", KT[:, 256 + s_ * TOK:256 + (s_ + 1) * TOK], G[0][s_ * 256:s_ * 256 + 128, :], reads=["gat0"], writes=[KTb + str(s_)])
                for hf in range(2):
                    P.dma("sp", vview(V, 2 + s_ * NT + hf * 8, 2 + s_ * NT + hf * 8 + 8),
                          G[3][s_ * 256:s_ * 256 + 128, hf * 1024:(hf + 1) * 1024].rearrange("p (t kv d) -> p t kv d", kv=2, d=64),
                          reads=["gat3"], writes=[Vb + str(s_)])

            def kbufs(kb):
                return [KTb + ("c" if kb < 2 else str((kb - 2) // NT))], [Vb + ("c" if kb < 2 else str((kb - 2) // NT)), Vb + "ones"]
        else:
            NKB = 2 + NT + 8
            KT, KTb = self.ar("KT", [128, NKB * 128], BF)
            V, Vb = self.ar("V", [128, NKB, 256], BF)
            P.op("pool", "memset", V[:, :, :].rearrange("p t (kv x) -> p t kv x", kv=2)[:, :, :, 64:128], 1.0, R=[], W=[Vb + "ones"])
            P.op("act", "copy", KT[:, 0:256], self.KTc[1][:], R=["KTc1"], W=[KTb])
            P.op("act", "copy", vview(V, 0, 2), self.Vc[1][:].rearrange("p t (kv d) -> p t kv d", kv=2), R=["Vc1"], W=[Vb])
            P.dma("sp", KT[:, 256:256 + TOK], pay[0].ap()[128:256, :], reads=["pay_k1"], writes=[KTb])
            for hf in range(2):
                P.dma("sp", vview(V, 2 + hf * 8, 2 + hf * 8 + 8),
                      pay[3].ap()[128:256, hf * 1024:(hf + 1) * 1024].rearrange("p (t kv d) -> p t kv d", kv=2, d=64),
                      reads=["pay_v1"], writes=[Vb])
            for s_ in range(4):
                for kb, c0 in ((2 + NT + s_, TOK - 128), (2 + NT + 4 + s_, 0)):
                    P.dma("sp", KT[:, kb * 128:(kb + 1) * 128], G[0][s_ * 256 + 128:s_ * 256 + 256, c0:c0 + 128], reads=["gat0"], writes=[KTb])
                    P.dma("sp", vview(V, kb, kb + 1),
                          G[3][s_ * 256 + 128:s_ * 256 + 256, c0:c0 + 128].rearrange("p (t kv d) -> p t kv d", kv=2, d=64),
                          reads=["gat3"], writes=[Vb])

            def kbufs(kb):
                return [KTb], [Vb, Vb + "ones"]

        tmp = [self.ar("hnt%d" % j, [128, 256], F32)[0] for j in range(3)]
        QR = [self.ar("QR%d" % j, [128, 256], BF) for j in range(2)]
        for i in range(ntl):
            ps, pb = self.proj_tm(WQ, WQb, 0, 256, i)
            qr, qrb = QR[i % 2]
            self.headnorm_rope(ps, pb, 0, 4, self.qnw[which], "qnw%d" % which, i, tmp, qr, qrb, permute=True)
            self.transpose_to(qr, qrb, 2, lambda j, i=i: QT[j][0][:, i * 128:(i + 1) * 128], lambda j, i=i: QT[j][1] + "_%d" % i)

        def z_consume(ps, pb, c, t0, n):
            P.op("act", "activation", YB[:, c, t0:t0 + n], ps[:, 0:n], AF.Silu, R=[pb], W=["YB%d_%d_%d" % (which, c, i) for i in range(t0 // 128, (t0 + n) // 128)])
        self.proj_fm(WZ, WZb, 0, 2, ntl, z_consume)
        if which == 0 and l == self.layers[0]:
            self.tap("L%dQT0" % l, QT[0][0], [128, TALL], [QT[0][1] + "_%d" % i for i in range(ntl)], BF)

        PT = [self.ar("PT%d" % j, [128, 1024], BF) for j in range(3)]
        fin = [self.ar("fin%d" % j, [128, 512], F32) for j in range(3)]

        work = []

        def add_pair(g, t0, n, keylist):
            cw = 512 // n
            chunks = [keylist[c:c + cw] for c in range(0, len(keylist), cw)]
            for ci, ch in enumerate(chunks):
                work.append(dict(g=g, t0=t0, n=n, keys=ch, first=(ci == 0), last=(ci == len(chunks) - 1)))

        if which == 0:
            for (t0, n) in nqg:
                keys = [(kb, None) for kb in range(NKB)] if t0 < TOK else [(0, None), (1, None)]
                for g in range(2):
                    add_pair(g, t0, n, keys)
        else:
            for i in range(ntl):
                if i >= NT:
                    keys = [(0, None), (1, None)]
                else:
                    keys = [(0, None), (1, None), (2 + i, None)]
                    if i > 0:
                        keys.append((2 + i - 1, 0))
                    else:
                        keys += [(2 + NT + s_, 2 + s_) for s_ in range(4)]
                    if i < NT - 1:
                        keys.append((2 + i + 1, 1))
                    else:
                        keys += [(2 + NT + 4 + s_, 6 + s_) for s_ in range(4)]
                for g in range(2):
                    add_pair(g, i * 128, 128, keys)

        accs = [(self.PS[4], "ps4"), (self.PS[5], "ps5")]

        def stage1(w, slot):
            g, t0, n = w["g"], w["t0"], w["n"]
            qt, qtb = QT[g]
            qbufs = [qtb + "_%d" % i for i in range(t0 // 128, (t0 + n) // 128)]
            psw = self.PSW[slot % 2]
            pbs = ["ps%d" % (2 * (slot % 2)), "ps%d" % (2 * (slot % 2) + 1)]
            pt, ptb = PT[slot % 3]
            cnt = len(w["keys"])
            for j, (kb, mi) in enumerate(w["keys"]):
                kb_k, _ = kbufs(kb)
                for kv in range(2):
                    krows = slice(kv * 64, (kv + 1) * 64)
                    P.op("pe", "matmul", psw[:, kv * 512 + j * n:kv * 512 + (j + 1) * n], KT[krows, kb * 128:(kb + 1) * 128],
                         qt[krows, t0:t0 + n], start=True, stop=True, R=kb_k + qbufs, W=[pbs[kv]])
            P.op("act", "activation", pt[:, :].rearrange("p (a b) -> p a b", a=2)[:, :, 0:cnt * n],
                 psw[:, :].rearrange("p (a b) -> p a b", a=2)[:, :, 0:cnt * n], AF.Exp, R=pbs, W=[ptb])
            for j, (kb, mi) in enumerate(w["keys"]):
                if mi is not None:
                    P.op("pool", "tensor_tensor", pt[:, :].rearrange("p (a b) -> p a b", a=2)[:, :, j * n:(j + 1) * n],
                         pt[:, :].rearrange("p (a b) -> p a b", a=2)[:, :, j * n:(j + 1) * n],
                         self.masks[:, mi, 0:n].unsqueeze(1).broadcast_to([128, 2, n]), ALU.mult, R=[ptb, "masks"], W=[ptb])

        def stage2(w, slot):
            g, t0, n = w["g"], w["t0"], w["n"]
            rows = slice(g * 64, (g + 1) * 64)
            pt, ptb = PT[slot % 3]
            cnt = len(w["keys"])
            for j, (kb, mi) in enumerate(w["keys"]):
                _, kb_v = kbufs(kb)
                st = w["first"] and j == 0
                sp = w["last"] and j == cnt - 1
                for kv in range(2):
                    acc, accb = accs[kv]
                    P.op("pe", "matmul", acc[:, 0:n], V[:, kb, kv * 128:(kv + 1) * 128], pt[:, kv * 512 + j * n:kv * 512 + (j + 1) * n],
                         start=st, stop=sp, R=kb_v + [ptb], W=[accb])
            if not w["last"]:
                return
            for kv in range(2):
                acc, accb = accs[kv]
                h = 2 * kv + g
                (f0, f0b), (f1, f1b), (f2, f2b) = fin
                if which == 1:
                    P.op("dve", "tensor_scalar", f2[64:128, 0:n], acc[64:128, 0:n], self.esink[64:128, h:h + 1], None, ALU.add,
                         R=[accb, "esink"], W=[f2b])
                    P.op("dve", "reciprocal", f0[rows, 0:n], f2[64:128, 0:n], R=[f2b], W=[f0b])
                else:
                    P.op("dve", "reciprocal", f0[rows, 0:n], acc[64:128, 0:n], R=[accb], W=[f0b])
                P.op("dve", "tensor_copy", f1[rows, 0:n], acc[0:64, 0:n], R=[accb], W=[f1b])
                P.op("pool", "tensor_tensor", f1[rows, 0:n], f1[rows, 0:n], f0[rows, 0:n], ALU.mult, R=[f1b, f0b], W=[f1b])
                ybb = ["YB%d_%d_%d" % (which, kv, i) for i in range(t0 // 128, (t0 + n) // 128)]
                P.op("pool", "tensor_tensor", YB[rows, kv, t0:t0 + n], f1[rows, 0:n], YB[rows, kv, t0:t0 + n], ALU.mult,
                     R=[f1b] + ybb, W=ybb)

        LOOK = 1
        for k in range(len(work) + LOOK):
            if k < len(work):
                stage1(work[k], k)
            if k >= LOOK:
                stage2(work[k - LOOK], k - LOOK)

    def fnet(self, l, ntl, G):
        P, W = self.P, self.W
        YB = self.YB[2]
        WZ, WZb = self.ar("WZ", [128, KC, 256], BF)
        self.load_w(self.w_cols("w_in", l, 1792, 256), WZ, WZb, (KC, 256))

        def z_consume(ps, pb, c, t0, n):
            P.op("act", "activation", YB[:, c, t0:t0 + n], ps[:, 0:n], AF.Silu, R=[pb], W=["YB2_%d_%d" % (c, i) for i in range(t0 // 128, (t0 + n) // 128)])
        self.proj_fm(WZ, WZb, 0, 2, ntl, z_consume)
        MC, MCb = self.ar("MC", [128, 64, 32], BF)
        MS, MSb = self.ar("MS", [128, 64, 32], BF)
        P.dma("sp", MC, self.C["mc"].rearrange("p (a b) -> p a b", a=64), writes=[MCb])
        P.dma("sp", MS, self.C["ms"].rearrange("p (a b) -> p a b", a=64), writes=[MSb])
        Z = [self.ar("Z%d" % j, [128, 64, 128], BF) for j in range(2)]
        T3, T3b = self.ar("T3", [128, 64, 128], BF)
        sp = 0
        for c in range(2):
            for hf in range(2):
                z, zb = Z[sp % 2]
                sp += 1
                for s in range(4):
                    for ri in range(2):
                        r0 = s * 256 + c * 128 + hf * 64
                        P.dma("sp", z[ri * 64 + s * 16:ri * 64 + (s + 1) * 16, :, :],
                              G[1 + ri][r0:r0 + 64, :].rearrange("ch (t n) -> t ch n", n=128), reads=["gat%d" % (1 + ri)], writes=[zb])
                for ch4 in range(16):
                    ps, pb = self.ps()
                    for j in range(4):
                        ch = ch4 * 4 + j
                        P.op("pe", "matmul", ps[:, j * 128:(j + 1) * 128], z[:, ch, :], self.m1[:, :],
                                                                            start=True, stop=True, R=[zb, "m1"], W=[pb])
                    if ch4 % 2 == 0:
                        P.op("act", "copy", T3[:, ch4 * 4:ch4 * 4 + 4, :], ps[:].rearrange("p (a b) -> p a b", a=4), R=[pb], W=[T3b])
                    else:
                        P.op("dve", "tensor_copy", T3[:, ch4 * 4:ch4 * 4 + 4, :], ps[:].rearrange("p (a b) -> p a b", a=4), R=[pb], W=[T3b])
                rows = slice(hf * 64, (hf + 1) * 64)
                for bank in range(4):
                    ps, pb = self.ps()
                    for kk in range(16):
                        k1 = bank * 16 + kk
                        P.op("pe", "matmul", ps[rows, kk * 32:(kk + 1) * 32], T3[:, :, k1], MC[:, k1, :],
                                                                         start=True, stop=False, tile_position=(0, hf * 64), R=[T3b, MCb], W=[pb])
                        P.op("pe", "matmul", ps[rows, kk * 32:(kk + 1) * 32], T3[:, :, 64 + k1], MS[:, k1, :],
                                                                         start=False, stop=True, tile_position=(0, hf * 64), R=[T3b, MSb], W=[pb])
                    yv = YB[rows, c, 0:TOK].rearrange("p (k2 k1) -> p k1 k2", k1=64)[:, bank * 16:(bank + 1) * 16, :]
                    ybb = ["YB2_%d_%d" % (c, i) for i in range(NT)]
                    P.op("dve", "tensor_tensor", yv, ps[rows, :].rearrange("p (a b) -> p a b", a=16), yv, ALU.mult, R=[pb] + ybb, W=ybb)
        if ntl > NT:
            for c in range(2):
                ps, pb = self.ps()
                k = 0
                for uv, tab in ((0, self.c256), (1, self.s256n)):
                    for j in range(2):
                        P.op("pe", "matmul", ps[:, 0:256], self.UVc[:, uv, j, c * 128:(c + 1) * 128], tab[:, j, :], start=(k == 0), stop=(k == 3), R=["UVc", "c256", "s256n"], W=[pb])
                        k += 1
                ybb = ["YB2_%d_%d" % (c, i) for i in range(NT, TT)]
                P.op("dve", "tensor_tensor", YB[:, c, TOK:TALL], ps[:, 0:256], YB[:, c, TOK:TALL], ALU.mult, R=[pb] + ybb, W=ybb)

    def sgu(self, l, ntl):
        P, W = self.P, self.W
        YB = self.YB[3]
        WU, WUb = self.ar("WU", [128, KC, 256], BF)
        WV, WVb = self.ar("WV", [128, KC, 256], BF)
        WZ, WZb = self.ar("WZ", [128, KC, 256], BF)
        self.load_w(self.w_cols("w_in", l, 2048, 256), WU, WUb, (KC, 256))
        self.load_w(self.w_cols("w_in", l, 2304, 256), WV, WVb, (KC, 256))
        self.load_w(self.w_cols("w_in", l, 2560, 256), WZ, WZb, (KC, 256))
        GU, GUb = self.ar("GU", [128, 2, TALL], BF)
        GV, GVb = self.ar("GV", [128, TT, 256], BF)
        wsp, wspb = self.ar("wsp", [128, 4, 128], BF)
        bsp, bspb = self.ar("bsp", [128, 2, 128], F32)
        self.load_w(W["w_spT"][l], wsp, wspb, (4, 128))
        for g in range(4):
            P.dma("sp", bsp[(g % 2) * 64:(g % 2) * 64 + 64, g // 2, :], W["b_sp"][l, g:g + 1, :].partition_broadcast(64), writes=[bspb])

        def u_consume(ps, pb, c, t0, n):
            P.op("act", "activation", GU[:, c, t0:t0 + n], ps[:, 0:n], AF.Gelu, R=[pb], W=[GUb])

        def z_consume(ps, pb, c, t0, n):
            P.op("act", "activation", YB[:, c, t0:t0 + n], ps[:, 0:n], AF.Silu, R=[pb], W=["YB3_%d_%d" % (c, i) for i in range(t0 // 128, (t0 + n) // 128)])
        self.proj_fm(WU, WUb, 0, 2, ntl, u_consume)
        self.proj_fm(WZ, WZb, 0, 2, ntl, z_consume)
        for i in range(ntl):
            ps, pb = self.proj_tm(WV, WVb, 0, 256, i)
            P.op("act", "activation", GV[:, i, :], ps[:, 0:256], AF.Gelu, R=[pb], W=[GVb + "_%d" % i])
        t1, t1b = self.ar("sg1", [128, 512], F32)
        t2, t2b = self.ar("sg2", [128, 512], F32)
        for i0 in range(0, ntl, 4):
            nt = min(4, ntl - i0)
            n = nt * 128
            for c in range(2):
                ps, pb = self.ps()
                for j in range(nt):
                    i = i0 + j
                    for gg in range(2):
                        g = 2 * c + gg
                        P.op("pe", "matmul", ps[gg * 64:(gg + 1) * 64, j * 128:(j + 1) * 128], GV[:, i, g * 64:(g + 1) * 64], wsp[:, g, :],
                            start=True, stop=True, tile_position=(0, gg * 64), R=[GVb + "_%d" % i, wspb], W=[pb])
                P.op("dve", "tensor_tensor", t1[:, 0:nt * 128].rearrange("p (a b) -> p a b", a=nt), ps[:, 0:nt * 128].rearrange("p (a b) -> p a b", a=nt),
                    bsp[:, c, :].unsqueeze(1).broadcast_to([128, nt, 128]), ALU.add, R=[pb, bspb, t1b], W=[t1b])
                P.op("pool", "tensor_tensor", t2[:, 0:n], t1[:, 0:n], GU[:, c, i0 * 128:i0 * 128 + n], ALU.mult, R=[t1b, GUb, t2b], W=[t2b])
                ybb = ["YB3_%d_%d" % (c, i) for i in range(i0, i0 + nt)]
                P.op("dve", "tensor_tensor", YB[:, c, i0 * 128:i0 * 128 + n], t2[:, 0:n], YB[:, c, i0 * 128:i0 * 128 + n], ALU.mult, R=[t2b] + ybb, W=ybb)

    def merge(self, l, ntl, is_last, x_tile_src):
        P, W = self.P, self.W
        groups = [g for g in GROUPS if g[0] // 128 < ntl]
        MT, MTb = self.ar("MT", [128, KC, TALL], BF)
        WBR = [self.ar("WBR%d" % r, [128, 2, D], BF) for r in range(4)]
        for r in range(4):
            self.load_w(W["w_br"][l, r].rearrange("(kc p) n -> p kc n", p=128), WBR[r][0], WBR[r][1], (2, D))
        bm, bmb = self.ar("bm", [128, 32], F32)
        P.dma("sp", bm, W["b_mergeT"][l], writes=[bmb])
        WM = [self.ar("WM%d" % j, [128, 4, KC, 128], BF) for j in range(2)]
        sg = [self.ar("sg%d" % j, [128, 512], F32) for j in range(2)]
        acc = [self.ar("acc%d" % j, [128, 512], F32) for j in range(2)]
        tmpm = [self.ar("tmpm%d" % j, [128, 512], F32) for j in range(2)]
        cnt = 0
        for dc in range(KC):
            wm, wmb = WM[dc % 2]
            for r in range(4):
                i = self.wsf_i % 2
                self.wsf_i += 1
                st = self.WSF[i][:, 0:1024].rearrange("p (a b) -> p a b", a=KC)
                P.dma("sp", st, self.w_cols("w_merge", l, r * D + dc * 128, 128), writes=["wsf%d" % i])
                P.op("pool", "tensor_copy", wm[:, r, :, :], st, R=["wsf%d" % i], W=[wmb + "_%d" % r])
            for (t0, n) in groups:
                htb = ["HT%d" % i for i in range(t0 // 128, (t0 + n) // 128)]
                a_, ab = acc[cnt % 2]
                for r in range(4):
                    psg, pgb = self.ps()
                    for kc in range(KC):
                        P.op("pe", "matmul", psg[:, 0:n], wm[:, r, kc, :], self.HT[:, kc, t0:t0 + n], start=(kc == 0), stop=(kc == KC - 1), R=[wmb + "_%d" % r] + htb, W=[pgb])
                    psy, pyb = self.ps()
                    ybb = ["YB%d_%d_%d" % (r, c, i) for c in range(2) for i in range(t0 // 128, (t0 + n) // 128)]
                    for k2 in range(2):
                        P.op("pe", "matmul", psy[:, 0:n], WBR[r][0][:, k2, dc * 128:(dc + 1) * 128], self.YB[r][:, k2, t0:t0 + n],
                            start=(k2 == 0), stop=(k2 == 1), R=[WBR[r][1]] + ybb, W=[pyb])
                    s_, sb_ = sg[(cnt * 4 + r) % 2]
                    P.op("act", "activation", s_[:, 0:n], psg[:, 0:n], AF.Sigmoid,
                                                                         bias=bm[:, r * 8 + dc:r * 8 + dc + 1], scale=1.0, R=[pgb, bmb], W=[sb_])
                    if r == 0:
                        P.op("dve", "tensor_tensor", a_[:, 0:n], psy[:, 0:n], s_[:, 0:n], ALU.mult, R=[pyb, sb_, ab], W=[ab])
                    else:
                        t_, tb_ = tmpm[r % 2]
                        P.op("dve", "tensor_tensor", t_[:, 0:n], psy[:, 0:n], s_[:, 0:n], ALU.mult, R=[pyb, sb_, tb_], W=[tb_])
                        if r < 3:
                            P.op("pool", "tensor_tensor", a_[:, 0:n], a_[:, 0:n], t_[:, 0:n], ALU.add, R=[ab, tb_], W=[ab])
                        else:
                            P.op("pool", "tensor_tensor", MT[:, dc, t0:t0 + n], a_[:, 0:n], t_[:, 0:n], ALU.add, R=[ab, tb_], W=[MTb + "_%d_%d" % (dc, t0)])
                cnt += 1
        self.tap("L%dMT" % l, MT, [128, KC, TALL], [MTb + "_%d_%d" % (dc, g[0]) for dc in range(KC) for g in groups], BF)
        if self.stop == "C2":
            return
        self.P.barrier()
        self.ar_off = (KC * TALL * 2 + 63) // 64 * 64
        self.ar_gen += 1
        WO, WOb = self.ar("WO", [128, KC, D], BF)
        for j in range(4):
            self.load_w(self.w_cols("w_out", l, j * 256, 256), WO[:, :, j * 256:(j + 1) * 256], WOb + "_%d" % j, (KC, 256))
        gt_bc, gtb = self.ar("gt_bc", [128, 2, D], F32)
        for r in range(2 if ntl > NT else 1):
            for half in range(2):
                ps, pb = self.ps()
                P.op("pe", "matmul", ps[:, :], self.sel2[0:2, r * 128:(r + 1) * 128], self.gate2[0:2, half * 512:(half + 1) * 512],
                    start=True, stop=True, R=["sel2", "gate2"], W=[pb])
                P.op("act", "copy", gt_bc[:, r, half * 512:(half + 1) * 512], ps[:, :], R=[pb], W=[gtb])
        XT = [self.ar("XT%d" % i, [128, D], F32) for i in range(2)]
        XO = [self.ar("XO%d" % i, [128, D], F32) for i in range(2)]
        for i in range(ntl):
            r = 0 if i < NT else 1
            xt, xtb = XT[i % 2]
            xo, xob = XO[i % 2]
            P.dma("sp", xt, x_tile_src(i), writes=[xtb])
            for half in range(2):
                ps, pb = self.ps()
                mtb = [MTb + "_%d_%d" % (dc, g[0]) for dc in range(KC) for g in GROUPS if g[0] <= i * 128 < g[0] + g[1]]
                for kc in range(KC):
                    P.op("pe", "matmul", ps[:, :], MT[:, kc, i * 128:(i + 1) * 128], WO[:, kc, half * 512:(half + 1) * 512],
                        start=(kc == 0), stop=(kc == KC - 1), R=mtb + [WOb + "_%d" % (2 * half), WOb + "_%d" % (2 * half + 1)], W=[pb])
                P.op("dve", "tensor_tensor", xo[:, half * 512:(half + 1) * 512], ps[:, :], gt_bc[:, r, half * 512:(half + 1) * 512], ALU.mult, R=[pb, gtb, xob], W=[xob + "h%d" % half])
                P.op("pool", "tensor_tensor", xo[:, half * 512:(half + 1) * 512], xo[:, half * 512:(half + 1) * 512], xt[:, half * 512:(half + 1) * 512], ALU.add, R=[xob + "h%d" % half, xtb], W=[xob + "h%d" % half])
            if is_last:
                dst = self.out[i * 128:(i + 1) * 128, :]
            elif self.last or l != self.layers[-1]:
                dst = self.xs[i * 128:(i + 1) * 128, :]
            else:
                dst = self.xs_out[i * 128:(i + 1) * 128, :]
            P.dma("sp", dst, xo, reads=[xob + "h0", xob + "h1"], writes=["xs%d" % i])


def is_last_layer(l):
    return l == 1


_CONST_CACHE = {}


def _consts(core):
    if core not in _CONST_CACHE:
        _CONST_CACHE[core] = host_consts(core)
    return _CONST_CACHE[core]


def _weights_layout(inp):
    f = lambda a: np.ascontiguousarray(np.asarray(a, dtype=np.float32))
    w = {k: f(inp[k]) for k in ("norm_w", "w_ada", "b_ada", "w_in", "qn_a", "kn_a", "qn_d", "kn_d", "sink_d",
                                "w_fnet", "b_sp", "w_br", "w_merge", "w_out")}
    w["w_spT"] = np.ascontiguousarray(f(inp["w_sp"]).transpose(0, 3, 1, 2))
    w["b_mergeT"] = np.ascontiguousarray(f(inp["b_merge"]).reshape(2, 32, 128).transpose(0, 2, 1))
    return w


def _run(builder_kwargs, per_core_extra, inp):
    b = Builder(**builder_kwargs)
    nc = b.build()
    w = _weights_layout(inp)
    c = np.asarray(inp["c"], np.float32)
    c_ctx = np.asarray(inp["c_ctx"], np.float32)
    in_maps = []
    for core in range(NCORES):
        bi = core // 4
        cv = np.stack([c[bi], c_ctx], 0)
        cvT = np.ascontiguousarray(cv.reshape(2, KC, 128).transpose(2, 1, 0))
        m = {"cvecT": cvT}
        m.update(w)
        m.update(_consts(core))
        m.update(per_core_extra(core))
        in_maps.append(m)
    res = run_bass_kernel_spmd(nc, in_maps, core_ids=list(range(NCORES)))
    return res.results, b


def kernel(**inp):
    x = np.asarray(inp["x"], np.float32)
    ctx = np.asarray(inp["ctx"], np.float32)

    def extra(core):
        bi, q = divmod(core, 4)
        return {"x": np.ascontiguousarray(x[bi, q * TOK:(q + 1) * TOK]), "ctx": np.ascontiguousarray(ctx[bi])}
    results, _ = _run(dict(layers=(0, 1), first=True, last=True), extra, inp)
    out = np.empty((2, SEQ, D), np.float32)
    for core in range(NCORES):
        bi, q = divmod(core, 4)
        out[bi, q * TOK:(q + 1) * TOK] = results[core]["out"]
    return out
```

```python
import numpy as np
import ml_dtypes
import concourse.bass as bass
import concourse.mybir as mybir
from concourse.bass_utils import run_bass_kernel_spmd

F32 = mybir.dt.float32
BF = mybir.dt.bfloat16
AF = mybir.ActivationFunctionType
ALU = mybir.AluOpType
AX = mybir.AxisListType
NPBF = ml_dtypes.bfloat16

NCORES = 8
TOK = 2048
NT = 16
CT = 2
TT = NT + CT
TALL = TT * 128
D = 1024
KC = 8
EPS = 1e-6
SEQ = 8192
INW = 2816
GROUPS = [(0, 512), (512, 512), (1024, 512), (1536, 512), (2048, 256)]


class Buf:
    __slots__ = ("name", "w", "rs")

    def __init__(self, name):
        self.name = name
        self.w = None
        self.rs = []


class Op:
    __slots__ = ("eng", "fn", "deps", "kind", "sem", "semval", "signal")

    def __init__(self, eng, fn, kind):
        self.eng = eng
        self.fn = fn
        self.kind = kind
        self.deps = []
        self.sem = None
        self.semval = 0
        self.signal = False


class Prog:
    ENGS = ("pe", "act", "dve", "pool", "sp")
    NS = 24

    def __init__(self, nc):
        self.nc = nc
        self.ops = {e: [] for e in self.ENGS}
        self.dma_count = 0
        self.dma_last = [None] * self.NS
        self.ncoll = 0
        self.bufs = {}
        self.pending = {e: [] for e in self.ENGS}
        self.last_c = {e: None for e in self.ENGS}
        self.colls = []

    def buf(self, name):
        b = self.bufs.get(name)
        if b is None:
            b = self.bufs[name] = Buf(name)
        return b

    def _add(self, op, d):
        if d is op:
            return
        op.deps.append(d)
        if d.kind == "c":
            d.signal = True

    def _deps(self, op, reads, writes):
        deps = []
        for b in reads:
            b = self.buf(b)
            if b.w is not None:
                deps.append(("raw", b.w))
            if op.kind == "c":
                b.rs = [r for r in b.rs if not (r.kind == "c" and r.eng == op.eng)]
            b.rs.append(op)
        for b in writes:
            b = self.buf(b)
            if b.w is not None:
                deps.append(("waw", b.w))
            for r in b.rs:
                if r is not op:
                    deps.append(("war", r))
            b.rs = []
            b.w = op
        for kind, d in deps:
            if d is op:
                continue
            if d.kind == "c" and op.kind == "c" and d.eng == op.eng:
                if op.eng == "pe":
                    continue
            self._add(op, d)
        if self.pending[op.eng]:
            for d in self.pending[op.eng]:
                self._add(op, d)
            self.pending[op.eng] = []

    def op(self, eng, meth, *args, R=(), W=(), **kw):
        fn = (lambda e, meth=meth, args=args, kw=kw: getattr(e, meth)(*args, **kw))
        o = Op(eng, fn, "c")
        self._deps(o, R, W)
        self.ops[eng].append(o)
        self.last_c[eng] = o
        return o

    def dma(self, q, out, in_, reads=(), writes=(), **kw):
        o = Op(q, (lambda e, out=out, in_=in_, kw=kw: e.dma_start(out=out, in_=in_, **kw)), "d")
        self._deps(o, reads, writes)
        slot = self.dma_count % self.NS
        self.dma_count += 1
        prev = self.dma_last[slot]
        o.sem = slot
        o.semval = (prev.semval if prev else 0) + 16
        if prev is not None:
            o.deps.append(prev)
        self.dma_last[slot] = o
        self.ops[q].append(o)
        return o

    def collective(self, fn, reads=(), writes=()):
        o = Op("pool", fn, "x")
        self._deps(o, reads, writes)
        o.sem = self.ncoll
        self.ncoll += 1
        self.ops["pool"].append(o)
        self.colls.append(o)
        return o

    def barrier(self):
        lasts = [o for o in self.last_c.values() if o is not None]
        lasts += [o for o in self.dma_last if o is not None]
        for e in self.ENGS:
            self.pending[e] = list(lasts)

    def emit(self):
        nc = self.nc
        for e in self.ENGS:
            cnt = 0
            for o in self.ops[e]:
                if o.kind == "c" and o.signal:
                    cnt += 1
                    o.semval = cnt
        esem = {e: nc.alloc_semaphore("es_" + e) for e in self.ENGS}
        dsem = [nc.alloc_semaphore("ds_%d" % i) for i in range(self.NS)]
        csem = [nc.alloc_semaphore("cs_%d" % i) for i in range(self.ncoll)]

        def semof(d):
            if d.kind == "c":
                return esem[d.eng], d.semval
            if d.kind == "d":
                return dsem[d.sem], d.semval
            return csem[d.sem], 1

        def gen(ename):
            def body(eng):
                waited = {}
                for o in self.ops[ename]:
                    need = {}
                    for d in o.deps:
                        s, v = semof(d)
                        if v > need.get(id(s), (None, 0))[1]:
                            need[id(s)] = (s, v)
                    for s, v in need.values():
                        if waited.get(id(s), 0) >= v:
                            continue
                        waited[id(s)] = v
                        eng.wait_ge(s, v)
                    ins = o.fn(eng)
                    if o.kind == "c":
                        if o.signal:
                            ins.then_inc(esem[ename], 1)
                    elif o.kind == "d":
                        ins.then_inc(dsem[o.sem], 16)
                    else:
                        ins.then_inc(csem[o.sem])
                last = {}
                for o in self.ops[ename]:
                    if o.kind in ("d", "x"):
                        s, v = semof(o)
                        if v > last.get(id(s), (None, 0))[1]:
                            last[id(s)] = (s, v)
                for s, v in last.values():
                    if waited.get(id(s), 0) < v:
                        eng.wait_ge(s, v)
            return body

        with nc.Block() as block:
            block.sync(gen("sp"))
            block.scalar(gen("act"))
            block.vector(gen("dve"))
            block.tensor(gen("pe"))
            block.gpsimd(gen("pool"))


def host_consts(core):
    b, q = divmod(core, 4)
    t = (q * TOK + np.arange(TOK)).astype(np.int64)
    r = (t // 64).astype(np.float32)
    col = (t % 64).astype(np.float32)
    inv = (np.float32(10000.0) ** (-np.arange(16, dtype=np.float32) / np.float32(16))).astype(np.float32)
    ar = r[:, None] * inv[None, :]
    ac = col[:, None] * inv[None, :]
    cr, sr, cc, sc = np.cos(ar), np.sin(ar), np.cos(ac), np.sin(ac)
    cos64 = np.concatenate([cr, cr, cc, cc], 1).astype(np.float32)
    sin64 = np.concatenate([-sr, sr, -sc, sc], 1).astype(np.float32)
    cos64 = np.ascontiguousarray(cos64.reshape(NT, 128, 64).transpose(1, 0, 2))
    sin64 = np.ascontiguousarray(sin64.reshape(NT, 128, 64).transpose(1, 0, 2))
    n1 = np.arange(64)
    th = 2 * np.pi * np.outer(n1, n1) / 64.0
    C, S = np.cos(th), np.sin(th)
    sc1 = 1.0 / np.sqrt(SEQ * 64.0)
    m1 = np.zeros((128, 128), np.float64)
    m1[:64, :64] = C
    m1[64:, :64] = -S
    m1[:64, 64:] = S
    m1[64:, 64:] = C
    m1 *= sc1
    n2 = np.arange(128)[:, None, None]
    k1 = np.arange(64)[None, :, None]
    k2 = (np.arange(32) + 32 * q)[None, None, :]
    ph = 2 * np.pi * n2 * (64 * k2 + k1) / float(SEQ)
    mc = np.cos(ph).reshape(128, 64 * 32)
    ms = (-np.sin(ph)).reshape(128, 64 * 32)
    n = np.arange(256)
    t256 = 2 * np.pi * np.outer(n, n) / 256.0
    sc2 = 1.0 / np.sqrt(256 * 64.0)
    c256 = (np.cos(t256) * sc2).reshape(2, 128, 256).transpose(1, 0, 2)
    s256n = (-np.sin(t256) * sc2).reshape(2, 128, 256).transpose(1, 0, 2)
    c64d = np.concatenate([C, C], 1).astype(np.float32)
    s64d = np.concatenate([S, S], 1).astype(np.float32)
    pk = np.arange(128)[:, None]
    pq = np.arange(128)[None, :]
    mP = (pk >= pq).astype(np.float32)
    mN = (pk <= pq).astype(np.float32)
    masks = np.zeros((128, 10, 128), np.float32)
    masks[:, 0] = mP
    masks[:, 1] = mN
    for s in range(4):
        if s == q - 1:
            masks[:, 2 + s] = mP
        if s == q + 1:
            masks[:, 6 + s] = mN
    sel2 = np.zeros((2, 256), np.float32)
    sel2[0, :128] = 1.0
    sel2[1, 128:] = 1.0
    return {
        "cos64": cos64, "sin64": sin64,
        "m1": m1.astype(NPBF), "mc": mc.astype(NPBF), "ms": ms.astype(NPBF),
        "c256": np.ascontiguousarray(c256).astype(NPBF), "s256n": np.ascontiguousarray(s256n).astype(NPBF),
        "c64d": c64d, "s64d": s64d,
        "masks": masks.astype(NPBF), "sel2": sel2,
        "ident": np.eye(128, dtype=np.float32),
    }


CONST_SPECS = {
    "cos64": ([128, NT, 64], F32), "sin64": ([128, NT, 64], F32),
    "m1": ([128, 128], BF), "mc": ([128, 2048], BF), "ms": ([128, 2048], BF),
    "c256": ([128, 2, 256], BF), "s256n": ([128, 2, 256], BF),
    "c64d": ([64, 128], F32), "s64d": ([64, 128], F32),
    "masks": ([128, 10, 128], BF), "sel2": ([2, 256], F32), "ident": ([128, 128], F32),
}

WEIGHT_SPECS = {
    "norm_w": [2, D], "w_ada": [2, D, 3 * D], "b_ada": [2, 3 * D], "w_in": [2, D, INW],
    "qn_a": [2, 64], "kn_a": [2, 64], "qn_d": [2, 64], "kn_d": [2, 64], "sink_d": [2, 4],
    "w_fnet": [2, 4, 64, 64], "w_spT": [2, 128, 4, 128], "b_sp": [2, 4, 128],
    "w_br": [2, 4, 256, D], "w_merge": [2, D, 4 * D], "b_mergeT": [2, 128, 32], "w_out": [2, D, D],
}


class Builder:
    def __init__(self, layers=(0, 1), first=True, last=True, debug=(), stop=None):
        self.stop = stop
        self.layers = tuple(layers)
        self.first = first
        self.last = last
        self.debug = set(debug)
        nc = self.nc = bass.Bass("TRN2", target_bir_lowering=False)
        self.P = Prog(nc)
        self.dbg_outs = {}
        self._uid = 0

    def uid(self, p="t"):
        self._uid += 1
        return "%s%d" % (p, self._uid)

    def dram_in(self, name, shape, dt=F32):
        return self.nc.dram_tensor(name, list(shape), dt, kind="ExternalInput").ap()

    def dram_out(self, name, shape, dt=F32):
        return self.nc.dram_tensor(name, list(shape), dt, kind="ExternalOutput").ap()

    def sb(self, name, shape, dt):
        return self.nc.alloc_sbuf_tensor("s_" + name, list(shape), dt)

    def tap(self, name, view, shape, reads, dt=F32):
        if name not in self.debug:
            return
        o = self.dram_out("dbg_" + name, shape, dt)
        self.dbg_outs["dbg_" + name] = (shape, dt)
        self.P.dma("sp", o, view, reads=reads)

    def build(self):
        nc, P = self.nc, self.P
        if self.first:
            self.x_in = self.dram_in("x", [TOK, D])
            self.ctx_in = self.dram_in("ctx", [256, D])
        else:
            self.xs_in = self.dram_in("xs_in", [TALL, D])
        self.cvecT = self.dram_in("cvecT", [128, KC, 2])
        self.W = {k: self.dram_in(k, s) for k, s in WEIGHT_SPECS.items()}
        self.C = {k: self.dram_in(k, s, dt) for k, (s, dt) in CONST_SPECS.items()}
        if self.last:
            self.out = self.dram_out("out", [TOK, D])
        else:
            self.xs_out = self.dram_out("xs_out", [TALL, D])
        self.xs = nc.dram_tensor("xs", [TALL, D], F32).ap()
        self.payload = [[nc.dram_tensor("payload%d_%d" % (l, j), [256, TOK], BF) for j in range(4)] for l in range(2)]
        self.gathered = [[nc.dram_tensor("gathered%d_%d" % (l, j), [1024, TOK], BF) for j in range(4)] for l in range(2)]

        self.HT = self.sb("HT", [128, KC, TALL], BF)
        self.YB = [self.sb("YB%d" % r, [128, 2, TALL], BF) for r in range(4)]
        self.WSF = [self.sb("WSF%d" % i, [128, 2048], F32) for i in range(2)]
        self.wsf_i = 0
        self.cos64 = self.sb("cos64", [128, NT, 64], F32)
        self.sin64 = self.sb("sin64", [128, NT, 64], F32)
        self.identf = self.sb("identf", [128, 128], F32)
        self.ident = self.sb("ident", [128, 128], BF)
        self.m1 = self.sb("m1", [128, 128], BF)
        self.c256 = self.sb("c256", [128, 2, 256], BF)
        self.s256n = self.sb("s256n", [128, 2, 256], BF)
        self.c64d = self.sb("c64d", [64, 128], F32)
        self.s64d = self.sb("s64d", [64, 128], F32)
        self.masks = self.sb("masks", [128, 10, 128], BF)
        self.sel2 = self.sb("sel2", [2, 256], F32)
        self.ones64 = self.sb("ones64", [128, 64], BF)
        self.scT = self.sb("scT", [128, KC, 2], F32)
        self.gate2 = self.sb("gate2", [2, D], F32)
        self.knw = [self.sb("knw%d" % i, [128, 128], F32) for i in range(2)]
        self.qnw = [self.sb("qnw%d" % i, [128, 256], F32) for i in range(2)]
        self.esink = self.sb("esink", [128, 4], F32)
        self.KTc = [self.sb("KTc%d" % i, [128, 256], BF) for i in range(2)]
        self.Vc = [self.sb("Vc%d" % i, [128, 2, 128], BF) for i in range(2)]
        self.UVc = self.sb("UVc", [128, 2, 2, 256], BF)
        self.BA = self.sb("BA", [128, 2, 2, 128], BF)
        self.stat = self.sb("stat", [128, 64], F32)
        self.stat_i = 0
        self.PSW = [nc.alloc_psum_tensor("psw%d" % i, [128, 1024], F32) for i in range(2)]
        self.PS = [self.PSW[i // 2][:, (i % 2) * 512:(i % 2 + 1) * 512] for i in range(4)]
        self.PS += [nc.alloc_psum_tensor("ps%d" % i, [128, 512], F32) for i in (4, 5)]
        self.PST = [nc.alloc_psum_tensor("pst%d" % i, [128, 1024], BF) for i in range(2)]
        self.ps_i = 0
        self.pst_i = 0
        self.ARENA_BYTES = 88 * 1024
        self.arena = self.sb("arena", [128, self.ARENA_BYTES // 2], BF)
        self.ar_off = 0
        self.ar_gen = 0

        for k in ("cos64", "sin64", "m1", "c256", "s256n", "c64d", "s64d", "masks", "sel2"):
            P.dma("sp", getattr(self, k)[:], self.C[k], writes=[k])
        P.dma("sp", self.identf[:], self.C["ident"], writes=["identf"])
        P.op("dve", "tensor_copy", self.ident[:], self.identf[:], R=["identf"], W=["ident"])
        P.op("pool", "memset", self.ones64[:], 1.0, R=[], W=["ones64"])
        P.dma("sp", self.scT[:], self.cvecT, writes=["scT"])
        P.op("act", "activation", self.scT[:], self.scT[:], AF.Silu, R=["scT"], W=["scT"])

        for li, l in enumerate(self.layers):
            is_first = self.first and li == 0
            is_last = self.last and li == len(self.layers) - 1
            self.layer(l, is_first, is_last, x_from_input=is_first,
                       x_src_xs_in=(not self.first and li == 0))
        if self.stop is not None:
            dst = self.out if self.last else self.xs_out
            P.dma("sp", dst[0:128, :], self.identf[:, :].unsqueeze(1).broadcast_to([128, 8, 128]) if False else self.cos64[:, :, :], reads=["cos64"])
        P.emit()
        return nc

    def phase(self):
        self.P.barrier()
        self.ar_off = 0
        self.ar_gen += 1

    def ar(self, name, shape, dt):
        n = int(np.prod(shape[1:]))
        nbytes = n * (4 if dt == F32 else 2)
        nbytes = (nbytes + 63) // 64 * 64
        assert self.ar_off + nbytes <= self.ARENA_BYTES, (name, self.ar_off, nbytes)
        e0 = self.ar_off // 2
        v = self.arena[:, e0:e0 + nbytes // 2]
        if dt == F32:
            v = v.bitcast(F32)
        v = v[:, 0:n]
        self.ar_off += nbytes
        if shape[0] != 128:
            v = v[0:shape[0]]
        if len(shape) == 3:
            v = v.rearrange("p (a b) -> p a b", a=shape[1])
        elif len(shape) == 4:
            v = v.rearrange("p (a b c) -> p a b c", a=shape[1], b=shape[2])
        return v, "%s_g%d" % (name, self.ar_gen)

    def ps(self):
        i = self.ps_i % 4
        self.ps_i += 1
        return self.PS[i], "ps%d" % i

    def pst(self):
        i = self.pst_i % 2
        self.pst_i += 1
        return self.PST[i], "pst%d" % i

    def statcol(self, n=1):
        c = (self.stat_i % 16) * 4
        self.stat_i += 1
        return self.stat[:, c:c + n], "stat%d" % c

    def load_w(self, src, dst, dstbuf, shape3, q="sp"):
        P = self.P
        i = self.wsf_i % 2
        self.wsf_i += 1
        a, b = shape3
        st = self.WSF[i][:, 0:a * b].rearrange("p (a b) -> p a b", a=a)
        P.dma(q, st, src, writes=["wsf%d" % i])
        P.op("pool", "tensor_copy", dst, st, R=["wsf%d" % i], W=[dstbuf])

    def w_cols(self, name, l, c0, n):
        return self.W[name][l, :, c0:c0 + n].rearrange("(kc p) n -> p kc n", p=128)

    def proj_fm(self, wb, wbuf, c0, nchunks, tiles, consume):
        P = self.P
        for c in range(nchunks):
            for (t0, n) in GROUPS:
                if t0 // 128 >= tiles:
                    continue
                ps, pb = self.ps()
                htb = ["HT%d" % i for i in range(t0 // 128, (t0 + n) // 128)]
                for kc in range(KC):
                    P.op("pe", "matmul", ps[:, 0:n], wb[:, kc, c0 + c * 128:c0 + (c + 1) * 128], self.HT[:, kc, t0:t0 + n],
                        start=(kc == 0), stop=(kc == KC - 1), R=[wbuf] + htb, W=[pb])
                consume(ps, pb, c, t0, n)

    def proj_tm(self, wb, wbuf, c0, ncols, tile):
        P = self.P
        ps, pb = self.ps()
        for kc in range(KC):
            P.op("pe", "matmul", ps[:, 0:ncols], self.HT[:, kc, tile * 128:(tile + 1) * 128], wb[:, kc, c0:c0 + ncols],
                start=(kc == 0), stop=(kc == KC - 1), R=[wbuf, "HT%d" % tile], W=[pb])
        return ps, pb

    def headnorm_rope(self, ps, pb, c0, nh, wtile, wbuf, tile, tmp, out_bf, outbuf, permute):
        P = self.P
        n = nh * 64
        (sq, sqb), (t1, t1b), (t2, t2b) = tmp
        P.op("act", "activation", sq[:, 0:n], ps[:, c0:c0 + n], AF.Square, R=[pb], W=[sqb])
        ssq, ssqb = self.statcol(4)
        rt, rtb = self.statcol(4)
        rr, rrb = self.statcol(4)
        P.op("dve", "tensor_reduce", ssq[:, 0:nh], sq[:, 0:n].rearrange("p (h d) -> p h d", h=nh), AX.X, ALU.add, R=[sqb], W=[ssqb])
        P.op("act", "activation", rt[:, 0:nh], ssq[:, 0:nh], AF.Sqrt, bias=64.0 * EPS, scale=1.0, R=[ssqb], W=[rtb])
        P.op("dve", "reciprocal", rr[:, 0:nh], rt[:, 0:nh], R=[rtb], W=[rrb])
        P.op("dve", "tensor_tensor", t1[:, 0:n].rearrange("p (h d) -> p h d", h=nh), ps[:, c0:c0 + n].rearrange("p (h d) -> p h d", h=nh),
            rr[:, 0:nh].unsqueeze(2).broadcast_to([128, nh, 64]), ALU.mult, R=[pb, rrb], W=[t1b])
        if permute:
            ov = out_bf[:, 0:n].rearrange("p (g kv d) -> p kv g d", g=2, kv=2)
        if tile >= NT:
            if permute:
                P.op("pool", "tensor_tensor", ov, t1[:, 0:n].rearrange("p (kv g d) -> p kv g d", kv=2, g=2),
                    wtile[:, 0:n].rearrange("p (kv g d) -> p kv g d", kv=2, g=2), ALU.mult, R=[t1b, wbuf], W=[outbuf])
            else:
                P.op("pool", "tensor_tensor", out_bf[:, 0:n], t1[:, 0:n], wtile[:, 0:n], ALU.mult, R=[t1b, wbuf], W=[outbuf])
            return
        P.op("pool", "tensor_tensor", sq[:, 0:n], t1[:, 0:n], wtile[:, 0:n], ALU.mult, R=[t1b, wbuf, sqb], W=[sqb])
        cosb = self.cos64[:, tile, :].unsqueeze(1).broadcast_to([128, nh, 64])
        sinv = self.sin64[:, tile, :].rearrange("p (t j) -> p t j", t=4)
        xv = sq[:, 0:n].rearrange("p (h t j) -> p h t j", h=nh, t=4)
        t2v = t2[:, 0:n].rearrange("p (h t j) -> p h t j", h=nh, t=4)
        P.op("dve", "tensor_tensor", t1[:, 0:n].rearrange("p (h d) -> p h d", h=nh),
                                              sq[:, 0:n].rearrange("p (h d) -> p h d", h=nh), cosb, ALU.mult, R=[sqb, "cos64", t1b], W=[t1b])
        P.op("pool", "tensor_tensor", t2v[:, :, 0::2, :], xv[:, :, 1::2, :],
                                               sinv[:, 0::2, :].unsqueeze(1).broadcast_to([128, nh, 2, 16]), ALU.mult, R=[sqb, "sin64"], W=[t2b + "a"])
        P.op("pool", "tensor_tensor", t2v[:, :, 1::2, :], xv[:, :, 0::2, :],
                                               sinv[:, 1::2, :].unsqueeze(1).broadcast_to([128, nh, 2, 16]), ALU.mult, R=[sqb, "sin64"], W=[t2b + "b"])
        if permute:
            P.op("dve", "tensor_tensor", ov, t1[:, 0:n].rearrange("p (kv g d) -> p kv g d", kv=2, g=2),
                t2[:, 0:n].rearrange("p (kv g d) -> p kv g d", kv=2, g=2), ALU.add, R=[t1b, t2b + "a", t2b + "b"], W=[outbuf])
        else:
            P.op("dve", "tensor_tensor", out_bf[:, 0:n], t1[:, 0:n], t2[:, 0:n], ALU.add, R=[t1b, t2b + "a", t2b + "b"], W=[outbuf])

    def transpose_to(self, src_bf, srcbuf, nblk, dst_fn, dstbuf_fn, eng="act"):
        P = self.P
        pt, ptb = self.pst()
        for j in range(nblk):
            P.op("pe", "transpose", pt[:, j * 128:(j + 1) * 128], src_bf[:, j * 128:(j + 1) * 128], self.ident[:], R=[srcbuf, "ident"], W=[ptb])
        for j in range(nblk):
            if eng == "act":
                P.op("act", "copy", dst_fn(j), pt[:, j * 128:(j + 1) * 128], R=[ptb], W=[dstbuf_fn(j)])
            else:
                P.op("dve", "tensor_copy", dst_fn(j), pt[:, j * 128:(j + 1) * 128], R=[ptb], W=[dstbuf_fn(j)])

    def layer(self, l, is_first, is_last, x_from_input, x_src_xs_in):
        nc, P, W = self.nc, self.P, self.W
        do_ctx_out = not is_last_layer(l)
        ntl = TT if do_ctx_out else NT
        pay, gat = self.payload[l], self.gathered[l]

        def x_tile_src(i):
            if x_from_input:
                return self.x_in[i * 128:(i + 1) * 128, :] if i < NT else self.ctx_in[(i - NT) * 128:(i - NT + 1) * 128, :]
            if x_src_xs_in:
                return self.xs_in[i * 128:(i + 1) * 128, :]
            return self.xs[i * 128:(i + 1) * 128, :]

        self.phase()
        L = "L%d" % l
        mod, modb = self.ar("mod", [2, 3 * D], F32)
        arow, arowb = self.ar("arow", [2, D], F32)
        normw2, normw2b = self.ar("normw2", [2, D], F32)
        bada2, bada2b = self.ar("bada2", [2, 3 * D], F32)
        P.dma("sp", normw2, W["norm_w"][l:l + 1, :].partition_broadcast(2), writes=[normw2b])
        P.dma("sp", bada2, W["b_ada"][l:l + 1, :].partition_broadcast(2), writes=[bada2b])
        for i, (kn, qn) in enumerate((("kn_a", "qn_a"), ("kn_d", "qn_d"))):
            for h in range(2):
                P.dma("sp", self.knw[i][:, h * 64:(h + 1) * 64], W[kn][l:l + 1, :].partition_broadcast(128), writes=["knw%d" % i])
            for h in range(4):
                P.dma("sp", self.qnw[i][:, h * 64:(h + 1) * 64], W[qn][l:l + 1, :].partition_broadcast(128), writes=["qnw%d" % i])
            P.op("act", "mul", self.knw[i][:], self.knw[i][:], 8.0, R=["knw%d" % i], W=["knw%d" % i])
        P.dma("sp", self.esink[:], W["sink_d"][l:l + 1, :].partition_broadcast(128), writes=["esink"])
        P.op("act", "activation", self.esink[:], self.esink[:], AF.Exp, R=["esink"], W=["esink"])

        for blk in range(6):
            ps, pb = self.ps()
            for half in range(2):
                i = self.wsf_i % 2
                self.wsf_i += 1
                st = self.WSF[i][:].rearrange("p (a b) -> p a b", a=KC)
                c0 = blk * 512 + half * 256
                P.dma("sp", st, self.w_cols("w_ada", l, c0, 256), writes=["wsf%d" % i])
                for kc in range(KC):
                    P.op("pe", "matmul", ps[0:2, half * 256:(half + 1) * 256], self.scT[:, kc, :], st[:, kc, :],
                        start=(kc == 0), stop=(kc == KC - 1), R=["scT", "wsf%d" % i], W=[pb])
            P.op("dve", "tensor_tensor", mod[0:2, blk * 512:(blk + 1) * 512], ps[0:2, :], bada2[0:2, blk * 512:(blk + 1) * 512], ALU.add, R=[pb, bada2b], W=[modb])
        self.tap(L + "mod", mod, [2, 3 * D], [modb])
        P.op("dve", "scalar_tensor_tensor", arow, mod[0:2, D:2 * D], 1.0, normw2, ALU.add, ALU.mult, R=[modb, normw2b], W=[arowb])
        P.op("act", "copy", self.gate2[:], mod[0:2, 2 * D:3 * D], R=[modb], W=["gate2"])
        a_bc, a_bcb = self.ar("a_bc", [128, 2, D], F32)
        sh_bc, sh_bcb = self.ar("sh_bc", [128, 2, D], F32)
        for r in range(2):
            for half in range(2):
                for which, (dst, dbuf, src, sbuf) in enumerate(((a_bc, a_bcb, arow, arowb), (sh_bc, sh_bcb, mod, modb))):
                    ps, pb = self.ps()
                    P.op("pe", "matmul", ps[:, :], self.sel2[0:2, r * 128:(r + 1) * 128], src[0:2, half * 512:(half + 1) * 512],
                        start=True, stop=True, R=["sel2", sbuf], W=[pb])
                    P.op("act", "copy", dst[:, r, half * 512:(half + 1) * 512], ps[:, :], R=[pb], W=[dbuf])

        XT = [self.ar("XT%d" % i, [128, D], F32) for i in range(2)]
        TMPF = [self.ar("TMPF%d" % i, [128, D], F32) for i in range(2)]
        HB = [self.ar("HB%d" % i, [128, D], BF) for i in range(2)]
        for i in range(TT):
            s = i % 2
            r = 0 if i < NT else 1
            (xt, xtb), (tf, tfb), (hb, hbb) = XT[s], TMPF[s], HB[s]
            P.dma("sp", xt, x_tile_src(i), writes=[xtb])
            ssq, ssqb = self.statcol()
            rt, rtb = self.statcol()
            rs, rsb = self.statcol()
            P.op("act", "activation", tf, xt, AF.Square, accum_out=ssq, R=[xtb], W=[tfb, ssqb])
            P.op("act", "activation", rt, ssq, AF.Sqrt, bias=EPS, scale=1.0 / D, R=[ssqb], W=[rtb])
            P.op("dve", "reciprocal", rs, rt, R=[rtb], W=[rsb])
            P.op("dve", "scalar_tensor_tensor", tf, xt, rs, a_bc[:, r, :], ALU.mult, ALU.mult, R=[xtb, rsb, a_bcb, tfb], W=[tfb])
            P.op("pool", "tensor_tensor", hb, tf, sh_bc[:, r, :], ALU.add, R=[tfb, sh_bcb], W=[hbb])
            if i == 0:
                self.tap(L + "h0", hb, [128, D], [hbb], BF)
            pt, ptb = self.pst()
            for kc in range(KC):
                P.op("pe", "transpose", pt[:, kc * 128:(kc + 1) * 128], hb[:, kc * 128:(kc + 1) * 128], self.ident[:], R=[hbb, "ident"], W=[ptb])
            P.op("act", "copy", self.HT[:, :, i * 128:(i + 1) * 128], pt[:].rearrange("p (k t) -> p k t", k=KC), R=[ptb], W=["HT%d" % i])

        if self.stop == "A1":
            return
        self.phase()
        WB, WBb = self.ar("WBkv", [128, KC, 256], BF)
        KR = [self.ar("kr%d" % j, [128, 128], BF) for j in range(2)]
        KTs = [self.ar("KTs%d" % a, [128, TOK], BF) for a in range(2)]
        Vs = [self.ar("Vs%d" % a, [128, NT, 128], BF) for a in range(2)]
        tmps = [[self.ar("hnt%d_%d" % (k, j), [128, 256], F32) for j in range(3)] for k in range(2)]
        for a, c0 in enumerate((256, 1024)):
            self.load_w(self.w_cols("w_in", l, c0, 256), WB, WBb, (KC, 256))
            for i in range(TT):
                ps, pb = self.proj_tm(WB, WBb, 0, 256, i)
                kr, krb = KR[i % 2]
                self.headnorm_rope(ps, pb, 0, 2, self.knw[a], "knw%d" % a, i, tmps[i % 2], kr, krb, permute=False)
                if i < NT:
                    kts, ktsb = KTs[a]
                    vs, vsb = Vs[a]
                    self.transpose_to(kr, krb, 1, lambda j, kts=kts, i=i: kts[:, i * 128:(i + 1) * 128], lambda j, ktsb=ktsb: ktsb)
                    P.op("act", "copy", vs[:, i, :], ps[:, 128:256], R=[pb], W=[vsb])
                else:
                    j = i - NT
                    self.transpose_to(kr, krb, 1, lambda jj, a=a, j=j: self.KTc[a][:, j * 128:(j + 1) * 128], lambda jj, a=a: "KTc%d" % a)
                    P.op("act", "copy", self.Vc[a][:, j, :], ps[:, 128:256], R=[pb], W=["Vc%d" % a])
            kts, ktsb = KTs[a]
            vs, vsb = Vs[a]
            P.dma("sp", pay[0].ap()[a * 128:(a + 1) * 128, :], kts, reads=[ktsb], writes=["pay_k%d" % a])
            P.dma("sp", pay[3].ap()[a * 128:(a + 1) * 128, :].rearrange("p (t d) -> p t d", d=128), vs, reads=[vsb],
                  writes=["pay_v%d" % a])
        if l == self.layers[0]:
            self.tap(L + "KTsA", KTs[0][0], [128, TOK], [KTs[0][1]], BF)
            self.tap(L + "VsA", Vs[0][0], [128, NT, 128], [Vs[0][1]], BF)
            self.tap(L + "KTcA", self.KTc[0][:], [128, 256], ["KTc0"], BF)

        self.load_w(self.w_cols("w_in", l, 1536, 256), WB, WBb, (KC, 256))
        BFT, BFTb = self.ar("BFT", [128, 2, TALL], BF)

        def bf_consume(ps, pb, c, t0, n):
            P.op("act", "copy", BFT[:, c, t0:t0 + n], ps[:, 0:n], R=[pb], W=[BFTb])
        self.proj_fm(WB, WBb, 0, 2, TT, bf_consume)
        wf, wfb = self.ar("wf", [64, 4, 64], F32)
        P.dma("sp", wf, W["w_fnet"][l].rearrange("g j c -> j g c"), writes=[wfb])
        P.op("pool", "memset", self.BA[:], 0.0, R=[], W=["BA"])
        for cs, tab in enumerate((self.c64d, self.s64d)):
            for g in range(4):
                ps, pb = self.ps()
                P.op("pe", "matmul", ps[:, 0:64], tab[0:64, :], wf[0:64, g, :], start=True, stop=True, R=[wfb, "c64d", "s64d"], W=[pb])
                rows = slice((g % 2) * 64, (g % 2) * 64 + 64)
                P.op("dve", "tensor_copy", self.BA[rows, cs, g // 2, (g % 2) * 64:(g % 2) * 64 + 64], ps[rows, 0:64], R=[pb, "BA"], W=["BA"])
        UVT, UVTb = self.ar("UVT", [128, 2, 2, TOK], BF)
        for cs in range(2):
            for c in range(2):
                for (t0, n) in GROUPS[:4]:
                    ps, pb = self.ps()
                    P.op("pe", "matmul", ps[:, 0:n], self.BA[:, cs, c, :], BFT[:, c, t0:t0 + n], start=True, stop=True, R=["BA", BFTb], W=[pb])
                    P.op("act", "copy", UVT[:, cs, c, t0:t0 + n], ps[:, 0:n], R=[pb], W=[UVTb])
                for j in range(2):
                    ps, pb = self.ps()
                    P.op("pe", "matmul", ps[:, 0:128], BFT[:, c, TOK + j * 128:TOK + (j + 1) * 128], self.BA[:, cs, c, :], start=True, stop=True, R=["BA", BFTb], W=[pb])
                    P.op("dve", "tensor_copy", self.UVc[:, cs, j, c * 128:(c + 1) * 128], ps[:, 0:128], R=[pb], W=["UVc"])
        for cs in range(2):
            P.dma("sp", pay[1 + cs].ap().rearrange("(c p) t -> p c t", p=128), UVT[:, cs, :, :],
                  reads=[UVTb], writes=["pay_uv%d" % cs])
        self.tap(L + "UVT", UVT, [128, 2, 2, TOK], [UVTb], BF)

        if self.stop == "A":
            return
        payb = [["pay_k0", "pay_k1"], ["pay_uv0"], ["pay_uv1"], ["pay_v0", "pay_v1"]]
        for j in (0, 3, 1, 2):
            P.collective(lambda e, j=j: e.collective_compute("AllGather", ALU.bypass, replica_groups=[[0, 1, 2, 3], [4, 5, 6, 7]],
                                                             ins=[pay[j].ap().opt()], outs=[gat[j].ap().opt()]),
                         reads=payb[j], writes=["gat%d" % j])
        G = [g.ap() for g in gat]

        if self.stop == "G":
            return
        self.phase()
        self.sgu(l, ntl)
        self.phase()
        self.attention(l, 0, ntl, G, pay)
        self.phase()
        self.attention(l, 1, ntl, G, pay)
        self.phase()
        self.fnet(l, ntl, G)
        if l == self.layers[0]:
            for r in range(4):
                self.tap(L + "YB%d" % r, self.YB[r][:], [128, 2, TALL], ["YB%d_%d_%d" % (r, c, i) for c in range(2) for i in range(TT)], BF)
        if self.stop == "B5":
            return
        self.phase()
        self.merge(l, ntl, is_last, x_tile_src)

    def attention(self, l, which, ntl, G, pay):
        P, W = self.P, self.W
        q0 = 0 if which == 0 else 768
        z0 = 512 if which == 0 else 1280
        YB = self.YB[which]
        nqg = [g for g in GROUPS if g[0] // 128 < ntl]
        WQ, WQb = self.ar("WQ", [128, KC, 256], BF)
        WZ, WZb = self.ar("WZ", [128, KC, 256], BF)
        self.load_w(self.w_cols("w_in", l, q0, 256), WQ, WQb, (KC, 256))
        self.load_w(self.w_cols("w_in", l, z0, 256), WZ, WZb, (KC, 256))
        QT = [self.ar("QT%d" % g, [128, TALL], BF) for g in range(2)]
        def vview(V, k0, k1):
            return V[:, k0:k1, :].rearrange("p t (kv x) -> p t kv x", kv=2)[:, :, :, 0:64]

        if which == 0:
            NKB = 2 + 64
            KT, KTb = self.ar("KT", [128, NKB * 128], BF)
            V, Vb = self.ar("V", [128, NKB, 256], BF)
            P.op("pool", "memset", V[:, :, :].rearrange("p t (kv x) -> p t kv x", kv=2)[:, :, :, 64:128], 1.0, R=[], W=[Vb + "ones"])
            P.op("act", "copy", KT[:, 0:256], self.KTc[0][:], R=["KTc0"], W=[KTb + "c"])
            P.op("act", "copy", vview(V, 0, 2), self.Vc[0][:].rearrange("p t (kv d) -> p t kv d", kv=2), R=["Vc0"], W=[Vb + "c"])
            for s_ in range(4):
                P.dma("sp", KT[:, 256 + s_ * TOK:256 + (s_ + 1) * TOK], G[0][s_ * 256:s_ * 256 + 128, :], reads=["gat0"], writes=[KTb + str(s_)])
                for hf in range(2):
                    P.dma("sp", vview(V, 2 + s_ * NT + hf * 8, 2 + s_ * NT + hf * 8 + 8),
                          G[3][s_ * 256:s_ * 256 + 128, hf * 1024:(hf + 1) * 1024].rearrange("p (t kv d) -> p t kv d", kv=2, d=64),
                          reads=["gat3"], writes=[Vb + str(s_)])

            def kbufs(kb):
                return [KTb + ("c" if kb < 2 else str((kb - 2) // NT))], [Vb + ("c" if kb < 2 else str((kb - 2) // NT)), Vb + "ones"]
        else:
            NKB = 2 + NT + 8
            KT, KTb = self.ar("KT", [128, NKB * 128], BF)
            V, Vb = self.ar("V", [128, NKB, 256], BF)
            P.op("pool", "memset", V[:, :, :].rearrange("p t (kv x) -> p t kv x", kv=2)[:, :, :, 64:128], 1.0, R=[], W=[Vb + "ones"])
            P.op("act", "copy", KT[:, 0:256], self.KTc[1][:], R=["KTc1"], W=[KTb])
            P.op("act", "copy", vview(V, 0, 2), self.Vc[1][:].rearrange("p t (kv d) -> p t kv d", kv=2), R=["Vc1"], W=[Vb])
            P.dma("sp", KT[:, 256:256 + TOK], pay[0].ap()[128:256, :], reads=["pay_k1"], writes=[KTb])
            for hf in range(2):
                P.dma("sp", vview(V, 2 + hf * 8, 2 + hf * 8 + 8),
                      pay[3].ap()[128:256, hf * 1024:(hf + 1) * 1024].rearrange("p (t kv d) -> p t kv d", kv=2, d=64),
                      reads=["pay_v1"], writes=[Vb])
            for s_ in range(4):
                for kb, c0 in ((2 + NT + s_, TOK - 128), (2 + NT + 4 + s_, 0)):
                    P.dma("sp", KT[:, kb * 128:(kb + 1) * 128], G[0][s_ * 256 + 128:s_ * 256 + 256, c0:c0 + 128], reads=["gat0"], writes=[KTb])
                    P.dma("sp", vview(V, kb, kb + 1),
                          G[3][s_ * 256 + 128:s_ * 256 + 256, c0:c0 + 128].rearrange("p (t kv d) -> p t kv d", kv=2, d=64),
                          reads=["gat3"], writes=[Vb])

            def kbufs(kb):
                return [KTb], [Vb, Vb + "ones"]

        tmps = [[self.ar("hnt%d_%d" % (k, j), [128, 256], F32) for j in range(3)] for k in range(2)]
        QR = [self.ar("QR%d" % j, [128, 256], BF) for j in range(2)]
        for i in range(ntl):
            ps, pb = self.proj_tm(WQ, WQb, 0, 256, i)
            qr, qrb = QR[i % 2]
            self.headnorm_rope(ps, pb, 0, 4, self.qnw[which], "qnw%d" % which, i, tmps[i % 2], qr, qrb, permute=True)
            self.transpose_to(qr, qrb, 2, lambda j, i=i: QT[j][0][:, i * 128:(i + 1) * 128], lambda j, i=i: QT[j][1] + "_%d" % i)

        def z_consume(ps, pb, c, t0, n):
            P.op("act", "activation", YB[:, c, t0:t0 + n], ps[:, 0:n], AF.Silu, R=[pb], W=["YB%d_%d_%d" % (which, c, i) for i in range(t0 // 128, (t0 + n) // 128)])
        self.proj_fm(WZ, WZb, 0, 2, ntl, z_consume)
        if which == 0 and l == self.layers[0]:
            self.tap("L%dQT0" % l, QT[0][0], [128, TALL], [QT[0][1] + "_%d" % i for i in range(ntl)], BF)

        PT = [self.ar("PT%d" % j, [128, 1024], BF) for j in range(3)]
        fin = [self.ar("fin%d" % j, [128, 512], F32) for j in range(3)]

        work = []

        def add_pair(g, t0, n, keylist):
            cw = 512 // n
            chunks = [keylist[c:c + cw] for c in range(0, len(keylist), cw)]
            for ci, ch in enumerate(chunks):
                work.append(dict(g=g, t0=t0, n=n, keys=ch, first=(ci == 0), last=(ci == len(chunks) - 1)))

        if which == 0:
            for (t0, n) in nqg:
                keys = [(kb, None) for kb in range(NKB)] if t0 < TOK else [(0, None), (1, None)]
                for g in range(2):
                    add_pair(g, t0, n, keys)
        else:
            for i in range(ntl):
                if i >= NT:
                    keys = [(0, None), (1, None)]
                else:
                    keys = [(0, None), (1, None), (2 + i, None)]
                    if i > 0:
                        keys.append((2 + i - 1, 0))
                    else:
                        keys += [(2 + NT + s_, 2 + s_) for s_ in range(4)]
                    if i < NT - 1:
                        keys.append((2 + i + 1, 1))
                    else:
                        keys += [(2 + NT + 4 + s_, 6 + s_) for s_ in range(4)]
                for g in range(2):
                    add_pair(g, i * 128, 128, keys)

        accs = [(self.PS[4], "ps4"), (self.PS[5], "ps5")]

        def stage1(w, slot):
            g, t0, n = w["g"], w["t0"], w["n"]
            qt, qtb = QT[g]
            qbufs = [qtb + "_%d" % i for i in range(t0 // 128, (t0 + n) // 128)]
            psw = self.PSW[slot % 2]
            pbs = ["ps%d" % (2 * (slot % 2)), "ps%d" % (2 * (slot % 2) + 1)]
            pt, ptb = PT[slot % 3]
            cnt = len(w["keys"])
            for j, (kb, mi) in enumerate(w["keys"]):
                kb_k, _ = kbufs(kb)
                for kv in range(2):
                    krows = slice(kv * 64, (kv + 1) * 64)
                    P.op("pe", "matmul", psw[:, kv * 512 + j * n:kv * 512 + (j + 1) * n], KT[krows, kb * 128:(kb + 1) * 128],
                         qt[krows, t0:t0 + n], start=True, stop=True, R=kb_k + qbufs, W=[pbs[kv]])
            P.op("act", "activation", pt[:, :].rearrange("p (a b) -> p a b", a=2)[:, :, 0:cnt * n],
                 psw[:, :].rearrange("p (a b) -> p a b", a=2)[:, :, 0:cnt * n], AF.Exp, R=pbs, W=[ptb])
            for j, (kb, mi) in enumerate(w["keys"]):
                if mi is not None:
                    P.op("pool", "tensor_tensor", pt[:, :].rearrange("p (a b) -> p a b", a=2)[:, :, j * n:(j + 1) * n],
                         pt[:, :].rearrange("p (a b) -> p a b", a=2)[:, :, j * n:(j + 1) * n],
                         self.masks[:, mi, 0:n].unsqueeze(1).broadcast_to([128, 2, n]), ALU.mult, R=[ptb, "masks"], W=[ptb])

        def stage2(w, slot):
            g, t0, n = w["g"], w["t0"], w["n"]
            rows = slice(g * 64, (g + 1) * 64)
            pt, ptb = PT[slot % 3]
            cnt = len(w["keys"])
            for j, (kb, mi) in enumerate(w["keys"]):
                _, kb_v = kbufs(kb)
                st = w["first"] and j == 0
                sp = w["last"] and j == cnt - 1
                for kv in range(2):
                    acc, accb = accs[kv]
                    P.op("pe", "matmul", acc[:, 0:n], V[:, kb, kv * 128:(kv + 1) * 128], pt[:, kv * 512 + j * n:kv * 512 + (j + 1) * n],
                         start=st, stop=sp, R=kb_v + [ptb], W=[accb])
            if not w["last"]:
                return
            for kv in range(2):
                acc, accb = accs[kv]
                h = 2 * kv + g
                (f0, f0b), (f1, f1b), (f2, f2b) = fin
                if which == 1:
                    P.op("dve", "tensor_scalar", f2[64:128, 0:n], acc[64:128, 0:n], self.esink[64:128, h:h + 1], None, ALU.add,
                         R=[accb, "esink"], W=[f2b])
                    P.op("dve", "reciprocal", f0[rows, 0:n], f2[64:128, 0:n], R=[f2b], W=[f0b])
                else:
                    P.op("dve", "reciprocal", f0[rows, 0:n], acc[64:128, 0:n], R=[accb], W=[f0b])
                P.op("dve", "tensor_copy", f1[rows, 0:n], acc[0:64, 0:n], R=[accb], W=[f1b])
                P.op("pool", "tensor_tensor", f1[rows, 0:n], f1[rows, 0:n], f0[rows, 0:n], ALU.mult, R=[f1b, f0b], W=[f1b])
                ybb = ["YB%d_%d_%d" % (which, kv, i) for i in range(t0 // 128, (t0 + n) // 128)]
                P.op("pool", "tensor_tensor", YB[rows, kv, t0:t0 + n], f1[rows, 0:n], YB[rows, kv, t0:t0 + n], ALU.mult,
                     R=[f1b] + ybb, W=ybb)

        LOOK = 1
        for k in range(len(work) + LOOK):
            if k < len(work):
                stage1(work[k], k)
            if k >= LOOK:
                stage2(work[k - LOOK], k - LOOK)

    def fnet(self, l, ntl, G):
        P, W = self.P, self.W
        YB = self.YB[2]
        WZ, WZb = self.ar("WZ", [128, KC, 256], BF)
        self.load_w(self.w_cols("w_in", l, 1792, 256), WZ, WZb, (KC, 256))

        def z_consume(ps, pb, c, t0, n):
            P.op("act", "activation", YB[:, c, t0:t0 + n], ps[:, 0:n], AF.Silu, R=[pb], W=["YB2_%d_%d" % (c, i) for i in range(t0 // 128, (t0 + n) // 128)])
        self.proj_fm(WZ, WZb, 0, 2, ntl, z_consume)
        MC, MCb = self.ar("MC", [128, 64, 32], BF)
        MS, MSb = self.ar("MS", [128, 64, 32], BF)
        P.dma("sp", MC, self.C["mc"].rearrange("p (a b) -> p a b", a=64), writes=[MCb])
        P.dma("sp", MS, self.C["ms"].rearrange("p (a b) -> p a b", a=64), writes=[MSb])
        Z = [self.ar("Z%d" % j, [128, 64, 128], BF) for j in range(2)]
        T3, T3b = self.ar("T3", [128, 64, 128], BF)
        sp = 0
        for c in range(2):
            for hf in range(2):
                z, zb = Z[sp % 2]
                sp += 1
                for s in range(4):
                    for ri in range(2):
                        r0 = s * 256 + c * 128 + hf * 64
                        P.dma("sp", z[ri * 64 + s * 16:ri * 64 + (s + 1) * 16, :, :],
                              G[1 + ri][r0:r0 + 64, :].rearrange("ch (t n) -> t ch n", n=128), reads=["gat%d" % (1 + ri)], writes=[zb])
                for ch4 in range(16):
                    ps, pb = self.ps()
                    for j in range(4):
                        ch = ch4 * 4 + j
                        P.op("pe", "matmul", ps[:, j * 128:(j + 1) * 128], z[:, ch, :], self.m1[:, :],
                                                                            start=True, stop=True, R=[zb, "m1"], W=[pb])
                    if ch4 % 2 == 0:
                        P.op("act", "copy", T3[:, ch4 * 4:ch4 * 4 + 4, :], ps[:].rearrange("p (a b) -> p a b", a=4), R=[pb], W=[T3b])
                    else:
                        P.op("dve", "tensor_copy", T3[:, ch4 * 4:ch4 * 4 + 4, :], ps[:].rearrange("p (a b) -> p a b", a=4), R=[pb], W=[T3b])
                rows = slice(hf * 64, (hf + 1) * 64)
                for bank in range(4):
                    ps, pb = self.ps()
                    for kk in range(16):
                        k1 = bank * 16 + kk
                        P.op("pe", "matmul", ps[rows, kk * 32:(kk + 1) * 32], T3[:, :, k1], MC[:, k1, :],
                                                                         start=True, stop=False, tile_position=(0, hf * 64), R=[T3b, MCb], W=[pb])
                        P.op("pe", "matmul", ps[rows, kk * 32:(kk + 1) * 32], T3[:, :, 64 + k1], MS[:, k1, :],
                                                                         start=False, stop=True, tile_position=(0, hf * 64), R=[T3b, MSb], W=[pb])
                    yv = YB[rows, c, 0:TOK].rearrange("p (k2 k1) -> p k1 k2", k1=64)[:, bank * 16:(bank + 1) * 16, :]
                    ybb = ["YB2_%d_%d" % (c, i) for i in range(NT)]
                    P.op("dve", "tensor_tensor", yv, ps[rows, :].rearrange("p (a b) -> p a b", a=16), yv, ALU.mult, R=[pb] + ybb, W=ybb)
        if ntl > NT:
            for c in range(2):
                ps, pb = self.ps()
                k = 0
                for uv, tab in ((0, self.c256), (1, self.s256n)):
                    for j in range(2):
                        P.op("pe", "matmul", ps[:, 0:256], self.UVc[:, uv, j, c * 128:(c + 1) * 128], tab[:, j, :], start=(k == 0), stop=(k == 3), R=["UVc", "c256", "s256n"], W=[pb])
                        k += 1
                ybb = ["YB2_%d_%d" % (c, i) for i in range(NT, TT)]
                P.op("dve", "tensor_tensor", YB[:, c, TOK:TALL], ps[:, 0:256], YB[:, c, TOK:TALL], ALU.mult, R=[pb] + ybb, W=ybb)

    def sgu(self, l, ntl):
        P, W = self.P, self.W
        YB = self.YB[3]
        WU, WUb = self.ar("WU", [128, KC, 256], BF)
        WV, WVb = self.ar("WV", [128, KC, 256], BF)
        WZ, WZb = self.ar("WZ", [128, KC, 256], BF)
        self.load_w(self.w_cols("w_in", l, 2048, 256), WU, WUb, (KC, 256))
        self.load_w(self.w_cols("w_in", l, 2304, 256), WV, WVb, (KC, 256))
        self.load_w(self.w_cols("w_in", l, 2560, 256), WZ, WZb, (KC, 256))
        GU, GUb = self.ar("GU", [128, 2, TALL], BF)
        GV, GVb = self.ar("GV", [128, TT, 256], BF)
        wsp, wspb = self.ar("wsp", [128, 4, 128], BF)
        bsp, bspb = self.ar("bsp", [128, 2, 128], F32)
        self.load_w(W["w_spT"][l], wsp, wspb, (4, 128))
        for g in range(4):
            P.dma("sp", bsp[(g % 2) * 64:(g % 2) * 64 + 64, g // 2, :], W["b_sp"][l, g:g + 1, :].partition_broadcast(64), writes=[bspb])

        def u_consume(ps, pb, c, t0, n):
            P.op("act", "activation", GU[:, c, t0:t0 + n], ps[:, 0:n], AF.Gelu, R=[pb], W=[GUb])

        def z_consume(ps, pb, c, t0, n):
            P.op("act", "activation", YB[:, c, t0:t0 + n], ps[:, 0:n], AF.Silu, R=[pb], W=["YB3_%d_%d" % (c, i) for i in range(t0 // 128, (t0 + n) // 128)])
        self.proj_fm(WU, WUb, 0, 2, ntl, u_consume)
        self.proj_fm(WZ, WZb, 0, 2, ntl, z_consume)
        for i in range(ntl):
            ps, pb = self.proj_tm(WV, WVb, 0, 256, i)
            P.op("act", "activation", GV[:, i, :], ps[:, 0:256], AF.Gelu, R=[pb], W=[GVb + "_%d" % i])
        t1, t1b = self.ar("sg1", [128, 512], F32)
        t2, t2b = self.ar("sg2", [128, 512], F32)
        for i0 in range(0, ntl, 4):
            nt = min(4, ntl - i0)
            n = nt * 128
            for c in range(2):
                ps, pb = self.ps()
                for j in range(nt):
                    i = i0 + j
                    for gg in range(2):
                        g = 2 * c + gg
                        P.op("pe", "matmul", ps[gg * 64:(gg + 1) * 64, j * 128:(j + 1) * 128], GV[:, i, g * 64:(g + 1) * 64], wsp[:, g, :],
                            start=True, stop=True, tile_position=(0, gg * 64), R=[GVb + "_%d" % i, wspb], W=[pb])
                P.op("dve", "tensor_tensor", t1[:, 0:nt * 128].rearrange("p (a b) -> p a b", a=nt), ps[:, 0:nt * 128].rearrange("p (a b) -> p a b", a=nt),
                    bsp[:, c, :].unsqueeze(1).broadcast_to([128, nt, 128]), ALU.add, R=[pb, bspb, t1b], W=[t1b])
                P.op("pool", "tensor_tensor", t2[:, 0:n], t1[:, 0:n], GU[:, c, i0 * 128:i0 * 128 + n], ALU.mult, R=[t1b, GUb, t2b], W=[t2b])
                ybb = ["YB3_%d_%d" % (c, i) for i in range(i0, i0 + nt)]
                P.op("dve", "tensor_tensor", YB[:, c, i0 * 128:i0 * 128 + n], t2[:, 0:n], YB[:, c, i0 * 128:i0 * 128 + n], ALU.mult, R=[t2b] + ybb, W=ybb)

    def merge(self, l, ntl, is_last, x_tile_src):
        P, W = self.P, self.W
        groups = [g for g in GROUPS if g[0] // 128 < ntl]
        MT, MTb = self.ar("MT", [128, KC, TALL], BF)
        WBR = [self.ar("WBR%d" % r, [128, 2, D], BF) for r in range(4)]
        for r in range(4):
            self.load_w(W["w_br"][l, r].rearrange("(kc p) n -> p kc n", p=128), WBR[r][0], WBR[r][1], (2, D))
        bm, bmb = self.ar("bm", [128, 32], F32)
        P.dma("sp", bm, W["b_mergeT"][l], writes=[bmb])
        WM = [self.ar("WM%d" % j, [128, 4, KC, 128], BF) for j in range(2)]
        sg = [self.ar("sg%d" % j, [128, 512], F32) for j in range(2)]
        acc = [self.ar("acc%d" % j, [128, 512], F32) for j in range(2)]
        tmpm = [self.ar("tmpm%d" % j, [128, 512], F32) for j in range(2)]
        cnt = 0
        for dc in range(KC):
            wm, wmb = WM[dc % 2]
            for r in range(4):
                i = self.wsf_i % 2
                self.wsf_i += 1
                st = self.WSF[i][:, 0:1024].rearrange("p (a b) -> p a b", a=KC)
                P.dma("sp", st, self.w_cols("w_merge", l, r * D + dc * 128, 128), writes=["wsf%d" % i])
                P.op("pool", "tensor_copy", wm[:, r, :, :], st, R=["wsf%d" % i], W=[wmb + "_%d" % r])
            for (t0, n) in groups:
                htb = ["HT%d" % i for i in range(t0 // 128, (t0 + n) // 128)]
                a_, ab = acc[cnt % 2]
                for r in range(4):
                    psg, pgb = self.ps()
                    for kc in range(KC):
                        P.op("pe", "matmul", psg[:, 0:n], wm[:, r, kc, :], self.HT[:, kc, t0:t0 + n], start=(kc == 0), stop=(kc == KC - 1), R=[wmb + "_%d" % r] + htb, W=[pgb])
                    psy, pyb = self.ps()
                    ybb = ["YB%d_%d_%d" % (r, c, i) for c in range(2) for i in range(t0 // 128, (t0 + n) // 128)]
                    for k2 in range(2):
                        P.op("pe", "matmul", psy[:, 0:n], WBR[r][0][:, k2, dc * 128:(dc + 1) * 128], self.YB[r][:, k2, t0:t0 + n],
                            start=(k2 == 0), stop=(k2 == 1), R=[WBR[r][1]] + ybb, W=[pyb])
                    s_, sb_ = sg[(cnt * 4 + r) % 2]
                    P.op("act", "activation", s_[:, 0:n], psg[:, 0:n], AF.Sigmoid,
                                                                         bias=bm[:, r * 8 + dc:r * 8 + dc + 1], scale=1.0, R=[pgb, bmb], W=[sb_])
                    if r == 0:
                        P.op("dve", "tensor_tensor", a_[:, 0:n], psy[:, 0:n], s_[:, 0:n], ALU.mult, R=[pyb, sb_, ab], W=[ab])
                    else:
                        t_, tb_ = tmpm[r % 2]
                        P.op("dve", "tensor_tensor", t_[:, 0:n], psy[:, 0:n], s_[:, 0:n], ALU.mult, R=[pyb, sb_, tb_], W=[tb_])
                        if r < 3:
                            P.op("pool", "tensor_tensor", a_[:, 0:n], a_[:, 0:n], t_[:, 0:n], ALU.add, R=[ab, tb_], W=[ab])
                        else:
                            P.op("pool", "tensor_tensor", MT[:, dc, t0:t0 + n], a_[:, 0:n], t_[:, 0:n], ALU.add, R=[ab, tb_], W=[MTb + "_%d_%d" % (dc, t0)])
                cnt += 1
        self.tap("L%dMT" % l, MT, [128, KC, TALL], [MTb + "_%d_%d" % (dc, g[0]) for dc in range(KC) for g in groups], BF)
        if self.stop == "C2":
            return
        self.P.barrier()
        self.ar_off = (KC * TALL * 2 + 63) // 64 * 64
        self.ar_gen += 1
        WO, WOb = self.ar("WO", [128, KC, D], BF)
        for j in range(4):
            self.load_w(self.w_cols("w_out", l, j * 256, 256), WO[:, :, j * 256:(j + 1) * 256], WOb + "_%d" % j, (KC, 256))
        gt_bc, gtb = self.ar("gt_bc", [128, 2, D], F32)
        for r in range(2 if ntl > NT else 1):
            for half in range(2):
                ps, pb = self.ps()
                P.op("pe", "matmul", ps[:, :], self.sel2[0:2, r * 128:(r + 1) * 128], self.gate2[0:2, half * 512:(half + 1) * 512],
                    start=True, stop=True, R=["sel2", "gate2"], W=[pb])
                P.op("act", "copy", gt_bc[:, r, half * 512:(half + 1) * 512], ps[:, :], R=[pb], W=[gtb])
        XT = [self.ar("XT%d" % i, [128, D], F32) for i in range(2)]
        XO = [self.ar("XO%d" % i, [128, D], F32) for i in range(2)]
        for i in range(ntl):
            r = 0 if i < NT else 1
            xt, xtb = XT[i % 2]
            xo, xob = XO[i % 2]
            P.dma("sp", xt, x_tile_src(i), writes=[xtb])
            for half in range(2):
                ps, pb = self.ps()
                mtb = [MTb + "_%d_%d" % (dc, g[0]) for dc in range(KC) for g in GROUPS if g[0] <= i * 128 < g[0] + g[1]]
                for kc in range(KC):
                    P.op("pe", "matmul", ps[:, :], MT[:, kc, i * 128:(i + 1) * 128], WO[:, kc, half * 512:(half + 1) * 512],
                        start=(kc == 0), stop=(kc == KC - 1), R=mtb + [WOb + "_%d" % (2 * half), WOb + "_%d" % (2 * half + 1)], W=[pb])
                P.op("dve", "tensor_tensor", xo[:, half * 512:(half + 1) * 512], ps[:, :], gt_bc[:, r, half * 512:(half + 1) * 512], ALU.mult, R=[pb, gtb, xob], W=[xob + "h%d" % half])
                P.op("pool", "tensor_tensor", xo[:, half * 512:(half + 1) * 512], xo[:, half * 512:(half + 1) * 512], xt[:, half * 512:(half + 1) * 512], ALU.add, R=[xob + "h%d" % half, xtb], W=[xob + "h%d" % half])
            if is_last:
                dst = self.out[i * 128:(i + 1) * 128, :]
            elif self.last or l != self.layers[-1]:
                dst = self.xs[i * 128:(i + 1) * 128, :]
            else:
                dst = self.xs_out[i * 128:(i + 1) * 128, :]
            P.dma("sp", dst, xo, reads=[xob + "h0", xob + "h1"], writes=["xs%d" % i])


def is_last_layer(l):
    return l == 1


_CONST_CACHE = {}


def _consts(core):
    if core not in _CONST_CACHE:
        _CONST_CACHE[core] = host_consts(core)
    return _CONST_CACHE[core]


def _weights_layout(inp):
    f = lambda a: np.ascontiguousarray(np.asarray(a, dtype=np.float32))
    w = {k: f(inp[k]) for k in ("norm_w", "w_ada", "b_ada", "w_in", "qn_a", "kn_a", "qn_d", "kn_d", "sink_d",
                                "w_fnet", "b_sp", "w_br", "w_merge", "w_out")}
    w["w_spT"] = np.ascontiguousarray(f(inp["w_sp"]).transpose(0, 3, 1, 2))
    w["b_mergeT"] = np.ascontiguousarray(f(inp["b_merge"]).reshape(2, 32, 128).transpose(0, 2, 1))
    return w


def _run(builder_kwargs, per_core_extra, inp):
    b = Builder(**builder_kwargs)
    nc = b.build()
    w = _weights_layout(inp)
    c = np.asarray(inp["c"], np.float32)
    c_ctx = np.asarray(inp["c_ctx"], np.float32)
    in_maps = []
    for core in range(NCORES):
        bi = core // 4
        cv = np.stack([c[bi], c_ctx], 0)
        cvT = np.ascontiguousarray(cv.reshape(2, KC, 128).transpose(2, 1, 0))
        m = {"cvecT": cvT}
        m.update(w)
        m.update(_consts(core))
        m.update(per_core_extra(core))
        in_maps.append(m)
    res = run_bass_kernel_spmd(nc, in_maps, core_ids=list(range(NCORES)))
    return res.results, b


def kernel(**inp):
    x = np.asarray(inp["x"], np.float32)
    ctx = np.asarray(inp["ctx"], np.float32)

    def extra(core):
        bi, q = divmod(core, 4)
        return {"x": np.ascontiguousarray(x[bi, q * TOK:(q + 1) * TOK]), "ctx": np.ascontiguousarray(ctx[bi])}
    results, _ = _run(dict(layers=(0, 1), first=True, last=True), extra, inp)
    out = np.empty((2, SEQ, D), np.float32)
    for core in range(NCORES):
        bi, q = divmod(core, 4)
        out[bi, q * TOK:(q + 1) * TOK] = results[core]["out"]
    return out
```

```python
import numpy as np
import ml_dtypes
import concourse.bass as bass
import concourse.mybir as mybir
from concourse.bass_utils import run_bass_kernel_spmd

F32 = mybir.dt.float32
BF = mybir.dt.bfloat16
AF = mybir.ActivationFunctionType
ALU = mybir.AluOpType
AX = mybir.AxisListType
NPBF = ml_dtypes.bfloat16

NCORES = 8
TOK = 2048
NT = 16
CT = 2
TT = NT + CT
TALL = TT * 128
D = 1024
KC = 8
EPS = 1e-6
SEQ = 8192
INW = 2816
GROUPS = [(0, 512), (512, 512), (1024, 512), (1536, 512), (2048, 256)]


class Buf:
    __slots__ = ("name", "w", "rs")

    def __init__(self, name):
        self.name = name
        self.w = None
        self.rs = []


class Op:
    __slots__ = ("eng", "fn", "deps", "kind", "sem", "semval", "signal")

    def __init__(self, eng, fn, kind):
        self.eng = eng
        self.fn = fn
        self.kind = kind
        self.deps = []
        self.sem = None
        self.semval = 0
        self.signal = False


class Prog:
    ENGS = ("pe", "act", "dve", "pool", "sp")
    NS = 24

    def __init__(self, nc):
        self.nc = nc
        self.ops = {e: [] for e in self.ENGS}
        self.dma_count = 0
        self.dma_last = [None] * self.NS
        self.ncoll = 0
        self.bufs = {}
        self.pending = {e: [] for e in self.ENGS}
        self.last_c = {e: None for e in self.ENGS}
        self.colls = []

    def buf(self, name):
        b = self.bufs.get(name)
        if b is None:
            b = self.bufs[name] = Buf(name)
        return b

    def _add(self, op, d):
        if d is op:
            return
        op.deps.append(d)
        if d.kind == "c":
            d.signal = True

    def _deps(self, op, reads, writes):
        deps = []
        for b in reads:
            b = self.buf(b)
            if b.w is not None:
                deps.append(("raw", b.w))
            if op.kind == "c":
                b.rs = [r for r in b.rs if not (r.kind == "c" and r.eng == op.eng)]
            b.rs.append(op)
        for b in writes:
            b = self.buf(b)
            if b.w is not None:
                deps.append(("waw", b.w))
            for r in b.rs:
                if r is not op:
                    deps.append(("war", r))
            b.rs = []
            b.w = op
        for kind, d in deps:
            if d is op:
                continue
            if d.kind == "c" and op.kind == "c" and d.eng == op.eng:
                if op.eng == "pe":
                    continue
            self._add(op, d)
        if self.pending[op.eng]:
            for d in self.pending[op.eng]:
                self._add(op, d)
            self.pending[op.eng] = []

    def op(self, eng, meth, *args, R=(), W=(), **kw):
        fn = (lambda e, meth=meth, args=args, kw=kw: getattr(e, meth)(*args, **kw))
        o = Op(eng, fn, "c")
        self._deps(o, R, W)
        self.ops[eng].append(o)
        self.last_c[eng] = o
        return o

    def dma(self, q, out, in_, reads=(), writes=(), **kw):
        o = Op(q, (lambda e, out=out, in_=in_, kw=kw: e.dma_start(out=out, in_=in_, **kw)), "d")
        self._deps(o, reads, writes)
        slot = self.dma_count % self.NS
        self.dma_count += 1
        prev = self.dma_last[slot]
        o.sem = slot
        o.semval = (prev.semval if prev else 0) + 16
        if prev is not None:
            o.deps.append(prev)
        self.dma_last[slot] = o
        self.ops[q].append(o)
        return o

    def collective(self, fn, reads=(), writes=()):
        o = Op("pool", fn, "x")
        self._deps(o, reads, writes)
        o.sem = self.ncoll
        self.ncoll += 1
        self.ops["pool"].append(o)
        self.colls.append(o)
        return o

    def barrier(self):
        lasts = [o for o in self.last_c.values() if o is not None]
        lasts += [o for o in self.dma_last if o is not None]
        for e in self.ENGS:
            self.pending[e] = list(lasts)

    def emit(self):
        nc = self.nc
        for e in self.ENGS:
            cnt = 0
            for o in self.ops[e]:
                if o.kind == "c" and o.signal:
                    cnt += 1
                    o.semval = cnt
        esem = {e: nc.alloc_semaphore("es_" + e) for e in self.ENGS}
        dsem = [nc.alloc_semaphore("ds_%d" % i) for i in range(self.NS)]
        csem = [nc.alloc_semaphore("cs_%d" % i) for i in range(self.ncoll)]

        def semof(d):
            if d.kind == "c":
                return esem[d.eng], d.semval
            if d.kind == "d":
                return dsem[d.sem], d.semval
            return csem[d.sem], 1

        def gen(ename):
            def body(eng):
                waited = {}
                for o in self.ops[ename]:
                    need = {}
                    for d in o.deps:
                        s, v = semof(d)
                        if v > need.get(id(s), (None, 0))[1]:
                            need[id(s)] = (s, v)
                    for s, v in need.values():
                        if waited.get(id(s), 0) >= v:
                            continue
                        waited[id(s)] = v
                        eng.wait_ge(s, v)
                    ins = o.fn(eng)
                    if o.kind == "c":
                        if o.signal:
                            ins.then_inc(esem[ename], 1)
                    elif o.kind == "d":
                        ins.then_inc(dsem[o.sem], 16)
                    else:
                        ins.then_inc(csem[o.sem])
                last = {}
                for o in self.ops[ename]:
                    if o.kind in ("d", "x"):
                        s, v = semof(o)
                        if v > last.get(id(s), (None, 0))[1]:
                            last[id(s)] = (s, v)
                for s, v in last.values():
                    if waited.get(id(s), 0) < v:
                        eng.wait_ge(s, v)
            return body

        with nc.Block() as block:
            block.sync(gen("sp"))
            block.scalar(gen("act"))
            block.vector(gen("dve"))
            block.tensor(gen("pe"))
            block.gpsimd(gen("pool"))


def host_consts(core):
    b, q = divmod(core, 4)
    t = (q * TOK + np.arange(TOK)).astype(np.int64)
    r = (t // 64).astype(np.float32)
    col = (t % 64).astype(np.float32)
    inv = (np.float32(10000.0) ** (-np.arange(16, dtype=np.float32) / np.float32(16))).astype(np.float32)
    ar = r[:, None] * inv[None, :]
    ac = col[:, None] * inv[None, :]
    cr, sr, cc, sc = np.cos(ar), np.sin(ar), np.cos(ac), np.sin(ac)
    cos64 = np.concatenate([cr, cr, cc, cc], 1).astype(np.float32)
    sin64 = np.concatenate([-sr, sr, -sc, sc], 1).astype(np.float32)
    cos64 = np.ascontiguousarray(cos64.reshape(NT, 128, 64).transpose(1, 0, 2))
    sin64 = np.ascontiguousarray(sin64.reshape(NT, 128, 64).transpose(1, 0, 2))
    n1 = np.arange(64)
    th = 2 * np.pi * np.outer(n1, n1) / 64.0
    C, S = np.cos(th), np.sin(th)
    sc1 = 1.0 / np.sqrt(SEQ * 64.0)
    m1 = np.zeros((128, 128), np.float64)
    m1[:64, :64] = C
    m1[64:, :64] = -S
    m1[:64, 64:] = S
    m1[64:, 64:] = C
    m1 *= sc1
    n2 = np.arange(128)[:, None, None]
    k1 = np.arange(64)[None, :, None]
    k2 = (np.arange(32) + 32 * q)[None, None, :]
    ph = 2 * np.pi * n2 * (64 * k2 + k1) / float(SEQ)
    mc = np.cos(ph).reshape(128, 64 * 32)
    ms = (-np.sin(ph)).reshape(128, 64 * 32)
    n = np.arange(256)
    t256 = 2 * np.pi * np.outer(n, n) / 256.0
    sc2 = 1.0 / np.sqrt(256 * 64.0)
    c256 = (np.cos(t256) * sc2).reshape(2, 128, 256).transpose(1, 0, 2)
    s256n = (-np.sin(t256) * sc2).reshape(2, 128, 256).transpose(1, 0, 2)
    c64d = np.concatenate([C, C], 1).astype(np.float32)
    s64d = np.concatenate([S, S], 1).astype(np.float32)
    pk = np.arange(128)[:, None]
    pq = np.arange(128)[None, :]
    mP = (pk >= pq).astype(np.float32)
    mN = (pk <= pq).astype(np.float32)
    masks = np.zeros((128, 10, 128), np.float32)
    masks[:, 0] = mP
    masks[:, 1] = mN
    for s in range(4):
        if s == q - 1:
            masks[:, 2 + s] = mP
        if s == q + 1:
            masks[:, 6 + s] = mN
    sel2 = np.zeros((2, 256), np.float32)
    sel2[0, :128] = 1.0
    sel2[1, 128:] = 1.0
    return {
        "cos64": cos64, "sin64": sin64,
        "m1": m1.astype(NPBF), "mc": mc.astype(NPBF), "ms": ms.astype(NPBF),
        "c256": np.ascontiguousarray(c256).astype(NPBF), "s256n": np.ascontiguousarray(s256n).astype(NPBF),
        "c64d": c64d, "s64d": s64d,
        "masks": masks.astype(NPBF), "sel2": sel2,
        "ident": np.eye(128, dtype=np.float32),
    }


CONST_SPECS = {
    "cos64": ([128, NT, 64], F32), "sin64": ([128, NT, 64], F32),
    "m1": ([128, 128], BF), "mc": ([128, 2048], BF), "ms": ([128, 2048], BF),
    "c256": ([128, 2, 256], BF), "s256n": ([128, 2, 256], BF),
    "c64d": ([64, 128], F32), "s64d": ([64, 128], F32),
    "masks": ([128, 10, 128], BF), "sel2": ([2, 256], F32), "ident": ([128, 128], F32),
}

WEIGHT_SPECS = {
    "norm_w": [2, D], "w_ada": [2, D, 3 * D], "b_ada": [2, 3 * D], "w_in": [2, D, INW],
    "qn_a": [2, 64], "kn_a": [2, 64], "qn_d": [2, 64], "kn_d": [2, 64], "sink_d": [2, 4],
    "w_fnet": [2, 4, 64, 64], "w_spT": [2, 128, 4, 128], "b_sp": [2, 4, 128],
    "w_br": [2, 4, 256, D], "w_merge": [2, D, 4 * D], "b_mergeT": [2, 128, 32], "w_out": [2, D, D],
}


class Builder:
    def __init__(self, layers=(0, 1), first=True, last=True, debug=(), stop=None):
        self.stop = stop
        self.layers = tuple(layers)
        self.first = first
        self.last = last
        self.debug = set(debug)
        nc = self.nc = bass.Bass("TRN2", target_bir_lowering=False)
        self.P = Prog(nc)
        self.dbg_outs = {}
        self._uid = 0

    def uid(self, p="t"):
        self._uid += 1
        return "%s%d" % (p, self._uid)

    def dram_in(self, name, shape, dt=F32):
        return self.nc.dram_tensor(name, list(shape), dt, kind="ExternalInput").ap()

    def dram_out(self, name, shape, dt=F32):
        return self.nc.dram_tensor(name, list(shape), dt, kind="ExternalOutput").ap()

    def sb(self, name, shape, dt):
        return self.nc.alloc_sbuf_tensor("s_" + name, list(shape), dt)

    def tap(self, name, view, shape, reads, dt=F32):
        if name not in self.debug:
            return
        o = self.dram_out("dbg_" + name, shape, dt)
        self.dbg_outs["dbg_" + name] = (shape, dt)
        self.P.dma("sp", o, view, reads=reads)

    def build(self):
        nc, P = self.nc, self.P
        if self.first:
            self.x_in = self.dram_in("x", [TOK, D])
            self.ctx_in = self.dram_in("ctx", [256, D])
        else:
            self.xs_in = self.dram_in("xs_in", [TALL, D])
        self.cvecT = self.dram_in("cvecT", [128, KC, 2])
        self.W = {k: self.dram_in(k, s) for k, s in WEIGHT_SPECS.items()}
        self.C = {k: self.dram_in(k, s, dt) for k, (s, dt) in CONST_SPECS.items()}
        if self.last:
            self.out = self.dram_out("out", [TOK, D])
        else:
            self.xs_out = self.dram_out("xs_out", [TALL, D])
        self.xs = nc.dram_tensor("xs", [TALL, D], F32).ap()
        self.payload = [[nc.dram_tensor("payload%d_%d" % (l, j), [256, TOK], BF) for j in range(4)] for l in range(2)]
        self.gathered = [[nc.dram_tensor("gathered%d_%d" % (l, j), [1024, TOK], BF) for j in range(4)] for l in range(2)]

        self.HT = self.sb("HT", [128, KC, TALL], BF)
        self.YB = [self.sb("YB%d" % r, [128, 2, TALL], BF) for r in range(4)]
        self.WSF = [self.sb("WSF%d" % i, [128, 2048], F32) for i in range(2)]
        self.wsf_i = 0
        self.cos64 = self.sb("cos64", [128, NT, 64], F32)
        self.sin64 = self.sb("sin64", [128, NT, 64], F32)
        self.identf = self.sb("identf", [128, 128], F32)
        self.ident = self.sb("ident", [128, 128], BF)
        self.m1 = self.sb("m1", [128, 128], BF)
        self.c256 = self.sb("c256", [128, 2, 256], BF)
        self.s256n = self.sb("s256n", [128, 2, 256], BF)
        self.c64d = self.sb("c64d", [64, 128], F32)
        self.s64d = self.sb("s64d", [64, 128], F32)
        self.masks = self.sb("masks", [128, 10, 128], BF)
        self.sel2 = self.sb("sel2", [2, 256], F32)
        self.ones64 = self.sb("ones64", [128, 64], BF)
        self.scT = self.sb("scT", [128, KC, 2], F32)
        self.gate2 = self.sb("gate2", [2, D], F32)
        self.knw = [self.sb("knw%d" % i, [128, 128], F32) for i in range(2)]
        self.qnw = [self.sb("qnw%d" % i, [128, 256], F32) for i in range(2)]
        self.esink = self.sb("esink", [128, 4], F32)
        self.KTc = [self.sb("KTc%d" % i, [128, 256], BF) for i in range(2)]
        self.Vc = [self.sb("Vc%d" % i, [128, 2, 128], BF) for i in range(2)]
        self.UVc = self.sb("UVc", [128, 2, 2, 256], BF)
        self.BA = self.sb("BA", [128, 2, 2, 128], BF)
        self.stat = self.sb("stat", [128, 64], F32)
        self.stat_i = 0
        self.PSW = [nc.alloc_psum_tensor("psw%d" % i, [128, 1024], F32) for i in range(2)]
        self.PS = [self.PSW[i // 2][:, (i % 2) * 512:(i % 2 + 1) * 512] for i in range(4)]
        self.PS += [nc.alloc_psum_tensor("ps%d" % i, [128, 512], F32) for i in (4, 5)]
        self.PST = [nc.alloc_psum_tensor("pst%d" % i, [128, 1024], BF) for i in range(2)]
        self.ps_i = 0
        self.pst_i = 0
        self.ARENA_BYTES = 88 * 1024
        self.arena = self.sb("arena", [128, self.ARENA_BYTES // 2], BF)
        self.ar_off = 0
        self.ar_gen = 0

        for k in ("cos64", "sin64", "m1", "c256", "s256n", "c64d", "s64d", "masks", "sel2"):
            P.dma("sp", getattr(self, k)[:], self.C[k], writes=[k])
        P.dma("sp", self.identf[:], self.C["ident"], writes=["identf"])
        P.op("dve", "tensor_copy", self.ident[:], self.identf[:], R=["identf"], W=["ident"])
        P.op("pool", "memset", self.ones64[:], 1.0, R=[], W=["ones64"])
        P.dma("sp", self.scT[:], self.cvecT, writes=["scT"])
        P.op("act", "activation", self.scT[:], self.scT[:], AF.Silu, R=["scT"], W=["scT"])

        for li, l in enumerate(self.layers):
            is_first = self.first and li == 0
            is_last = self.last and li == len(self.layers) - 1
            self.layer(l, is_first, is_last, x_from_input=is_first,
                       x_src_xs_in=(not self.first and li == 0))
        if self.stop is not None:
            dst = self.out if self.last else self.xs_out
            P.dma("sp", dst[0:128, :], self.identf[:, :].unsqueeze(1).broadcast_to([128, 8, 128]) if False else self.cos64[:, :, :], reads=["cos64"])
        P.emit()
        return nc

    def phase(self):
        self.P.barrier()
        self.ar_off = 0
        self.ar_gen += 1

    def ar(self, name, shape, dt):
        n = int(np.prod(shape[1:]))
        nbytes = n * (4 if dt == F32 else 2)
        nbytes = (nbytes + 63) // 64 * 64
        assert self.ar_off + nbytes <= self.ARENA_BYTES, (name, self.ar_off, nbytes)
        e0 = self.ar_off // 2
        v = self.arena[:, e0:e0 + nbytes // 2]
        if dt == F32:
            v = v.bitcast(F32)
        v = v[:, 0:n]
        self.ar_off += nbytes
        if shape[0] != 128:
            v = v[0:shape[0]]
        if len(shape) == 3:
            v = v.rearrange("p (a b) -> p a b", a=shape[1])
        elif len(shape) == 4:
            v = v.rearrange("p (a b c) -> p a b c", a=shape[1], b=shape[2])
        return v, "%s_g%d" % (name, self.ar_gen)

    def ps(self):
        i = self.ps_i % 4
        self.ps_i += 1
        return self.PS[i], "ps%d" % i

    def pst(self):
        i = self.pst_i % 2
        self.pst_i += 1
        return self.PST[i], "pst%d" % i

    def statcol(self, n=1):
        c = (self.stat_i % 16) * 4
        self.stat_i += 1
        return self.stat[:, c:c + n], "stat%d" % c

    def load_w(self, src, dst, dstbuf, shape3, q="sp"):
        P = self.P
        i = self.wsf_i % 2
        self.wsf_i += 1
        a, b = shape3
        st = self.WSF[i][:, 0:a * b].rearrange("p (a b) -> p a b", a=a)
        P.dma(q, st, src, writes=["wsf%d" % i])
        P.op("pool", "tensor_copy", dst, st, R=["wsf%d" % i], W=[dstbuf])

    def w_cols(self, name, l, c0, n):
        return self.W[name][l, :, c0:c0 + n].rearrange("(kc p) n -> p kc n", p=128)

    @staticmethod
    def interleave(gens, width):
        gens = iter(gens)
        active = []
        while True:
            while len(active) < width:
                g = next(gens, None)
                if g is None:
                    break
                active.append(g)
            if not active:
                break
            for g in list(active):
                try:
                    next(g)
                except StopIteration:
                    active.remove(g)

    def proj_fm(self, wb, wbuf, c0, nchunks, tiles, consume):
        P = self.P
        for c in range(nchunks):
            for (t0, n) in GROUPS:
                if t0 // 128 >= tiles:
                    continue
                ps, pb = self.ps()
                htb = ["HT%d" % i for i in range(t0 // 128, (t0 + n) // 128)]
                for kc in range(KC):
                    P.op("pe", "matmul", ps[:, 0:n], wb[:, kc, c0 + c * 128:c0 + (c + 1) * 128], self.HT[:, kc, t0:t0 + n],
                        start=(kc == 0), stop=(kc == KC - 1), R=[wbuf] + htb, W=[pb])
                consume(ps, pb, c, t0, n)

    def proj_tm(self, wb, wbuf, c0, ncols, tile):
        P = self.P
        ps, pb = self.ps()
        for kc in range(KC):
            P.op("pe", "matmul", ps[:, 0:ncols], self.HT[:, kc, tile * 128:(tile + 1) * 128], wb[:, kc, c0:c0 + ncols],
                start=(kc == 0), stop=(kc == KC - 1), R=[wbuf, "HT%d" % tile], W=[pb])
        return ps, pb

    def headnorm_rope(self, ps, pb, c0, nh, wtile, wbuf, tile, tmp, out_bf, outbuf, permute):
        P = self.P
        n = nh * 64
        (sq, sqb), (t1, t1b), (t2, t2b) = tmp
        P.op("act", "activation", sq[:, 0:n], ps[:, c0:c0 + n], AF.Square, R=[pb], W=[sqb])
        yield
        ssq, ssqb = self.statcol(4)
        rt, rtb = self.statcol(4)
        rr, rrb = self.statcol(4)
        P.op("dve", "tensor_reduce", ssq[:, 0:nh], sq[:, 0:n].rearrange("p (h d) -> p h d", h=nh), AX.X, ALU.add, R=[sqb], W=[ssqb])
        yield
        P.op("act", "activation", rt[:, 0:nh], ssq[:, 0:nh], AF.Sqrt, bias=64.0 * EPS, scale=1.0, R=[ssqb], W=[rtb])
        yield
        P.op("dve", "reciprocal", rr[:, 0:nh], rt[:, 0:nh], R=[rtb], W=[rrb])
        yield
        P.op("dve", "tensor_tensor", t1[:, 0:n].rearrange("p (h d) -> p h d", h=nh), ps[:, c0:c0 + n].rearrange("p (h d) -> p h d", h=nh),
            rr[:, 0:nh].unsqueeze(2).broadcast_to([128, nh, 64]), ALU.mult, R=[pb, rrb], W=[t1b])
        yield
        if permute:
            ov = out_bf[:, 0:n].rearrange("p (g kv d) -> p kv g d", g=2, kv=2)
        if tile >= NT:
            if permute:
                P.op("pool", "tensor_tensor", ov, t1[:, 0:n].rearrange("p (kv g d) -> p kv g d", kv=2, g=2),
                    wtile[:, 0:n].rearrange("p (kv g d) -> p kv g d", kv=2, g=2), ALU.mult, R=[t1b, wbuf], W=[outbuf])
                yield
            else:
                P.op("pool", "tensor_tensor", out_bf[:, 0:n], t1[:, 0:n], wtile[:, 0:n], ALU.mult, R=[t1b, wbuf], W=[outbuf])
                yield
            return
        P.op("pool", "tensor_tensor", sq[:, 0:n], t1[:, 0:n], wtile[:, 0:n], ALU.mult, R=[t1b, wbuf, sqb], W=[sqb])
        yield
        cosb = self.cos64[:, tile, :].unsqueeze(1).broadcast_to([128, nh, 64])
        sinv = self.sin64[:, tile, :].rearrange("p (t j) -> p t j", t=4)
        xv = sq[:, 0:n].rearrange("p (h t j) -> p h t j", h=nh, t=4)
        t2v = t2[:, 0:n].rearrange("p (h t j) -> p h t j", h=nh, t=4)
        P.op("dve", "tensor_tensor", t1[:, 0:n].rearrange("p (h d) -> p h d", h=nh),
                                              sq[:, 0:n].rearrange("p (h d) -> p h d", h=nh), cosb, ALU.mult, R=[sqb, "cos64", t1b], W=[t1b])
        yield
        P.op("pool", "tensor_tensor", t2v[:, :, 0::2, :], xv[:, :, 1::2, :],
                                               sinv[:, 0::2, :].unsqueeze(1).broadcast_to([128, nh, 2, 16]), ALU.mult, R=[sqb, "sin64"], W=[t2b + "a"])
        yield
        P.op("pool", "tensor_tensor", t2v[:, :, 1::2, :], xv[:, :, 0::2, :],
                                               sinv[:, 1::2, :].unsqueeze(1).broadcast_to([128, nh, 2, 16]), ALU.mult, R=[sqb, "sin64"], W=[t2b + "b"])
        yield
        if permute:
            P.op("dve", "tensor_tensor", ov, t1[:, 0:n].rearrange("p (kv g d) -> p kv g d", kv=2, g=2),
                t2[:, 0:n].rearrange("p (kv g d) -> p kv g d", kv=2, g=2), ALU.add, R=[t1b, t2b + "a", t2b + "b"], W=[outbuf])
            yield
        else:
            P.op("dve", "tensor_tensor", out_bf[:, 0:n], t1[:, 0:n], t2[:, 0:n], ALU.add, R=[t1b, t2b + "a", t2b + "b"], W=[outbuf])
            yield

    def transpose_to(self, src_bf, srcbuf, nblk, dst_fn, dstbuf_fn, eng="act"):
        P = self.P
        pt, ptb = self.pst()
        for j in range(nblk):
            P.op("pe", "transpose", pt[:, j * 128:(j + 1) * 128], src_bf[:, j * 128:(j + 1) * 128], self.ident[:], R=[srcbuf, "ident"], W=[ptb])
        yield
        for j in range(nblk):
            if eng == "act":
                P.op("act", "copy", dst_fn(j), pt[:, j * 128:(j + 1) * 128], R=[ptb], W=[dstbuf_fn(j)])
            else:
                P.op("dve", "tensor_copy", dst_fn(j), pt[:, j * 128:(j + 1) * 128], R=[ptb], W=[dstbuf_fn(j)])
        yield

    def layer(self, l, is_first, is_last, x_from_input, x_src_xs_in):
        nc, P, W = self.nc, self.P, self.W
        do_ctx_out = not is_last_layer(l)
        ntl = TT if do_ctx_out else NT
        pay, gat = self.payload[l], self.gathered[l]

        def x_tile_src(i):
            if x_from_input:
                return self.x_in[i * 128:(i + 1) * 128, :] if i < NT else self.ctx_in[(i - NT) * 128:(i - NT + 1) * 128, :]
            if x_src_xs_in:
                return self.xs_in[i * 128:(i + 1) * 128, :]
            return self.xs[i * 128:(i + 1) * 128, :]

        self.phase()
        L = "L%d" % l
        mod, modb = self.ar("mod", [2, 3 * D], F32)
        arow, arowb = self.ar("arow", [2, D], F32)
        normw2, normw2b = self.ar("normw2", [2, D], F32)
        bada2, bada2b = self.ar("bada2", [2, 3 * D], F32)
        P.dma("sp", normw2, W["norm_w"][l:l + 1, :].partition_broadcast(2), writes=[normw2b])
        P.dma("sp", bada2, W["b_ada"][l:l + 1, :].partition_broadcast(2), writes=[bada2b])
        for i, (kn, qn) in enumerate((("kn_a", "qn_a"), ("kn_d", "qn_d"))):
            for h in range(2):
                P.dma("sp", self.knw[i][:, h * 64:(h + 1) * 64], W[kn][l:l + 1, :].partition_broadcast(128), writes=["knw%d" % i])
            for h in range(4):
                P.dma("sp", self.qnw[i][:, h * 64:(h + 1) * 64], W[qn][l:l + 1, :].partition_broadcast(128), writes=["qnw%d" % i])
            P.op("act", "mul", self.knw[i][:], self.knw[i][:], 8.0, R=["knw%d" % i], W=["knw%d" % i])
        P.dma("sp", self.esink[:], W["sink_d"][l:l + 1, :].partition_broadcast(128), writes=["esink"])
        P.op("act", "activation", self.esink[:], self.esink[:], AF.Exp, R=["esink"], W=["esink"])

        for blk in range(6):
            ps, pb = self.ps()
            for half in range(2):
                i = self.wsf_i % 2
                self.wsf_i += 1
                st = self.WSF[i][:].rearrange("p (a b) -> p a b", a=KC)
                c0 = blk * 512 + half * 256
                P.dma("sp", st, self.w_cols("w_ada", l, c0, 256), writes=["wsf%d" % i])
                for kc in range(KC):
                    P.op("pe", "matmul", ps[0:2, half * 256:(half + 1) * 256], self.scT[:, kc, :], st[:, kc, :],
                        start=(kc == 0), stop=(kc == KC - 1), R=["scT", "wsf%d" % i], W=[pb])
            P.op("dve", "tensor_tensor", mod[0:2, blk * 512:(blk + 1) * 512], ps[0:2, :], bada2[0:2, blk * 512:(blk + 1) * 512], ALU.add, R=[pb, bada2b], W=[modb])
        self.tap(L + "mod", mod, [2, 3 * D], [modb])
        P.op("dve", "scalar_tensor_tensor", arow, mod[0:2, D:2 * D], 1.0, normw2, ALU.add, ALU.mult, R=[modb, normw2b], W=[arowb])
        P.op("act", "copy", self.gate2[:], mod[0:2, 2 * D:3 * D], R=[modb], W=["gate2"])
        a_bc, a_bcb = self.ar("a_bc", [128, 2, D], F32)
        sh_bc, sh_bcb = self.ar("sh_bc", [128, 2, D], F32)
        for r in range(2):
            for half in range(2):
                for which, (dst, dbuf, src, sbuf) in enumerate(((a_bc, a_bcb, arow, arowb), (sh_bc, sh_bcb, mod, modb))):
                    ps, pb = self.ps()
                    P.op("pe", "matmul", ps[:, :], self.sel2[0:2, r * 128:(r + 1) * 128], src[0:2, half * 512:(half + 1) * 512],
                        start=True, stop=True, R=["sel2", sbuf], W=[pb])
                    P.op("act", "copy", dst[:, r, half * 512:(half + 1) * 512], ps[:, :], R=[pb], W=[dbuf])

        XT = [self.ar("XT%d" % i, [128, D], F32) for i in range(2)]
        TMPF = [self.ar("TMPF%d" % i, [128, D], F32) for i in range(2)]
        HB = [self.ar("HB%d" % i, [128, D], BF) for i in range(2)]
        def a1_chain(i):
            s = i % 2
            r = 0 if i < NT else 1
            (xt, xtb), (tf, tfb), (hb, hbb) = XT[s], TMPF[s], HB[s]
            P.dma("sp", xt, x_tile_src(i), writes=[xtb])
            ssq, ssqb = self.statcol()
            rt, rtb = self.statcol()
            rs, rsb = self.statcol()
            P.op("act", "activation", tf, xt, AF.Square, accum_out=ssq, R=[xtb], W=[tfb, ssqb])
            yield
            P.op("act", "activation", rt, ssq, AF.Sqrt, bias=EPS, scale=1.0 / D, R=[ssqb], W=[rtb])
            yield
            P.op("dve", "reciprocal", rs, rt, R=[rtb], W=[rsb])
            yield
            P.op("dve", "scalar_tensor_tensor", tf, xt, rs, a_bc[:, r, :], ALU.mult, ALU.mult, R=[xtb, rsb, a_bcb, tfb], W=[tfb])
            yield
            P.op("pool", "tensor_tensor", hb, tf, sh_bc[:, r, :], ALU.add, R=[tfb, sh_bcb], W=[hbb])
            yield
            if i == 0:
                self.tap(L + "h0", hb, [128, D], [hbb], BF)
            pt, ptb = self.pst()
            for kc in range(KC):
                P.op("pe", "transpose", pt[:, kc * 128:(kc + 1) * 128], hb[:, kc * 128:(kc + 1) * 128], self.ident[:], R=[hbb, "ident"], W=[ptb])
            yield
            P.op("act", "copy", self.HT[:, :, i * 128:(i + 1) * 128], pt[:].rearrange("p (k t) -> p k t", k=KC), R=[ptb], W=["HT%d" % i])
            yield
        self.interleave((a1_chain(i) for i in range(TT)), 2)

        if self.stop == "A1":
            return
        self.phase()
        WB, WBb = self.ar("WBkv", [128, KC, 256], BF)
        KR = [self.ar("kr%d" % j, [128, 128], BF) for j in range(2)]
        KTs = [self.ar("KTs%d" % a, [128, TOK], BF) for a in range(2)]
        Vs = [self.ar("Vs%d" % a, [128, NT, 128], BF) for a in range(2)]
        tmps = [[self.ar("hnt%d_%d" % (k, j), [128, 256], F32) for j in range(3)] for k in range(2)]
        WBs = [(WB, WBb), self.ar("WBkv2", [128, KC, 256], BF)]
        for a, c0 in enumerate((256, 1024)):
            self.load_w(self.w_cols("w_in", l, c0, 256), WBs[a][0], WBs[a][1], (KC, 256))

        def a2_chain(a, i, slot):
            wb_, wbb_ = WBs[a]
            ps, pb = self.proj_tm(wb_, wbb_, 0, 256, i)
            yield
            kr, krb = KR[slot]
            yield from self.headnorm_rope(ps, pb, 0, 2, self.knw[a], "knw%d" % a, i, tmps[slot], kr, krb, permute=False)
            if i < NT:
                kts, ktsb = KTs[a]
                vs, vsb = Vs[a]
                P.op("act", "copy", vs[:, i, :], ps[:, 128:256], R=[pb], W=[vsb])
                yield from self.transpose_to(kr, krb, 1, lambda j: kts[:, i * 128:(i + 1) * 128], lambda j: ktsb)
            else:
                j_ = i - NT
                P.op("act", "copy", self.Vc[a][:, j_, :], ps[:, 128:256], R=[pb], W=["Vc%d" % a])
                yield from self.transpose_to(kr, krb, 1, lambda jj: self.KTc[a][:, j_ * 128:(j_ + 1) * 128], lambda jj: "KTc%d" % a)

        chains = [(a, i) for i in range(TT) for a in range(2)]
        self.interleave((a2_chain(a, i, k % 2) for k, (a, i) in enumerate(chains)), 2)
        for a in range(2):
            kts, ktsb = KTs[a]
            vs, vsb = Vs[a]
            P.dma("sp", pay[0].ap()[a * 128:(a + 1) * 128, :], kts, reads=[ktsb], writes=["pay_k%d" % a])
            P.dma("sp", pay[3].ap()[a * 128:(a + 1) * 128, :].rearrange("p (t d) -> p t d", d=128), vs, reads=[vsb],
                  writes=["pay_v%d" % a])
        if l == self.layers[0]:
            self.tap(L + "KTsA", KTs[0][0], [128, TOK], [KTs[0][1]], BF)
            self.tap(L + "VsA", Vs[0][0], [128, NT, 128], [Vs[0][1]], BF)
            self.tap(L + "KTcA", self.KTc[0][:], [128, 256], ["KTc0"], BF)

        self.load_w(self.w_cols("w_in", l, 1536, 256), WB, WBb, (KC, 256))
        BFT, BFTb = self.ar("BFT", [128, 2, TALL], BF)

        def bf_consume(ps, pb, c, t0, n):
            P.op("act", "copy", BFT[:, c, t0:t0 + n], ps[:, 0:n], R=[pb], W=[BFTb])
        self.proj_fm(WB, WBb, 0, 2, TT, bf_consume)
        wf, wfb = self.ar("wf", [64, 4, 64], F32)
        P.dma("sp", wf, W["w_fnet"][l].rearrange("g j c -> j g c"), writes=[wfb])
        P.op("pool", "memset", self.BA[:], 0.0, R=[], W=["BA"])
        for cs, tab in enumerate((self.c64d, self.s64d)):
            for g in range(4):
                ps, pb = self.ps()
                P.op("pe", "matmul", ps[:, 0:64], tab[0:64, :], wf[0:64, g, :], start=True, stop=True, R=[wfb, "c64d", "s64d"], W=[pb])
                rows = slice((g % 2) * 64, (g % 2) * 64 + 64)
                P.op("dve", "tensor_copy", self.BA[rows, cs, g // 2, (g % 2) * 64:(g % 2) * 64 + 64], ps[rows, 0:64], R=[pb, "BA"], W=["BA"])
        UVT, UVTb = self.ar("UVT", [128, 2, 2, TOK], BF)
        for cs in range(2):
            for c in range(2):
                for (t0, n) in GROUPS[:4]:
                    ps, pb = self.ps()
                    P.op("pe", "matmul", ps[:, 0:n], self.BA[:, cs, c, :], BFT[:, c, t0:t0 + n], start=True, stop=True, R=["BA", BFTb], W=[pb])
                    P.op("act", "copy", UVT[:, cs, c, t0:t0 + n], ps[:, 0:n], R=[pb], W=[UVTb])
                for j in range(2):
                    ps, pb = self.ps()
                    P.op("pe", "matmul", ps[:, 0:128], BFT[:, c, TOK + j * 128:TOK + (j + 1) * 128], self.BA[:, cs, c, :], start=True, stop=True, R=["BA", BFTb], W=[pb])
                    P.op("dve", "tensor_copy", self.UVc[:, cs, j, c * 128:(c + 1) * 128], ps[:, 0:128], R=[pb], W=["UVc"])
        for cs in range(2):
            P.dma("sp", pay[1 + cs].ap().rearrange("(c p) t -> p c t", p=128), UVT[:, cs, :, :],
                  reads=[UVTb], writes=["pay_uv%d" % cs])
        self.tap(L + "UVT", UVT, [128, 2, 2, TOK], [UVTb], BF)

        if self.stop == "A":
            return
        payb = [["pay_k0", "pay_k1"], ["pay_uv0"], ["pay_uv1"], ["pay_v0", "pay_v1"]]
        for j in (0, 3, 1, 2):
            P.collective(lambda e, j=j: e.collective_compute("AllGather", ALU.bypass, replica_groups=[[0, 1, 2, 3], [4, 5, 6, 7]],
                                                             ins=[pay[j].ap().opt()], outs=[gat[j].ap().opt()]),
                         reads=payb[j], writes=["gat%d" % j])
        G = [g.ap() for g in gat]

        if self.stop == "G":
            return
        self.phase()
        self.sgu(l, ntl)
        self.phase()
        self.attention(l, 0, ntl, G, pay)
        self.phase()
        self.attention(l, 1, ntl, G, pay)
        self.phase()
        self.fnet(l, ntl, G)
        if l == self.layers[0]:
            for r in range(4):
                self.tap(L + "YB%d" % r, self.YB[r][:], [128, 2, TALL], ["YB%d_%d_%d" % (r, c, i) for c in range(2) for i in range(TT)], BF)
        if self.stop == "B5":
            return
        self.phase()
        self.merge(l, ntl, is_last, x_tile_src)

    def attention(self, l, which, ntl, G, pay):
        P, W = self.P, self.W
        q0 = 0 if which == 0 else 768
        z0 = 512 if which == 0 else 1280
        YB = self.YB[which]
        nqg = [g for g in GROUPS if g[0] // 128 < ntl]
        WQ, WQb = self.ar("WQ", [128, KC, 256], BF)
        WZ, WZb = self.ar("WZ", [128, KC, 256], BF)
        self.load_w(self.w_cols("w_in", l, q0, 256), WQ, WQb, (KC, 256))
        self.load_w(self.w_cols("w_in", l, z0, 256), WZ, WZb, (KC, 256))
        QT = [self.ar("QT%d" % g, [128, TALL], BF) for g in range(2)]
        def vview(V, k0, k1):
            return V[:, k0:k1, :].rearrange("p t (kv x) -> p t kv x", kv=2)[:, :, :, 0:64]

        if which == 0:
            NKB = 2 + 64
            KT, KTb = self.ar("KT", [128, NKB * 128], BF)
            V, Vb = self.ar("V", [128, NKB, 256], BF)
            P.op("pool", "memset", V[:, :, :].rearrange("p t (kv x) -> p t kv x", kv=2)[:, :, :, 64:128], 1.0, R=[], W=[Vb + "ones"])
            P.op("act", "copy", KT[:, 0:256], self.KTc[0][:], R=["KTc0"], W=[KTb + "c"])
            P.op("act", "copy", vview(V, 0, 2), self.Vc[0][:].rearrange("p t (kv d) -> p t kv d", kv=2), R=["Vc0"], W=[Vb + "c"])
            for s_ in range(4):
                P.dma("sp", KT[:, 256 + s_ * TOK:256 + (s_ + 1) * TOK], G[0][s_ * 256:s_ * 256 + 128, :], reads=["gat0"], writes=[KTb + str(s_)])
                for hf in range(2):
                    P.dma("sp", vview(V, 2 + s_ * NT + hf * 8, 2 + s_ * NT + hf * 8 + 8),
                          G[3][s_ * 256:s_ * 256 + 128, hf * 1024:(hf + 1) * 1024].rearrange("p (t kv d) -> p t kv d", kv=2, d=64),
                          reads=["gat3"], writes=[Vb + str(s_)])

            def kbufs(kb):
                return [KTb + ("c" if kb < 2 else str((kb - 2) // NT))], [Vb + ("c" if kb < 2 else str((kb - 2) // NT)), Vb + "ones"]
        else:
            NKB = 2 + NT + 8
            KT, KTb = self.ar("KT", [128, NKB * 128], BF)
            V, Vb = self.ar("V", [128, NKB, 256], BF)
            P.op("pool", "memset", V[:, :, :].rearrange("p t (kv x) -> p t kv x", kv=2)[:, :, :, 64:128], 1.0, R=[], W=[Vb + "ones"])
            P.op("act", "copy", KT[:, 0:256], self.KTc[1][:], R=["KTc1"], W=[KTb])
            P.op("act", "copy", vview(V, 0, 2), self.Vc[1][:].rearrange("p t (kv d) -> p t kv d", kv=2), R=["Vc1"], W=[Vb])
            P.dma("sp", KT[:, 256:256 + TOK], pay[0].ap()[128:256, :], reads=["pay_k1"], writes=[KTb])
            for hf in range(2):
                P.dma("sp", vview(V, 2 + hf * 8, 2 + hf * 8 + 8),
                      pay[3].ap()[128:256, hf * 1024:(hf + 1) * 1024].rearrange("p (t kv d) -> p t kv d", kv=2, d=64),
                      reads=["pay_v1"], writes=[Vb])
            for s_ in range(4):
                for kb, c0 in ((2 + NT + s_, TOK - 128), (2 + NT + 4 + s_, 0)):
                    P.dma("sp", KT[:, kb * 128:(kb + 1) * 128], G[0][s_ * 256 + 128:s_ * 256 + 256, c0:c0 + 128], reads=["gat0"], writes=[KTb])
                    P.dma("sp", vview(V, kb, kb + 1),
                          G[3][s_ * 256 + 128:s_ * 256 + 256, c0:c0 + 128].rearrange("p (t kv d) -> p t kv d", kv=2, d=64),
                          reads=["gat3"], writes=[Vb])

            def kbufs(kb):
                return [KTb], [Vb, Vb + "ones"]

        tmps = [[self.ar("hnt%d_%d" % (k, j), [128, 256], F32) for j in range(3)] for k in range(2)]
        QR = [self.ar("QR%d" % j, [128, 256], BF) for j in range(2)]
        def q_chain(i, slot):
            ps, pb = self.proj_tm(WQ, WQb, 0, 256, i)
            yield
            qr, qrb = QR[slot]
            yield from self.headnorm_rope(ps, pb, 0, 4, self.qnw[which], "qnw%d" % which, i, tmps[slot], qr, qrb, permute=True)
            yield from self.transpose_to(qr, qrb, 2, lambda j: QT[j][0][:, i * 128:(i + 1) * 128], lambda j: QT[j][1] + "_%d" % i)
        self.interleave((q_chain(i, i % 2) for i in range(ntl)), 2)

        def z_consume(ps, pb, c, t0, n):
            P.op("act", "activation", YB[:, c, t0:t0 + n], ps[:, 0:n], AF.Silu, R=[pb], W=["YB%d_%d_%d" % (which, c, i) for i in range(t0 // 128, (t0 + n) // 128)])
        self.proj_fm(WZ, WZb, 0, 2, ntl, z_consume)
        if which == 0 and l == self.layers[0]:
            self.tap("L%dQT0" % l, QT[0][0], [128, TALL], [QT[0][1] + "_%d" % i for i in range(ntl)], BF)

        PT = [self.ar("PT%d" % j, [128, 1024], BF) for j in range(3)]
        fin = [self.ar("fin%d" % j, [128, 512], F32) for j in range(3)]

        work = []

        def add_pair(g, t0, n, keylist):
            cw = 512 // n
            chunks = [keylist[c:c + cw] for c in range(0, len(keylist), cw)]
            for ci, ch in enumerate(chunks):
                work.append(dict(g=g, t0=t0, n=n, keys=ch, first=(ci == 0), last=(ci == len(chunks) - 1)))

        if which == 0:
            for (t0, n) in nqg:
                keys = [(kb, None) for kb in range(NKB)] if t0 < TOK else [(0, None), (1, None)]
                for g in range(2):
                    add_pair(g, t0, n, keys)
        else:
            for i in range(ntl):
                if i >= NT:
                    keys = [(0, None), (1, None)]
                else:
                    keys = [(0, None), (1, None), (2 + i, None)]
                    if i > 0:
                        keys.append((2 + i - 1, 0))
                    else:
                        keys += [(2 + NT + s_, 2 + s_) for s_ in range(4)]
                    if i < NT - 1:
                        keys.append((2 + i + 1, 1))
                    else:
                        keys += [(2 + NT + 4 + s_, 6 + s_) for s_ in range(4)]
                for g in range(2):
                    add_pair(g, i * 128, 128, keys)

        accs = [(self.PS[4], "ps4"), (self.PS[5], "ps5")]

        def stage1(w, slot):
            g, t0, n = w["g"], w["t0"], w["n"]
            qt, qtb = QT[g]
            qbufs = [qtb + "_%d" % i for i in range(t0 // 128, (t0 + n) // 128)]
            psw = self.PSW[slot % 2]
            pbs = ["ps%d" % (2 * (slot % 2)), "ps%d" % (2 * (slot % 2) + 1)]
            pt, ptb = PT[slot % 3]
            cnt = len(w["keys"])
            for j, (kb, mi) in enumerate(w["keys"]):
                kb_k, _ = kbufs(kb)
                for kv in range(2):
                    krows = slice(kv * 64, (kv + 1) * 64)
                    P.op("pe", "matmul", psw[:, kv * 512 + j * n:kv * 512 + (j + 1) * n], KT[krows, kb * 128:(kb + 1) * 128],
                         qt[krows, t0:t0 + n], start=True, stop=True, R=kb_k + qbufs, W=[pbs[kv]])
            P.op("act", "activation", pt[:, :].rearrange("p (a b) -> p a b", a=2)[:, :, 0:cnt * n],
                 psw[:, :].rearrange("p (a b) -> p a b", a=2)[:, :, 0:cnt * n], AF.Exp, R=pbs, W=[ptb])
            for j, (kb, mi) in enumerate(w["keys"]):
                if mi is not None:
                    P.op("pool", "tensor_tensor", pt[:, :].rearrange("p (a b) -> p a b", a=2)[:, :, j * n:(j + 1) * n],
                         pt[:, :].rearrange("p (a b) -> p a b", a=2)[:, :, j * n:(j + 1) * n],
                         self.masks[:, mi, 0:n].unsqueeze(1).broadcast_to([128, 2, n]), ALU.mult, R=[ptb, "masks"], W=[ptb])

        def stage2(w, slot):
            g, t0, n = w["g"], w["t0"], w["n"]
            rows = slice(g * 64, (g + 1) * 64)
            pt, ptb = PT[slot % 3]
            cnt = len(w["keys"])
            for j, (kb, mi) in enumerate(w["keys"]):
                _, kb_v = kbufs(kb)
                st = w["first"] and j == 0
                sp = w["last"] and j == cnt - 1
                for kv in range(2):
                    acc, accb = accs[kv]
                    P.op("pe", "matmul", acc[:, 0:n], V[:, kb, kv * 128:(kv + 1) * 128], pt[:, kv * 512 + j * n:kv * 512 + (j + 1) * n],
                         start=st, stop=sp, R=kb_v + [ptb], W=[accb])
            if not w["last"]:
                return
            for kv in range(2):
                acc, accb = accs[kv]
                h = 2 * kv + g
                (f0, f0b), (f1, f1b), (f2, f2b) = fin
                if which == 1:
                    P.op("dve", "tensor_scalar", f2[64:128, 0:n], acc[64:128, 0:n], self.esink[64:128, h:h + 1], None, ALU.add,
                         R=[accb, "esink"], W=[f2b])
                    P.op("dve", "reciprocal", f0[rows, 0:n], f2[64:128, 0:n], R=[f2b], W=[f0b])
                else:
                    P.op("dve", "reciprocal", f0[rows, 0:n], acc[64:128, 0:n], R=[accb], W=[f0b])
                P.op("dve", "tensor_copy", f1[rows, 0:n], acc[0:64, 0:n], R=[accb], W=[f1b])
                P.op("pool", "tensor_tensor", f1[rows, 0:n], f1[rows, 0:n], f0[rows, 0:n], ALU.mult, R=[f1b, f0b], W=[f1b])
                ybb = ["YB%d_%d_%d" % (which, kv, i) for i in range(t0 // 128, (t0 + n) // 128)]
                P.op("pool", "tensor_tensor", YB[rows, kv, t0:t0 + n], f1[rows, 0:n], YB[rows, kv, t0:t0 + n], ALU.mult,
                     R=[f1b] + ybb, W=ybb)

        LOOK = 1
        for k in range(len(work) + LOOK):
            if k < len(work):
                stage1(work[k], k)
            if k >= LOOK:
                stage2(work[k - LOOK], k - LOOK)

    def fnet(self, l, ntl, G):
        P, W = self.P, self.W
        YB = self.YB[2]
        WZ, WZb = self.ar("WZ", [128, KC, 256], BF)
        self.load_w(self.w_cols("w_in", l, 1792, 256), WZ, WZb, (KC, 256))

        def z_consume(ps, pb, c, t0, n):
            P.op("act", "activation", YB[:, c, t0:t0 + n], ps[:, 0:n], AF.Silu, R=[pb], W=["YB2_%d_%d" % (c, i) for i in range(t0 // 128, (t0 + n) // 128)])
        self.proj_fm(WZ, WZb, 0, 2, ntl, z_consume)
        MC, MCb = self.ar("MC", [128, 64, 32], BF)
        MS, MSb = self.ar("MS", [128, 64, 32], BF)
        P.dma("sp", MC, self.C["mc"].rearrange("p (a b) -> p a b", a=64), writes=[MCb])
        P.dma("sp", MS, self.C["ms"].rearrange("p (a b) -> p a b", a=64), writes=[MSb])
        Z = [self.ar("Z%d" % j, [128, 64, 128], BF) for j in range(2)]
        T3, T3b = self.ar("T3", [128, 64, 128], BF)
        sp = 0
        for c in range(2):
            for hf in range(2):
                z, zb = Z[sp % 2]
                sp += 1
                for s in range(4):
                    for ri in range(2):
                        r0 = s * 256 + c * 128 + hf * 64
                        P.dma("sp", z[ri * 64 + s * 16:ri * 64 + (s + 1) * 16, :, :],
                              G[1 + ri][r0:r0 + 64, :].rearrange("ch (t n) -> t ch n", n=128), reads=["gat%d" % (1 + ri)], writes=[zb])
                for ch4 in range(16):
                    ps, pb = self.ps()
                    for j in range(4):
                        ch = ch4 * 4 + j
                        P.op("pe", "matmul", ps[:, j * 128:(j + 1) * 128], z[:, ch, :], self.m1[:, :],
                                                                            start=True, stop=True, R=[zb, "m1"], W=[pb])
                    if ch4 % 2 == 0:
                        P.op("act", "copy", T3[:, ch4 * 4:ch4 * 4 + 4, :], ps[:].rearrange("p (a b) -> p a b", a=4), R=[pb], W=[T3b])
                    else:
                        P.op("dve", "tensor_copy", T3[:, ch4 * 4:ch4 * 4 + 4, :], ps[:].rearrange("p (a b) -> p a b", a=4), R=[pb], W=[T3b])
                rows = slice(hf * 64, (hf + 1) * 64)
                for bank in range(4):
                    ps, pb = self.ps()
                    for kk in range(16):
                        k1 = bank * 16 + kk
                        P.op("pe", "matmul", ps[rows, kk * 32:(kk + 1) * 32], T3[:, :, k1], MC[:, k1, :],
                                                                         start=True, stop=False, tile_position=(0, hf * 64), R=[T3b, MCb], W=[pb])
                        P.op("pe", "matmul", ps[rows, kk * 32:(kk + 1) * 32], T3[:, :, 64 + k1], MS[:, k1, :],
                                                                         start=False, stop=True, tile_position=(0, hf * 64), R=[T3b, MSb], W=[pb])
                    yv = YB[rows, c, 0:TOK].rearrange("p (k2 k1) -> p k1 k2", k1=64)[:, bank * 16:(bank + 1) * 16, :]
                    ybb = ["YB2_%d_%d" % (c, i) for i in range(NT)]
                    P.op("dve", "tensor_tensor", yv, ps[rows, :].rearrange("p (a b) -> p a b", a=16), yv, ALU.mult, R=[pb] + ybb, W=ybb)
        if ntl > NT:
            for c in range(2):
                ps, pb = self.ps()
                k = 0
                for uv, tab in ((0, self.c256), (1, self.s256n)):
                    for j in range(2):
                        P.op("pe", "matmul", ps[:, 0:256], self.UVc[:, uv, j, c * 128:(c + 1) * 128], tab[:, j, :], start=(k == 0), stop=(k == 3), R=["UVc", "c256", "s256n"], W=[pb])
                        k += 1
                ybb = ["YB2_%d_%d" % (c, i) for i in range(NT, TT)]
                P.op("dve", "tensor_tensor", YB[:, c, TOK:TALL], ps[:, 0:256], YB[:, c, TOK:TALL], ALU.mult, R=[pb] + ybb, W=ybb)

    def sgu(self, l, ntl):
        P, W = self.P, self.W
        YB = self.YB[3]
        WU, WUb = self.ar("WU", [128, KC, 256], BF)
        WV, WVb = self.ar("WV", [128, KC, 256], BF)
        WZ, WZb = self.ar("WZ", [128, KC, 256], BF)
        self.load_w(self.w_cols("w_in", l, 2048, 256), WU, WUb, (KC, 256))
        self.load_w(self.w_cols("w_in", l, 2304, 256), WV, WVb, (KC, 256))
        self.load_w(self.w_cols("w_in", l, 2560, 256), WZ, WZb, (KC, 256))
        GU, GUb = self.ar("GU", [128, 2, TALL], BF)
        GV, GVb = self.ar("GV", [128, TT, 256], BF)
        wsp, wspb = self.ar("wsp", [128, 4, 128], BF)
        bsp, bspb = self.ar("bsp", [128, 2, 128], F32)
        self.load_w(W["w_spT"][l], wsp, wspb, (4, 128))
        for g in range(4):
            P.dma("sp", bsp[(g % 2) * 64:(g % 2) * 64 + 64, g // 2, :], W["b_sp"][l, g:g + 1, :].partition_broadcast(64), writes=[bspb])

        def u_consume(ps, pb, c, t0, n):
            P.op("act", "activation", GU[:, c, t0:t0 + n], ps[:, 0:n], AF.Gelu, R=[pb], W=[GUb])

        def z_consume(ps, pb, c, t0, n):
            P.op("act", "activation", YB[:, c, t0:t0 + n], ps[:, 0:n], AF.Silu, R=[pb], W=["YB3_%d_%d" % (c, i) for i in range(t0 // 128, (t0 + n) // 128)])
        self.proj_fm(WU, WUb, 0, 2, ntl, u_consume)
        self.proj_fm(WZ, WZb, 0, 2, ntl, z_consume)
        for i in range(ntl):
            ps, pb = self.proj_tm(WV, WVb, 0, 256, i)
            P.op("act", "activation", GV[:, i, :], ps[:, 0:256], AF.Gelu, R=[pb], W=[GVb + "_%d" % i])
        t1, t1b = self.ar("sg1", [128, 512], F32)
        t2, t2b = self.ar("sg2", [128, 512], F32)
        for i0 in range(0, ntl, 4):
            nt = min(4, ntl - i0)
            n = nt * 128
            for c in range(2):
                ps, pb = self.ps()
                for j in range(nt):
                    i = i0 + j
                    for gg in range(2):
                        g = 2 * c + gg
                        P.op("pe", "matmul", ps[gg * 64:(gg + 1) * 64, j * 128:(j + 1) * 128], GV[:, i, g * 64:(g + 1) * 64], wsp[:, g, :],
                            start=True, stop=True, tile_position=(0, gg * 64), R=[GVb + "_%d" % i, wspb], W=[pb])
                P.op("dve", "tensor_tensor", t1[:, 0:nt * 128].rearrange("p (a b) -> p a b", a=nt), ps[:, 0:nt * 128].rearrange("p (a b) -> p a b", a=nt),
                    bsp[:, c, :].unsqueeze(1).broadcast_to([128, nt, 128]), ALU.add, R=[pb, bspb, t1b], W=[t1b])
                P.op("pool", "tensor_tensor", t2[:, 0:n], t1[:, 0:n], GU[:, c, i0 * 128:i0 * 128 + n], ALU.mult, R=[t1b, GUb, t2b], W=[t2b])
                ybb = ["YB3_%d_%d" % (c, i) for i in range(i0, i0 + nt)]
                P.op("dve", "tensor_tensor", YB[:, c, i0 * 128:i0 * 128 + n], t2[:, 0:n], YB[:, c, i0 * 128:i0 * 128 + n], ALU.mult, R=[t2b] + ybb, W=ybb)

    def merge(self, l, ntl, is_last, x_tile_src):
        P, W = self.P, self.W
        groups = [g for g in GROUPS if g[0] // 128 < ntl]
        MT, MTb = self.ar("MT", [128, KC, TALL], BF)
        WBR = [self.ar("WBR%d" % r, [128, 2, D], BF) for r in range(4)]
        for r in range(4):
            self.load_w(W["w_br"][l, r].rearrange("(kc p) n -> p kc n", p=128), WBR[r][0], WBR[r][1], (2, D))
        bm, bmb = self.ar("bm", [128, 32], F32)
        P.dma("sp", bm, W["b_mergeT"][l], writes=[bmb])
        WM = [self.ar("WM%d" % j, [128, 4, KC, 128], BF) for j in range(2)]
        sg = [self.ar("sg%d" % j, [128, 512], F32) for j in range(2)]
        acc = [self.ar("acc%d" % j, [128, 512], F32) for j in range(2)]
        tmpm = [self.ar("tmpm%d" % j, [128, 512], F32) for j in range(2)]
        cnt = 0
        for dc in range(KC):
            wm, wmb = WM[dc % 2]
            for r in range(4):
                i = self.wsf_i % 2
                self.wsf_i += 1
                st = self.WSF[i][:, 0:1024].rearrange("p (a b) -> p a b", a=KC)
                P.dma("sp", st, self.w_cols("w_merge", l, r * D + dc * 128, 128), writes=["wsf%d" % i])
                P.op("pool", "tensor_copy", wm[:, r, :, :], st, R=["wsf%d" % i], W=[wmb + "_%d" % r])
            for (t0, n) in groups:
                htb = ["HT%d" % i for i in range(t0 // 128, (t0 + n) // 128)]
                a_, ab = acc[cnt % 2]
                for r in range(4):
                    psg, pgb = self.ps()
                    for kc in range(KC):
                        P.op("pe", "matmul", psg[:, 0:n], wm[:, r, kc, :], self.HT[:, kc, t0:t0 + n], start=(kc == 0), stop=(kc == KC - 1), R=[wmb + "_%d" % r] + htb, W=[pgb])
                    psy, pyb = self.ps()
                    ybb = ["YB%d_%d_%d" % (r, c, i) for c in range(2) for i in range(t0 // 128, (t0 + n) // 128)]
                    for k2 in range(2):
                        P.op("pe", "matmul", psy[:, 0:n], WBR[r][0][:, k2, dc * 128:(dc + 1) * 128], self.YB[r][:, k2, t0:t0 + n],
                            start=(k2 == 0), stop=(k2 == 1), R=[WBR[r][1]] + ybb, W=[pyb])
                    s_, sb_ = sg[(cnt * 4 + r) % 2]
                    P.op("act", "activation", s_[:, 0:n], psg[:, 0:n], AF.Sigmoid,
                                                                         bias=bm[:, r * 8 + dc:r * 8 + dc + 1], scale=1.0, R=[pgb, bmb], W=[sb_])
                    if r == 0:
                        P.op("dve", "tensor_tensor", a_[:, 0:n], psy[:, 0:n], s_[:, 0:n], ALU.mult, R=[pyb, sb_, ab], W=[ab])
                    else:
                        t_, tb_ = tmpm[r % 2]
                        P.op("dve", "tensor_tensor", t_[:, 0:n], psy[:, 0:n], s_[:, 0:n], ALU.mult, R=[pyb, sb_, tb_], W=[tb_])
                        if r < 3:
                            P.op("pool", "tensor_tensor", a_[:, 0:n], a_[:, 0:n], t_[:, 0:n], ALU.add, R=[ab, tb_], W=[ab])
                        else:
                            P.op("pool", "tensor_tensor", MT[:, dc, t0:t0 + n], a_[:, 0:n], t_[:, 0:n], ALU.add, R=[ab, tb_], W=[MTb + "_%d_%d" % (dc, t0)])
                cnt += 1
        self.tap("L%dMT" % l, MT, [128, KC, TALL], [MTb + "_%d_%d" % (dc, g[0]) for dc in range(KC) for g in groups], BF)
        if self.stop == "C2":
            return
        self.P.barrier()
        self.ar_off = (KC * TALL * 2 + 63) // 64 * 64
        self.ar_gen += 1
        WO, WOb = self.ar("WO", [128, KC, D], BF)
        for j in range(4):
            self.load_w(self.w_cols("w_out", l, j * 256, 256), WO[:, :, j * 256:(j + 1) * 256], WOb + "_%d" % j, (KC, 256))
        gt_bc, gtb = self.ar("gt_bc", [128, 2, D], F32)
        for r in range(2 if ntl > NT else 1):
            for half in range(2):
                ps, pb = self.ps()
                P.op("pe", "matmul", ps[:, :], self.sel2[0:2, r * 128:(r + 1) * 128], self.gate2[0:2, half * 512:(half + 1) * 512],
                    start=True, stop=True, R=["sel2", "gate2"], W=[pb])
                P.op("act", "copy", gt_bc[:, r, half * 512:(half + 1) * 512], ps[:, :], R=[pb], W=[gtb])
        XT = [self.ar("XT%d" % i, [128, D], F32) for i in range(2)]
        XO = [self.ar("XO%d" % i, [128, D], F32) for i in range(2)]
        for i in range(ntl):
            r = 0 if i < NT else 1
            xt, xtb = XT[i % 2]
            xo, xob = XO[i % 2]
            P.dma("sp", xt, x_tile_src(i), writes=[xtb])
            for half in range(2):
                ps, pb = self.ps()
                mtb = [MTb + "_%d_%d" % (dc, g[0]) for dc in range(KC) for g in GROUPS if g[0] <= i * 128 < g[0] + g[1]]
                for kc in range(KC):
                    P.op("pe", "matmul", ps[:, :], MT[:, kc, i * 128:(i + 1) * 128], WO[:, kc, half * 512:(half + 1) * 512],
                        start=(kc == 0), stop=(kc == KC - 1), R=mtb + [WOb + "_%d" % (2 * half), WOb + "_%d" % (2 * half + 1)], W=[pb])
                P.op("dve", "tensor_tensor", xo[:, half * 512:(half + 1) * 512], ps[:, :], gt_bc[:, r, half * 512:(half + 1) * 512], ALU.mult, R=[pb, gtb, xob], W=[xob + "h%d" % half])
                P.op("pool", "tensor_tensor", xo[:, half * 512:(half + 1) * 512], xo[:, half * 512:(half + 1) * 512], xt[:, half * 512:(half + 1) * 512], ALU.add, R=[xob + "h%d" % half, xtb], W=[xob + "h%d" % half])
            if is_last:
                dst = self.out[i * 128:(i + 1) * 128, :]
            elif self.last or l != self.layers[-1]:
                dst = self.xs[i * 128:(i + 1) * 128, :]
            else:
                dst = self.xs_out[i * 128:(i + 1) * 128, :]
            P.dma("sp", dst, xo, reads=[xob + "h0", xob + "h1"], writes=["xs%d" % i])


def is_last_layer(l):
    return l == 1


_CONST_CACHE = {}


def _consts(core):
    if core not in _CONST_CACHE:
        _CONST_CACHE[core] = host_consts(core)
    return _CONST_CACHE[core]


def _weights_layout(inp):
    f = lambda a: np.ascontiguousarray(np.asarray(a, dtype=np.float32))
    w = {k: f(inp[k]) for k in ("norm_w", "w_ada", "b_ada", "w_in", "qn_a", "kn_a", "qn_d", "kn_d", "sink_d",
                                "w_fnet", "b_sp", "w_br", "w_merge", "w_out")}
    w["w_spT"] = np.ascontiguousarray(f(inp["w_sp"]).transpose(0, 3, 1, 2))
    w["b_mergeT"] = np.ascontiguousarray(f(inp["b_merge"]).reshape(2, 32, 128).transpose(0, 2, 1))
    return w


def _run(builder_kwargs, per_core_extra, inp):
    b = Builder(**builder_kwargs)
    nc = b.build()
    w = _weights_layout(inp)
    c = np.asarray(inp["c"], np.float32)
    c_ctx = np.asarray(inp["c_ctx"], np.float32)
    in_maps = []
    for core in range(NCORES):
        bi = core // 4
        cv = np.stack([c[bi], c_ctx], 0)
        cvT = np.ascontiguousarray(cv.reshape(2, KC, 128).transpose(2, 1, 0))
        m = {"cvecT": cvT}
        m.update(w)
        m.update(_consts(core))
        m.update(per_core_extra(core))
        in_maps.append(m)
    res = run_bass_kernel_spmd(nc, in_maps, core_ids=list(range(NCORES)))
    return res.results, b


def kernel(**inp):
    x = np.asarray(inp["x"], np.float32)
    ctx = np.asarray(inp["ctx"], np.float32)

    def extra(core):
        bi, q = divmod(core, 4)
        return {"x": np.ascontiguousarray(x[bi, q * TOK:(q + 1) * TOK]), "ctx": np.ascontiguousarray(ctx[bi])}
    results, _ = _run(dict(layers=(0, 1), first=True, last=True), extra, inp)
    out = np.empty((2, SEQ, D), np.float32)
    for core in range(NCORES):
        bi, q = divmod(core, 4)
        out[bi, q * TOK:(q + 1) * TOK] = results[core]["out"]
    return out
```

```python
import numpy as np
import ml_dtypes
import concourse.bass as bass
import concourse.mybir as mybir
from concourse.bass_utils import run_bass_kernel_spmd

F32 = mybir.dt.float32
BF = mybir.dt.bfloat16
AF = mybir.ActivationFunctionType
ALU = mybir.AluOpType
AX = mybir.AxisListType
NPBF = ml_dtypes.bfloat16

NCORES = 8
TOK = 2048
NT = 16
CT = 2
TT = NT + CT
TALL = TT * 128
D = 1024
KC = 8
EPS = 1e-6
SEQ = 8192
INW = 2816
GROUPS = [(0, 512), (512, 512), (1024, 512), (1536, 512), (2048, 256)]


class Buf:
    __slots__ = ("name", "w", "rs")

    def __init__(self, name):
        self.name = name
        self.w = None
        self.rs = []


class Op:
    __slots__ = ("eng", "fn", "deps", "kind", "sem", "semval", "signal")

    def __init__(self, eng, fn, kind):
        self.eng = eng
        self.fn = fn
        self.kind = kind
        self.deps = []
        self.sem = None
        self.semval = 0
        self.signal = False


class Prog:
    ENGS = ("pe", "act", "dve", "pool", "sp")
    NS = 24

    def __init__(self, nc):
        self.nc = nc
        self.ops = {e: [] for e in self.ENGS}
        self.dma_count = 0
        self.dma_last = [None] * self.NS
        self.ncoll = 0
        self.bufs = {}
        self.pending = {e: [] for e in self.ENGS}
        self.last_c = {e: None for e in self.ENGS}
        self.colls = []

    def buf(self, name):
        b = self.bufs.get(name)
        if b is None:
            b = self.bufs[name] = Buf(name)
        return b

    def _add(self, op, d):
        if d is op:
            return
        op.deps.append(d)
        if d.kind == "c":
            d.signal = True

    def _deps(self, op, reads, writes):
        deps = []
        for b in reads:
            b = self.buf(b)
            if b.w is not None:
                deps.append(("raw", b.w))
            if op.kind == "c":
                b.rs = [r for r in b.rs if not (r.kind == "c" and r.eng == op.eng)]
            b.rs.append(op)
        for b in writes:
            b = self.buf(b)
            if b.w is not None:
                deps.append(("waw", b.w))
            for r in b.rs:
                if r is not op:
                    deps.append(("war", r))
            b.rs = []
            b.w = op
        for kind, d in deps:
            if d is op:
                continue
            if d.kind == "c" and op.kind == "c" and d.eng == op.eng:
                if op.eng == "pe":
                    continue
            self._add(op, d)
        if self.pending[op.eng]:
            for d in self.pending[op.eng]:
                self._add(op, d)
            self.pending[op.eng] = []

    def op(self, eng, meth, *args, R=(), W=(), **kw):
        fn = (lambda e, meth=meth, args=args, kw=kw: getattr(e, meth)(*args, **kw))
        o = Op(eng, fn, "c")
        self._deps(o, R, W)
        self.ops[eng].append(o)
        self.last_c[eng] = o
        return o

    def dma(self, q, out, in_, reads=(), writes=(), **kw):
        o = Op(q, (lambda e, out=out, in_=in_, kw=kw: e.dma_start(out=out, in_=in_, **kw)), "d")
        self._deps(o, reads, writes)
        slot = self.dma_count % self.NS
        self.dma_count += 1
        prev = self.dma_last[slot]
        o.sem = slot
        o.semval = (prev.semval if prev else 0) + 16
        if prev is not None:
            o.deps.append(prev)
        self.dma_last[slot] = o
        self.ops[q].append(o)
        return o

    def collective(self, fn, reads=(), writes=()):
        o = Op("pool", fn, "x")
        self._deps(o, reads, writes)
        o.sem = self.ncoll
        self.ncoll += 1
        self.ops["pool"].append(o)
        self.colls.append(o)
        return o

    def barrier(self):
        lasts = [o for o in self.last_c.values() if o is not None]
        lasts += [o for o in self.dma_last if o is not None]
        for e in self.ENGS:
            self.pending[e] = list(lasts)

    def emit(self):
        nc = self.nc
        for e in self.ENGS:
            cnt = 0
            for o in self.ops[e]:
                if o.kind == "c" and o.signal:
                    cnt += 1
                    o.semval = cnt
        esem = {e: nc.alloc_semaphore("es_" + e) for e in self.ENGS}
        dsem = [nc.alloc_semaphore("ds_%d" % i) for i in range(self.NS)]
        csem = [nc.alloc_semaphore("cs_%d" % i) for i in range(self.ncoll)]

        def semof(d):
            if d.kind == "c":
                return esem[d.eng], d.semval
            if d.kind == "d":
                return dsem[d.sem], d.semval
            return csem[d.sem], 1

        def gen(ename):
            def body(eng):
                waited = {}
                for o in self.ops[ename]:
                    need = {}
                    for d in o.deps:
                        s, v = semof(d)
                        if v > need.get(id(s), (None, 0))[1]:
                            need[id(s)] = (s, v)
                    for s, v in need.values():
                        if waited.get(id(s), 0) >= v:
                            continue
                        waited[id(s)] = v
                        eng.wait_ge(s, v)
                    ins = o.fn(eng)
                    if o.kind == "c":
                        if o.signal:
                            ins.then_inc(esem[ename], 1)
                    elif o.kind == "d":
                        ins.then_inc(dsem[o.sem], 16)
                    else:
                        ins.then_inc(csem[o.sem])
                last = {}
                for o in self.ops[ename]:
                    if o.kind in ("d", "x"):
                        s, v = semof(o)
                        if v > last.get(id(s), (None, 0))[1]:
                            last[id(s)] = (s, v)
                for s, v in last.values():
                    if waited.get(id(s), 0) < v:
                        eng.wait_ge(s, v)
            return body

        with nc.Block() as block:
            block.sync(gen("sp"))
            block.scalar(gen("act"))
            block.vector(gen("dve"))
            block.tensor(gen("pe"))
            block.gpsimd(gen("pool"))


def host_consts(core):
    b, q = divmod(core, 4)
    t = (q * TOK + np.arange(TOK)).astype(np.int64)
    r = (t // 64).astype(np.float32)
    col = (t % 64).astype(np.float32)
    inv = (np.float32(10000.0) ** (-np.arange(16, dtype=np.float32) / np.float32(16))).astype(np.float32)
    ar = r[:, None] * inv[None, :]
    ac = col[:, None] * inv[None, :]
    cr, sr, cc, sc = np.cos(ar), np.sin(ar), np.cos(ac), np.sin(ac)
    cos64 = np.concatenate([cr, cr, cc, cc], 1).astype(np.float32)
    sin64 = np.concatenate([-sr, sr, -sc, sc], 1).astype(np.float32)
    cos64 = np.ascontiguousarray(cos64.reshape(NT, 128, 64).transpose(1, 0, 2))
    sin64 = np.ascontiguousarray(sin64.reshape(NT, 128, 64).transpose(1, 0, 2))
    n1 = np.arange(64)
    th = 2 * np.pi * np.outer(n1, n1) / 64.0
    C, S = np.cos(th), np.sin(th)
    sc1 = 1.0 / np.sqrt(SEQ * 64.0)
    m1 = np.zeros((128, 128), np.float64)
    m1[:64, :64] = C
    m1[64:, :64] = -S
    m1[:64, 64:] = S
    m1[64:, 64:] = C
    m1 *= sc1
    n2 = np.arange(128)[:, None, None]
    k1 = np.arange(64)[None, :, None]
    k2 = (np.arange(32) + 32 * q)[None, None, :]
    ph = 2 * np.pi * n2 * (64 * k2 + k1) / float(SEQ)
    mc = np.cos(ph).reshape(128, 64 * 32)
    ms = (-np.sin(ph)).reshape(128, 64 * 32)
    n = np.arange(256)
    t256 = 2 * np.pi * np.outer(n, n) / 256.0
    sc2 = 1.0 / np.sqrt(256 * 64.0)
    c256 = (np.cos(t256) * sc2).reshape(2, 128, 256).transpose(1, 0, 2)
    s256n = (-np.sin(t256) * sc2).reshape(2, 128, 256).transpose(1, 0, 2)
    c64d = np.concatenate([C, C], 1).astype(np.float32)
    s64d = np.concatenate([S, S], 1).astype(np.float32)
    pk = np.arange(128)[:, None]
    pq = np.arange(128)[None, :]
    mP = (pk >= pq).astype(np.float32)
    mN = (pk <= pq).astype(np.float32)
    masks = np.zeros((128, 10, 128), np.float32)
    masks[:, 0] = mP
    masks[:, 1] = mN
    for s in range(4):
        if s == q - 1:
            masks[:, 2 + s] = mP
        if s == q + 1:
            masks[:, 6 + s] = mN
    sel2 = np.zeros((2, 256), np.float32)
    sel2[0, :128] = 1.0
    sel2[1, 128:] = 1.0
    return {
        "cos64": cos64, "sin64": sin64,
        "m1": m1.astype(NPBF), "mc": mc.astype(NPBF), "ms": ms.astype(NPBF),
        "c256": np.ascontiguousarray(c256).astype(NPBF), "s256n": np.ascontiguousarray(s256n).astype(NPBF),
        "c64d": c64d, "s64d": s64d,
        "masks": masks.astype(NPBF), "sel2": sel2,
        "ident": np.eye(128, dtype=np.float32),
    }


CONST_SPECS = {
    "cos64": ([128, NT, 64], F32), "sin64": ([128, NT, 64], F32),
    "m1": ([128, 128], BF), "mc": ([128, 2048], BF), "ms": ([128, 2048], BF),
    "c256": ([128, 2, 256], BF), "s256n": ([128, 2, 256], BF),
    "c64d": ([64, 128], F32), "s64d": ([64, 128], F32),
    "masks": ([128, 10, 128], BF), "sel2": ([2, 256], F32), "ident": ([128, 128], F32),
}

WEIGHT_SPECS = {
    "norm_w": [2, D], "w_ada": [2, D, 3 * D], "b_ada": [2, 3 * D], "w_in": [2, D, INW],
    "qn_a": [2, 64], "kn_a": [2, 64], "qn_d": [2, 64], "kn_d": [2, 64], "sink_d": [2, 4],
    "w_fnet": [2, 4, 64, 64], "w_spT": [2, 128, 4, 128], "b_sp": [2, 4, 128],
    "w_br": [2, 4, 256, D], "w_merge": [2, D, 4 * D], "b_mergeT": [2, 128, 32], "w_out": [2, D, D],
}


class Builder:
    def __init__(self, layers=(0, 1), first=True, last=True, debug=(), stop=None):
        self.stop = stop
        self.layers = tuple(layers)
        self.first = first
        self.last = last
        self.debug = set(debug)
        nc = self.nc = bass.Bass("TRN2", target_bir_lowering=False)
        self.P = Prog(nc)
        self.dbg_outs = {}
        self._uid = 0

    def uid(self, p="t"):
        self._uid += 1
        return "%s%d" % (p, self._uid)

    def dram_in(self, name, shape, dt=F32):
        return self.nc.dram_tensor(name, list(shape), dt, kind="ExternalInput").ap()

    def dram_out(self, name, shape, dt=F32):
        return self.nc.dram_tensor(name, list(shape), dt, kind="ExternalOutput").ap()

    def sb(self, name, shape, dt):
        return self.nc.alloc_sbuf_tensor("s_" + name, list(shape), dt)

    def tap(self, name, view, shape, reads, dt=F32):
        if name not in self.debug:
            return
        o = self.dram_out("dbg_" + name, shape, dt)
        self.dbg_outs["dbg_" + name] = (shape, dt)
        self.P.dma("sp", o, view, reads=reads)

    def build(self):
        nc, P = self.nc, self.P
        if self.first:
            self.x_in = self.dram_in("x", [TOK, D])
            self.ctx_in = self.dram_in("ctx", [256, D])
        else:
            self.xs_in = self.dram_in("xs_in", [TALL, D])
        self.cvecT = self.dram_in("cvecT", [128, KC, 2])
        self.W = {k: self.dram_in(k, s) for k, s in WEIGHT_SPECS.items()}
        self.C = {k: self.dram_in(k, s, dt) for k, (s, dt) in CONST_SPECS.items()}
        if self.last:
            self.out = self.dram_out("out", [TOK, D])
        else:
            self.xs_out = self.dram_out("xs_out", [TALL, D])
        self.xs = nc.dram_tensor("xs", [TALL, D], F32).ap()
        self.payload = [[nc.dram_tensor("payload%d_%d" % (l, j), [256, TOK], BF) for j in range(4)] for l in range(2)]
        self.gathered = [[nc.dram_tensor("gathered%d_%d" % (l, j), [1024, TOK], BF) for j in range(4)] for l in range(2)]

        self.HT = self.sb("HT", [128, KC, TALL], BF)
        self.YB = [self.sb("YB%d" % r, [128, 2, TALL], BF) for r in range(4)]
        self.WSF = [self.sb("WSF%d" % i, [128, 2048], F32) for i in range(2)]
        self.wsf_i = 0
        self.cos64 = self.sb("cos64", [128, NT, 64], F32)
        self.sin64 = self.sb("sin64", [128, NT, 64], F32)
        self.identf = self.sb("identf", [128, 128], F32)
        self.ident = self.sb("ident", [128, 128], BF)
        self.m1 = self.sb("m1", [128, 128], BF)
        self.c256 = self.sb("c256", [128, 2, 256], BF)
        self.s256n = self.sb("s256n", [128, 2, 256], BF)
        self.c64d = self.sb("c64d", [64, 128], F32)
        self.s64d = self.sb("s64d", [64, 128], F32)
        self.masks = self.sb("masks", [128, 10, 128], BF)
        self.sel2 = self.sb("sel2", [2, 256], F32)
        self.ones64 = self.sb("ones64", [128, 64], BF)
        self.scT = self.sb("scT", [128, KC, 2], F32)
        self.gate2 = self.sb("gate2", [2, D], F32)
        self.knw = [self.sb("knw%d" % i, [128, 128], F32) for i in range(2)]
        self.qnw = [self.sb("qnw%d" % i, [128, 256], F32) for i in range(2)]
        self.esink = self.sb("esink", [128, 4], F32)
        self.KTc = [self.sb("KTc%d" % i, [128, 256], BF) for i in range(2)]
        self.Vc = [self.sb("Vc%d" % i, [128, 2, 128], BF) for i in range(2)]
        self.UVc = self.sb("UVc", [128, 2, 2, 256], BF)
        self.BA = self.sb("BA", [128, 2, 2, 128], BF)
        self.stat = self.sb("stat", [128, 64], F32)
        self.stat_i = 0
        self.PSW = [nc.alloc_psum_tensor("psw%d" % i, [128, 1024], F32) for i in range(2)]
        self.PS = [self.PSW[i // 2][:, (i % 2) * 512:(i % 2 + 1) * 512] for i in range(4)]
        self.PS += [nc.alloc_psum_tensor("ps%d" % i, [128, 512], F32) for i in (4, 5)]
        self.PST = [nc.alloc_psum_tensor("pst%d" % i, [128, 1024], BF) for i in range(2)]
        self.ps_i = 0
        self.pst_i = 0
        self.ARENA_BYTES = 88 * 1024
        self.arena = self.sb("arena", [128, self.ARENA_BYTES // 2], BF)
        self.ar_off = 0
        self.ar_gen = 0

        for k in ("cos64", "sin64", "m1", "c256", "s256n", "c64d", "s64d", "masks", "sel2"):
            P.dma("sp", getattr(self, k)[:], self.C[k], writes=[k])
        P.dma("sp", self.identf[:], self.C["ident"], writes=["identf"])
        P.op("dve", "tensor_copy", self.ident[:], self.identf[:], R=["identf"], W=["ident"])
        P.op("pool", "memset", self.ones64[:], 1.0, R=[], W=["ones64"])
        P.dma("sp", self.scT[:], self.cvecT, writes=["scT"])
        P.op("act", "activation", self.scT[:], self.scT[:], AF.Silu, R=["scT"], W=["scT"])

        for li, l in enumerate(self.layers):
            is_first = self.first and li == 0
            is_last = self.last and li == len(self.layers) - 1
            self.layer(l, is_first, is_last, x_from_input=is_first,
                       x_src_xs_in=(not self.first and li == 0))
        if self.stop is not None:
            dst = self.out if self.last else self.xs_out
            P.dma("sp", dst[0:128, :], self.identf[:, :].unsqueeze(1).broadcast_to([128, 8, 128]) if False else self.cos64[:, :, :], reads=["cos64"])
        P.emit()
        return nc

    def phase(self):
        self.P.barrier()
        self.ar_off = 0
        self.ar_gen += 1

    def ar(self, name, shape, dt):
        n = int(np.prod(shape[1:]))
        nbytes = n * (4 if dt == F32 else 2)
        nbytes = (nbytes + 63) // 64 * 64
        assert self.ar_off + nbytes <= self.ARENA_BYTES, (name, self.ar_off, nbytes)
        e0 = self.ar_off // 2
        v = self.arena[:, e0:e0 + nbytes // 2]
        if dt == F32:
            v = v.bitcast(F32)
        v = v[:, 0:n]
        self.ar_off += nbytes
        if shape[0] != 128:
            v = v[0:shape[0]]
        if len(shape) == 3:
            v = v.rearrange("p (a b) -> p a b", a=shape[1])
        elif len(shape) == 4:
            v = v.rearrange("p (a b c) -> p a b c", a=shape[1], b=shape[2])
        return v, "%s_g%d" % (name, self.ar_gen)

    def ps(self):
        i = self.ps_i % 4
        self.ps_i += 1
        return self.PS[i], "ps%d" % i

    def pst(self):
        i = self.pst_i % 2
        self.pst_i += 1
        return self.PST[i], "pst%d" % i

    def statcol(self, n=1):
        c = (self.stat_i % 16) * 4
        self.stat_i += 1
        return self.stat[:, c:c + n], "stat%d" % c

    def load_w(self, src, dst, dstbuf, shape3, q="sp"):
        P = self.P
        i = self.wsf_i % 2
        self.wsf_i += 1
        a, b = shape3
        st = self.WSF[i][:, 0:a * b].rearrange("p (a b) -> p a b", a=a)
        P.dma(q, st, src, writes=["wsf%d" % i])
        P.op("dve", "tensor_copy", dst, st, R=["wsf%d" % i], W=[dstbuf])

    def w_cols(self, name, l, c0, n):
        return self.W[name][l, :, c0:c0 + n].rearrange("(kc p) n -> p kc n", p=128)

    @staticmethod
    def interleave(gens, width):
        gens = iter(gens)
        active = []
        while True:
            while len(active) < width:
                g = next(gens, None)
                if g is None:
                    break
                active.append(g)
            if not active:
                break
            for g in list(active):
                try:
                    next(g)
                except StopIteration:
                    active.remove(g)

    def proj_fm(self, wb, wbuf, c0, nchunks, tiles, consume):
        P = self.P
        for c in range(nchunks):
            for (t0, n) in GROUPS:
                if t0 // 128 >= tiles:
                    continue
                ps, pb = self.ps()
                htb = ["HT%d" % i for i in range(t0 // 128, (t0 + n) // 128)]
                for kc in range(KC):
                    P.op("pe", "matmul", ps[:, 0:n], wb[:, kc, c0 + c * 128:c0 + (c + 1) * 128], self.HT[:, kc, t0:t0 + n],
                        start=(kc == 0), stop=(kc == KC - 1), R=[wbuf] + htb, W=[pb])
                consume(ps, pb, c, t0, n)

    def proj_tm(self, wb, wbuf, c0, ncols, tile):
        P = self.P
        ps, pb = self.ps()
        for kc in range(KC):
            P.op("pe", "matmul", ps[:, 0:ncols], self.HT[:, kc, tile * 128:(tile + 1) * 128], wb[:, kc, c0:c0 + ncols],
                start=(kc == 0), stop=(kc == KC - 1), R=[wbuf, "HT%d" % tile], W=[pb])
        return ps, pb

    def headnorm_rope(self, ps, pb, c0, nh, wtile, wbuf, tile, tmp, out_bf, outbuf, permute):
        P = self.P
        n = nh * 64
        (sq, sqb), (t1, t1b), (t2, t2b) = tmp
        P.op("act", "activation", sq[:, 0:n], ps[:, c0:c0 + n], AF.Square, R=[pb], W=[sqb])
        yield
        ssq, ssqb = self.statcol(4)
        rt, rtb = self.statcol(4)
        rr, rrb = self.statcol(4)
        P.op("dve", "tensor_reduce", ssq[:, 0:nh], sq[:, 0:n].rearrange("p (h d) -> p h d", h=nh), AX.X, ALU.add, R=[sqb], W=[ssqb])
        yield
        P.op("act", "activation", rt[:, 0:nh], ssq[:, 0:nh], AF.Sqrt, bias=64.0 * EPS, scale=1.0, R=[ssqb], W=[rtb])
        yield
        P.op("dve", "reciprocal", rr[:, 0:nh], rt[:, 0:nh], R=[rtb], W=[rrb])
        yield
        P.op("dve", "tensor_tensor", t1[:, 0:n].rearrange("p (h d) -> p h d", h=nh), ps[:, c0:c0 + n].rearrange("p (h d) -> p h d", h=nh),
            rr[:, 0:nh].unsqueeze(2).broadcast_to([128, nh, 64]), ALU.mult, R=[pb, rrb], W=[t1b])
        yield
        if permute:
            ov = out_bf[:, 0:n].rearrange("p (g kv d) -> p kv g d", g=2, kv=2)
        if tile >= NT:
            if permute:
                P.op("pool", "tensor_tensor", ov, t1[:, 0:n].rearrange("p (kv g d) -> p kv g d", kv=2, g=2),
                    wtile[:, 0:n].rearrange("p (kv g d) -> p kv g d", kv=2, g=2), ALU.mult, R=[t1b, wbuf], W=[outbuf])
                yield
            else:
                P.op("pool", "tensor_tensor", out_bf[:, 0:n], t1[:, 0:n], wtile[:, 0:n], ALU.mult, R=[t1b, wbuf], W=[outbuf])
                yield
            return
        P.op("pool", "tensor_tensor", sq[:, 0:n], t1[:, 0:n], wtile[:, 0:n], ALU.mult, R=[t1b, wbuf, sqb], W=[sqb])
        yield
        cosb = self.cos64[:, tile, :].unsqueeze(1).broadcast_to([128, nh, 64])
        sinv = self.sin64[:, tile, :].rearrange("p (t j) -> p t j", t=4)
        xv = sq[:, 0:n].rearrange("p (h t j) -> p h t j", h=nh, t=4)
        t2v = t2[:, 0:n].rearrange("p (h t j) -> p h t j", h=nh, t=4)
        P.op("dve", "tensor_tensor", t1[:, 0:n].rearrange("p (h d) -> p h d", h=nh),
                                              sq[:, 0:n].rearrange("p (h d) -> p h d", h=nh), cosb, ALU.mult, R=[sqb, "cos64", t1b], W=[t1b])
        yield
        P.op("pool", "tensor_tensor", t2v[:, :, 0::2, :], xv[:, :, 1::2, :],
                                               sinv[:, 0::2, :].unsqueeze(1).broadcast_to([128, nh, 2, 16]), ALU.mult, R=[sqb, "sin64"], W=[t2b + "a"])
        yield
        P.op("pool", "tensor_tensor", t2v[:, :, 1::2, :], xv[:, :, 0::2, :],
                                               sinv[:, 1::2, :].unsqueeze(1).broadcast_to([128, nh, 2, 16]), ALU.mult, R=[sqb, "sin64"], W=[t2b + "b"])
        yield
        if permute:
            P.op("dve", "tensor_tensor", ov, t1[:, 0:n].rearrange("p (kv g d) -> p kv g d", kv=2, g=2),
                t2[:, 0:n].rearrange("p (kv g d) -> p kv g d", kv=2, g=2), ALU.add, R=[t1b, t2b + "a", t2b + "b"], W=[outbuf])
            yield
        else:
            P.op("dve", "tensor_tensor", out_bf[:, 0:n], t1[:, 0:n], t2[:, 0:n], ALU.add, R=[t1b, t2b + "a", t2b + "b"], W=[outbuf])
            yield

    def transpose_to(self, src_bf, srcbuf, nblk, dst_fn, dstbuf_fn, eng="act"):
        P = self.P
        pt, ptb = self.pst()
        for j in range(nblk):
            P.op("pe", "transpose", pt[:, j * 128:(j + 1) * 128], src_bf[:, j * 128:(j + 1) * 128], self.ident[:], R=[srcbuf, "ident"], W=[ptb])
        yield
        for j in range(nblk):
            if eng == "act":
                P.op("act", "copy", dst_fn(j), pt[:, j * 128:(j + 1) * 128], R=[ptb], W=[dstbuf_fn(j)])
            else:
                P.op("dve", "tensor_copy", dst_fn(j), pt[:, j * 128:(j + 1) * 128], R=[ptb], W=[dstbuf_fn(j)])
        yield

    def layer(self, l, is_first, is_last, x_from_input, x_src_xs_in):
        nc, P, W = self.nc, self.P, self.W
        do_ctx_out = not is_last_layer(l)
        ntl = TT if do_ctx_out else NT
        pay, gat = self.payload[l], self.gathered[l]

        def x_tile_src(i):
            if x_from_input:
                return self.x_in[i * 128:(i + 1) * 128, :] if i < NT else self.ctx_in[(i - NT) * 128:(i - NT + 1) * 128, :]
            if x_src_xs_in:
                return self.xs_in[i * 128:(i + 1) * 128, :]
            return self.xs[i * 128:(i + 1) * 128, :]

        self.phase()
        L = "L%d" % l
        mod, modb = self.ar("mod", [2, 3 * D], F32)
        arow, arowb = self.ar("arow", [2, D], F32)
        normw2, normw2b = self.ar("normw2", [2, D], F32)
        bada2, bada2b = self.ar("bada2", [2, 3 * D], F32)
        P.dma("sp", normw2, W["norm_w"][l:l + 1, :].partition_broadcast(2), writes=[normw2b])
        P.dma("sp", bada2, W["b_ada"][l:l + 1, :].partition_broadcast(2), writes=[bada2b])
        for i, (kn, qn) in enumerate((("kn_a", "qn_a"), ("kn_d", "qn_d"))):
            for h in range(2):
                P.dma("sp", self.knw[i][:, h * 64:(h + 1) * 64], W[kn][l:l + 1, :].partition_broadcast(128), writes=["knw%d" % i])
            for h in range(4):
                P.dma("sp", self.qnw[i][:, h * 64:(h + 1) * 64], W[qn][l:l + 1, :].partition_broadcast(128), writes=["qnw%d" % i])
            P.op("act", "mul", self.knw[i][:], self.knw[i][:], 8.0, R=["knw%d" % i], W=["knw%d" % i])
        P.dma("sp", self.esink[:], W["sink_d"][l:l + 1, :].partition_broadcast(128), writes=["esink"])
        P.op("act", "activation", self.esink[:], self.esink[:], AF.Exp, R=["esink"], W=["esink"])

        for blk in range(6):
            ps, pb = self.ps()
            for half in range(2):
                i = self.wsf_i % 2
                self.wsf_i += 1
                st = self.WSF[i][:].rearrange("p (a b) -> p a b", a=KC)
                c0 = blk * 512 + half * 256
                P.dma("sp", st, self.w_cols("w_ada", l, c0, 256), writes=["wsf%d" % i])
                for kc in range(KC):
                    P.op("pe", "matmul", ps[0:2, half * 256:(half + 1) * 256], self.scT[:, kc, :], st[:, kc, :],
                        start=(kc == 0), stop=(kc == KC - 1), R=["scT", "wsf%d" % i], W=[pb])
            P.op("dve", "tensor_tensor", mod[0:2, blk * 512:(blk + 1) * 512], ps[0:2, :], bada2[0:2, blk * 512:(blk + 1) * 512], ALU.add, R=[pb, bada2b], W=[modb])
        self.tap(L + "mod", mod, [2, 3 * D], [modb])
        P.op("dve", "scalar_tensor_tensor", arow, mod[0:2, D:2 * D], 1.0, normw2, ALU.add, ALU.mult, R=[modb, normw2b], W=[arowb])
        P.op("act", "copy", self.gate2[:], mod[0:2, 2 * D:3 * D], R=[modb], W=["gate2"])
        a_bc, a_bcb = self.ar("a_bc", [128, 2, D], F32)
        sh_bc, sh_bcb = self.ar("sh_bc", [128, 2, D], F32)
        for r in range(2):
            for half in range(2):
                for which, (dst, dbuf, src, sbuf) in enumerate(((a_bc, a_bcb, arow, arowb), (sh_bc, sh_bcb, mod, modb))):
                    ps, pb = self.ps()
                    P.op("pe", "matmul", ps[:, :], self.sel2[0:2, r * 128:(r + 1) * 128], src[0:2, half * 512:(half + 1) * 512],
                        start=True, stop=True, R=["sel2", sbuf], W=[pb])
                    P.op("act", "copy", dst[:, r, half * 512:(half + 1) * 512], ps[:, :], R=[pb], W=[dbuf])

        XT = [self.ar("XT%d" % i, [128, D], F32) for i in range(2)]
        TMPF = [self.ar("TMPF%d" % i, [128, D], F32) for i in range(2)]
        HB = [self.ar("HB%d" % i, [128, D], BF) for i in range(2)]
        def a1_chain(i):
            s = i % 2
            r = 0 if i < NT else 1
            (xt, xtb), (tf, tfb), (hb, hbb) = XT[s], TMPF[s], HB[s]
            P.dma("sp", xt, x_tile_src(i), writes=[xtb])
            ssq, ssqb = self.statcol()
            rt, rtb = self.statcol()
            rs, rsb = self.statcol()
            P.op("act", "activation", tf, xt, AF.Square, accum_out=ssq, R=[xtb], W=[tfb, ssqb])
            yield
            P.op("act", "activation", rt, ssq, AF.Sqrt, bias=EPS, scale=1.0 / D, R=[ssqb], W=[rtb])
            yield
            P.op("dve", "reciprocal", rs, rt, R=[rtb], W=[rsb])
            yield
            P.op("dve", "scalar_tensor_tensor", tf, xt, rs, a_bc[:, r, :], ALU.mult, ALU.mult, R=[xtb, rsb, a_bcb, tfb], W=[tfb])
            yield
            P.op("pool", "tensor_tensor", hb, tf, sh_bc[:, r, :], ALU.add, R=[tfb, sh_bcb], W=[hbb])
            yield
            if i == 0:
                self.tap(L + "h0", hb, [128, D], [hbb], BF)
            pt, ptb = self.pst()
            for kc in range(KC):
                P.op("pe", "transpose", pt[:, kc * 128:(kc + 1) * 128], hb[:, kc * 128:(kc + 1) * 128], self.ident[:], R=[hbb, "ident"], W=[ptb])
            yield
            P.op("act", "copy", self.HT[:, :, i * 128:(i + 1) * 128], pt[:].rearrange("p (k t) -> p k t", k=KC), R=[ptb], W=["HT%d" % i])
            yield
        self.interleave((a1_chain(i) for i in range(TT)), 2)

        if self.stop == "A1":
            return
        self.phase()
        WB, WBb = self.ar("WBkv", [128, KC, 256], BF)
        KR = [self.ar("kr%d" % j, [128, 128], BF) for j in range(2)]
        KTs = [self.ar("KTs%d" % a, [128, TOK], BF) for a in range(2)]
        Vs = [self.ar("Vs%d" % a, [128, NT, 128], BF) for a in range(2)]
        tmps = [[self.ar("hnt%d_%d" % (k, j), [128, 256], F32) for j in range(3)] for k in range(2)]
        WBs = [(WB, WBb), self.ar("WBkv2", [128, KC, 256], BF)]
        for a, c0 in enumerate((256, 1024)):
            self.load_w(self.w_cols("w_in", l, c0, 256), WBs[a][0], WBs[a][1], (KC, 256))

        def a2_chain(a, i, slot):
            wb_, wbb_ = WBs[a]
            ps, pb = self.proj_tm(wb_, wbb_, 0, 256, i)
            yield
            kr, krb = KR[slot]
            yield from self.headnorm_rope(ps, pb, 0, 2, self.knw[a], "knw%d" % a, i, tmps[slot], kr, krb, permute=False)
            if i < NT:
                kts, ktsb = KTs[a]
                vs, vsb = Vs[a]
                P.op("act", "copy", vs[:, i, :], ps[:, 128:256], R=[pb], W=[vsb])
                yield from self.transpose_to(kr, krb, 1, lambda j: kts[:, i * 128:(i + 1) * 128], lambda j: ktsb)
            else:
                j_ = i - NT
                P.op("act", "copy", self.Vc[a][:, j_, :], ps[:, 128:256], R=[pb], W=["Vc%d" % a])
                yield from self.transpose_to(kr, krb, 1, lambda jj: self.KTc[a][:, j_ * 128:(j_ + 1) * 128], lambda jj: "KTc%d" % a)

        chains = [(a, i) for i in range(TT) for a in range(2)]
        self.interleave((a2_chain(a, i, k % 2) for k, (a, i) in enumerate(chains)), 2)
        for a in range(2):
            kts, ktsb = KTs[a]
            vs, vsb = Vs[a]
            P.dma("sp", pay[0].ap()[a * 128:(a + 1) * 128, :], kts, reads=[ktsb], writes=["pay_k%d" % a])
            P.dma("sp", pay[3].ap()[a * 128:(a + 1) * 128, :].rearrange("p (t d) -> p t d", d=128), vs, reads=[vsb],
                  writes=["pay_v%d" % a])
        if l == self.layers[0]:
            self.tap(L + "KTsA", KTs[0][0], [128, TOK], [KTs[0][1]], BF)
            self.tap(L + "VsA", Vs[0][0], [128, NT, 128], [Vs[0][1]], BF)
            self.tap(L + "KTcA", self.KTc[0][:], [128, 256], ["KTc0"], BF)

        self.load_w(self.w_cols("w_in", l, 1536, 256), WB, WBb, (KC, 256))
        BFT, BFTb = self.ar("BFT", [128, 2, TALL], BF)

        def bf_consume(ps, pb, c, t0, n):
            P.op("act", "copy", BFT[:, c, t0:t0 + n], ps[:, 0:n], R=[pb], W=[BFTb])
        self.proj_fm(WB, WBb, 0, 2, TT, bf_consume)
        wf, wfb = self.ar("wf", [64, 4, 64], F32)
        P.dma("sp", wf, W["w_fnet"][l].rearrange("g j c -> j g c"), writes=[wfb])
        P.op("pool", "memset", self.BA[:], 0.0, R=[], W=["BA"])
        for cs, tab in enumerate((self.c64d, self.s64d)):
            for g in range(4):
                ps, pb = self.ps()
                P.op("pe", "matmul", ps[:, 0:64], tab[0:64, :], wf[0:64, g, :], start=True, stop=True, R=[wfb, "c64d", "s64d"], W=[pb])
                rows = slice((g % 2) * 64, (g % 2) * 64 + 64)
                P.op("dve", "tensor_copy", self.BA[rows, cs, g // 2, (g % 2) * 64:(g % 2) * 64 + 64], ps[rows, 0:64], R=[pb, "BA"], W=["BA"])
        UVT, UVTb = self.ar("UVT", [128, 2, 2, TOK], BF)
        for cs in range(2):
            for c in range(2):
                for (t0, n) in GROUPS[:4]:
                    ps, pb = self.ps()
                    P.op("pe", "matmul", ps[:, 0:n], self.BA[:, cs, c, :], BFT[:, c, t0:t0 + n], start=True, stop=True, R=["BA", BFTb], W=[pb])
                    P.op("act", "copy", UVT[:, cs, c, t0:t0 + n], ps[:, 0:n], R=[pb], W=[UVTb])
                for j in range(2):
                    ps, pb = self.ps()
                    P.op("pe", "matmul", ps[:, 0:128], BFT[:, c, TOK + j * 128:TOK + (j + 1) * 128], self.BA[:, cs, c, :], start=True, stop=True, R=["BA", BFTb], W=[pb])
                    P.op("dve", "tensor_copy", self.UVc[:, cs, j, c * 128:(c + 1) * 128], ps[:, 0:128], R=[pb], W=["UVc"])
        for cs in range(2):
            P.dma("sp", pay[1 + cs].ap().rearrange("(c p) t -> p c t", p=128), UVT[:, cs, :, :],
                  reads=[UVTb], writes=["pay_uv%d" % cs])
        self.tap(L + "UVT", UVT, [128, 2, 2, TOK], [UVTb], BF)

        if self.stop == "A":
            return
        payb = [["pay_k0", "pay_k1"], ["pay_uv0"], ["pay_uv1"], ["pay_v0", "pay_v1"]]
        for j in (0, 3, 1, 2):
            P.collective(lambda e, j=j: e.collective_compute("AllGather", ALU.bypass, replica_groups=[[0, 1, 2, 3], [4, 5, 6, 7]],
                                                             ins=[pay[j].ap().opt()], outs=[gat[j].ap().opt()]),
                         reads=payb[j], writes=["gat%d" % j])
        G = [g.ap() for g in gat]

        if self.stop == "G":
            return
        self.phase()
        self.sgu(l, ntl)
        self.phase()
        self.attention(l, 0, ntl, G, pay)
        self.phase()
        self.attention(l, 1, ntl, G, pay)
        self.phase()
        self.fnet(l, ntl, G)
        if l == self.layers[0]:
            for r in range(4):
                self.tap(L + "YB%d" % r, self.YB[r][:], [128, 2, TALL], ["YB%d_%d_%d" % (r, c, i) for c in range(2) for i in range(TT)], BF)
        if self.stop == "B5":
            return
        self.phase()
        self.merge(l, ntl, is_last, x_tile_src)

    def attention(self, l, which, ntl, G, pay):
        P, W = self.P, self.W
        q0 = 0 if which == 0 else 768
        z0 = 512 if which == 0 else 1280
        YB = self.YB[which]
        nqg = [g for g in GROUPS if g[0] // 128 < ntl]
        WQ, WQb = self.ar("WQ", [128, KC, 256], BF)
        WZ, WZb = self.ar("WZ", [128, KC, 256], BF)
        self.load_w(self.w_cols("w_in", l, q0, 256), WQ, WQb, (KC, 256))
        self.load_w(self.w_cols("w_in", l, z0, 256), WZ, WZb, (KC, 256))
        QT = [self.ar("QT%d" % g, [128, TALL], BF) for g in range(2)]
        def vview(V, k0, k1):
            return V[:, k0:k1, :].rearrange("p t (kv x) -> p t kv x", kv=2)[:, :, :, 0:64]

        if which == 0:
            NKB = 2 + 64
            KT, KTb = self.ar("KT", [128, NKB * 128], BF)
            V, Vb = self.ar("V", [128, NKB, 256], BF)
            P.op("pool", "memset", V[:, :, :].rearrange("p t (kv x) -> p t kv x", kv=2)[:, :, :, 64:128], 1.0, R=[], W=[Vb + "ones"])
            P.op("act", "copy", KT[:, 0:256], self.KTc[0][:], R=["KTc0"], W=[KTb + "c"])
            P.op("act", "copy", vview(V, 0, 2), self.Vc[0][:].rearrange("p t (kv d) -> p t kv d", kv=2), R=["Vc0"], W=[Vb + "c"])
            for s_ in range(4):
                P.dma("sp", KT[:, 256 + s_ * TOK:256 + (s_ + 1) * TOK], G[0][s_ * 256:s_ * 256 + 128, :], reads=["gat0"], writes=[KTb + str(s_)])
                for hf in range(2):
                    P.dma("sp", vview(V, 2 + s_ * NT + hf * 8, 2 + s_ * NT + hf * 8 + 8),
                          G[3][s_ * 256:s_ * 256 + 128, hf * 1024:(hf + 1) * 1024].rearrange("p (t kv d) -> p t kv d", kv=2, d=64),
                          reads=["gat3"], writes=[Vb + str(s_)])

            def kbufs(kb):
                return [KTb + ("c" if kb < 2 else str((kb - 2) // NT))], [Vb + ("c" if kb < 2 else str((kb - 2) // NT)), Vb + "ones"]
        else:
            NKB = 2 + NT + 8
            KT, KTb = self.ar("KT", [128, NKB * 128], BF)
            V, Vb = self.ar("V", [128, NKB, 256], BF)
            P.op("pool", "memset", V[:, :, :].rearrange("p t (kv x) -> p t kv x", kv=2)[:, :, :, 64:128], 1.0, R=[], W=[Vb + "ones"])
            P.op("act", "copy", KT[:, 0:256], self.KTc[1][:], R=["KTc1"], W=[KTb])
            P.op("act", "copy", vview(V, 0, 2), self.Vc[1][:].rearrange("p t (kv d) -> p t kv d", kv=2), R=["Vc1"], W=[Vb])
            P.dma("sp", KT[:, 256:256 + TOK], pay[0].ap()[128:256, :], reads=["pay_k1"], writes=[KTb])
            for hf in range(2):
                P.dma("sp", vview(V, 2 + hf * 8, 2 + hf * 8 + 8),
                      pay[3].ap()[128:256, hf * 1024:(hf + 1) * 1024].rearrange("p (t kv d) -> p t kv d", kv=2, d=64),
                      reads=["pay_v1"], writes=[Vb])
            for s_ in range(4):
                for kb, c0 in ((2 + NT + s_, TOK - 128), (2 + NT + 4 + s_, 0)):
                    P.dma("sp", KT[:, kb * 128:(kb + 1) * 128], G[0][s_ * 256 + 128:s_ * 256 + 256, c0:c0 + 128], reads=["gat0"], writes=[KTb])
                    P.dma("sp", vview(V, kb, kb + 1),
                          G[3][s_ * 256 + 128:s_ * 256 + 256, c0:c0 + 128].rearrange("p (t kv d) -> p t kv d", kv=2, d=64),
                          reads=["gat3"], writes=[Vb])

            def kbufs(kb):
                return [KTb], [Vb, Vb + "ones"]

        tmps = [[self.ar("hnt%d_%d" % (k, j), [128, 256], F32) for j in range(3)] for k in range(2)]
        QR = [self.ar("QR%d" % j, [128, 256], BF) for j in range(2)]
        def q_chain(i, slot):
            ps, pb = self.proj_tm(WQ, WQb, 0, 256, i)
            yield
            qr, qrb = QR[slot]
            yield from self.headnorm_rope(ps, pb, 0, 4, self.qnw[which], "qnw%d" % which, i, tmps[slot], qr, qrb, permute=True)
            yield from self.transpose_to(qr, qrb, 2, lambda j: QT[j][0][:, i * 128:(i + 1) * 128], lambda j: QT[j][1] + "_%d" % i)
        self.interleave((q_chain(i, i % 2) for i in range(ntl)), 2)

        def z_consume(ps, pb, c, t0, n):
            P.op("act", "activation", YB[:, c, t0:t0 + n], ps[:, 0:n], AF.Silu, R=[pb], W=["YB%d_%d_%d" % (which, c, i) for i in range(t0 // 128, (t0 + n) // 128)])
        self.proj_fm(WZ, WZb, 0, 2, ntl, z_consume)
        if which == 0 and l == self.layers[0]:
            self.tap("L%dQT0" % l, QT[0][0], [128, TALL], [QT[0][1] + "_%d" % i for i in range(ntl)], BF)

        PT = [self.ar("PT%d" % j, [128, 1024], BF) for j in range(3)]
        fin = [self.ar("fin%d" % j, [128, 512], F32) for j in range(3)]

        work = []

        def add_pair(g, t0, n, keylist):
            cw = 512 // n
            chunks = [keylist[c:c + cw] for c in range(0, len(keylist), cw)]
            for ci, ch in enumerate(chunks):
                work.append(dict(g=g, t0=t0, n=n, keys=ch, first=(ci == 0), last=(ci == len(chunks) - 1)))

        if which == 0:
            for (t0, n) in nqg:
                keys = [(kb, None) for kb in range(NKB)] if t0 < TOK else [(0, None), (1, None)]
                for g in range(2):
                    add_pair(g, t0, n, keys)
        else:
            for i in range(ntl):
                if i >= NT:
                    keys = [(0, None), (1, None)]
                else:
                    keys = [(0, None), (1, None), (2 + i, None)]
                    if i > 0:
                        keys.append((2 + i - 1, 0))
                    else:
                        keys += [(2 + NT + s_, 2 + s_) for s_ in range(4)]
                    if i < NT - 1:
                        keys.append((2 + i + 1, 1))
                    else:
                        keys += [(2 + NT + 4 + s_, 6 + s_) for s_ in range(4)]
                for g in range(2):
                    add_pair(g, i * 128, 128, keys)

        accs = [(self.PS[4], "ps4"), (self.PS[5], "ps5")]

        def stage1(w, slot):
            g, t0, n = w["g"], w["t0"], w["n"]
            qt, qtb = QT[g]
            qbufs = [qtb + "_%d" % i for i in range(t0 // 128, (t0 + n) // 128)]
            psw = self.PSW[slot % 2]
            pbs = ["ps%d" % (2 * (slot % 2)), "ps%d" % (2 * (slot % 2) + 1)]
            pt, ptb = PT[slot % 3]
            cnt = len(w["keys"])
            for j, (kb, mi) in enumerate(w["keys"]):
                kb_k, _ = kbufs(kb)
                for kv in range(2):
                    krows = slice(kv * 64, (kv + 1) * 64)
                    P.op("pe", "matmul", psw[:, kv * 512 + j * n:kv * 512 + (j + 1) * n], KT[krows, kb * 128:(kb + 1) * 128],
                         qt[krows, t0:t0 + n], start=True, stop=True, R=kb_k + qbufs, W=[pbs[kv]])
            P.op("act", "activation", pt[:, :].rearrange("p (a b) -> p a b", a=2)[:, :, 0:cnt * n],
                 psw[:, :].rearrange("p (a b) -> p a b", a=2)[:, :, 0:cnt * n], AF.Exp, R=pbs, W=[ptb])
            for j, (kb, mi) in enumerate(w["keys"]):
                if mi is not None:
                    P.op("dve", "tensor_tensor", pt[:, :].rearrange("p (a b) -> p a b", a=2)[:, :, j * n:(j + 1) * n],
                         pt[:, :].rearrange("p (a b) -> p a b", a=2)[:, :, j * n:(j + 1) * n],
                         self.masks[:, mi, 0:n].unsqueeze(1).broadcast_to([128, 2, n]), ALU.mult, R=[ptb, "masks"], W=[ptb])

        def stage2(w, slot):
            g, t0, n = w["g"], w["t0"], w["n"]
            rows = slice(g * 64, (g + 1) * 64)
            pt, ptb = PT[slot % 3]
            cnt = len(w["keys"])
            for j, (kb, mi) in enumerate(w["keys"]):
                _, kb_v = kbufs(kb)
                st = w["first"] and j == 0
                sp = w["last"] and j == cnt - 1
                for kv in range(2):
                    acc, accb = accs[kv]
                    P.op("pe", "matmul", acc[:, 0:n], V[:, kb, kv * 128:(kv + 1) * 128], pt[:, kv * 512 + j * n:kv * 512 + (j + 1) * n],
                         start=st, stop=sp, R=kb_v + [ptb], W=[accb])
            if not w["last"]:
                return
            for kv in range(2):
                acc, accb = accs[kv]
                h = 2 * kv + g
                (f0, f0b), (f1, f1b), (f2, f2b) = fin
                if which == 1:
                    P.op("dve", "tensor_scalar", f2[64:128, 0:n], acc[64:128, 0:n], self.esink[64:128, h:h + 1], None, ALU.add,
                         R=[accb, "esink"], W=[f2b])
                    P.op("dve", "reciprocal", f0[rows, 0:n], f2[64:128, 0:n], R=[f2b], W=[f0b])
                else:
                    P.op("dve", "reciprocal", f0[rows, 0:n], acc[64:128, 0:n], R=[accb], W=[f0b])
                P.op("dve", "tensor_copy", f1[rows, 0:n], acc[0:64, 0:n], R=[accb], W=[f1b])
                fe = "dve" if which == 1 else "pool"
                P.op(fe, "tensor_tensor", f1[rows, 0:n], f1[rows, 0:n], f0[rows, 0:n], ALU.mult, R=[f1b, f0b], W=[f1b])
                ybb = ["YB%d_%d_%d" % (which, kv, i) for i in range(t0 // 128, (t0 + n) // 128)]
                P.op(fe, "tensor_tensor", YB[rows, kv, t0:t0 + n], f1[rows, 0:n], YB[rows, kv, t0:t0 + n], ALU.mult,
                     R=[f1b] + ybb, W=ybb)

        LOOK = 1
        for k in range(len(work) + LOOK):
            if k < len(work):
                stage1(work[k], k)
            if k >= LOOK:
                stage2(work[k - LOOK], k - LOOK)

    def fnet(self, l, ntl, G):
        P, W = self.P, self.W
        YB = self.YB[2]
        WZ, WZb = self.ar("WZ", [128, KC, 256], BF)
        self.load_w(self.w_cols("w_in", l, 1792, 256), WZ, WZb, (KC, 256))

        def z_consume(ps, pb, c, t0, n):
            P.op("act", "activation", YB[:, c, t0:t0 + n], ps[:, 0:n], AF.Silu, R=[pb], W=["YB2_%d_%d" % (c, i) for i in range(t0 // 128, (t0 + n) // 128)])
        self.proj_fm(WZ, WZb, 0, 2, ntl, z_consume)
        MC, MCb = self.ar("MC", [128, 64, 32], BF)
        MS, MSb = self.ar("MS", [128, 64, 32], BF)
        P.dma("sp", MC, self.C["mc"].rearrange("p (a b) -> p a b", a=64), writes=[MCb])
        P.dma("sp", MS, self.C["ms"].rearrange("p (a b) -> p a b", a=64), writes=[MSb])
        Z = [self.ar("Z%d" % j, [128, 64, 128], BF) for j in range(2)]
        T3, T3b = self.ar("T3", [128, 64, 128], BF)
        sp = 0
        for c in range(2):
            for hf in range(2):
                z, zb = Z[sp % 2]
                sp += 1
                for s in range(4):
                    for ri in range(2):
                        r0 = s * 256 + c * 128 + hf * 64
                        P.dma("sp", z[ri * 64 + s * 16:ri * 64 + (s + 1) * 16, :, :],
                              G[1 + ri][r0:r0 + 64, :].rearrange("ch (t n) -> t ch n", n=128), reads=["gat%d" % (1 + ri)], writes=[zb])
                for ch4 in range(16):
                    ps, pb = self.ps()
                    for j in range(4):
                        ch = ch4 * 4 + j
                        P.op("pe", "matmul", ps[:, j * 128:(j + 1) * 128], z[:, ch, :], self.m1[:, :],
                                                                            start=True, stop=True, R=[zb, "m1"], W=[pb])
                    if ch4 % 2 == 0:
                        P.op("act", "copy", T3[:, ch4 * 4:ch4 * 4 + 4, :], ps[:].rearrange("p (a b) -> p a b", a=4), R=[pb], W=[T3b])
                    else:
                        P.op("dve", "tensor_copy", T3[:, ch4 * 4:ch4 * 4 + 4, :], ps[:].rearrange("p (a b) -> p a b", a=4), R=[pb], W=[T3b])
                rows = slice(hf * 64, (hf + 1) * 64)
                for bank in range(4):
                    ps, pb = self.ps()
                    for kk in range(16):
                        k1 = bank * 16 + kk
                        P.op("pe", "matmul", ps[rows, kk * 32:(kk + 1) * 32], T3[:, :, k1], MC[:, k1, :],
                                                                         start=True, stop=False, tile_position=(0, hf * 64), R=[T3b, MCb], W=[pb])
                        P.op("pe", "matmul", ps[rows, kk * 32:(kk + 1) * 32], T3[:, :, 64 + k1], MS[:, k1, :],
                                                                         start=False, stop=True, tile_position=(0, hf * 64), R=[T3b, MSb], W=[pb])
                    yv = YB[rows, c, 0:TOK].rearrange("p (k2 k1) -> p k1 k2", k1=64)[:, bank * 16:(bank + 1) * 16, :]
                    ybb = ["YB2_%d_%d" % (c, i) for i in range(NT)]
                    P.op("dve", "tensor_tensor", yv, ps[rows, :].rearrange("p (a b) -> p a b", a=16), yv, ALU.mult, R=[pb] + ybb, W=ybb)
        if ntl > NT:
            for c in range(2):
                ps, pb = self.ps()
                k = 0
                for uv, tab in ((0, self.c256), (1, self.s256n)):
                    for j in range(2):
                        P.op("pe", "matmul", ps[:, 0:256], self.UVc[:, uv, j, c * 128:(c + 1) * 128], tab[:, j, :], start=(k == 0), stop=(k == 3), R=["UVc", "c256", "s256n"], W=[pb])
                        k += 1
                ybb = ["YB2_%d_%d" % (c, i) for i in range(NT, TT)]
                P.op("dve", "tensor_tensor", YB[:, c, TOK:TALL], ps[:, 0:256], YB[:, c, TOK:TALL], ALU.mult, R=[pb] + ybb, W=ybb)

    def sgu(self, l, ntl):
        P, W = self.P, self.W
        YB = self.YB[3]
        WU, WUb = self.ar("WU", [128, KC, 256], BF)
        WV, WVb = self.ar("WV", [128, KC, 256], BF)
        WZ, WZb = self.ar("WZ", [128, KC, 256], BF)
        self.load_w(self.w_cols("w_in", l, 2048, 256), WU, WUb, (KC, 256))
        self.load_w(self.w_cols("w_in", l, 2304, 256), WV, WVb, (KC, 256))
        self.load_w(self.w_cols("w_in", l, 2560, 256), WZ, WZb, (KC, 256))
        GU, GUb = self.ar("GU", [128, 2, TALL], BF)
        GV, GVb = self.ar("GV", [128, TT, 256], BF)
        wsp, wspb = self.ar("wsp", [128, 4, 128], BF)
        bsp, bspb = self.ar("bsp", [128, 2, 128], F32)
        self.load_w(W["w_spT"][l], wsp, wspb, (4, 128))
        for g in range(4):
            P.dma("sp", bsp[(g % 2) * 64:(g % 2) * 64 + 64, g // 2, :], W["b_sp"][l, g:g + 1, :].partition_broadcast(64), writes=[bspb])

        def u_consume(ps, pb, c, t0, n):
            P.op("act", "activation", GU[:, c, t0:t0 + n], ps[:, 0:n], AF.Gelu, R=[pb], W=[GUb])

        def z_consume(ps, pb, c, t0, n):
            P.op("act", "activation", YB[:, c, t0:t0 + n], ps[:, 0:n], AF.Silu, R=[pb], W=["YB3_%d_%d" % (c, i) for i in range(t0 // 128, (t0 + n) // 128)])
        self.proj_fm(WU, WUb, 0, 2, ntl, u_consume)
        self.proj_fm(WZ, WZb, 0, 2, ntl, z_consume)
        for i in range(ntl):
            ps, pb = self.proj_tm(WV, WVb, 0, 256, i)
            P.op("act", "activation", GV[:, i, :], ps[:, 0:256], AF.Gelu, R=[pb], W=[GVb + "_%d" % i])
        t1, t1b = self.ar("sg1", [128, 512], F32)
        t2, t2b = self.ar("sg2", [128, 512], F32)
        for i0 in range(0, ntl, 4):
            nt = min(4, ntl - i0)
            n = nt * 128
            for c in range(2):
                ps, pb = self.ps()
                for j in range(nt):
                    i = i0 + j
                    for gg in range(2):
                        g = 2 * c + gg
                        P.op("pe", "matmul", ps[gg * 64:(gg + 1) * 64, j * 128:(j + 1) * 128], GV[:, i, g * 64:(g + 1) * 64], wsp[:, g, :],
                            start=True, stop=True, tile_position=(0, gg * 64), R=[GVb + "_%d" % i, wspb], W=[pb])
                P.op("dve", "tensor_tensor", t1[:, 0:nt * 128].rearrange("p (a b) -> p a b", a=nt), ps[:, 0:nt * 128].rearrange("p (a b) -> p a b", a=nt),
                    bsp[:, c, :].unsqueeze(1).broadcast_to([128, nt, 128]), ALU.add, R=[pb, bspb, t1b], W=[t1b])
                P.op("pool", "tensor_tensor", t2[:, 0:n], t1[:, 0:n], GU[:, c, i0 * 128:i0 * 128 + n], ALU.mult, R=[t1b, GUb, t2b], W=[t2b])
                ybb = ["YB3_%d_%d" % (c, i) for i in range(i0, i0 + nt)]
                P.op("dve", "tensor_tensor", YB[:, c, i0 * 128:i0 * 128 + n], t2[:, 0:n], YB[:, c, i0 * 128:i0 * 128 + n], ALU.mult, R=[t2b] + ybb, W=ybb)

    def merge(self, l, ntl, is_last, x_tile_src):
        P, W = self.P, self.W
        groups = [g for g in GROUPS if g[0] // 128 < ntl]
        MT, MTb = self.ar("MT", [128, KC, TALL], BF)
        WBR = [self.ar("WBR%d" % r, [128, 2, D], BF) for r in range(4)]
        for r in range(4):
            self.load_w(W["w_br"][l, r].rearrange("(kc p) n -> p kc n", p=128), WBR[r][0], WBR[r][1], (2, D))
        bm, bmb = self.ar("bm", [128, 32], F32)
        P.dma("sp", bm, W["b_mergeT"][l], writes=[bmb])
        WM = [self.ar("WM%d" % j, [128, 4, KC, 128], BF) for j in range(2)]
        sg = [self.ar("sg%d" % j, [128, 512], F32) for j in range(2)]
        acc = [self.ar("acc%d" % j, [128, 512], F32) for j in range(2)]
        tmpm = [self.ar("tmpm%d" % j, [128, 512], F32) for j in range(2)]
        cnt = 0
        for dc in range(KC):
            wm, wmb = WM[dc % 2]
            for r in range(4):
                i = self.wsf_i % 2
                self.wsf_i += 1
                st = self.WSF[i][:, 0:1024].rearrange("p (a b) -> p a b", a=KC)
                P.dma("sp", st, self.w_cols("w_merge", l, r * D + dc * 128, 128), writes=["wsf%d" % i])
                P.op("dve", "tensor_copy", wm[:, r, :, :], st, R=["wsf%d" % i], W=[wmb + "_%d" % r])
            for (t0, n) in groups:
                htb = ["HT%d" % i for i in range(t0 // 128, (t0 + n) // 128)]
                a_, ab = acc[cnt % 2]
                for r in range(4):
                    psg, pgb = self.ps()
                    for kc in range(KC):
                        P.op("pe", "matmul", psg[:, 0:n], wm[:, r, kc, :], self.HT[:, kc, t0:t0 + n], start=(kc == 0), stop=(kc == KC - 1), R=[wmb + "_%d" % r] + htb, W=[pgb])
                    psy, pyb = self.ps()
                    ybb = ["YB%d_%d_%d" % (r, c, i) for c in range(2) for i in range(t0 // 128, (t0 + n) // 128)]
                    for k2 in range(2):
                        P.op("pe", "matmul", psy[:, 0:n], WBR[r][0][:, k2, dc * 128:(dc + 1) * 128], self.YB[r][:, k2, t0:t0 + n],
                            start=(k2 == 0), stop=(k2 == 1), R=[WBR[r][1]] + ybb, W=[pyb])
                    s_, sb_ = sg[(cnt * 4 + r) % 2]
                    P.op("act", "activation", s_[:, 0:n], psg[:, 0:n], AF.Sigmoid,
                                                                         bias=bm[:, r * 8 + dc:r * 8 + dc + 1], scale=1.0, R=[pgb, bmb], W=[sb_])
                    if r == 0:
                        P.op("dve", "tensor_tensor", a_[:, 0:n], psy[:, 0:n], s_[:, 0:n], ALU.mult, R=[pyb, sb_, ab], W=[ab])
                    else:
                        t_, tb_ = tmpm[r % 2]
                        P.op("dve", "tensor_tensor", t_[:, 0:n], psy[:, 0:n], s_[:, 0:n], ALU.mult, R=[pyb, sb_, tb_], W=[tb_])
                        if r < 3:
                            P.op("pool", "tensor_tensor", a_[:, 0:n], a_[:, 0:n], t_[:, 0:n], ALU.add, R=[ab, tb_], W=[ab])
                        else:
                            P.op("pool", "tensor_tensor", MT[:, dc, t0:t0 + n], a_[:, 0:n], t_[:, 0:n], ALU.add, R=[ab, tb_], W=[MTb + "_%d_%d" % (dc, t0)])
                cnt += 1
        self.tap("L%dMT" % l, MT, [128, KC, TALL], [MTb + "_%d_%d" % (dc, g[0]) for dc in range(KC) for g in groups], BF)
        if self.stop == "C2":
            return
        self.P.barrier()
        self.ar_off = (KC * TALL * 2 + 63) // 64 * 64
        self.ar_gen += 1
        WO, WOb = self.ar("WO", [128, KC, D], BF)
        for j in range(4):
            self.load_w(self.w_cols("w_out", l, j * 256, 256), WO[:, :, j * 256:(j + 1) * 256], WOb + "_%d" % j, (KC, 256))
        gt_bc, gtb = self.ar("gt_bc", [128, 2, D], F32)
        for r in range(2 if ntl > NT else 1):
            for half in range(2):
                ps, pb = self.ps()
                P.op("pe", "matmul", ps[:, :], self.sel2[0:2, r * 128:(r + 1) * 128], self.gate2[0:2, half * 512:(half + 1) * 512],
                    start=True, stop=True, R=["sel2", "gate2"], W=[pb])
                P.op("act", "copy", gt_bc[:, r, half * 512:(half + 1) * 512], ps[:, :], R=[pb], W=[gtb])
        XT = [self.ar("XT%d" % i, [128, D], F32) for i in range(2)]
        XO = [self.ar("XO%d" % i, [128, D], F32) for i in range(2)]
        for i in range(ntl):
            r = 0 if i < NT else 1
            xt, xtb = XT[i % 2]
            xo, xob = XO[i % 2]
            P.dma("sp", xt, x_tile_src(i), writes=[xtb])
            for half in range(2):
                ps, pb = self.ps()
                mtb = [MTb + "_%d_%d" % (dc, g[0]) for dc in range(KC) for g in GROUPS if g[0] <= i * 128 < g[0] + g[1]]
                for kc in range(KC):
                    P.op("pe", "matmul", ps[:, :], MT[:, kc, i * 128:(i + 1) * 128], WO[:, kc, half * 512:(half + 1) * 512],
                        start=(kc == 0), stop=(kc == KC - 1), R=mtb + [WOb + "_%d" % (2 * half), WOb + "_%d" % (2 * half + 1)], W=[pb])
                P.op("dve", "tensor_tensor", xo[:, half * 512:(half + 1) * 512], ps[:, :], gt_bc[:, r, half * 512:(half + 1) * 512], ALU.mult, R=[pb, gtb, xob], W=[xob + "h%d" % half])
                P.op("pool", "tensor_tensor", xo[:, half * 512:(half + 1) * 512], xo[:, half * 512:(half + 1) * 512], xt[:, half * 512:(half + 1) * 512], ALU.add, R=[xob + "h%d" % half, xtb], W=[xob + "h%d" % half])
            if is_last:
                dst = self.out[i * 128:(i + 1) * 128, :]
            elif self.last or l != self.layers[-1]:
                dst = self.xs[i * 128:(i + 1) * 128, :]
            else:
                dst = self.xs_out[i * 128:(i + 1) * 128, :]
            P.dma("sp", dst, xo, reads=[xob + "h0", xob + "h1"], writes=["xs%d" % i])


def is_last_layer(l):
    return l == 1


_CONST_CACHE = {}


def _consts(core):
    if core not in _CONST_CACHE:
        _CONST_CACHE[core] = host_consts(core)
    return _CONST_CACHE[core]


def _weights_layout(inp):
    f = lambda a: np.ascontiguousarray(np.asarray(a, dtype=np.float32))
    w = {k: f(inp[k]) for k in ("norm_w", "w_ada", "b_ada", "w_in", "qn_a", "kn_a", "qn_d", "kn_d", "sink_d",
                                "w_fnet", "b_sp", "w_br", "w_merge", "w_out")}
    w["w_spT"] = np.ascontiguousarray(f(inp["w_sp"]).transpose(0, 3, 1, 2))
    w["b_mergeT"] = np.ascontiguousarray(f(inp["b_merge"]).reshape(2, 32, 128).transpose(0, 2, 1))
    return w


def _run(builder_kwargs, per_core_extra, inp):
    b = Builder(**builder_kwargs)
    nc = b.build()
    w = _weights_layout(inp)
    c = np.asarray(inp["c"], np.float32)
    c_ctx = np.asarray(inp["c_ctx"], np.float32)
    in_maps = []
    for core in range(NCORES):
        bi = core // 4
        cv = np.stack([c[bi], c_ctx], 0)
        cvT = np.ascontiguousarray(cv.reshape(2, KC, 128).transpose(2, 1, 0))
        m = {"cvecT": cvT}
        m.update(w)
        m.update(_consts(core))
        m.update(per_core_extra(core))
        in_maps.append(m)
    res = run_bass_kernel_spmd(nc, in_maps, core_ids=list(range(NCORES)))
    return res.results, b


def kernel(**inp):
    x = np.asarray(inp["x"], np.float32)
    ctx = np.asarray(inp["ctx"], np.float32)

    def extra(core):
        bi, q = divmod(core, 4)
        return {"x": np.ascontiguousarray(x[bi, q * TOK:(q + 1) * TOK]), "ctx": np.ascontiguousarray(ctx[bi])}
    results, _ = _run(dict(layers=(0, 1), first=True, last=True), extra, inp)
    out = np.empty((2, SEQ, D), np.float32)
    for core in range(NCORES):
        bi, q = divmod(core, 4)
        out[bi, q * TOK:(q + 1) * TOK] = results[core]["out"]
    return out
```

```python
import numpy as np
import ml_dtypes
import concourse.bass as bass
import concourse.mybir as mybir
from concourse.bass_utils import run_bass_kernel_spmd

F32 = mybir.dt.float32
BF = mybir.dt.bfloat16
AF = mybir.ActivationFunctionType
ALU = mybir.AluOpType
AX = mybir.AxisListType
NPBF = ml_dtypes.bfloat16

NCORES = 8
TOK = 2048
NT = 16
CT = 2
TT = NT + CT
TALL = TT * 128
D = 1024
KC = 8
EPS = 1e-6
SEQ = 8192
INW = 2816
import os as _os
WIDTHS = [int(c) for c in _os.environ.get('KW', '333')]
GROUPS = [(0, 512), (512, 512), (1024, 512), (1536, 512), (2048, 256)]


class Buf:
    __slots__ = ("name", "w", "rs")

    def __init__(self, name):
        self.name = name
        self.w = None
        self.rs = []


class Op:
    __slots__ = ("eng", "fn", "deps", "kind", "sem", "semval", "signal")

    def __init__(self, eng, fn, kind):
        self.eng = eng
        self.fn = fn
        self.kind = kind
        self.deps = []
        self.sem = None
        self.semval = 0
        self.signal = False


class Prog:
    ENGS = ("pe", "act", "dve", "pool", "sp")
    NS = 24

    def __init__(self, nc):
        self.nc = nc
        self.ops = {e: [] for e in self.ENGS}
        self.dma_count = 0
        self.dma_last = [None] * self.NS
        self.ncoll = 0
        self.bufs = {}
        self.pending = {e: [] for e in self.ENGS}
        self.last_c = {e: None for e in self.ENGS}
        self.colls = []

    def buf(self, name):
        b = self.bufs.get(name)
        if b is None:
            b = self.bufs[name] = Buf(name)
        return b

    def _add(self, op, d):
        if d is op:
            return
        op.deps.append(d)
        if d.kind == "c":
            d.signal = True

    def _deps(self, op, reads, writes):
        deps = []
        for b in reads:
            b = self.buf(b)
            if b.w is not None:
                deps.append(("raw", b.w))
            if op.kind == "c":
                b.rs = [r for r in b.rs if not (r.kind == "c" and r.eng == op.eng)]
            b.rs.append(op)
        for b in writes:
            b = self.buf(b)
            if b.w is not None:
                deps.append(("waw", b.w))
            for r in b.rs:
                if r is not op:
                    deps.append(("war", r))
            b.rs = []
            b.w = op
        for kind, d in deps:
            if d is op:
                continue
            if d.kind == "c" and op.kind == "c" and d.eng == op.eng:
                if op.eng == "pe":
                    continue
            self._add(op, d)
        if self.pending[op.eng]:
            for d in self.pending[op.eng]:
                self._add(op, d)
            self.pending[op.eng] = []

    def op(self, eng, meth, *args, R=(), W=(), **kw):
        fn = (lambda e, meth=meth, args=args, kw=kw: getattr(e, meth)(*args, **kw))
        o = Op(eng, fn, "c")
        self._deps(o, R, W)
        self.ops[eng].append(o)
        self.last_c[eng] = o
        return o

    def dma(self, q, out, in_, reads=(), writes=(), **kw):
        o = Op(q, (lambda e, out=out, in_=in_, kw=kw: e.dma_start(out=out, in_=in_, **kw)), "d")
        self._deps(o, reads, writes)
        slot = self.dma_count % self.NS
        self.dma_count += 1
        prev = self.dma_last[slot]
        o.sem = slot
        o.semval = (prev.semval if prev else 0) + 16
        if prev is not None:
            o.deps.append(prev)
        self.dma_last[slot] = o
        self.ops[q].append(o)
        return o

    def collective(self, fn, reads=(), writes=()):
        o = Op("pool", fn, "x")
        self._deps(o, reads, writes)
        o.sem = self.ncoll
        self.ncoll += 1
        self.ops["pool"].append(o)
        self.colls.append(o)
        return o

    def barrier(self):
        lasts = [o for o in self.last_c.values() if o is not None]
        lasts += [o for o in self.dma_last if o is not None]
        for e in self.ENGS:
            self.pending[e] = list(lasts)

    def emit(self):
        nc = self.nc
        for e in self.ENGS:
            cnt = 0
            for o in self.ops[e]:
                if o.kind == "c" and o.signal:
                    cnt += 1
                    o.semval = cnt
        esem = {e: nc.alloc_semaphore("es_" + e) for e in self.ENGS}
        dsem = [nc.alloc_semaphore("ds_%d" % i) for i in range(self.NS)]
        csem = [nc.alloc_semaphore("cs_%d" % i) for i in range(self.ncoll)]

        def semof(d):
            if d.kind == "c":
                return esem[d.eng], d.semval
            if d.kind == "d":
                return dsem[d.sem], d.semval
            return csem[d.sem], 1

        def gen(ename):
            def body(eng):
                waited = {}
                for o in self.ops[ename]:
                    need = {}
                    for d in o.deps:
                        s, v = semof(d)
                        if v > need.get(id(s), (None, 0))[1]:
                            need[id(s)] = (s, v)
                    for s, v in need.values():
                        if waited.get(id(s), 0) >= v:
                            continue
                        waited[id(s)] = v
                        eng.wait_ge(s, v)
                    ins = o.fn(eng)
                    if o.kind == "c":
                        if o.signal:
                            ins.then_inc(esem[ename], 1)
                    elif o.kind == "d":
                        ins.then_inc(dsem[o.sem], 16)
                    else:
                        ins.then_inc(csem[o.sem])
                last = {}
                for o in self.ops[ename]:
                    if o.kind in ("d", "x"):
                        s, v = semof(o)
                        if v > last.get(id(s), (None, 0))[1]:
                            last[id(s)] = (s, v)
                for s, v in last.values():
                    if waited.get(id(s), 0) < v:
                        eng.wait_ge(s, v)
            return body

        with nc.Block() as block:
            block.sync(gen("sp"))
            block.scalar(gen("act"))
            block.vector(gen("dve"))
            block.tensor(gen("pe"))
            block.gpsimd(gen("pool"))


def host_consts(core):
    b, q = divmod(core, 4)
    t = (q * TOK + np.arange(TOK)).astype(np.int64)
    r = (t // 64).astype(np.float32)
    col = (t % 64).astype(np.float32)
    inv = (np.float32(10000.0) ** (-np.arange(16, dtype=np.float32) / np.float32(16))).astype(np.float32)
    ar = r[:, None] * inv[None, :]
    ac = col[:, None] * inv[None, :]
    cr, sr, cc, sc = np.cos(ar), np.sin(ar), np.cos(ac), np.sin(ac)
    cos64 = np.concatenate([cr, cr, cc, cc], 1).astype(np.float32)
    sin64 = np.concatenate([-sr, sr, -sc, sc], 1).astype(np.float32)
    cos64 = np.ascontiguousarray(cos64.reshape(NT, 128, 64).transpose(1, 0, 2))
    sin64 = np.ascontiguousarray(sin64.reshape(NT, 128, 64).transpose(1, 0, 2))
    n1 = np.arange(64)
    th = 2 * np.pi * np.outer(n1, n1) / 64.0
    C, S = np.cos(th), np.sin(th)
    sc1 = 1.0 / np.sqrt(SEQ * 64.0)
    m1 = np.zeros((128, 128), np.float64)
    m1[:64, :64] = C
    m1[64:, :64] = -S
    m1[:64, 64:] = S
    m1[64:, 64:] = C
    m1 *= sc1
    n2 = np.arange(128)[:, None, None]
    k1 = np.arange(64)[None, :, None]
    k2 = (np.arange(32) + 32 * q)[None, None, :]
    ph = 2 * np.pi * n2 * (64 * k2 + k1) / float(SEQ)
    mc = np.cos(ph).reshape(128, 64 * 32)
    ms = (-np.sin(ph)).reshape(128, 64 * 32)
    n = np.arange(256)
    t256 = 2 * np.pi * np.outer(n, n) / 256.0
    sc2 = 1.0 / np.sqrt(256 * 64.0)
    c256 = (np.cos(t256) * sc2).reshape(2, 128, 256).transpose(1, 0, 2)
    s256n = (-np.sin(t256) * sc2).reshape(2, 128, 256).transpose(1, 0, 2)
    c64d = np.concatenate([C, C], 1).astype(np.float32)
    s64d = np.concatenate([S, S], 1).astype(np.float32)
    pk = np.arange(128)[:, None]
    pq = np.arange(128)[None, :]
    mP = (pk >= pq).astype(np.float32)
    mN = (pk <= pq).astype(np.float32)
    masks = np.zeros((128, 10, 128), np.float32)
    masks[:, 0] = mP
    masks[:, 1] = mN
    for s in range(4):
        if s == q - 1:
            masks[:, 2 + s] = mP
        if s == q + 1:
            masks[:, 6 + s] = mN
    sel2 = np.zeros((2, 256), np.float32)
    sel2[0, :128] = 1.0
    sel2[1, 128:] = 1.0
    return {
        "cos64": cos64, "sin64": sin64,
        "m1": m1.astype(np.float32), "mc": mc.astype(np.float32), "ms": ms.astype(np.float32),
        "c256": np.ascontiguousarray(c256).astype(np.float32), "s256n": np.ascontiguousarray(s256n).astype(np.float32),
        "c64d": c64d, "s64d": s64d,
        "masks": masks.astype(NPBF), "sel2": sel2,
        "ident": np.eye(128, dtype=np.float32),
    }


CONST_SPECS = {
    "cos64": ([128, NT, 64], F32), "sin64": ([128, NT, 64], F32),
    "m1": ([128, 128], F32), "mc": ([128, 2048], F32), "ms": ([128, 2048], F32),
    "c256": ([128, 2, 256], F32), "s256n": ([128, 2, 256], F32),
    "c64d": ([64, 128], F32), "s64d": ([64, 128], F32),
    "masks": ([128, 10, 128], BF), "sel2": ([2, 256], F32), "ident": ([128, 128], F32),
}

WEIGHT_SPECS = {
    "norm_w": [2, D], "w_ada": [2, D, 3 * D], "b_ada": [2, 3 * D], "w_in": [2, D, INW],
    "qn_a": [2, 64], "kn_a": [2, 64], "qn_d": [2, 64], "kn_d": [2, 64], "sink_d": [2, 4],
    "w_fnet": [2, 4, 64, 64], "w_spT": [2, 128, 4, 128], "b_sp": [2, 4, 128],
    "w_br": [2, 4, 256, D], "w_merge": [2, D, 4 * D], "b_mergeT": [2, 128, 32], "w_out": [2, D, D],
}


class Builder:
    def __init__(self, layers=(0, 1), first=True, last=True, debug=(), stop=None):
        self.stop = stop
        self.layers = tuple(layers)
        self.first = first
        self.last = last
        self.debug = set(debug)
        nc = self.nc = bass.Bass("TRN2", target_bir_lowering=False)
        self.P = Prog(nc)
        self.dbg_outs = {}
        self._uid = 0

    def uid(self, p="t"):
        self._uid += 1
        return "%s%d" % (p, self._uid)

    def dram_in(self, name, shape, dt=F32):
        return self.nc.dram_tensor(name, list(shape), dt, kind="ExternalInput").ap()

    def dram_out(self, name, shape, dt=F32):
        return self.nc.dram_tensor(name, list(shape), dt, kind="ExternalOutput").ap()

    def sb(self, name, shape, dt):
        return self.nc.alloc_sbuf_tensor("s_" + name, list(shape), dt)

    def tap(self, name, view, shape, reads, dt=F32):
        if name not in self.debug:
            return
        o = self.dram_out("dbg_" + name, shape, dt)
        self.dbg_outs["dbg_" + name] = (shape, dt)
        self.P.dma("sp", o, view, reads=reads)

    def build(self):
        nc, P = self.nc, self.P
        if self.first:
            self.x_in = self.dram_in("x", [TOK, D])
            self.ctx_in = self.dram_in("ctx", [256, D])
        else:
            self.xs_in = self.dram_in("xs_in", [TALL, D])
        self.cvecT = self.dram_in("cvecT", [128, KC, 2])
        self.W = {k: self.dram_in(k, s) for k, s in WEIGHT_SPECS.items()}
        self.C = {k: self.dram_in(k, s, dt) for k, (s, dt) in CONST_SPECS.items()}
        if self.last:
            self.out = self.dram_out("out", [TOK, D])
        else:
            self.xs_out = self.dram_out("xs_out", [TALL, D])
        self.xs = nc.dram_tensor("xs", [TALL, D], F32).ap()
        self.payload = [[nc.dram_tensor("payload%d_%d" % (l, j), [256, TOK], BF) for j in range(4)] for l in range(2)]
        self.gathered = [[nc.dram_tensor("gathered%d_%d" % (l, j), [1024, TOK], BF) for j in range(4)] for l in range(2)]

        self.HT = self.sb("HT", [128, KC, TALL], BF)
        self.YB = [self.sb("YB%d" % r, [128, 2, TALL], BF) for r in range(4)]
        self.WSF = [self.sb("WSF%d" % i, [128, 2048], F32) for i in range(2)]
        self.wsf_i = 0
        self.cos64 = self.sb("cos64", [128, NT, 64], F32)
        self.sin64 = self.sb("sin64", [128, NT, 64], F32)
        self.identf = self.sb("identf", [128, 128], F32)
        self.ident = self.sb("ident", [128, 128], BF)
        self.m1 = self.sb("m1", [128, 128], BF)
        self.c256 = self.sb("c256", [128, 2, 256], BF)
        self.s256n = self.sb("s256n", [128, 2, 256], BF)
        self.c64d = self.sb("c64d", [64, 128], F32)
        self.s64d = self.sb("s64d", [64, 128], F32)
        self.masks = self.sb("masks", [128, 10, 128], BF)
        self.sel2 = self.sb("sel2", [2, 256], F32)
        self.ones64 = self.sb("ones64", [128, 64], BF)
        self.scT = self.sb("scT", [128, KC, 2], F32)
        self.gate2 = self.sb("gate2", [2, D], F32)
        self.knw = [self.sb("knw%d" % i, [128, 128], F32) for i in range(2)]
        self.qnw = [self.sb("qnw%d" % i, [128, 256], F32) for i in range(2)]
        self.esink = self.sb("esink", [128, 4], F32)
        self.KTc = [self.sb("KTc%d" % i, [128, 256], BF) for i in range(2)]
        self.Vc = [self.sb("Vc%d" % i, [128, 2, 128], BF) for i in range(2)]
        self.UVc = self.sb("UVc", [128, 2, 2, 256], BF)
        self.BA = self.sb("BA", [128, 2, 2, 128], BF)
        self.stat = self.sb("stat", [128, 64], F32)
        self.stat_i = 0
        self.PSW = [nc.alloc_psum_tensor("psw%d" % i, [128, 1024], F32) for i in range(2)]
        self.PS = [self.PSW[i // 2][:, (i % 2) * 512:(i % 2 + 1) * 512] for i in range(4)]
        self.PS += [nc.alloc_psum_tensor("ps%d" % i, [128, 512], F32) for i in (4, 5)]
        self.PST = [nc.alloc_psum_tensor("pst%d" % i, [128, 1024], BF) for i in range(2)]
        self.PS4BF = self.PS[4][:, :].bitcast(BF)
        self.ps_i = 0
        self.pst_i = 0
        self.ARENA_BYTES = 88 * 1024
        self.arena = self.sb("arena", [128, self.ARENA_BYTES // 2], BF)
        self.ar_off = 0
        self.ar_gen = 0

        for k in ("cos64", "sin64", "c64d", "s64d", "masks", "sel2"):
            P.dma("sp", getattr(self, k)[:], self.C[k], writes=[k])
        for k, shp in (("m1", [128, 128]), ("c256", [128, 2, 256]), ("s256n", [128, 2, 256])):
            tf_, tfb_ = self.ar(k + "_f32", shp, F32)
            P.dma("sp", tf_, self.C[k], writes=[tfb_])
            P.op("dve", "tensor_copy", getattr(self, k)[:], tf_, R=[tfb_], W=[k])
        P.dma("sp", self.identf[:], self.C["ident"], writes=["identf"])
        P.op("dve", "tensor_copy", self.ident[:], self.identf[:], R=["identf"], W=["ident"])
        P.op("pool", "memset", self.ones64[:], 1.0, R=[], W=["ones64"])
        P.dma("sp", self.scT[:], self.cvecT, writes=["scT"])
        P.op("act", "activation", self.scT[:], self.scT[:], AF.Silu, R=["scT"], W=["scT"])

        for li, l in enumerate(self.layers):
            is_first = self.first and li == 0
            is_last = self.last and li == len(self.layers) - 1
            self.layer(l, is_first, is_last, x_from_input=is_first,
                       x_src_xs_in=(not self.first and li == 0))
        if self.stop is not None:
            dst = self.out if self.last else self.xs_out
            P.dma("sp", dst[0:128, :], self.identf[:, :].unsqueeze(1).broadcast_to([128, 8, 128]) if False else self.cos64[:, :, :], reads=["cos64"])
        P.emit()
        return nc

    def phase(self):
        self.P.barrier()
        self.ar_off = 0
        self.ar_gen += 1

    def ar(self, name, shape, dt):
        n = int(np.prod(shape[1:]))
        nbytes = n * (4 if dt == F32 else 2)
        nbytes = (nbytes + 63) // 64 * 64
        assert self.ar_off + nbytes <= self.ARENA_BYTES, (name, self.ar_off, nbytes)
        e0 = self.ar_off // 2
        v = self.arena[:, e0:e0 + nbytes // 2]
        if dt == F32:
            v = v.bitcast(F32)
        v = v[:, 0:n]
        self.ar_off += nbytes
        if shape[0] != 128:
            v = v[0:shape[0]]
        if len(shape) == 3:
            v = v.rearrange("p (a b) -> p a b", a=shape[1])
        elif len(shape) == 4:
            v = v.rearrange("p (a b c) -> p a b c", a=shape[1], b=shape[2])
        return v, "%s_g%d" % (name, self.ar_gen)

    def ps(self):
        i = self.ps_i % 4
        self.ps_i += 1
        return self.PS[i], "ps%d" % i

    def pst(self):
        i = self.pst_i % 3
        self.pst_i += 1
        if i == 2:
            return self.PS4BF, "ps4"
        return self.PST[i], "pst%d" % i

    def statcol(self, n=1):
        c = (self.stat_i % 16) * 4
        self.stat_i += 1
        return self.stat[:, c:c + n], "stat%d" % c

    def load_w(self, src, dst, dstbuf, shape3, q="sp"):
        P = self.P
        i = self.wsf_i % 2
        self.wsf_i += 1
        a, b = shape3
        st = self.WSF[i][:, 0:a * b].rearrange("p (a b) -> p a b", a=a)
        P.dma(q, st, src, writes=["wsf%d" % i])
        P.op("dve", "tensor_copy", dst, st, R=["wsf%d" % i], W=[dstbuf])

    def w_cols(self, name, l, c0, n):
        return self.W[name][l, :, c0:c0 + n].rearrange("(kc p) n -> p kc n", p=128)

    @staticmethod
    def interleave(gens, width):
        gens = iter(gens)
        active = []
        while True:
            while len(active) < width:
                g = next(gens, None)
                if g is None:
                    break
                active.append(g)
            if not active:
                break
            for g in list(active):
                try:
                    next(g)
                except StopIteration:
                    active.remove(g)

    def proj_fm(self, wb, wbuf, c0, nchunks, tiles, consume):
        P = self.P
        for c in range(nchunks):
            for (t0, n) in GROUPS:
                if t0 // 128 >= tiles:
                    continue
                ps, pb = self.ps()
                htb = ["HT%d" % i for i in range(t0 // 128, (t0 + n) // 128)]
                for kc in range(KC):
                    P.op("pe", "matmul", ps[:, 0:n], wb[:, kc, c0 + c * 128:c0 + (c + 1) * 128], self.HT[:, kc, t0:t0 + n],
                        start=(kc == 0), stop=(kc == KC - 1), R=[wbuf] + htb, W=[pb])
                consume(ps, pb, c, t0, n)

    def proj_tm(self, wb, wbuf, c0, ncols, tile):
        P = self.P
        ps, pb = self.ps()
        for kc in range(KC):
            P.op("pe", "matmul", ps[:, 0:ncols], self.HT[:, kc, tile * 128:(tile + 1) * 128], wb[:, kc, c0:c0 + ncols],
                start=(kc == 0), stop=(kc == KC - 1), R=[wbuf, "HT%d" % tile], W=[pb])
        return ps, pb

    def headnorm_rope(self, ps, pb, c0, nh, wtile, wbuf, tile, tmp, out_bf, outbuf, permute):
        P = self.P
        n = nh * 64
        (sq, sqb), (t1, t1b), (t2, t2b) = tmp
        P.op("act", "activation", sq[:, 0:n], ps[:, c0:c0 + n], AF.Square, R=[pb], W=[sqb])
        yield
        ssq, ssqb = self.statcol(4)
        rt, rtb = self.statcol(4)
        rr, rrb = self.statcol(4)
        P.op("dve", "tensor_reduce", ssq[:, 0:nh], sq[:, 0:n].rearrange("p (h d) -> p h d", h=nh), AX.X, ALU.add, R=[sqb], W=[ssqb])
        yield
        P.op("act", "activation", rt[:, 0:nh], ssq[:, 0:nh], AF.Sqrt, bias=64.0 * EPS, scale=1.0, R=[ssqb], W=[rtb])
        yield
        P.op("dve", "reciprocal", rr[:, 0:nh], rt[:, 0:nh], R=[rtb], W=[rrb])
        yield
        P.op("dve", "tensor_tensor", t1[:, 0:n].rearrange("p (h d) -> p h d", h=nh), ps[:, c0:c0 + n].rearrange("p (h d) -> p h d", h=nh),
            rr[:, 0:nh].unsqueeze(2).broadcast_to([128, nh, 64]), ALU.mult, R=[pb, rrb], W=[t1b])
        yield
        if permute:
            ov = out_bf[:, 0:n].rearrange("p (g kv d) -> p kv g d", g=2, kv=2)
        if tile >= NT:
            if permute:
                P.op("pool", "tensor_tensor", ov, t1[:, 0:n].rearrange("p (kv g d) -> p kv g d", kv=2, g=2),
                    wtile[:, 0:n].rearrange("p (kv g d) -> p kv g d", kv=2, g=2), ALU.mult, R=[t1b, wbuf], W=[outbuf])
                yield
            else:
                P.op("pool", "tensor_tensor", out_bf[:, 0:n], t1[:, 0:n], wtile[:, 0:n], ALU.mult, R=[t1b, wbuf], W=[outbuf])
                yield
            return
        P.op("pool", "tensor_tensor", sq[:, 0:n], t1[:, 0:n], wtile[:, 0:n], ALU.mult, R=[t1b, wbuf, sqb], W=[sqb])
        yield
        cosb = self.cos64[:, tile, :].unsqueeze(1).broadcast_to([128, nh, 64])
        sinv = self.sin64[:, tile, :].rearrange("p (t j) -> p t j", t=4)
        xv = sq[:, 0:n].rearrange("p (h t j) -> p h t j", h=nh, t=4)
        t2v = t2[:, 0:n].rearrange("p (h t j) -> p h t j", h=nh, t=4)
        P.op("dve", "tensor_tensor", t1[:, 0:n].rearrange("p (h d) -> p h d", h=nh),
                                              sq[:, 0:n].rearrange("p (h d) -> p h d", h=nh), cosb, ALU.mult, R=[sqb, "cos64", t1b], W=[t1b])
        yield
        P.op("pool", "tensor_tensor", t2v[:, :, 0::2, :], xv[:, :, 1::2, :],
                                               sinv[:, 0::2, :].unsqueeze(1).broadcast_to([128, nh, 2, 16]), ALU.mult, R=[sqb, "sin64"], W=[t2b + "a"])
        yield
        P.op("pool", "tensor_tensor", t2v[:, :, 1::2, :], xv[:, :, 0::2, :],
                                               sinv[:, 1::2, :].unsqueeze(1).broadcast_to([128, nh, 2, 16]), ALU.mult, R=[sqb, "sin64"], W=[t2b + "b"])
        yield
        if permute:
            P.op("dve", "tensor_tensor", ov, t1[:, 0:n].rearrange("p (kv g d) -> p kv g d", kv=2, g=2),
                t2[:, 0:n].rearrange("p (kv g d) -> p kv g d", kv=2, g=2), ALU.add, R=[t1b, t2b + "a", t2b + "b"], W=[outbuf])
            yield
        else:
            P.op("dve", "tensor_tensor", out_bf[:, 0:n], t1[:, 0:n], t2[:, 0:n], ALU.add, R=[t1b, t2b + "a", t2b + "b"], W=[outbuf])
            yield

    def transpose_to(self, src_bf, srcbuf, nblk, dst_fn, dstbuf_fn, eng="act"):
        P = self.P
        pt, ptb = self.pst()
        for j in range(nblk):
            P.op("pe", "transpose", pt[:, j * 128:(j + 1) * 128], src_bf[:, j * 128:(j + 1) * 128], self.ident[:], R=[srcbuf, "ident"], W=[ptb])
        yield
        for j in range(nblk):
            if eng == "act":
                P.op("act", "copy", dst_fn(j), pt[:, j * 128:(j + 1) * 128], R=[ptb], W=[dstbuf_fn(j)])
            else:
                P.op("dve", "tensor_copy", dst_fn(j), pt[:, j * 128:(j + 1) * 128], R=[ptb], W=[dstbuf_fn(j)])
        yield

    def layer(self, l, is_first, is_last, x_from_input, x_src_xs_in):
        nc, P, W = self.nc, self.P, self.W
        do_ctx_out = not is_last_layer(l)
        ntl = TT if do_ctx_out else NT
        pay, gat = self.payload[l], self.gathered[l]

        def x_tile_src(i):
            if x_from_input:
                return self.x_in[i * 128:(i + 1) * 128, :] if i < NT else self.ctx_in[(i - NT) * 128:(i - NT + 1) * 128, :]
            if x_src_xs_in:
                return self.xs_in[i * 128:(i + 1) * 128, :]
            return self.xs[i * 128:(i + 1) * 128, :]

        self.phase()
        L = "L%d" % l
        mod, modb = self.ar("mod", [2, 3 * D], F32)
        arow, arowb = self.ar("arow", [2, D], F32)
        normw2, normw2b = self.ar("normw2", [2, D], F32)
        bada2, bada2b = self.ar("bada2", [2, 3 * D], F32)
        P.dma("sp", normw2, W["norm_w"][l:l + 1, :].partition_broadcast(2), writes=[normw2b])
        P.dma("sp", bada2, W["b_ada"][l:l + 1, :].partition_broadcast(2), writes=[bada2b])
        for i, (kn, qn) in enumerate((("kn_a", "qn_a"), ("kn_d", "qn_d"))):
            for h in range(2):
                P.dma("sp", self.knw[i][:, h * 64:(h + 1) * 64], W[kn][l:l + 1, :].partition_broadcast(128), writes=["knw%d" % i])
            for h in range(4):
                P.dma("sp", self.qnw[i][:, h * 64:(h + 1) * 64], W[qn][l:l + 1, :].partition_broadcast(128), writes=["qnw%d" % i])
            P.op("act", "mul", self.knw[i][:], self.knw[i][:], 8.0, R=["knw%d" % i], W=["knw%d" % i])
        P.dma("sp", self.esink[:], W["sink_d"][l:l + 1, :].partition_broadcast(128), writes=["esink"])
        P.op("act", "activation", self.esink[:], self.esink[:], AF.Exp, R=["esink"], W=["esink"])

        for blk in range(6):
            ps, pb = self.ps()
            for half in range(2):
                i = self.wsf_i % 2
                self.wsf_i += 1
                st = self.WSF[i][:].rearrange("p (a b) -> p a b", a=KC)
                c0 = blk * 512 + half * 256
                P.dma("sp", st, self.w_cols("w_ada", l, c0, 256), writes=["wsf%d" % i])
                for kc in range(KC):
                    P.op("pe", "matmul", ps[0:2, half * 256:(half + 1) * 256], self.scT[:, kc, :], st[:, kc, :],
                        start=(kc == 0), stop=(kc == KC - 1), R=["scT", "wsf%d" % i], W=[pb])
            P.op("dve", "tensor_tensor", mod[0:2, blk * 512:(blk + 1) * 512], ps[0:2, :], bada2[0:2, blk * 512:(blk + 1) * 512], ALU.add, R=[pb, bada2b], W=[modb])
        self.tap(L + "mod", mod, [2, 3 * D], [modb])
        P.op("dve", "scalar_tensor_tensor", arow, mod[0:2, D:2 * D], 1.0, normw2, ALU.add, ALU.mult, R=[modb, normw2b], W=[arowb])
        P.op("act", "copy", self.gate2[:], mod[0:2, 2 * D:3 * D], R=[modb], W=["gate2"])
        a_bc, a_bcb = self.ar("a_bc", [128, 2, D], F32)
        sh_bc, sh_bcb = self.ar("sh_bc", [128, 2, D], F32)
        for r in range(2):
            for half in range(2):
                for which, (dst, dbuf, src, sbuf) in enumerate(((a_bc, a_bcb, arow, arowb), (sh_bc, sh_bcb, mod, modb))):
                    ps, pb = self.ps()
                    P.op("pe", "matmul", ps[:, :], self.sel2[0:2, r * 128:(r + 1) * 128], src[0:2, half * 512:(half + 1) * 512],
                        start=True, stop=True, R=["sel2", sbuf], W=[pb])
                    P.op("act", "copy", dst[:, r, half * 512:(half + 1) * 512], ps[:, :], R=[pb], W=[dbuf])

        XT = [self.ar("XT%d" % i, [128, D], F32) for i in range(3)]
        TMPF = [self.ar("TMPF%d" % i, [128, D], F32) for i in range(3)]
        HB = [self.ar("HB%d" % i, [128, D], BF) for i in range(3)]
        def a1_chain(i):
            s = i % 3
            r = 0 if i < NT else 1
            (xt, xtb), (tf, tfb), (hb, hbb) = XT[s], TMPF[s], HB[s]
            P.dma("sp", xt, x_tile_src(i), writes=[xtb])
            ssq, ssqb = self.statcol()
            rt, rtb = self.statcol()
            rs, rsb = self.statcol()
            P.op("act", "activation", tf, xt, AF.Square, accum_out=ssq, R=[xtb], W=[tfb, ssqb])
            yield
            P.op("act", "activation", rt, ssq, AF.Sqrt, bias=EPS, scale=1.0 / D, R=[ssqb], W=[rtb])
            yield
            P.op("dve", "reciprocal", rs, rt, R=[rtb], W=[rsb])
            yield
            P.op("dve", "scalar_tensor_tensor", tf, xt, rs, a_bc[:, r, :], ALU.mult, ALU.mult, R=[xtb, rsb, a_bcb, tfb], W=[tfb])
            yield
            P.op("pool", "tensor_tensor", hb, tf, sh_bc[:, r, :], ALU.add, R=[tfb, sh_bcb], W=[hbb])
            yield
            if i == 0:
                self.tap(L + "h0", hb, [128, D], [hbb], BF)
            pt, ptb = self.pst()
            for kc in range(KC):
                P.op("pe", "transpose", pt[:, kc * 128:(kc + 1) * 128], hb[:, kc * 128:(kc + 1) * 128], self.ident[:], R=[hbb, "ident"], W=[ptb])
            yield
            P.op("act", "copy", self.HT[:, :, i * 128:(i + 1) * 128], pt[:].rearrange("p (k t) -> p k t", k=KC), R=[ptb], W=["HT%d" % i])
            yield
        self.interleave((a1_chain(i) for i in range(TT)), WIDTHS[0])

        if self.stop == "A1":
            return
        self.phase()
        WB, WBb = self.ar("WBkv", [128, KC, 256], BF)
        KR = [self.ar("kr%d" % j, [128, 128], BF) for j in range(3)]
        KTs = [self.ar("KTs%d" % a, [128, TOK], BF) for a in range(2)]
        Vs = [self.ar("Vs%d" % a, [128, NT, 128], BF) for a in range(2)]
        tmps = [[self.ar("hnt%d_%d" % (k, j), [128, 256], F32) for j in range(3)] for k in range(3)]
        WBs = [(WB, WBb), self.ar("WBkv2", [128, KC, 256], BF)]
        for a, c0 in enumerate((256, 1024)):
            self.load_w(self.w_cols("w_in", l, c0, 256), WBs[a][0], WBs[a][1], (KC, 256))

        def a2_chain(a, i, slot):
            wb_, wbb_ = WBs[a]
            ps, pb = self.proj_tm(wb_, wbb_, 0, 256, i)
            yield
            kr, krb = KR[slot]
            yield from self.headnorm_rope(ps, pb, 0, 2, self.knw[a], "knw%d" % a, i, tmps[slot], kr, krb, permute=False)
            if i < NT:
                kts, ktsb = KTs[a]
                vs, vsb = Vs[a]
                P.op("act", "copy", vs[:, i, :], ps[:, 128:256], R=[pb], W=[vsb])
                yield from self.transpose_to(kr, krb, 1, lambda j: kts[:, i * 128:(i + 1) * 128], lambda j: ktsb)
            else:
                j_ = i - NT
                P.op("act", "copy", self.Vc[a][:, j_, :], ps[:, 128:256], R=[pb], W=["Vc%d" % a])
                yield from self.transpose_to(kr, krb, 1, lambda jj: self.KTc[a][:, j_ * 128:(j_ + 1) * 128], lambda jj: "KTc%d" % a)

        chains = [(a, i) for i in range(TT) for a in range(2)]
        self.interleave((a2_chain(a, i, k % 3) for k, (a, i) in enumerate(chains)), WIDTHS[1])
        for a in range(2):
            kts, ktsb = KTs[a]
            vs, vsb = Vs[a]
            P.dma("sp", pay[0].ap()[a * 128:(a + 1) * 128, :], kts, reads=[ktsb], writes=["pay_k%d" % a])
            P.dma("sp", pay[3].ap()[a * 128:(a + 1) * 128, :].rearrange("p (t d) -> p t d", d=128), vs, reads=[vsb],
                  writes=["pay_v%d" % a])
        if l == self.layers[0]:
            self.tap(L + "KTsA", KTs[0][0], [128, TOK], [KTs[0][1]], BF)
            self.tap(L + "VsA", Vs[0][0], [128, NT, 128], [Vs[0][1]], BF)
            self.tap(L + "KTcA", self.KTc[0][:], [128, 256], ["KTc0"], BF)

        self.load_w(self.w_cols("w_in", l, 1536, 256), WB, WBb, (KC, 256))
        BFT, BFTb = self.ar("BFT", [128, 2, TALL], BF)

        def bf_consume(ps, pb, c, t0, n):
            P.op("act", "copy", BFT[:, c, t0:t0 + n], ps[:, 0:n], R=[pb], W=[BFTb])
        self.proj_fm(WB, WBb, 0, 2, TT, bf_consume)
        wf, wfb = self.ar("wf", [64, 4, 64], F32)
        P.dma("sp", wf, W["w_fnet"][l].rearrange("g j c -> j g c"), writes=[wfb])
        P.op("pool", "memset", self.BA[:], 0.0, R=[], W=["BA"])
        for cs, tab in enumerate((self.c64d, self.s64d)):
            for g in range(4):
                ps, pb = self.ps()
                P.op("pe", "matmul", ps[:, 0:64], tab[0:64, :], wf[0:64, g, :], start=True, stop=True, R=[wfb, "c64d", "s64d"], W=[pb])
                rows = slice((g % 2) * 64, (g % 2) * 64 + 64)
                P.op("dve", "tensor_copy", self.BA[rows, cs, g // 2, (g % 2) * 64:(g % 2) * 64 + 64], ps[rows, 0:64], R=[pb, "BA"], W=["BA"])
        UVT, UVTb = self.ar("UVT", [128, 2, 2, TOK], BF)
        for cs in range(2):
            for c in range(2):
                for (t0, n) in GROUPS[:4]:
                    ps, pb = self.ps()
                    P.op("pe", "matmul", ps[:, 0:n], self.BA[:, cs, c, :], BFT[:, c, t0:t0 + n], start=True, stop=True, R=["BA", BFTb], W=[pb])
                    P.op("act", "copy", UVT[:, cs, c, t0:t0 + n], ps[:, 0:n], R=[pb], W=[UVTb])
                for j in range(2):
                    ps, pb = self.ps()
                    P.op("pe", "matmul", ps[:, 0:128], BFT[:, c, TOK + j * 128:TOK + (j + 1) * 128], self.BA[:, cs, c, :], start=True, stop=True, R=["BA", BFTb], W=[pb])
                    P.op("dve", "tensor_copy", self.UVc[:, cs, j, c * 128:(c + 1) * 128], ps[:, 0:128], R=[pb], W=["UVc"])
        for cs in range(2):
            P.dma("sp", pay[1 + cs].ap().rearrange("(c p) t -> p c t", p=128), UVT[:, cs, :, :],
                  reads=[UVTb], writes=["pay_uv%d" % cs])
        self.tap(L + "UVT", UVT, [128, 2, 2, TOK], [UVTb], BF)

        if self.stop == "A":
            return
        payb = [["pay_k0", "pay_k1"], ["pay_uv0"], ["pay_uv1"], ["pay_v0", "pay_v1"]]
        for j in (0, 3, 1, 2):
            P.collective(lambda e, j=j: e.collective_compute("AllGather", ALU.bypass, replica_groups=[[0, 1, 2, 3], [4, 5, 6, 7]],
                                                             ins=[pay[j].ap().opt()], outs=[gat[j].ap().opt()]),
                         reads=payb[j], writes=["gat%d" % j])
        G = [g.ap() for g in gat]

        if self.stop == "G":
            return
        self.phase()
        self.sgu(l, ntl)
        self.phase()
        self.attention(l, 0, ntl, G, pay)
        self.phase()
        self.attention(l, 1, ntl, G, pay)
        self.phase()
        self.fnet(l, ntl, G)
        if l == self.layers[0]:
            for r in range(4):
                self.tap(L + "YB%d" % r, self.YB[r][:], [128, 2, TALL], ["YB%d_%d_%d" % (r, c, i) for c in range(2) for i in range(TT)], BF)
        if self.stop == "B5":
            return
        self.phase()
        self.merge(l, ntl, is_last, x_tile_src)

    def attention(self, l, which, ntl, G, pay):
        P, W = self.P, self.W
        q0 = 0 if which == 0 else 768
        z0 = 512 if which == 0 else 1280
        YB = self.YB[which]
        nqg = [g for g in GROUPS if g[0] // 128 < ntl]
        WQ, WQb = self.ar("WQ", [128, KC, 256], BF)
        WZ, WZb = self.ar("WZ", [128, KC, 256], BF)
        self.load_w(self.w_cols("w_in", l, q0, 256), WQ, WQb, (KC, 256))
        self.load_w(self.w_cols("w_in", l, z0, 256), WZ, WZb, (KC, 256))
        QT = [self.ar("QT%d" % g, [128, TALL], BF) for g in range(2)]
        def vview(V, k0, k1):
            return V[:, k0:k1, :].rearrange("p t (kv x) -> p t kv x", kv=2)[:, :, :, 0:64]

        if which == 0:
            NKB = 2 + 64
            KT, KTb = self.ar("KT", [128, NKB * 128], BF)
            V, Vb = self.ar("V", [128, NKB, 256], BF)
            P.op("pool", "memset", V[:, :, :].rearrange("p t (kv x) -> p t kv x", kv=2)[:, :, :, 64:128], 1.0, R=[], W=[Vb + "ones"])
            P.op("act", "copy", KT[:, 0:256], self.KTc[0][:], R=["KTc0"], W=[KTb + "c"])
            P.op("act", "copy", vview(V, 0, 2), self.Vc[0][:].rearrange("p t (kv d) -> p t kv d", kv=2), R=["Vc0"], W=[Vb + "c"])
            for s_ in range(4):
                P.dma("sp", KT[:, 256 + s_ * TOK:256 + (s_ + 1) * TOK], G[0][s_ * 256:s_ * 256 + 128, :], reads=["gat0"], writes=[KTb + str(s_)])
                for hf in range(2):
                    P.dma("sp", vview(V, 2 + s_ * NT + hf * 8, 2 + s_ * NT + hf * 8 + 8),
                          G[3][s_ * 256:s_ * 256 + 128, hf * 1024:(hf + 1) * 1024].rearrange("p (t kv d) -> p t kv d", kv=2, d=64),
                          reads=["gat3"], writes=[Vb + str(s_)])

            def kbufs(kb):
                return [KTb + ("c" if kb < 2 else str((kb - 2) // NT))], [Vb + ("c" if kb < 2 else str((kb - 2) // NT)), Vb + "ones"]
        else:
            NKB = 2 + NT + 8
            KT, KTb = self.ar("KT", [128, NKB * 128], BF)
            V, Vb = self.ar("V", [128, NKB, 256], BF)
            P.op("pool", "memset", V[:, :, :].rearrange("p t (kv x) -> p t kv x", kv=2)[:, :, :, 64:128], 1.0, R=[], W=[Vb + "ones"])
            P.op("act", "copy", KT[:, 0:256], self.KTc[1][:], R=["KTc1"], W=[KTb])
            P.op("act", "copy", vview(V, 0, 2), self.Vc[1][:].rearrange("p t (kv d) -> p t kv d", kv=2), R=["Vc1"], W=[Vb])
            P.dma("sp", KT[:, 256:256 + TOK], pay[0].ap()[128:256, :], reads=["pay_k1"], writes=[KTb])
            for hf in range(2):
                P.dma("sp", vview(V, 2 + hf * 8, 2 + hf * 8 + 8),
                      pay[3].ap()[128:256, hf * 1024:(hf + 1) * 1024].rearrange("p (t kv d) -> p t kv d", kv=2, d=64),
                      reads=["pay_v1"], writes=[Vb])
            for s_ in range(4):
                for kb, c0 in ((2 + NT + s_, TOK - 128), (2 + NT + 4 + s_, 0)):
                    P.dma("sp", KT[:, kb * 128:(kb + 1) * 128], G[0][s_ * 256 + 128:s_ * 256 + 256, c0:c0 + 128], reads=["gat0"], writes=[KTb])
                    P.dma("sp", vview(V, kb, kb + 1),
                          G[3][s_ * 256 + 128:s_ * 256 + 256, c0:c0 + 128].rearrange("p (t kv d) -> p t kv d", kv=2, d=64),
                          reads=["gat3"], writes=[Vb])

            def kbufs(kb):
                return [KTb], [Vb, Vb + "ones"]

        tmps = [[self.ar("hnt%d_%d" % (k, j), [128, 256], F32) for j in range(3)] for k in range(3)]
        QR = [self.ar("QR%d" % j, [128, 256], BF) for j in range(3)]
        def q_chain(i, slot):
            ps, pb = self.proj_tm(WQ, WQb, 0, 256, i)
            yield
            qr, qrb = QR[slot]
            yield from self.headnorm_rope(ps, pb, 0, 4, self.qnw[which], "qnw%d" % which, i, tmps[slot], qr, qrb, permute=True)
            yield from self.transpose_to(qr, qrb, 2, lambda j: QT[j][0][:, i * 128:(i + 1) * 128], lambda j: QT[j][1] + "_%d" % i)
        self.interleave((q_chain(i, i % 3) for i in range(ntl)), WIDTHS[2])

        def z_consume(ps, pb, c, t0, n):
            P.op("act", "activation", YB[:, c, t0:t0 + n], ps[:, 0:n], AF.Silu, R=[pb], W=["YB%d_%d_%d" % (which, c, i) for i in range(t0 // 128, (t0 + n) // 128)])
        self.proj_fm(WZ, WZb, 0, 2, ntl, z_consume)
        if which == 0 and l == self.layers[0]:
            self.tap("L%dQT0" % l, QT[0][0], [128, TALL], [QT[0][1] + "_%d" % i for i in range(ntl)], BF)

        PT = [self.ar("PT%d" % j, [128, 1024], BF) for j in range(2)]
        fin = [self.ar("fin%d" % j, [128, 512], F32) for j in range(3)]

        work = []

        def add_pair(g, t0, n, keylist):
            cw = 512 // n
            chunks = [keylist[c:c + cw] for c in range(0, len(keylist), cw)]
            for ci, ch in enumerate(chunks):
                work.append(dict(g=g, t0=t0, n=n, keys=ch, first=(ci == 0), last=(ci == len(chunks) - 1)))

        if which == 0:
            for (t0, n) in nqg:
                keys = [(kb, None) for kb in range(NKB)] if t0 < TOK else [(0, None), (1, None)]
                for g in range(2):
                    add_pair(g, t0, n, keys)
        else:
            for i in range(ntl):
                if i >= NT:
                    keys = [(0, None), (1, None)]
                else:
                    keys = [(0, None), (1, None), (2 + i, None)]
                    if i > 0:
                        keys.append((2 + i - 1, 0))
                    else:
                        keys += [(2 + NT + s_, 2 + s_) for s_ in range(4)]
                    if i < NT - 1:
                        keys.append((2 + i + 1, 1))
                    else:
                        keys += [(2 + NT + 4 + s_, 6 + s_) for s_ in range(4)]
                for g in range(2):
                    add_pair(g, i * 128, 128, keys)

        accs = [(self.PS[4], "ps4"), (self.PS[5], "ps5")]

        def stage1(w, slot):
            g, t0, n = w["g"], w["t0"], w["n"]
            qt, qtb = QT[g]
            qbufs = [qtb + "_%d" % i for i in range(t0 // 128, (t0 + n) // 128)]
            psw = self.PSW[slot % 2]
            pbs = ["ps%d" % (2 * (slot % 2)), "ps%d" % (2 * (slot % 2) + 1)]
            pt, ptb = PT[slot % 2]
            cnt = len(w["keys"])
            for j, (kb, mi) in enumerate(w["keys"]):
                kb_k, _ = kbufs(kb)
                for kv in range(2):
                    krows = slice(kv * 64, (kv + 1) * 64)
                    P.op("pe", "matmul", psw[:, kv * 512 + j * n:kv * 512 + (j + 1) * n], KT[krows, kb * 128:(kb + 1) * 128],
                         qt[krows, t0:t0 + n], start=True, stop=True, R=kb_k + qbufs, W=[pbs[kv]])
            P.op("act", "activation", pt[:, :].rearrange("p (a b) -> p a b", a=2)[:, :, 0:cnt * n],
                 psw[:, :].rearrange("p (a b) -> p a b", a=2)[:, :, 0:cnt * n], AF.Exp, R=pbs, W=[ptb])
            for j, (kb, mi) in enumerate(w["keys"]):
                if mi is not None:
                    P.op("dve", "tensor_tensor", pt[:, :].rearrange("p (a b) -> p a b", a=2)[:, :, j * n:(j + 1) * n],
                         pt[:, :].rearrange("p (a b) -> p a b", a=2)[:, :, j * n:(j + 1) * n],
                         self.masks[:, mi, 0:n].unsqueeze(1).broadcast_to([128, 2, n]), ALU.mult, R=[ptb, "masks"], W=[ptb])

        def stage2(w, slot):
            g, t0, n = w["g"], w["t0"], w["n"]
            rows = slice(g * 64, (g + 1) * 64)
            pt, ptb = PT[slot % 2]
            cnt = len(w["keys"])
            for j, (kb, mi) in enumerate(w["keys"]):
                _, kb_v = kbufs(kb)
                st = w["first"] and j == 0
                sp = w["last"] and j == cnt - 1
                for kv in range(2):
                    acc, accb = accs[kv]
                    P.op("pe", "matmul", acc[:, 0:n], V[:, kb, kv * 128:(kv + 1) * 128], pt[:, kv * 512 + j * n:kv * 512 + (j + 1) * n],
                         start=st, stop=sp, R=kb_v + [ptb], W=[accb])
            if not w["last"]:
                return
            for kv in range(2):
                acc, accb = accs[kv]
                h = 2 * kv + g
                (f0, f0b), (f1, f1b), (f2, f2b) = fin
                if which == 1:
                    P.op("dve", "tensor_scalar", f2[64:128, 0:n], acc[64:128, 0:n], self.esink[64:128, h:h + 1], None, ALU.add,
                         R=[accb, "esink"], W=[f2b])
                    P.op("dve", "reciprocal", f0[rows, 0:n], f2[64:128, 0:n], R=[f2b], W=[f0b])
                else:
                    P.op("dve", "reciprocal", f0[rows, 0:n], acc[64:128, 0:n], R=[accb], W=[f0b])
                P.op("dve", "tensor_copy", f1[rows, 0:n], acc[0:64, 0:n], R=[accb], W=[f1b])
                fe = "dve" if which == 1 else "pool"
                P.op(fe, "tensor_tensor", f1[rows, 0:n], f1[rows, 0:n], f0[rows, 0:n], ALU.mult, R=[f1b, f0b], W=[f1b])
                ybb = ["YB%d_%d_%d" % (which, kv, i) for i in range(t0 // 128, (t0 + n) // 128)]
                P.op(fe, "tensor_tensor", YB[rows, kv, t0:t0 + n], f1[rows, 0:n], YB[rows, kv, t0:t0 + n], ALU.mult,
                     R=[f1b] + ybb, W=ybb)

        LOOK = 1
        for k in range(len(work) + LOOK):
            if k < len(work):
                stage1(work[k], k)
            if k >= LOOK:
                stage2(work[k - LOOK], k - LOOK)

    def fnet(self, l, ntl, G):
        P, W = self.P, self.W
        YB = self.YB[2]
        WZ, WZb = self.ar("WZ", [128, KC, 256], BF)
        self.load_w(self.w_cols("w_in", l, 1792, 256), WZ, WZb, (KC, 256))

        def z_consume(ps, pb, c, t0, n):
            P.op("act", "activation", YB[:, c, t0:t0 + n], ps[:, 0:n], AF.Silu, R=[pb], W=["YB2_%d_%d" % (c, i) for i in range(t0 // 128, (t0 + n) // 128)])
        self.proj_fm(WZ, WZb, 0, 2, ntl, z_consume)
        MC, MCb = self.ar("MC", [128, 64, 32], BF)
        MS, MSb = self.ar("MS", [128, 64, 32], BF)
        self.load_w(self.C["mc"].rearrange("p (a b) -> p a b", a=64), MC, MCb, (64, 32))
        self.load_w(self.C["ms"].rearrange("p (a b) -> p a b", a=64), MS, MSb, (64, 32))
        Z = [self.ar("Z%d" % j, [128, 64, 128], BF) for j in range(2)]
        T3, T3b = self.ar("T3", [128, 64, 128], BF)
        sp = 0
        for c in range(2):
            for hf in range(2):
                z, zb = Z[sp % 2]
                sp += 1
                for s in range(4):
                    for ri in range(2):
                        r0 = s * 256 + c * 128 + hf * 64
                        P.dma("sp", z[ri * 64 + s * 16:ri * 64 + (s + 1) * 16, :, :],
                              G[1 + ri][r0:r0 + 64, :].rearrange("ch (t n) -> t ch n", n=128), reads=["gat%d" % (1 + ri)], writes=[zb])
                for ch4 in range(16):
                    ps, pb = self.ps()
                    for j in range(4):
                        ch = ch4 * 4 + j
                        P.op("pe", "matmul", ps[:, j * 128:(j + 1) * 128], z[:, ch, :], self.m1[:, :],
                                                                            start=True, stop=True, R=[zb, "m1"], W=[pb])
                    if ch4 % 2 == 0:
                        P.op("act", "copy", T3[:, ch4 * 4:ch4 * 4 + 4, :], ps[:].rearrange("p (a b) -> p a b", a=4), R=[pb], W=[T3b])
                    else:
                        P.op("dve", "tensor_copy", T3[:, ch4 * 4:ch4 * 4 + 4, :], ps[:].rearrange("p (a b) -> p a b", a=4), R=[pb], W=[T3b])
                rows = slice(hf * 64, (hf + 1) * 64)
                for bank in range(4):
                    ps, pb = self.ps()
                    for kk in range(16):
                        k1 = bank * 16 + kk
                        P.op("pe", "matmul", ps[rows, kk * 32:(kk + 1) * 32], T3[:, :, k1], MC[:, k1, :],
                                                                         start=True, stop=False, tile_position=(0, hf * 64), R=[T3b, MCb], W=[pb])
                        P.op("pe", "matmul", ps[rows, kk * 32:(kk + 1) * 32], T3[:, :, 64 + k1], MS[:, k1, :],
                                                                         start=False, stop=True, tile_position=(0, hf * 64), R=[T3b, MSb], W=[pb])
                    yv = YB[rows, c, 0:TOK].rearrange("p (k2 k1) -> p k1 k2", k1=64)[:, bank * 16:(bank + 1) * 16, :]
                    ybb = ["YB2_%d_%d" % (c, i) for i in range(NT)]
                    P.op("dve", "tensor_tensor", yv, ps[rows, :].rearrange("p (a b) -> p a b", a=16), yv, ALU.mult, R=[pb] + ybb, W=ybb)
        if ntl > NT:
            for c in range(2):
                ps, pb = self.ps()
                k = 0
                for uv, tab in ((0, self.c256), (1, self.s256n)):
                    for j in range(2):
                        P.op("pe", "matmul", ps[:, 0:256], self.UVc[:, uv, j, c * 128:(c + 1) * 128], tab[:, j, :], start=(k == 0), stop=(k == 3), R=["UVc", "c256", "s256n"], W=[pb])
                        k += 1
                ybb = ["YB2_%d_%d" % (c, i) for i in range(NT, TT)]
                P.op("dve", "tensor_tensor", YB[:, c, TOK:TALL], ps[:, 0:256], YB[:, c, TOK:TALL], ALU.mult, R=[pb] + ybb, W=ybb)

    def sgu(self, l, ntl):
        P, W = self.P, self.W
        YB = self.YB[3]
        WU, WUb = self.ar("WU", [128, KC, 256], BF)
        WV, WVb = self.ar("WV", [128, KC, 256], BF)
        WZ, WZb = self.ar("WZ", [128, KC, 256], BF)
        self.load_w(self.w_cols("w_in", l, 2048, 256), WU, WUb, (KC, 256))
        self.load_w(self.w_cols("w_in", l, 2304, 256), WV, WVb, (KC, 256))
        self.load_w(self.w_cols("w_in", l, 2560, 256), WZ, WZb, (KC, 256))
        GU, GUb = self.ar("GU", [128, 2, TALL], BF)
        GV, GVb = self.ar("GV", [128, TT, 256], BF)
        wsp, wspb = self.ar("wsp", [128, 4, 128], BF)
        bsp, bspb = self.ar("bsp", [128, 2, 128], F32)
        self.load_w(W["w_spT"][l], wsp, wspb, (4, 128))
        for g in range(4):
            P.dma("sp", bsp[(g % 2) * 64:(g % 2) * 64 + 64, g // 2, :], W["b_sp"][l, g:g + 1, :].partition_broadcast(64), writes=[bspb])

        def u_consume(ps, pb, c, t0, n):
            P.op("act", "activation", GU[:, c, t0:t0 + n], ps[:, 0:n], AF.Gelu, R=[pb], W=[GUb])

        def z_consume(ps, pb, c, t0, n):
            P.op("act", "activation", YB[:, c, t0:t0 + n], ps[:, 0:n], AF.Silu, R=[pb], W=["YB3_%d_%d" % (c, i) for i in range(t0 // 128, (t0 + n) // 128)])
        self.proj_fm(WU, WUb, 0, 2, ntl, u_consume)
        self.proj_fm(WZ, WZb, 0, 2, ntl, z_consume)
        for i in range(ntl):
            ps, pb = self.proj_tm(WV, WVb, 0, 256, i)
            P.op("act", "activation", GV[:, i, :], ps[:, 0:256], AF.Gelu, R=[pb], W=[GVb + "_%d" % i])
        t1, t1b = self.ar("sg1", [128, 512], F32)
        t2, t2b = self.ar("sg2", [128, 512], F32)
        for i0 in range(0, ntl, 4):
            nt = min(4, ntl - i0)
            n = nt * 128
            for c in range(2):
                ps, pb = self.ps()
                for j in range(nt):
                    i = i0 + j
                    for gg in range(2):
                        g = 2 * c + gg
                        P.op("pe", "matmul", ps[gg * 64:(gg + 1) * 64, j * 128:(j + 1) * 128], GV[:, i, g * 64:(g + 1) * 64], wsp[:, g, :],
                            start=True, stop=True, tile_position=(0, gg * 64), R=[GVb + "_%d" % i, wspb], W=[pb])
                P.op("dve", "tensor_tensor", t1[:, 0:nt * 128].rearrange("p (a b) -> p a b", a=nt), ps[:, 0:nt * 128].rearrange("p (a b) -> p a b", a=nt),
                    bsp[:, c, :].unsqueeze(1).broadcast_to([128, nt, 128]), ALU.add, R=[pb, bspb, t1b], W=[t1b])
                P.op("pool", "tensor_tensor", t2[:, 0:n], t1[:, 0:n], GU[:, c, i0 * 128:i0 * 128 + n], ALU.mult, R=[t1b, GUb, t2b], W=[t2b])
                ybb = ["YB3_%d_%d" % (c, i) for i in range(i0, i0 + nt)]
                P.op("dve", "tensor_tensor", YB[:, c, i0 * 128:i0 * 128 + n], t2[:, 0:n], YB[:, c, i0 * 128:i0 * 128 + n], ALU.mult, R=[t2b] + ybb, W=ybb)

    def merge(self, l, ntl, is_last, x_tile_src):
        P, W = self.P, self.W
        groups = [g for g in GROUPS if g[0] // 128 < ntl]
        MT, MTb = self.ar("MT", [128, KC, TALL], BF)
        WBR = [self.ar("WBR%d" % r, [128, 2, D], BF) for r in range(4)]
        for r in range(4):
            self.load_w(W["w_br"][l, r].rearrange("(kc p) n -> p kc n", p=128), WBR[r][0], WBR[r][1], (2, D))
        bm, bmb = self.ar("bm", [128, 32], F32)
        P.dma("sp", bm, W["b_mergeT"][l], writes=[bmb])
        WM = [self.ar("WM%d" % j, [128, 4, KC, 128], BF) for j in range(2)]
        sg = [self.ar("sg%d" % j, [128, 512], F32) for j in range(2)]
        acc = [self.ar("acc%d" % j, [128, 512], F32) for j in range(2)]
        tmpm = [self.ar("tmpm%d" % j, [128, 512], F32) for j in range(2)]
        cnt = 0
        for dc in range(KC):
            wm, wmb = WM[dc % 2]
            for r in range(4):
                i = self.wsf_i % 2
                self.wsf_i += 1
                st = self.WSF[i][:, 0:1024].rearrange("p (a b) -> p a b", a=KC)
                P.dma("sp", st, self.w_cols("w_merge", l, r * D + dc * 128, 128), writes=["wsf%d" % i])
                P.op("dve", "tensor_copy", wm[:, r, :, :], st, R=["wsf%d" % i], W=[wmb + "_%d" % r])
            for (t0, n) in groups:
                htb = ["HT%d" % i for i in range(t0 // 128, (t0 + n) // 128)]
                a_, ab = acc[cnt % 2]
                for r in range(4):
                    psg, pgb = self.ps()
                    for kc in range(KC):
                        P.op("pe", "matmul", psg[:, 0:n], wm[:, r, kc, :], self.HT[:, kc, t0:t0 + n], start=(kc == 0), stop=(kc == KC - 1), R=[wmb + "_%d" % r] + htb, W=[pgb])
                    psy, pyb = self.ps()
                    ybb = ["YB%d_%d_%d" % (r, c, i) for c in range(2) for i in range(t0 // 128, (t0 + n) // 128)]
                    for k2 in range(2):
                        P.op("pe", "matmul", psy[:, 0:n], WBR[r][0][:, k2, dc * 128:(dc + 1) * 128], self.YB[r][:, k2, t0:t0 + n],
                            start=(k2 == 0), stop=(k2 == 1), R=[WBR[r][1]] + ybb, W=[pyb])
                    s_, sb_ = sg[(cnt * 4 + r) % 2]
                    P.op("act", "activation", s_[:, 0:n], psg[:, 0:n], AF.Sigmoid,
                                                                         bias=bm[:, r * 8 + dc:r * 8 + dc + 1], scale=1.0, R=[pgb, bmb], W=[sb_])
                    if r == 0:
                        P.op("dve", "tensor_tensor", a_[:, 0:n], psy[:, 0:n], s_[:, 0:n], ALU.mult, R=[pyb, sb_, ab], W=[ab])
                    else:
                        t_, tb_ = tmpm[r % 2]
                        P.op("dve", "tensor_tensor", t_[:, 0:n], psy[:, 0:n], s_[:, 0:n], ALU.mult, R=[pyb, sb_, tb_], W=[tb_])
                        if r < 3:
                            P.op("pool", "tensor_tensor", a_[:, 0:n], a_[:, 0:n], t_[:, 0:n], ALU.add, R=[ab, tb_], W=[ab])
                        else:
                            P.op("pool", "tensor_tensor", MT[:, dc, t0:t0 + n], a_[:, 0:n], t_[:, 0:n], ALU.add, R=[ab, tb_], W=[MTb + "_%d_%d" % (dc, t0)])
                cnt += 1
        self.tap("L%dMT" % l, MT, [128, KC, TALL], [MTb + "_%d_%d" % (dc, g[0]) for dc in range(KC) for g in groups], BF)
        if self.stop == "C2":
            return
        self.P.barrier()
        self.ar_off = (KC * TALL * 2 + 63) // 64 * 64
        self.ar_gen += 1
        WO, WOb = self.ar("WO", [128, KC, D], BF)
        for j in range(4):
            self.load_w(self.w_cols("w_out", l, j * 256, 256), WO[:, :, j * 256:(j + 1) * 256], WOb + "_%d" % j, (KC, 256))
        gt_bc, gtb = self.ar("gt_bc", [128, 2, D], F32)
        for r in range(2 if ntl > NT else 1):
            for half in range(2):
                ps, pb = self.ps()
                P.op("pe", "matmul", ps[:, :], self.sel2[0:2, r * 128:(r + 1) * 128], self.gate2[0:2, half * 512:(half + 1) * 512],
                    start=True, stop=True, R=["sel2", "gate2"], W=[pb])
                P.op("act", "copy", gt_bc[:, r, half * 512:(half + 1) * 512], ps[:, :], R=[pb], W=[gtb])
        XT = [self.ar("XT%d" % i, [128, D], F32) for i in range(2)]
        XO = [self.ar("XO%d" % i, [128, D], F32) for i in range(2)]
        for i in range(ntl):
            r = 0 if i < NT else 1
            xt, xtb = XT[i % 2]
            xo, xob = XO[i % 2]
            P.dma("sp", xt, x_tile_src(i), writes=[xtb])
            for half in range(2):
                ps, pb = self.ps()
                mtb = [MTb + "_%d_%d" % (dc, g[0]) for dc in range(KC) for g in GROUPS if g[0] <= i * 128 < g[0] + g[1]]
                for kc in range(KC):
                    P.op("pe", "matmul", ps[:, :], MT[:, kc, i * 128:(i + 1) * 128], WO[:, kc, half * 512:(half + 1) * 512],
                        start=(kc == 0), stop=(kc == KC - 1), R=mtb + [WOb + "_%d" % (2 * half), WOb + "_%d" % (2 * half + 1)], W=[pb])
                P.op("dve", "tensor_tensor", xo[:, half * 512:(half + 1) * 512], ps[:, :], gt_bc[:, r, half * 512:(half + 1) * 512], ALU.mult, R=[pb, gtb, xob], W=[xob + "h%d" % half])
                P.op("pool", "tensor_tensor", xo[:, half * 512:(half + 1) * 512], xo[:, half * 512:(half + 1) * 512], xt[:, half * 512:(half + 1) * 512], ALU.add, R=[xob + "h%d" % half, xtb], W=[xob + "h%d" % half])
            if is_last:
                dst = self.out[i * 128:(i + 1) * 128, :]
            elif self.last or l != self.layers[-1]:
                dst = self.xs[i * 128:(i + 1) * 128, :]
            else:
                dst = self.xs_out[i * 128:(i + 1) * 128, :]
            P.dma("sp", dst, xo, reads=[xob + "h0", xob + "h1"], writes=["xs%d" % i])


def is_last_layer(l):
    return l == 1


_CONST_CACHE = {}


def _consts(core):
    if core not in _CONST_CACHE:
        _CONST_CACHE[core] = host_consts(core)
    return _CONST_CACHE[core]


def _weights_layout(inp):
    f = lambda a: np.ascontiguousarray(np.asarray(a, dtype=np.float32))
    w = {k: f(inp[k]) for k in ("norm_w", "w_ada", "b_ada", "w_in", "qn_a", "kn_a", "qn_d", "kn_d", "sink_d",
                                "w_fnet", "b_sp", "w_br", "w_merge", "w_out")}
    w["w_spT"] = np.ascontiguousarray(f(inp["w_sp"]).transpose(0, 3, 1, 2))
    w["b_mergeT"] = np.ascontiguousarray(f(inp["b_merge"]).reshape(2, 32, 128).transpose(0, 2, 1))
    return w


def _run(builder_kwargs, per_core_extra, inp):
    b = Builder(**builder_kwargs)
    nc = b.build()
    w = _weights_layout(inp)
    c = np.asarray(inp["c"], np.float32)
    c_ctx = np.asarray(inp["c_ctx"], np.float32)
    in_maps = []
    for core in range(NCORES):
        bi = core // 4
        cv = np.stack([c[bi], c_ctx], 0)
        cvT = np.ascontiguousarray(cv.reshape(2, KC, 128).transpose(2, 1, 0))
        m = {"cvecT": cvT}
        m.update(w)
        m.update(_consts(core))
        m.update(per_core_extra(core))
        in_maps.append(m)
    res = run_bass_kernel_spmd(nc, in_maps, core_ids=list(range(NCORES)))
    return res.results, b


def kernel(**inp):
    x = np.asarray(inp["x"], np.float32)
    ctx = np.asarray(inp["ctx"], np.float32)

    def extra(core):
        bi, q = divmod(core, 4)
        return {"x": np.ascontiguousarray(x[bi, q * TOK:(q + 1) * TOK]), "ctx": np.ascontiguousarray(ctx[bi])}
    results, _ = _run(dict(layers=(0, 1), first=True, last=True), extra, inp)
    out = np.empty((2, SEQ, D), np.float32)
    for core in range(NCORES):
        bi, q = divmod(core, 4)
        out[bi, q * TOK:(q + 1) * TOK] = results[core]["out"]
    return out
```

```python
import numpy as np
import ml_dtypes
import concourse.bass as bass
import concourse.mybir as mybir
from concourse.bass_utils import run_bass_kernel_spmd

F32 = mybir.dt.float32
BF = mybir.dt.bfloat16
AF = mybir.ActivationFunctionType
ALU = mybir.AluOpType
AX = mybir.AxisListType
NPBF = ml_dtypes.bfloat16

NCORES = 8
TOK = 2048
NT = 16
CT = 2
TT = NT + CT
TALL = TT * 128
D = 1024
KC = 8
EPS = 1e-6
SEQ = 8192
INW = 2816
WIDTHS = [3, 3, 3]
GROUPS = [(0, 512), (512, 512), (1024, 512), (1536, 512), (2048, 256)]


class Buf:
    __slots__ = ("name", "w", "rs")

    def __init__(self, name):
        self.name = name
        self.w = None
        self.rs = []


class Op:
    __slots__ = ("eng", "fn", "deps", "kind", "sem", "semval", "signal")

    def __init__(self, eng, fn, kind):
        self.eng = eng
        self.fn = fn
        self.kind = kind
        self.deps = []
        self.sem = None
        self.semval = 0
        self.signal = False


class Prog:
    ENGS = ("pe", "act", "dve", "pool", "sp")
    NS = 24

    def __init__(self, nc):
        self.nc = nc
        self.ops = {e: [] for e in self.ENGS}
        self.dma_count = 0
        self.dma_last = [None] * self.NS
        self.ncoll = 0
        self.bufs = {}
        self.pending = {e: [] for e in self.ENGS}
        self.last_c = {e: None for e in self.ENGS}
        self.colls = []

    def buf(self, name):
        b = self.bufs.get(name)
        if b is None:
            b = self.bufs[name] = Buf(name)
        return b

    def _add(self, op, d):
        if d is op:
            return
        op.deps.append(d)
        if d.kind == "c":
            d.signal = True

    def _deps(self, op, reads, writes):
        deps = []
        for b in reads:
            b = self.buf(b)
            if b.w is not None:
                deps.append(("raw", b.w))
            if op.kind == "c":
                b.rs = [r for r in b.rs if not (r.kind == "c" and r.eng == op.eng)]
            b.rs.append(op)
        for b in writes:
            b = self.buf(b)
            if b.w is not None:
                deps.append(("waw", b.w))
            for r in b.rs:
                if r is not op:
                    deps.append(("war", r))
            b.rs = []
            b.w = op
        for kind, d in deps:
            if d is op:
                continue
            if d.kind == "c" and op.kind == "c" and d.eng == op.eng:
                if op.eng == "pe":
                    continue
            self._add(op, d)
        if self.pending[op.eng]:
            for d in self.pending[op.eng]:
                self._add(op, d)
            self.pending[op.eng] = []

    def op(self, eng, meth, *args, R=(), W=(), **kw):
        fn = (lambda e, meth=meth, args=args, kw=kw: getattr(e, meth)(*args, **kw))
        o = Op(eng, fn, "c")
        self._deps(o, R, W)
        self.ops[eng].append(o)
        self.last_c[eng] = o
        return o

    def dma(self, q, out, in_, reads=(), writes=(), **kw):
        o = Op(q, (lambda e, out=out, in_=in_, kw=kw: e.dma_start(out=out, in_=in_, **kw)), "d")
        self._deps(o, reads, writes)
        slot = self.dma_count % self.NS
        self.dma_count += 1
        prev = self.dma_last[slot]
        o.sem = slot
        o.semval = (prev.semval if prev else 0) + 16
        if prev is not None:
            o.deps.append(prev)
        self.dma_last[slot] = o
        self.ops[q].append(o)
        return o

    def collective(self, fn, reads=(), writes=()):
        o = Op("pool", fn, "x")
        self._deps(o, reads, writes)
        o.sem = self.ncoll
        self.ncoll += 1
        self.ops["pool"].append(o)
        self.colls.append(o)
        return o

    def barrier(self):
        lasts = [o for o in self.last_c.values() if o is not None]
        lasts += [o for o in self.dma_last if o is not None]
        for e in self.ENGS:
            self.pending[e] = list(lasts)

    def emit(self):
        nc = self.nc
        for e in self.ENGS:
            cnt = 0
            for o in self.ops[e]:
                if o.kind == "c" and o.signal:
                    cnt += 1
                    o.semval = cnt
        esem = {e: nc.alloc_semaphore("es_" + e) for e in self.ENGS}
        dsem = [nc.alloc_semaphore("ds_%d" % i) for i in range(self.NS)]
        csem = [nc.alloc_semaphore("cs_%d" % i) for i in range(self.ncoll)]

        def semof(d):
            if d.kind == "c":
                return esem[d.eng], d.semval
            if d.kind == "d":
                return dsem[d.sem], d.semval
            return csem[d.sem], 1

        def gen(ename):
            def body(eng):
                waited = {}
                for o in self.ops[ename]:
                    need = {}
                    for d in o.deps:
                        s, v = semof(d)
                        if v > need.get(id(s), (None, 0))[1]:
                            need[id(s)] = (s, v)
                    for s, v in need.values():
                        if waited.get(id(s), 0) >= v:
                            continue
                        waited[id(s)] = v
                        eng.wait_ge(s, v)
                    ins = o.fn(eng)
                    if o.kind == "c":
                        if o.signal:
                            ins.then_inc(esem[ename], 1)
                    elif o.kind == "d":
                        ins.then_inc(dsem[o.sem], 16)
                    else:
                        ins.then_inc(csem[o.sem])
                last = {}
                for o in self.ops[ename]:
                    if o.kind in ("d", "x"):
                        s, v = semof(o)
                        if v > last.get(id(s), (None, 0))[1]:
                            last[id(s)] = (s, v)
                for s, v in last.values():
                    if waited.get(id(s), 0) < v:
                        eng.wait_ge(s, v)
            return body

        with nc.Block() as block:
            block.sync(gen("sp"))
            block.scalar(gen("act"))
            block.vector(gen("dve"))
            block.tensor(gen("pe"))
            block.gpsimd(gen("pool"))


def host_consts(core):
    b, q = divmod(core, 4)
    t = (q * TOK + np.arange(TOK)).astype(np.int64)
    r = (t // 64).astype(np.float32)
    col = (t % 64).astype(np.float32)
    inv = (np.float32(10000.0) ** (-np.arange(16, dtype=np.float32) / np.float32(16))).astype(np.float32)
    ar = r[:, None] * inv[None, :]
    ac = col[:, None] * inv[None, :]
    cr, sr, cc, sc = np.cos(ar), np.sin(ar), np.cos(ac), np.sin(ac)
    cos64 = np.concatenate([cr, cr, cc, cc], 1).astype(np.float32)
    sin64 = np.concatenate([-sr, sr, -sc, sc], 1).astype(np.float32)
    cos64 = np.ascontiguousarray(cos64.reshape(NT, 128, 64).transpose(1, 0, 2))
    sin64 = np.ascontiguousarray(sin64.reshape(NT, 128, 64).transpose(1, 0, 2))
    n1 = np.arange(64)
    th = 2 * np.pi * np.outer(n1, n1) / 64.0
    C, S = np.cos(th), np.sin(th)
    sc1 = 1.0 / np.sqrt(SEQ * 64.0)
    m1 = np.zeros((128, 128), np.float64)
    m1[:64, :64] = C
    m1[64:, :64] = -S
    m1[:64, 64:] = S
    m1[64:, 64:] = C
    m1 *= sc1
    n2 = np.arange(128)[:, None, None]
    k1 = np.arange(64)[None, :, None]
    k2 = (np.arange(32) + 32 * q)[None, None, :]
    ph = 2 * np.pi * n2 * (64 * k2 + k1) / float(SEQ)
    mc = np.cos(ph).reshape(128, 64 * 32)
    ms = (-np.sin(ph)).reshape(128, 64 * 32)
    n = np.arange(256)
    t256 = 2 * np.pi * np.outer(n, n) / 256.0
    sc2 = 1.0 / np.sqrt(256 * 64.0)
    c256 = (np.cos(t256) * sc2).reshape(2, 128, 256).transpose(1, 0, 2)
    s256n = (-np.sin(t256) * sc2).reshape(2, 128, 256).transpose(1, 0, 2)
    c64d = np.concatenate([C, C], 1).astype(np.float32)
    s64d = np.concatenate([S, S], 1).astype(np.float32)
    pk = np.arange(128)[:, None]
    pq = np.arange(128)[None, :]
    mP = (pk >= pq).astype(np.float32)
    mN = (pk <= pq).astype(np.float32)
    masks = np.zeros((128, 10, 128), np.float32)
    masks[:, 0] = mP
    masks[:, 1] = mN
    for s in range(4):
        if s == q - 1:
            masks[:, 2 + s] = mP
        if s == q + 1:
            masks[:, 6 + s] = mN
    sel2 = np.zeros((2, 256), np.float32)
    sel2[0, :128] = 1.0
    sel2[1, 128:] = 1.0
    return {
        "cos64": cos64, "sin64": sin64,
        "m1": m1.astype(np.float32), "mc": mc.astype(np.float32), "ms": ms.astype(np.float32),
        "c256": np.ascontiguousarray(c256).astype(np.float32), "s256n": np.ascontiguousarray(s256n).astype(np.float32),
        "c64d": c64d, "s64d": s64d,
        "masks": masks.astype(NPBF), "sel2": sel2,
        "ident": np.eye(128, dtype=np.float32),
    }


CONST_SPECS = {
    "cos64": ([128, NT, 64], F32), "sin64": ([128, NT, 64], F32),
    "m1": ([128, 128], F32), "mc": ([128, 2048], F32), "ms": ([128, 2048], F32),
    "c256": ([128, 2, 256], F32), "s256n": ([128, 2, 256], F32),
    "c64d": ([64, 128], F32), "s64d": ([64, 128], F32),
    "masks": ([128, 10, 128], BF), "sel2": ([2, 256], F32), "ident": ([128, 128], F32),
}

WEIGHT_SPECS = {
    "norm_w": [2, D], "w_ada": [2, D, 3 * D], "b_ada": [2, 3 * D], "w_in": [2, D, INW],
    "qn_a": [2, 64], "kn_a": [2, 64], "qn_d": [2, 64], "kn_d": [2, 64], "sink_d": [2, 4],
    "w_fnet": [2, 4, 64, 64], "w_spT": [2, 128, 4, 128], "b_sp": [2, 4, 128],
    "w_br": [2, 4, 256, D], "w_merge": [2, D, 4 * D], "b_mergeT": [2, 128, 32], "w_out": [2, D, D],
}


class Builder:
    def __init__(self, layers=(0, 1), first=True, last=True, debug=(), stop=None):
        self.stop = stop
        self.layers = tuple(layers)
        self.first = first
        self.last = last
        self.debug = set(debug)
        nc = self.nc = bass.Bass("TRN2", target_bir_lowering=False)
        self.P = Prog(nc)
        self.dbg_outs = {}
        self._uid = 0

    def uid(self, p="t"):
        self._uid += 1
        return "%s%d" % (p, self._uid)

    def dram_in(self, name, shape, dt=F32):
        return self.nc.dram_tensor(name, list(shape), dt, kind="ExternalInput").ap()

    def dram_out(self, name, shape, dt=F32):
        return self.nc.dram_tensor(name, list(shape), dt, kind="ExternalOutput").ap()

    def sb(self, name, shape, dt):
        return self.nc.alloc_sbuf_tensor("s_" + name, list(shape), dt)

    def tap(self, name, view, shape, reads, dt=F32):
        if name not in self.debug:
            return
        o = self.dram_out("dbg_" + name, shape, dt)
        self.dbg_outs["dbg_" + name] = (shape, dt)
        self.P.dma("sp", o, view, reads=reads)

    def build(self):
        nc, P = self.nc, self.P
        if self.first:
            self.x_in = self.dram_in("x", [TOK, D])
            self.ctx_in = self.dram_in("ctx", [256, D])
        else:
            self.xs_in = self.dram_in("xs_in", [TALL, D])
        self.cvecT = self.dram_in("cvecT", [128, KC, 2])
        self.W = {k: self.dram_in(k, s) for k, s in WEIGHT_SPECS.items()}
        self.C = {k: self.dram_in(k, s, dt) for k, (s, dt) in CONST_SPECS.items()}
        if self.last:
            self.out = self.dram_out("out", [TOK, D])
        else:
            self.xs_out = self.dram_out("xs_out", [TALL, D])
        self.xs = nc.dram_tensor("xs", [TALL, D], F32).ap()
        self.payload = [[nc.dram_tensor("payload%d_%d" % (l, j), [256, TOK], BF) for j in range(4)] for l in range(2)]
        self.gathered = [[nc.dram_tensor("gathered%d_%d" % (l, j), [1024, TOK], BF) for j in range(4)] for l in range(2)]

        self.HT = self.sb("HT", [128, KC, TALL], BF)
        self.YB = [self.sb("YB%d" % r, [128, 2, TALL], BF) for r in range(4)]
        self.WSF = [self.sb("WSF%d" % i, [128, 2048], F32) for i in range(2)]
        self.wsf_i = 0
        self.cos64 = self.sb("cos64", [128, NT, 64], F32)
        self.sin64 = self.sb("sin64", [128, NT, 64], F32)
        self.identf = self.sb("identf", [128, 128], F32)
        self.ident = self.sb("ident", [128, 128], BF)
        self.m1 = self.sb("m1", [128, 128], BF)
        self.c256 = self.sb("c256", [128, 2, 256], BF)
        self.s256n = self.sb("s256n", [128, 2, 256], BF)
        self.c64d = self.sb("c64d", [64, 128], F32)
        self.s64d = self.sb("s64d", [64, 128], F32)
        self.masks = self.sb("masks", [128, 10, 128], BF)
        self.sel2 = self.sb("sel2", [2, 256], F32)
        self.ones64 = self.sb("ones64", [128, 64], BF)
        self.scT = self.sb("scT", [128, KC, 2], F32)
        self.gate2 = self.sb("gate2", [2, D], F32)
        self.knw = [self.sb("knw%d" % i, [128, 128], F32) for i in range(2)]
        self.qnw = [self.sb("qnw%d" % i, [128, 256], F32) for i in range(2)]
        self.esink = self.sb("esink", [128, 4], F32)
        self.KTc = [self.sb("KTc%d" % i, [128, 256], BF) for i in range(2)]
        self.Vc = [self.sb("Vc%d" % i, [128, 2, 128], BF) for i in range(2)]
        self.UVc = self.sb("UVc", [128, 2, 2, 256], BF)
        self.BA = self.sb("BA", [128, 2, 2, 128], BF)
        self.stat = self.sb("stat", [128, 64], F32)
        self.stat_i = 0
        self.PSW = [nc.alloc_psum_tensor("psw%d" % i, [128, 1024], F32) for i in range(2)]
        self.PS = [self.PSW[i // 2][:, (i % 2) * 512:(i % 2 + 1) * 512] for i in range(4)]
        self.PS += [nc.alloc_psum_tensor("ps%d" % i, [128, 512], F32) for i in (4, 5)]
        self.PST = [nc.alloc_psum_tensor("pst%d" % i, [128, 1024], BF) for i in range(2)]
        self.PS4BF = self.PS[4][:, :].bitcast(BF)
        self.ps_i = 0
        self.pst_i = 0
        self.ARENA_BYTES = 88 * 1024
        self.arena = self.sb("arena", [128, self.ARENA_BYTES // 2], BF)
        self.ar_off = 0
        self.ar_gen = 0

        for k in ("cos64", "sin64", "c64d", "s64d", "masks", "sel2"):
            P.dma("sp", getattr(self, k)[:], self.C[k], writes=[k])
        for k, shp in (("m1", [128, 128]), ("c256", [128, 2, 256]), ("s256n", [128, 2, 256])):
            tf_, tfb_ = self.ar(k + "_f32", shp, F32)
            P.dma("sp", tf_, self.C[k], writes=[tfb_])
            P.op("dve", "tensor_copy", getattr(self, k)[:], tf_, R=[tfb_], W=[k])
        P.dma("sp", self.identf[:], self.C["ident"], writes=["identf"])
        P.op("dve", "tensor_copy", self.ident[:], self.identf[:], R=["identf"], W=["ident"])
        P.op("pool", "memset", self.ones64[:], 1.0, R=[], W=["ones64"])
        P.dma("sp", self.scT[:], self.cvecT, writes=["scT"])
        P.op("act", "activation", self.scT[:], self.scT[:], AF.Silu, R=["scT"], W=["scT"])

        for li, l in enumerate(self.layers):
            is_first = self.first and li == 0
            is_last = self.last and li == len(self.layers) - 1
            self.layer(l, is_first, is_last, x_from_input=is_first,
                       x_src_xs_in=(not self.first and li == 0))
        if self.stop is not None:
            dst = self.out if self.last else self.xs_out
            P.dma("sp", dst[0:128, :], self.identf[:, :].unsqueeze(1).broadcast_to([128, 8, 128]) if False else self.cos64[:, :, :], reads=["cos64"])
        P.emit()
        return nc

    def phase(self):
        self.P.barrier()
        self.ar_off = 0
        self.ar_gen += 1

    def ar(self, name, shape, dt):
        n = int(np.prod(shape[1:]))
        nbytes = n * (4 if dt == F32 else 2)
        nbytes = (nbytes + 63) // 64 * 64
        assert self.ar_off + nbytes <= self.ARENA_BYTES, (name, self.ar_off, nbytes)
        e0 = self.ar_off // 2
        v = self.arena[:, e0:e0 + nbytes // 2]
        if dt == F32:
            v = v.bitcast(F32)
        v = v[:, 0:n]
        self.ar_off += nbytes
        if shape[0] != 128:
            v = v[0:shape[0]]
        if len(shape) == 3:
            v = v.rearrange("p (a b) -> p a b", a=shape[1])
        elif len(shape) == 4:
            v = v.rearrange("p (a b c) -> p a b c", a=shape[1], b=shape[2])
        return v, "%s_g%d" % (name, self.ar_gen)

    def ps(self):
        i = self.ps_i % 4
        self.ps_i += 1
        return self.PS[i], "ps%d" % i

    def pst(self):
        i = self.pst_i % 3
        self.pst_i += 1
        if i == 2:
            return self.PS4BF, "ps4"
        return self.PST[i], "pst%d" % i

    def statcol(self, n=1):
        c = (self.stat_i % 16) * 4
        self.stat_i += 1
        return self.stat[:, c:c + n], "stat%d" % c

    def load_w(self, src, dst, dstbuf, shape3, q="sp"):
        P = self.P
        i = self.wsf_i % 2
        self.wsf_i += 1
        a, b = shape3
        st = self.WSF[i][:, 0:a * b].rearrange("p (a b) -> p a b", a=a)
        P.dma(q, st, src, writes=["wsf%d" % i])
        P.op("dve", "tensor_copy", dst, st, R=["wsf%d" % i], W=[dstbuf])

    def w_cols(self, name, l, c0, n):
        return self.W[name][l, :, c0:c0 + n].rearrange("(kc p) n -> p kc n", p=128)

    @staticmethod
    def interleave(gens, width):
        gens = iter(gens)
        active = []
        while True:
            while len(active) < width:
                g = next(gens, None)
                if g is None:
                    break
                active.append(g)
            if not active:
                break
            for g in list(active):
                try:
                    next(g)
                except StopIteration:
                    active.remove(g)

    def proj_fm(self, wb, wbuf, c0, nchunks, tiles, consume):
        P = self.P
        for c in range(nchunks):
            for (t0, n) in GROUPS:
                if t0 // 128 >= tiles:
                    continue
                ps, pb = self.ps()
                htb = ["HT%d" % i for i in range(t0 // 128, (t0 + n) // 128)]
                for kc in range(KC):
                    P.op("pe", "matmul", ps[:, 0:n], wb[:, kc, c0 + c * 128:c0 + (c + 1) * 128], self.HT[:, kc, t0:t0 + n],
                        start=(kc == 0), stop=(kc == KC - 1), R=[wbuf] + htb, W=[pb])
                consume(ps, pb, c, t0, n)

    def proj_tm(self, wb, wbuf, c0, ncols, tile):
        P = self.P
        ps, pb = self.ps()
        for kc in range(KC):
            P.op("pe", "matmul", ps[:, 0:ncols], self.HT[:, kc, tile * 128:(tile + 1) * 128], wb[:, kc, c0:c0 + ncols],
                start=(kc == 0), stop=(kc == KC - 1), R=[wbuf, "HT%d" % tile], W=[pb])
        return ps, pb

    def headnorm_rope(self, ps, pb, c0, nh, wtile, wbuf, tile, tmp, out_bf, outbuf, permute):
        P = self.P
        n = nh * 64
        (sq, sqb), (t1, t1b), (t2, t2b) = tmp
        P.op("act", "activation", sq[:, 0:n], ps[:, c0:c0 + n], AF.Square, R=[pb], W=[sqb])
        yield
        ssq, ssqb = self.statcol(4)
        rt, rtb = self.statcol(4)
        rr, rrb = self.statcol(4)
        P.op("dve", "tensor_reduce", ssq[:, 0:nh], sq[:, 0:n].rearrange("p (h d) -> p h d", h=nh), AX.X, ALU.add, R=[sqb], W=[ssqb])
        yield
        P.op("act", "activation", rt[:, 0:nh], ssq[:, 0:nh], AF.Sqrt, bias=64.0 * EPS, scale=1.0, R=[ssqb], W=[rtb])
        yield
        P.op("dve", "reciprocal", rr[:, 0:nh], rt[:, 0:nh], R=[rtb], W=[rrb])
        yield
        P.op("dve", "tensor_tensor", t1[:, 0:n].rearrange("p (h d) -> p h d", h=nh), ps[:, c0:c0 + n].rearrange("p (h d) -> p h d", h=nh),
            rr[:, 0:nh].unsqueeze(2).broadcast_to([128, nh, 64]), ALU.mult, R=[pb, rrb], W=[t1b])
        yield
        if permute:
            ov = out_bf[:, 0:n].rearrange("p (g kv d) -> p kv g d", g=2, kv=2)
        if tile >= NT:
            if permute:
                P.op("pool", "tensor_tensor", ov, t1[:, 0:n].rearrange("p (kv g d) -> p kv g d", kv=2, g=2),
                    wtile[:, 0:n].rearrange("p (kv g d) -> p kv g d", kv=2, g=2), ALU.mult, R=[t1b, wbuf], W=[outbuf])
                yield
            else:
                P.op("pool", "tensor_tensor", out_bf[:, 0:n], t1[:, 0:n], wtile[:, 0:n], ALU.mult, R=[t1b, wbuf], W=[outbuf])
                yield
            return
        P.op("pool", "tensor_tensor", sq[:, 0:n], t1[:, 0:n], wtile[:, 0:n], ALU.mult, R=[t1b, wbuf, sqb], W=[sqb])
        yield
        cosb = self.cos64[:, tile, :].unsqueeze(1).broadcast_to([128, nh, 64])
        sinv = self.sin64[:, tile, :].rearrange("p (t j) -> p t j", t=4)
        xv = sq[:, 0:n].rearrange("p (h t j) -> p h t j", h=nh, t=4)
        t2v = t2[:, 0:n].rearrange("p (h t j) -> p h t j", h=nh, t=4)
        P.op("dve", "tensor_tensor", t1[:, 0:n].rearrange("p (h d) -> p h d", h=nh),
                                              sq[:, 0:n].rearrange("p (h d) -> p h d", h=nh), cosb, ALU.mult, R=[sqb, "cos64", t1b], W=[t1b])
        yield
        P.op("pool", "tensor_tensor", t2v[:, :, 0::2, :], xv[:, :, 1::2, :],
                                               sinv[:, 0::2, :].unsqueeze(1).broadcast_to([128, nh, 2, 16]), ALU.mult, R=[sqb, "sin64"], W=[t2b + "a"])
        yield
        P.op("pool", "tensor_tensor", t2v[:, :, 1::2, :], xv[:, :, 0::2, :],
                                               sinv[:, 1::2, :].unsqueeze(1).broadcast_to([128, nh, 2, 16]), ALU.mult, R=[sqb, "sin64"], W=[t2b + "b"])
        yield
        if permute:
            P.op("dve", "tensor_tensor", ov, t1[:, 0:n].rearrange("p (kv g d) -> p kv g d", kv=2, g=2),
                t2[:, 0:n].rearrange("p (kv g d) -> p kv g d", kv=2, g=2), ALU.add, R=[t1b, t2b + "a", t2b + "b"], W=[outbuf])
            yield
        else:
            P.op("dve", "tensor_tensor", out_bf[:, 0:n], t1[:, 0:n], t2[:, 0:n], ALU.add, R=[t1b, t2b + "a", t2b + "b"], W=[outbuf])
            yield

    def transpose_to(self, src_bf, srcbuf, nblk, dst_fn, dstbuf_fn, eng="act"):
        P = self.P
        pt, ptb = self.pst()
        for j in range(nblk):
            P.op("pe", "transpose", pt[:, j * 128:(j + 1) * 128], src_bf[:, j * 128:(j + 1) * 128], self.ident[:], R=[srcbuf, "ident"], W=[ptb])
        yield
        for j in range(nblk):
            if eng == "act":
                P.op("act", "copy", dst_fn(j), pt[:, j * 128:(j + 1) * 128], R=[ptb], W=[dstbuf_fn(j)])
            else:
                P.op("dve", "tensor_copy", dst_fn(j), pt[:, j * 128:(j + 1) * 128], R=[ptb], W=[dstbuf_fn(j)])
        yield

    def layer(self, l, is_first, is_last, x_from_input, x_src_xs_in):
        nc, P, W = self.nc, self.P, self.W
        do_ctx_out = not is_last_layer(l)
        ntl = TT if do_ctx_out else NT
        pay, gat = self.payload[l], self.gathered[l]

        def x_tile_src(i):
            if x_from_input:
                return self.x_in[i * 128:(i + 1) * 128, :] if i < NT else self.ctx_in[(i - NT) * 128:(i - NT + 1) * 128, :]
            if x_src_xs_in:
                return self.xs_in[i * 128:(i + 1) * 128, :]
            return self.xs[i * 128:(i + 1) * 128, :]

        self.phase()
        L = "L%d" % l
        mod, modb = self.ar("mod", [2, 3 * D], F32)
        arow, arowb = self.ar("arow", [2, D], F32)
        normw2, normw2b = self.ar("normw2", [2, D], F32)
        bada2, bada2b = self.ar("bada2", [2, 3 * D], F32)
        P.dma("sp", normw2, W["norm_w"][l:l + 1, :].partition_broadcast(2), writes=[normw2b])
        P.dma("sp", bada2, W["b_ada"][l:l + 1, :].partition_broadcast(2), writes=[bada2b])
        for i, (kn, qn) in enumerate((("kn_a", "qn_a"), ("kn_d", "qn_d"))):
            for h in range(2):
                P.dma("sp", self.knw[i][:, h * 64:(h + 1) * 64], W[kn][l:l + 1, :].partition_broadcast(128), writes=["knw%d" % i])
            for h in range(4):
                P.dma("sp", self.qnw[i][:, h * 64:(h + 1) * 64], W[qn][l:l + 1, :].partition_broadcast(128), writes=["qnw%d" % i])
            P.op("act", "mul", self.knw[i][:], self.knw[i][:], 8.0, R=["knw%d" % i], W=["knw%d" % i])
        P.dma("sp", self.esink[:], W["sink_d"][l:l + 1, :].partition_broadcast(128), writes=["esink"])
        P.op("act", "activation", self.esink[:], self.esink[:], AF.Exp, R=["esink"], W=["esink"])

        for blk in range(6):
            ps, pb = self.ps()
            for half in range(2):
                i = self.wsf_i % 2
                self.wsf_i += 1
                st = self.WSF[i][:].rearrange("p (a b) -> p a b", a=KC)
                c0 = blk * 512 + half * 256
                P.dma("sp", st, self.w_cols("w_ada", l, c0, 256), writes=["wsf%d" % i])
                for kc in range(KC):
                    P.op("pe", "matmul", ps[0:2, half * 256:(half + 1) * 256], self.scT[:, kc, :], st[:, kc, :],
                        start=(kc == 0), stop=(kc == KC - 1), R=["scT", "wsf%d" % i], W=[pb])
            P.op("dve", "tensor_tensor", mod[0:2, blk * 512:(blk + 1) * 512], ps[0:2, :], bada2[0:2, blk * 512:(blk + 1) * 512], ALU.add, R=[pb, bada2b], W=[modb])
        self.tap(L + "mod", mod, [2, 3 * D], [modb])
        P.op("dve", "scalar_tensor_tensor", arow, mod[0:2, D:2 * D], 1.0, normw2, ALU.add, ALU.mult, R=[modb, normw2b], W=[arowb])
        P.op("act", "copy", self.gate2[:], mod[0:2, 2 * D:3 * D], R=[modb], W=["gate2"])
        a_bc, a_bcb = self.ar("a_bc", [128, 2, D], F32)
        sh_bc, sh_bcb = self.ar("sh_bc", [128, 2, D], F32)
        for r in range(2):
            for half in range(2):
                for which, (dst, dbuf, src, sbuf) in enumerate(((a_bc, a_bcb, arow, arowb), (sh_bc, sh_bcb, mod, modb))):
                    ps, pb = self.ps()
                    P.op("pe", "matmul", ps[:, :], self.sel2[0:2, r * 128:(r + 1) * 128], src[0:2, half * 512:(half + 1) * 512],
                        start=True, stop=True, R=["sel2", sbuf], W=[pb])
                    P.op("act", "copy", dst[:, r, half * 512:(half + 1) * 512], ps[:, :], R=[pb], W=[dbuf])

        XT = [self.ar("XT%d" % i, [128, D], F32) for i in range(3)]
        TMPF = [self.ar("TMPF%d" % i, [128, D], F32) for i in range(3)]
        HB = [self.ar("HB%d" % i, [128, D], BF) for i in range(3)]
        def a1_chain(i):
            s = i % 3
            r = 0 if i < NT else 1
            (xt, xtb), (tf, tfb), (hb, hbb) = XT[s], TMPF[s], HB[s]
            P.dma("sp", xt, x_tile_src(i), writes=[xtb])
            ssq, ssqb = self.statcol()
            rt, rtb = self.statcol()
            rs, rsb = self.statcol()
            P.op("act", "activation", tf, xt, AF.Square, accum_out=ssq, R=[xtb], W=[tfb, ssqb])
            yield
            P.op("act", "activation", rt, ssq, AF.Sqrt, bias=EPS, scale=1.0 / D, R=[ssqb], W=[rtb])
            yield
            P.op("dve", "reciprocal", rs, rt, R=[rtb], W=[rsb])
            yield
            P.op("dve", "scalar_tensor_tensor", tf, xt, rs, a_bc[:, r, :], ALU.mult, ALU.mult, R=[xtb, rsb, a_bcb, tfb], W=[tfb])
            yield
            P.op("pool", "tensor_tensor", hb, tf, sh_bc[:, r, :], ALU.add, R=[tfb, sh_bcb], W=[hbb])
            yield
            if i == 0:
                self.tap(L + "h0", hb, [128, D], [hbb], BF)
            pt, ptb = self.pst()
            for kc in range(KC):
                P.op("pe", "transpose", pt[:, kc * 128:(kc + 1) * 128], hb[:, kc * 128:(kc + 1) * 128], self.ident[:], R=[hbb, "ident"], W=[ptb])
            yield
            P.op("act", "copy", self.HT[:, :, i * 128:(i + 1) * 128], pt[:].rearrange("p (k t) -> p k t", k=KC), R=[ptb], W=["HT%d" % i])
            yield
        self.interleave((a1_chain(i) for i in range(TT)), WIDTHS[0])

        if self.stop == "A1":
            return
        self.phase()
        WB, WBb = self.ar("WBkv", [128, KC, 256], BF)
        KR = [self.ar("kr%d" % j, [128, 128], BF) for j in range(3)]
        KTs = [self.ar("KTs%d" % a, [128, TOK], BF) for a in range(2)]
        Vs = [self.ar("Vs%d" % a, [128, NT, 128], BF) for a in range(2)]
        tmps = [[self.ar("hnt%d_%d" % (k, j), [128, 256], F32) for j in range(3)] for k in range(3)]
        WBs = [(WB, WBb), self.ar("WBkv2", [128, KC, 256], BF)]
        for a, c0 in enumerate((256, 1024)):
            self.load_w(self.w_cols("w_in", l, c0, 256), WBs[a][0], WBs[a][1], (KC, 256))

        def a2_chain(a, i, slot):
            wb_, wbb_ = WBs[a]
            ps, pb = self.proj_tm(wb_, wbb_, 0, 256, i)
            yield
            kr, krb = KR[slot]
            yield from self.headnorm_rope(ps, pb, 0, 2, self.knw[a], "knw%d" % a, i, tmps[slot], kr, krb, permute=False)
            if i < NT:
                kts, ktsb = KTs[a]
                vs, vsb = Vs[a]
                P.op("act", "copy", vs[:, i, :], ps[:, 128:256], R=[pb], W=[vsb])
                yield from self.transpose_to(kr, krb, 1, lambda j: kts[:, i * 128:(i + 1) * 128], lambda j: ktsb)
            else:
                j_ = i - NT
                P.op("act", "copy", self.Vc[a][:, j_, :], ps[:, 128:256], R=[pb], W=["Vc%d" % a])
                yield from self.transpose_to(kr, krb, 1, lambda jj: self.KTc[a][:, j_ * 128:(j_ + 1) * 128], lambda jj: "KTc%d" % a)

        chains = [(a, i) for i in range(TT) for a in range(2)]
        self.interleave((a2_chain(a, i, k % 3) for k, (a, i) in enumerate(chains)), WIDTHS[1])
        for a in range(2):
            kts, ktsb = KTs[a]
            vs, vsb = Vs[a]
            P.dma("sp", pay[0].ap()[a * 128:(a + 1) * 128, :], kts, reads=[ktsb], writes=["pay_k%d" % a])
            P.dma("sp", pay[3].ap()[a * 128:(a + 1) * 128, :].rearrange("p (t d) -> p t d", d=128), vs, reads=[vsb],
                  writes=["pay_v%d" % a])
        if l == self.layers[0]:
            self.tap(L + "KTsA", KTs[0][0], [128, TOK], [KTs[0][1]], BF)
            self.tap(L + "VsA", Vs[0][0], [128, NT, 128], [Vs[0][1]], BF)
            self.tap(L + "KTcA", self.KTc[0][:], [128, 256], ["KTc0"], BF)

        self.load_w(self.w_cols("w_in", l, 1536, 256), WB, WBb, (KC, 256))
        BFT, BFTb = self.ar("BFT", [128, 2, TALL], BF)

        def bf_consume(ps, pb, c, t0, n):
            P.op("act", "copy", BFT[:, c, t0:t0 + n], ps[:, 0:n], R=[pb], W=[BFTb])
        self.proj_fm(WB, WBb, 0, 2, TT, bf_consume)
        wf, wfb = self.ar("wf", [64, 4, 64], F32)
        P.dma("sp", wf, W["w_fnet"][l].rearrange("g j c -> j g c"), writes=[wfb])
        P.op("pool", "memset", self.BA[:], 0.0, R=[], W=["BA"])
        for cs, tab in enumerate((self.c64d, self.s64d)):
            for g in range(4):
                ps, pb = self.ps()
                P.op("pe", "matmul", ps[:, 0:64], tab[0:64, :], wf[0:64, g, :], start=True, stop=True, R=[wfb, "c64d", "s64d"], W=[pb])
                rows = slice((g % 2) * 64, (g % 2) * 64 + 64)
                P.op("dve", "tensor_copy", self.BA[rows, cs, g // 2, (g % 2) * 64:(g % 2) * 64 + 64], ps[rows, 0:64], R=[pb, "BA"], W=["BA"])
        UVT, UVTb = self.ar("UVT", [128, 2, 2, TOK], BF)
        for cs in range(2):
            for c in range(2):
                for (t0, n) in GROUPS[:4]:
                    ps, pb = self.ps()
                    P.op("pe", "matmul", ps[:, 0:n], self.BA[:, cs, c, :], BFT[:, c, t0:t0 + n], start=True, stop=True, R=["BA", BFTb], W=[pb])
                    P.op("act", "copy", UVT[:, cs, c, t0:t0 + n], ps[:, 0:n], R=[pb], W=[UVTb])
                for j in range(2):
                    ps, pb = self.ps()
                    P.op("pe", "matmul", ps[:, 0:128], BFT[:, c, TOK + j * 128:TOK + (j + 1) * 128], self.BA[:, cs, c, :], start=True, stop=True, R=["BA", BFTb], W=[pb])
                    P.op("dve", "tensor_copy", self.UVc[:, cs, j, c * 128:(c + 1) * 128], ps[:, 0:128], R=[pb], W=["UVc"])
        for cs in range(2):
            P.dma("sp", pay[1 + cs].ap().rearrange("(c p) t -> p c t", p=128), UVT[:, cs, :, :],
                  reads=[UVTb], writes=["pay_uv%d" % cs])
        self.tap(L + "UVT", UVT, [128, 2, 2, TOK], [UVTb], BF)

        if self.stop == "A":
            return
        payb = [["pay_k0", "pay_k1"], ["pay_uv0"], ["pay_uv1"], ["pay_v0", "pay_v1"]]
        for j in (0, 3, 1, 2):
            P.collective(lambda e, j=j: e.collective_compute("AllGather", ALU.bypass, replica_groups=[[0, 1, 2, 3], [4, 5, 6, 7]],
                                                             ins=[pay[j].ap().opt()], outs=[gat[j].ap().opt()]),
                         reads=payb[j], writes=["gat%d" % j])
        G = [g.ap() for g in gat]

        if self.stop == "G":
            return
        self.phase()
        self.sgu(l, ntl)
        self.phase()
        self.attention(l, 0, ntl, G, pay)
        self.phase()
        self.attention(l, 1, ntl, G, pay)
        self.phase()
        self.fnet(l, ntl, G)
        if l == self.layers[0]:
            for r in range(4):
                self.tap(L + "YB%d" % r, self.YB[r][:], [128, 2, TALL], ["YB%d_%d_%d" % (r, c, i) for c in range(2) for i in range(TT)], BF)
        if self.stop == "B5":
            return
        self.phase()
        self.merge(l, ntl, is_last, x_tile_src)

    def attention(self, l, which, ntl, G, pay):
        P, W = self.P, self.W
        q0 = 0 if which == 0 else 768
        z0 = 512 if which == 0 else 1280
        YB = self.YB[which]
        nqg = [g for g in GROUPS if g[0] // 128 < ntl]
        WQ, WQb = self.ar("WQ", [128, KC, 256], BF)
        WZ, WZb = self.ar("WZ", [128, KC, 256], BF)
        self.load_w(self.w_cols("w_in", l, q0, 256), WQ, WQb, (KC, 256))
        self.load_w(self.w_cols("w_in", l, z0, 256), WZ, WZb, (KC, 256))
        QT = [self.ar("QT%d" % g, [128, TALL], BF) for g in range(2)]
        def vview(V, k0, k1):
            return V[:, k0:k1, :].rearrange("p t (kv x) -> p t kv x", kv=2)[:, :, :, 0:64]

        if which == 0:
            NKB = 2 + 64
            KT, KTb = self.ar("KT", [128, NKB * 128], BF)
            V, Vb = self.ar("V", [128, NKB, 256], BF)
            P.op("pool", "memset", V[:, :, :].rearrange("p t (kv x) -> p t kv x", kv=2)[:, :, :, 64:128], 1.0, R=[], W=[Vb + "ones"])
            P.op("act", "copy", KT[:, 0:256], self.KTc[0][:], R=["KTc0"], W=[KTb + "c"])
            P.op("act", "copy", vview(V, 0, 2), self.Vc[0][:].rearrange("p t (kv d) -> p t kv d", kv=2), R=["Vc0"], W=[Vb + "c"])
            for s_ in range(4):
                P.dma("sp", KT[:, 256 + s_ * TOK:256 + (s_ + 1) * TOK], G[0][s_ * 256:s_ * 256 + 128, :], reads=["gat0"], writes=[KTb + str(s_)])
                for hf in range(2):
                    P.dma("sp", vview(V, 2 + s_ * NT + hf * 8, 2 + s_ * NT + hf * 8 + 8),
                          G[3][s_ * 256:s_ * 256 + 128, hf * 1024:(hf + 1) * 1024].rearrange("p (t kv d) -> p t kv d", kv=2, d=64),
                          reads=["gat3"], writes=[Vb + str(s_)])

            def kbufs(kb):
                return [KTb + ("c" if kb < 2 else str((kb - 2) // NT))], [Vb + ("c" if kb < 2 else str((kb - 2) // NT)), Vb + "ones"]
        else:
            NKB = 2 + NT + 8
            KT, KTb = self.ar("KT", [128, NKB * 128], BF)
            V, Vb = self.ar("V", [128, NKB, 256], BF)
            P.op("pool", "memset", V[:, :, :].rearrange("p t (kv x) -> p t kv x", kv=2)[:, :, :, 64:128], 1.0, R=[], W=[Vb + "ones"])
            P.op("act", "copy", KT[:, 0:256], self.KTc[1][:], R=["KTc1"], W=[KTb])
            P.op("act", "copy", vview(V, 0, 2), self.Vc[1][:].rearrange("p t (kv d) -> p t kv d", kv=2), R=["Vc1"], W=[Vb])
            P.dma("sp", KT[:, 256:256 + TOK], pay[0].ap()[128:256, :], reads=["pay_k1"], writes=[KTb])
            for hf in range(2):
                P.dma("sp", vview(V, 2 + hf * 8, 2 + hf * 8 + 8),
                      pay[3].ap()[128:256, hf * 1024:(hf + 1) * 1024].rearrange("p (t kv d) -> p t kv d", kv=2, d=64),
                      reads=["pay_v1"], writes=[Vb])
            for s_ in range(4):
                for kb, c0 in ((2 + NT + s_, TOK - 128), (2 + NT + 4 + s_, 0)):
                    P.dma("sp", KT[:, kb * 128:(kb + 1) * 128], G[0][s_ * 256 + 128:s_ * 256 + 256, c0:c0 + 128], reads=["gat0"], writes=[KTb])
                    P.dma("sp", vview(V, kb, kb + 1),
                          G[3][s_ * 256 + 128:s_ * 256 + 256, c0:c0 + 128].rearrange("p (t kv d) -> p t kv d", kv=2, d=64),
                          reads=["gat3"], writes=[Vb])

            def kbufs(kb):
                return [KTb], [Vb, Vb + "ones"]

        tmps = [[self.ar("hnt%d_%d" % (k, j), [128, 256], F32) for j in range(3)] for k in range(3)]
        QR = [self.ar("QR%d" % j, [128, 256], BF) for j in range(3)]
        def q_chain(i, slot):
            ps, pb = self.proj_tm(WQ, WQb, 0, 256, i)
            yield
            qr, qrb = QR[slot]
            yield from self.headnorm_rope(ps, pb, 0, 4, self.qnw[which], "qnw%d" % which, i, tmps[slot], qr, qrb, permute=True)
            yield from self.transpose_to(qr, qrb, 2, lambda j: QT[j][0][:, i * 128:(i + 1) * 128], lambda j: QT[j][1] + "_%d" % i)
        self.interleave((q_chain(i, i % 3) for i in range(ntl)), WIDTHS[2])

        def z_consume(ps, pb, c, t0, n):
            P.op("act", "activation", YB[:, c, t0:t0 + n], ps[:, 0:n], AF.Silu, R=[pb], W=["YB%d_%d_%d" % (which, c, i) for i in range(t0 // 128, (t0 + n) // 128)])
        self.proj_fm(WZ, WZb, 0, 2, ntl, z_consume)
        if which == 0 and l == self.layers[0]:
            self.tap("L%dQT0" % l, QT[0][0], [128, TALL], [QT[0][1] + "_%d" % i for i in range(ntl)], BF)

        PT = [self.ar("PT%d" % j, [128, 1024], BF) for j in range(2)]
        fin = [self.ar("fin%d" % j, [128, 512 if which == 0 else 128], F32) for j in range(3 if which == 0 else 6)]

        work = []

        def add_pair(g, t0, n, keylist):
            cw = 512 // n
            chunks = [keylist[c:c + cw] for c in range(0, len(keylist), cw)]
            for ci, ch in enumerate(chunks):
                work.append(dict(g=g, t0=t0, n=n, keys=ch, first=(ci == 0), last=(ci == len(chunks) - 1)))

        if which == 0:
            for (t0, n) in nqg:
                keys = [(kb, None) for kb in range(NKB)] if t0 < TOK else [(0, None), (1, None)]
                for g in range(2):
                    add_pair(g, t0, n, keys)
        else:
            for i in range(ntl):
                if i >= NT:
                    keys = [(0, None), (1, None)]
                else:
                    keys = [(0, None), (1, None), (2 + i, None)]
                    if i > 0:
                        keys.append((2 + i - 1, 0))
                    else:
                        keys += [(2 + NT + s_, 2 + s_) for s_ in range(4)]
                    if i < NT - 1:
                        keys.append((2 + i + 1, 1))
                    else:
                        keys += [(2 + NT + 4 + s_, 6 + s_) for s_ in range(4)]
                for g in range(2):
                    add_pair(g, i * 128, 128, keys)

        accs = [(self.PS[4], "ps4"), (self.PS[5], "ps5")]

        def stage1(w, slot):
            g, t0, n = w["g"], w["t0"], w["n"]
            qt, qtb = QT[g]
            qbufs = [qtb + "_%d" % i for i in range(t0 // 128, (t0 + n) // 128)]
            psw = self.PSW[slot % 2]
            pbs = ["ps%d" % (2 * (slot % 2)), "ps%d" % (2 * (slot % 2) + 1)]
            pt, ptb = PT[slot % 2]
            cnt = len(w["keys"])
            for j, (kb, mi) in enumerate(w["keys"]):
                kb_k, _ = kbufs(kb)
                for kv in range(2):
                    krows = slice(kv * 64, (kv + 1) * 64)
                    P.op("pe", "matmul", psw[:, kv * 512 + j * n:kv * 512 + (j + 1) * n], KT[krows, kb * 128:(kb + 1) * 128],
                         qt[krows, t0:t0 + n], start=True, stop=True, R=kb_k + qbufs, W=[pbs[kv]])
            P.op("act", "activation", pt[:, :].rearrange("p (a b) -> p a b", a=2)[:, :, 0:cnt * n],
                 psw[:, :].rearrange("p (a b) -> p a b", a=2)[:, :, 0:cnt * n], AF.Exp, R=pbs, W=[ptb])
            for j, (kb, mi) in enumerate(w["keys"]):
                if mi is not None:
                    P.op("pool", "tensor_tensor", pt[:, :].rearrange("p (a b) -> p a b", a=2)[:, :, j * n:(j + 1) * n],
                         pt[:, :].rearrange("p (a b) -> p a b", a=2)[:, :, j * n:(j + 1) * n],
                         self.masks[:, mi, 0:n].unsqueeze(1).broadcast_to([128, 2, n]), ALU.mult, R=[ptb, "masks"], W=[ptb])

        def stage2(w, slot):
            g, t0, n = w["g"], w["t0"], w["n"]
            rows = slice(g * 64, (g + 1) * 64)
            pt, ptb = PT[slot % 2]
            cnt = len(w["keys"])
            for j, (kb, mi) in enumerate(w["keys"]):
                _, kb_v = kbufs(kb)
                st = w["first"] and j == 0
                sp = w["last"] and j == cnt - 1
                for kv in range(2):
                    acc, accb = accs[kv]
                    P.op("pe", "matmul", acc[:, 0:n], V[:, kb, kv * 128:(kv + 1) * 128], pt[:, kv * 512 + j * n:kv * 512 + (j + 1) * n],
                         start=st, stop=sp, R=kb_v + [ptb], W=[accb])
            if not w["last"]:
                return
            for kv in range(2):
                acc, accb = accs[kv]
                h = 2 * kv + g
                fo = 0 if which == 0 else 3 * kv
                (f0, f0b), (f1, f1b), (f2, f2b) = fin[fo:fo + 3]
                if which == 1:
                    P.op("dve", "tensor_scalar", f2[64:128, 0:n], acc[64:128, 0:n], self.esink[64:128, h:h + 1], None, ALU.add,
                         R=[accb, "esink"], W=[f2b])
                    P.op("dve", "reciprocal", f0[rows, 0:n], f2[64:128, 0:n], R=[f2b], W=[f0b])
                else:
                    P.op("dve", "reciprocal", f0[rows, 0:n], acc[64:128, 0:n], R=[accb], W=[f0b])
                if which == 1:
                    P.op("act", "copy", f1[rows, 0:n], acc[0:64, 0:n], R=[accb], W=[f1b])
                    P.op("dve", "tensor_tensor", f1[rows, 0:n], f1[rows, 0:n], f0[rows, 0:n], ALU.mult, R=[f1b, f0b], W=[f1b])
                else:
                    P.op("dve", "tensor_copy", f1[rows, 0:n], acc[0:64, 0:n], R=[accb], W=[f1b])
                    P.op("pool", "tensor_tensor", f1[rows, 0:n], f1[rows, 0:n], f0[rows, 0:n], ALU.mult, R=[f1b, f0b], W=[f1b])
                ybb = ["YB%d_%d_%d" % (which, kv, i) for i in range(t0 // 128, (t0 + n) // 128)]
                P.op("pool", "tensor_tensor", YB[rows, kv, t0:t0 + n], f1[rows, 0:n], YB[rows, kv, t0:t0 + n], ALU.mult,
                     R=[f1b] + ybb, W=ybb)

        LOOK = 1
        for k in range(len(work) + LOOK):
            if k < len(work):
                stage1(work[k], k)
            if k >= LOOK:
                stage2(work[k - LOOK], k - LOOK)

    def fnet(self, l, ntl, G):
        P, W = self.P, self.W
        YB = self.YB[2]
        WZ, WZb = self.ar("WZ", [128, KC, 256], BF)
        self.load_w(self.w_cols("w_in", l, 1792, 256), WZ, WZb, (KC, 256))

        def z_consume(ps, pb, c, t0, n):
            P.op("act", "activation", YB[:, c, t0:t0 + n], ps[:, 0:n], AF.Silu, R=[pb], W=["YB2_%d_%d" % (c, i) for i in range(t0 // 128, (t0 + n) // 128)])
        self.proj_fm(WZ, WZb, 0, 2, ntl, z_consume)
        MC, MCb = self.ar("MC", [128, 64, 32], BF)
        MS, MSb = self.ar("MS", [128, 64, 32], BF)
        self.load_w(self.C["mc"].rearrange("p (a b) -> p a b", a=64), MC, MCb, (64, 32))
        self.load_w(self.C["ms"].rearrange("p (a b) -> p a b", a=64), MS, MSb, (64, 32))
        Z = [self.ar("Z%d" % j, [128, 64, 128], BF) for j in range(2)]
        T3, T3b = self.ar("T3", [128, 64, 128], BF)
        sp = 0
        for c in range(2):
            for hf in range(2):
                z, zb = Z[sp % 2]
                sp += 1
                for s in range(4):
                    for ri in range(2):
                        r0 = s * 256 + c * 128 + hf * 64
                        P.dma("sp", z[ri * 64 + s * 16:ri * 64 + (s + 1) * 16, :, :],
                              G[1 + ri][r0:r0 + 64, :].rearrange("ch (t n) -> t ch n", n=128), reads=["gat%d" % (1 + ri)], writes=[zb])
                for ch4 in range(16):
                    ps, pb = self.ps()
                    for j in range(4):
                        ch = ch4 * 4 + j
                        P.op("pe", "matmul", ps[:, j * 128:(j + 1) * 128], z[:, ch, :], self.m1[:, :],
                                                                            start=True, stop=True, R=[zb, "m1"], W=[pb])
                    if ch4 % 2 == 0:
                        P.op("act", "copy", T3[:, ch4 * 4:ch4 * 4 + 4, :], ps[:].rearrange("p (a b) -> p a b", a=4), R=[pb], W=[T3b])
                    else:
                        P.op("dve", "tensor_copy", T3[:, ch4 * 4:ch4 * 4 + 4, :], ps[:].rearrange("p (a b) -> p a b", a=4), R=[pb], W=[T3b])
                rows = slice(hf * 64, (hf + 1) * 64)
                for bank in range(4):
                    ps, pb = self.ps()
                    for kk in range(16):
                        k1 = bank * 16 + kk
                        P.op("pe", "matmul", ps[rows, kk * 32:(kk + 1) * 32], T3[:, :, k1], MC[:, k1, :],
                                                                         start=True, stop=False, tile_position=(0, hf * 64), R=[T3b, MCb], W=[pb])
                        P.op("pe", "matmul", ps[rows, kk * 32:(kk + 1) * 32], T3[:, :, 64 + k1], MS[:, k1, :],
                                                                         start=False, stop=True, tile_position=(0, hf * 64), R=[T3b, MSb], W=[pb])
                    yv = YB[rows, c, 0:TOK].rearrange("p (k2 k1) -> p k1 k2", k1=64)[:, bank * 16:(bank + 1) * 16, :]
                    ybb = ["YB2_%d_%d" % (c, i) for i in range(NT)]
                    P.op("dve", "tensor_tensor", yv, ps[rows, :].rearrange("p (a b) -> p a b", a=16), yv, ALU.mult, R=[pb] + ybb, W=ybb)
        if ntl > NT:
            for c in range(2):
                ps, pb = self.ps()
                k = 0
                for uv, tab in ((0, self.c256), (1, self.s256n)):
                    for j in range(2):
                        P.op("pe", "matmul", ps[:, 0:256], self.UVc[:, uv, j, c * 128:(c + 1) * 128], tab[:, j, :], start=(k == 0), stop=(k == 3), R=["UVc", "c256", "s256n"], W=[pb])
                        k += 1
                ybb = ["YB2_%d_%d" % (c, i) for i in range(NT, TT)]
                P.op("dve", "tensor_tensor", YB[:, c, TOK:TALL], ps[:, 0:256], YB[:, c, TOK:TALL], ALU.mult, R=[pb] + ybb, W=ybb)

    def sgu(self, l, ntl):
        P, W = self.P, self.W
        YB = self.YB[3]
        WU, WUb = self.ar("WU", [128, KC, 256], BF)
        WV, WVb = self.ar("WV", [128, KC, 256], BF)
        WZ, WZb = self.ar("WZ", [128, KC, 256], BF)
        self.load_w(self.w_cols("w_in", l, 2048, 256), WU, WUb, (KC, 256))
        self.load_w(self.w_cols("w_in", l, 2304, 256), WV, WVb, (KC, 256))
        self.load_w(self.w_cols("w_in", l, 2560, 256), WZ, WZb, (KC, 256))
        GU, GUb = self.ar("GU", [128, 2, TALL], BF)
        GV, GVb = self.ar("GV", [128, TT, 256], BF)
        wsp, wspb = self.ar("wsp", [128, 4, 128], BF)
        bsp, bspb = self.ar("bsp", [128, 2, 128], F32)
        self.load_w(W["w_spT"][l], wsp, wspb, (4, 128))
        for g in range(4):
            P.dma("sp", bsp[(g % 2) * 64:(g % 2) * 64 + 64, g // 2, :], W["b_sp"][l, g:g + 1, :].partition_broadcast(64), writes=[bspb])

        def u_consume(ps, pb, c, t0, n):
            P.op("act", "activation", GU[:, c, t0:t0 + n], ps[:, 0:n], AF.Gelu, R=[pb], W=[GUb])

        def z_consume(ps, pb, c, t0, n):
            P.op("act", "activation", YB[:, c, t0:t0 + n], ps[:, 0:n], AF.Silu, R=[pb], W=["YB3_%d_%d" % (c, i) for i in range(t0 // 128, (t0 + n) // 128)])
        self.proj_fm(WU, WUb, 0, 2, ntl, u_consume)
        self.proj_fm(WZ, WZb, 0, 2, ntl, z_consume)
        for i in range(ntl):
            ps, pb = self.proj_tm(WV, WVb, 0, 256, i)
            P.op("act", "activation", GV[:, i, :], ps[:, 0:256], AF.Gelu, R=[pb], W=[GVb + "_%d" % i])
        t1, t1b = self.ar("sg1", [128, 512], F32)
        t2, t2b = self.ar("sg2", [128, 512], F32)
        for i0 in range(0, ntl, 4):
            nt = min(4, ntl - i0)
            n = nt * 128
            for c in range(2):
                ps, pb = self.ps()
                for j in range(nt):
                    i = i0 + j
                    for gg in range(2):
                        g = 2 * c + gg
                        P.op("pe", "matmul", ps[gg * 64:(gg + 1) * 64, j * 128:(j + 1) * 128], GV[:, i, g * 64:(g + 1) * 64], wsp[:, g, :],
                            start=True, stop=True, tile_position=(0, gg * 64), R=[GVb + "_%d" % i, wspb], W=[pb])
                P.op("dve", "tensor_tensor", t1[:, 0:nt * 128].rearrange("p (a b) -> p a b", a=nt), ps[:, 0:nt * 128].rearrange("p (a b) -> p a b", a=nt),
                    bsp[:, c, :].unsqueeze(1).broadcast_to([128, nt, 128]), ALU.add, R=[pb, bspb, t1b], W=[t1b])
                P.op("pool", "tensor_tensor", t2[:, 0:n], t1[:, 0:n], GU[:, c, i0 * 128:i0 * 128 + n], ALU.mult, R=[t1b, GUb, t2b], W=[t2b])
                ybb = ["YB3_%d_%d" % (c, i) for i in range(i0, i0 + nt)]
                P.op("dve", "tensor_tensor", YB[:, c, i0 * 128:i0 * 128 + n], t2[:, 0:n], YB[:, c, i0 * 128:i0 * 128 + n], ALU.mult, R=[t2b] + ybb, W=ybb)

    def merge(self, l, ntl, is_last, x_tile_src):
        P, W = self.P, self.W
        groups = [g for g in GROUPS if g[0] // 128 < ntl]
        MT, MTb = self.ar("MT", [128, KC, TALL], BF)
        WBR = [self.ar("WBR%d" % r, [128, 2, D], BF) for r in range(4)]
        for r in range(4):
            self.load_w(W["w_br"][l, r].rearrange("(kc p) n -> p kc n", p=128), WBR[r][0], WBR[r][1], (2, D))
        bm, bmb = self.ar("bm", [128, 32], F32)
        P.dma("sp", bm, W["b_mergeT"][l], writes=[bmb])
        WM = [self.ar("WM%d" % j, [128, 4, KC, 128], BF) for j in range(2)]
        sg = [self.ar("sg%d" % j, [128, 512], F32) for j in range(2)]
        acc = [self.ar("acc%d" % j, [128, 512], F32) for j in range(2)]
        tmpm = [self.ar("tmpm%d" % j, [128, 512], F32) for j in range(2)]
        cnt = 0
        for dc in range(KC):
            wm, wmb = WM[dc % 2]
            for r in range(4):
                i = self.wsf_i % 2
                self.wsf_i += 1
                st = self.WSF[i][:, 0:1024].rearrange("p (a b) -> p a b", a=KC)
                P.dma("sp", st, self.w_cols("w_merge", l, r * D + dc * 128, 128), writes=["wsf%d" % i])
                P.op("dve", "tensor_copy", wm[:, r, :, :], st, R=["wsf%d" % i], W=[wmb + "_%d" % r])
            for (t0, n) in groups:
                htb = ["HT%d" % i for i in range(t0 // 128, (t0 + n) // 128)]
                a_, ab = acc[cnt % 2]
                for r in range(4):
                    psg, pgb = self.ps()
                    for kc in range(KC):
                        P.op("pe", "matmul", psg[:, 0:n], wm[:, r, kc, :], self.HT[:, kc, t0:t0 + n], start=(kc == 0), stop=(kc == KC - 1), R=[wmb + "_%d" % r] + htb, W=[pgb])
                    psy, pyb = self.ps()
                    ybb = ["YB%d_%d_%d" % (r, c, i) for c in range(2) for i in range(t0 // 128, (t0 + n) // 128)]
                    for k2 in range(2):
                        P.op("pe", "matmul", psy[:, 0:n], WBR[r][0][:, k2, dc * 128:(dc + 1) * 128], self.YB[r][:, k2, t0:t0 + n],
                            start=(k2 == 0), stop=(k2 == 1), R=[WBR[r][1]] + ybb, W=[pyb])
                    s_, sb_ = sg[(cnt * 4 + r) % 2]
                    P.op("act", "activation", s_[:, 0:n], psg[:, 0:n], AF.Sigmoid,
                                                                         bias=bm[:, r * 8 + dc:r * 8 + dc + 1], scale=1.0, R=[pgb, bmb], W=[sb_])
                    if r == 0:
                        P.op("dve", "tensor_tensor", a_[:, 0:n], psy[:, 0:n], s_[:, 0:n], ALU.mult, R=[pyb, sb_, ab], W=[ab])
                    else:
                        t_, tb_ = tmpm[r % 2]
                        P.op("dve", "tensor_tensor", t_[:, 0:n], psy[:, 0:n], s_[:, 0:n], ALU.mult, R=[pyb, sb_, tb_], W=[tb_])
                        if r < 3:
                            P.op("pool", "tensor_tensor", a_[:, 0:n], a_[:, 0:n], t_[:, 0:n], ALU.add, R=[ab, tb_], W=[ab])
                        else:
                            P.op("pool", "tensor_tensor", MT[:, dc, t0:t0 + n], a_[:, 0:n], t_[:, 0:n], ALU.add, R=[ab, tb_], W=[MTb + "_%d_%d" % (dc, t0)])
                cnt += 1
        self.tap("L%dMT" % l, MT, [128, KC, TALL], [MTb + "_%d_%d" % (dc, g[0]) for dc in range(KC) for g in groups], BF)
        if self.stop == "C2":
            return
        self.P.barrier()
        self.ar_off = (KC * TALL * 2 + 63) // 64 * 64
        self.ar_gen += 1
        WO, WOb = self.ar("WO", [128, KC, D], BF)
        for j in range(4):
            self.load_w(self.w_cols("w_out", l, j * 256, 256), WO[:, :, j * 256:(j + 1) * 256], WOb + "_%d" % j, (KC, 256))
        gt_bc, gtb = self.ar("gt_bc", [128, 2, D], F32)
        for r in range(2 if ntl > NT else 1):
            for half in range(2):
                ps, pb = self.ps()
                P.op("pe", "matmul", ps[:, :], self.sel2[0:2, r * 128:(r + 1) * 128], self.gate2[0:2, half * 512:(half + 1) * 512],
                    start=True, stop=True, R=["sel2", "gate2"], W=[pb])
                P.op("act", "copy", gt_bc[:, r, half * 512:(half + 1) * 512], ps[:, :], R=[pb], W=[gtb])
        XT = [self.ar("XT%d" % i, [128, D], F32) for i in range(2)]
        XO = [self.ar("XO%d" % i, [128, D], F32) for i in range(2)]
        xs_reads = (lambda i: []) if (l == 0) else (lambda i: ["xs%d" % i])
        P.dma("sp", XT[0][0], x_tile_src(0), reads=xs_reads(0), writes=[XT[0][1]])
        for i in range(ntl):
            r = 0 if i < NT else 1
            xt, xtb = XT[i % 2]
            xo, xob = XO[i % 2]
            if i + 1 < ntl:
                P.dma("sp", XT[(i + 1) % 2][0], x_tile_src(i + 1), reads=xs_reads(i + 1), writes=[XT[(i + 1) % 2][1]])
            for half in range(2):
                ps, pb = self.ps()
                mtb = [MTb + "_%d_%d" % (dc, g[0]) for dc in range(KC) for g in GROUPS if g[0] <= i * 128 < g[0] + g[1]]
                for kc in range(KC):
                    P.op("pe", "matmul", ps[:, :], MT[:, kc, i * 128:(i + 1) * 128], WO[:, kc, half * 512:(half + 1) * 512],
                        start=(kc == 0), stop=(kc == KC - 1), R=mtb + [WOb + "_%d" % (2 * half), WOb + "_%d" % (2 * half + 1)], W=[pb])
                P.op("dve", "tensor_tensor", xo[:, half * 512:(half + 1) * 512], ps[:, :], gt_bc[:, r, half * 512:(half + 1) * 512], ALU.mult, R=[pb, gtb, xob], W=[xob + "h%d" % half])
                P.op("pool", "tensor_tensor", xo[:, half * 512:(half + 1) * 512], xo[:, half * 512:(half + 1) * 512], xt[:, half * 512:(half + 1) * 512], ALU.add, R=[xob + "h%d" % half, xtb], W=[xob + "h%d" % half])
            if is_last:
                dst = self.out[i * 128:(i + 1) * 128, :]
            elif self.last or l != self.layers[-1]:
                dst = self.xs[i * 128:(i + 1) * 128, :]
            else:
                dst = self.xs_out[i * 128:(i + 1) * 128, :]
            P.dma("sp", dst, xo, reads=[xob + "h0", xob + "h1"], writes=["xs%d" % i] if not is_last else [])


def is_last_layer(l):
    return l == 1


_CONST_CACHE = {}


def _consts(core):
    if core not in _CONST_CACHE:
        _CONST_CACHE[core] = host_consts(core)
    return _CONST_CACHE[core]


def _weights_layout(inp):
    f = lambda a: np.ascontiguousarray(np.asarray(a, dtype=np.float32))
    w = {k: f(inp[k]) for k in ("norm_w", "w_ada", "b_ada", "w_in", "qn_a", "kn_a", "qn_d", "kn_d", "sink_d",
                                "w_fnet", "b_sp", "w_br", "w_merge", "w_out")}
    w["w_spT"] = np.ascontiguousarray(f(inp["w_sp"]).transpose(0, 3, 1, 2))
    w["b_mergeT"] = np.ascontiguousarray(f(inp["b_merge"]).reshape(2, 32, 128).transpose(0, 2, 1))
    return w


def _run(builder_kwargs, per_core_extra, inp):
    b = Builder(**builder_kwargs)
    nc = b.build()
    w = _weights_layout(inp)
    c = np.asarray(inp["c"], np.float32)
    c_ctx = np.asarray(inp["c_ctx"], np.float32)
    in_maps = []
    for core in range(NCORES):
        bi = core // 4
        cv = np.stack([c[bi], c_ctx], 0)
        cvT = np.ascontiguousarray(cv.reshape(2, KC, 128).transpose(2, 1, 0))
        m = {"cvecT": cvT}
        m.update(w)
        m.update(_consts(core))
        m.update(per_core_extra(core))
        in_maps.append(m)
    res = run_bass_kernel_spmd(nc, in_maps, core_ids=list(range(NCORES)))
    return res.results, b


def kernel(**inp):
    x = np.asarray(inp["x"], np.float32)
    ctx = np.asarray(inp["ctx"], np.float32)

    def extra(core):
        bi, q = divmod(core, 4)
        return {"x": np.ascontiguousarray(x[bi, q * TOK:(q + 1) * TOK]), "ctx": np.ascontiguousarray(ctx[bi])}
    results, _ = _run(dict(layers=(0, 1), first=True, last=True), extra, inp)
    out = np.empty((2, SEQ, D), np.float32)
    for core in range(NCORES):
        bi, q = divmod(core, 4)
        out[bi, q * TOK:(q + 1) * TOK] = results[core]["out"]
    return out
```
